# Optimizing a Trainium2 kernel written in Bass

```python
import jax, jax.numpy as jnp
from jax import lax
import numpy as np

D_MODEL = 1024
BATCH = 4
SEQ = 4096
DEPTH = 1

SSD_HEADS = 8
SSD_HEAD_DIM = 64
SSD_D = SSD_HEADS * SSD_HEAD_DIM
SSD_GROUPS = 2
SSD_STATE = 128
SSD_CONV = 4
SSD_CHUNK = 128
SSD_CONV_DIM = SSD_D + 2 * SSD_GROUPS * SSD_STATE
SSD_COLS = SSD_D + SSD_CONV_DIM + SSD_HEADS

RWKV_HEADS = 8
RWKV_HEAD_DIM = 64
RWKV_D = RWKV_HEADS * RWKV_HEAD_DIM
DECAY_LORA = 64
AAA_LORA = 64
GATE_LORA = 128
RWKV_COLS = 3 * RWKV_D + DECAY_LORA + AAA_LORA + GATE_LORA
RWKV_GN_EPS = 64e-5

D_MIX = SSD_D + RWKV_D
D_IN = SSD_COLS + RWKV_COLS

D_FF = 2816
FFN_CONV = 3
NORM_EPS = 1e-6

kernel_name = "hymba_ssd_rwkv7_sandwich_convglu"


def _rms_norm(x, g):
    xf = x.astype(jnp.float32)
    y = xf * lax.rsqrt(jnp.mean(xf * xf, axis=-1, keepdims=True) + NORM_EPS)
    return (y * g.astype(jnp.float32)).astype(x.dtype)


def _causal_dwconv(x, w, b):
    K, C = w.shape
    y = lax.conv_general_dilated(
        x, w[:, None, :].astype(x.dtype), window_strides=(1,), padding=[(K - 1, 0)],
        dimension_numbers=("NWC", "WIO", "NWC"), feature_group_count=C)
    return y + b.astype(x.dtype)


def _token_shift(x):
    return jnp.pad(x, ((0, 0), (1, 0), (0, 0)))[:, :-1]


def _segsum(a):
    cs = jnp.cumsum(a, axis=-1)
    diff = cs[..., :, None] - cs[..., None, :]
    L = a.shape[-1]
    mask = jnp.tril(jnp.ones((L, L), dtype=bool))
    return jnp.where(mask, diff, -jnp.inf)


def _ssd_chunked(xh, dt, A, Bg, Cg):
    b, T, h, p = xh.shape
    g, n = Bg.shape[2], Bg.shape[3]
    c, L = T // SSD_CHUNK, SSD_CHUNK
    Bh = jnp.repeat(Bg, h // g, axis=2).reshape(b, c, L, h, n)
    Ch = jnp.repeat(Cg, h // g, axis=2).reshape(b, c, L, h, n)
    X = (xh * dt[..., None]).reshape(b, c, L, h, p)
    a = (dt * A).reshape(b, c, L, h).transpose(0, 3, 1, 2)
    a_cs = jnp.cumsum(a, axis=-1)
    decay_in = jnp.exp(_segsum(a))
    scores = jnp.einsum("bclhn,bcshn->bhcls", Ch, Bh)
    y_diag = jnp.einsum("bhcls,bcshp->bclhp", scores * decay_in, X)
    decay_states = jnp.exp(a_cs[..., -1:] - a_cs)
    states = jnp.einsum("bclhn,bhcl,bclhp->bchpn", Bh, decay_states, X)
    chunk_decay = jnp.pad(a_cs[..., -1], ((0, 0), (0, 0), (1, 0)))
    decay_chunk = jnp.exp(_segsum(chunk_decay))
    states_p = jnp.concatenate([jnp.zeros_like(states[:, :1]), states], axis=1)
    states_in = jnp.einsum("bhzc,bchpn->bzhpn", decay_chunk, states_p)[:, :-1]
    y_off = jnp.einsum("bclhn,bchpn,bhcl->bclhp", Ch, states_in, jnp.exp(a_cs))
    return (y_diag + y_off).reshape(b, T, h, p)


def _ssd_mixer(p, conv_w, conv_b, dt_bias, a_log, d_skip, norm_g):
    b, T, _ = p.shape
    z, xbc, dt = jnp.split(p, [SSD_D, SSD_D + SSD_CONV_DIM], axis=-1)
    xbc = jax.nn.silu(_causal_dwconv(xbc, conv_w.astype(jnp.float32), conv_b.astype(jnp.float32)))
    xs, Bm, Cm = jnp.split(xbc, [SSD_D, SSD_D + SSD_GROUPS * SSD_STATE], axis=-1)
    xh = xs.reshape(b, T, SSD_HEADS, SSD_HEAD_DIM)
    Bm = Bm.reshape(b, T, SSD_GROUPS, SSD_STATE)
    Cm = Cm.reshape(b, T, SSD_GROUPS, SSD_STATE)
    dt = jax.nn.softplus(dt + dt_bias.astype(jnp.float32))
    A = -jnp.exp(a_log.astype(jnp.float32))
    y = _ssd_chunked(xh, dt, A, Bm, Cm) + d_skip.astype(jnp.float32)[:, None] * xh
    y = y.reshape(b, T, SSD_D) * jax.nn.silu(z)
    y = y.reshape(b, T, SSD_GROUPS, SSD_D // SSD_GROUPS)
    y = y * lax.rsqrt(jnp.mean(y * y, axis=-1, keepdims=True) + NORM_EPS)
    return y.reshape(b, T, SSD_D) * norm_g.astype(jnp.float32)


def _rwkv7_scan(r, w, k, v, za, zb):
    b, T, h, n = r.shape

    def step(S, inp):
        r_t, w_t, k_t, v_t, a_t, b_t = inp
        sa = jnp.einsum("bhij,bhj->bhi", S, a_t)
        S = S * w_t[:, :, None, :] + sa[..., None] * b_t[:, :, None, :] + v_t[..., None] * k_t[:, :, None, :]
        return S, jnp.einsum("bhij,bhj->bhi", S, r_t)

    xs = tuple(jnp.moveaxis(t, 1, 0) for t in (r, w, k, v, za, zb))
    S0 = jnp.zeros((b, h, n, n), dtype=r.dtype)
    _, ys = lax.scan(step, S0, xs)
    return jnp.moveaxis(ys, 0, 1)


def _rwkv7_mixer(p, mu, w0, w2, a0, a2, g2, k_k, k_a, r_k, ln_w, ln_b):
    b, T, _ = p.shape
    f32 = lambda t: t.astype(jnp.float32)
    p = p + (_token_shift(p) - p) * f32(mu)
    i1 = RWKV_D
    i2 = i1 + DECAY_LORA
    i3 = i2 + RWKV_D
    i4 = i3 + RWKV_D
    i5 = i4 + AAA_LORA
    r, w_lo, k, v, a_lo, g_lo = jnp.split(p, [i1, i2, i3, i4, i5], axis=-1)
    w_log = -jax.nn.softplus(-(f32(w0) + jnp.tanh(w_lo) @ f32(w2))) - 0.5
    decay = jnp.exp(-jnp.exp(w_log))
    a = jax.nn.sigmoid(f32(a0) + a_lo @ f32(a2))
    g = jax.nn.sigmoid(g_lo) @ f32(g2)
    heads = lambda t: t.reshape(b, T, RWKV_HEADS, RWKV_HEAD_DIM)
    kk = heads(k * f32(k_k))
    kk = kk / jnp.maximum(jnp.sqrt(jnp.sum(kk * kk, axis=-1, keepdims=True)), 1e-12)
    k = k * (1.0 + (a - 1.0) * f32(k_a))
    rh, kh, vh, ah = heads(r), heads(k), heads(v), heads(a)
    y = _rwkv7_scan(rh, heads(decay), kh, vh, -kk, kk * ah)
    mean = jnp.mean(y, axis=-1, keepdims=True)
    var = jnp.mean(jnp.square(y - mean), axis=-1, keepdims=True)
    y = ((y - mean) * lax.rsqrt(var + RWKV_GN_EPS)).reshape(b, T, RWKV_D) * f32(ln_w) + f32(ln_b)
    bonus = jnp.sum(rh * kh * f32(r_k), axis=-1, keepdims=True) * vh
    return (y + bonus.reshape(b, T, RWKV_D)) * g


def setup_inputs(seed: int = 0) -> dict:
    key = jax.random.key(seed)
    ks = jax.random.split(key, 32)
    nrm = lambda k, shape, s: jax.random.normal(k, shape, jnp.float32) * s
    L = DEPTH
    dt = jnp.exp(jax.random.uniform(ks[5], (L, SSD_HEADS), jnp.float32) * (np.log(0.1) - np.log(0.001)) + np.log(0.001))
    dt = jnp.clip(dt, 1e-4)
    return {
        "x": nrm(ks[0], (BATCH, SEQ, D_MODEL), 1.0),
        "pre_mix_norm": 1.0 + nrm(ks[1], (L, D_MODEL), 0.05),
        "w_in": nrm(ks[2], (L, D_MODEL, D_IN), D_MODEL ** -0.5),
        "ssd_conv_w": nrm(ks[3], (L, SSD_CONV, SSD_CONV_DIM), SSD_CONV ** -0.5),
        "ssd_conv_b": nrm(ks[4], (L, SSD_CONV_DIM), 0.02),
        "ssd_dt_bias": dt + jnp.log(-jnp.expm1(-dt)),
        "ssd_a_log": jnp.log(jax.random.uniform(ks[6], (L, SSD_HEADS), jnp.float32, 1.0, 16.0)),
        "ssd_d": 1.0 + nrm(ks[7], (L, SSD_HEADS), 0.1),
        "ssd_norm": 1.0 + nrm(ks[8], (L, SSD_D), 0.05),
        "rwkv_mu": jax.random.uniform(ks[9], (L, RWKV_COLS), jnp.float32),
        "rwkv_w0": jax.random.uniform(ks[10], (L, RWKV_D), jnp.float32, -6.0, 1.0),
        "rwkv_w2": nrm(ks[11], (L, DECAY_LORA, RWKV_D), 0.1 * DECAY_LORA ** -0.5),
        "rwkv_a0": nrm(ks[12], (L, RWKV_D), 0.1),
        "rwkv_a2": nrm(ks[13], (L, AAA_LORA, RWKV_D), 0.5 * AAA_LORA ** -0.5),
        "rwkv_g2": nrm(ks[14], (L, GATE_LORA, RWKV_D), GATE_LORA ** -0.5),
        "rwkv_k_k": 0.85 + nrm(ks[15], (L, RWKV_D), 0.05),
        "rwkv_k_a": 1.0 + nrm(ks[16], (L, RWKV_D), 0.05),
        "rwkv_r_k": nrm(ks[17], (L, RWKV_HEADS, RWKV_HEAD_DIM), 0.1),
        "rwkv_ln_w": 1.0 + nrm(ks[18], (L, RWKV_D), 0.05),
        "rwkv_ln_b": nrm(ks[19], (L, RWKV_D), 0.02),
        "w_out": nrm(ks[20], (L, D_MIX, D_MODEL), D_MIX ** -0.5),
        "post_mix_norm": 1.0 + nrm(ks[21], (L, D_MODEL), 0.05),
        "pre_ffn_norm": 1.0 + nrm(ks[22], (L, D_MODEL), 0.05),
        "ffn_w_up": nrm(ks[23], (L, D_MODEL, 2 * D_FF), D_MODEL ** -0.5),
        "ffn_conv_w": nrm(ks[24], (L, FFN_CONV, 2 * D_FF), FFN_CONV ** -0.5),
        "ffn_conv_b": nrm(ks[25], (L, 2 * D_FF), 0.02),
        "ffn_w_down": nrm(ks[26], (L, D_FF, D_MODEL), D_FF ** -0.5),
        "post_ffn_norm": 1.0 + nrm(ks[27], (L, D_MODEL), 0.05),
    }


def reference(x, pre_mix_norm, w_in, ssd_conv_w, ssd_conv_b, ssd_dt_bias, ssd_a_log, ssd_d, ssd_norm,
              rwkv_mu, rwkv_w0, rwkv_w2, rwkv_a0, rwkv_a2, rwkv_g2, rwkv_k_k, rwkv_k_a, rwkv_r_k,
              rwkv_ln_w, rwkv_ln_b, w_out, post_mix_norm, pre_ffn_norm, ffn_w_up, ffn_conv_w,
              ffn_conv_b, ffn_w_down, post_ffn_norm):
    h = x
    for l in range(DEPTH):
        xn = _rms_norm(h, pre_mix_norm[l])
        proj = (xn @ w_in[l]).astype(jnp.float32)
        p_ssd, p_rwkv = jnp.split(proj, [SSD_COLS], axis=-1)
        y_ssd = _ssd_mixer(p_ssd, ssd_conv_w[l], ssd_conv_b[l], ssd_dt_bias[l], ssd_a_log[l],
                           ssd_d[l], ssd_norm[l])
        y_rwkv = _rwkv7_mixer(p_rwkv, rwkv_mu[l], rwkv_w0[l], rwkv_w2[l], rwkv_a0[l], rwkv_a2[l],
                              rwkv_g2[l], rwkv_k_k[l], rwkv_k_a[l], rwkv_r_k[l], rwkv_ln_w[l], rwkv_ln_b[l])
        mix = jnp.concatenate([y_ssd, y_rwkv], axis=-1).astype(h.dtype) @ w_out[l]
        h = h + _rms_norm(mix, post_mix_norm[l])
        hn = _rms_norm(h, pre_ffn_norm[l])
        u = _causal_dwconv(hn @ ffn_w_up[l], ffn_conv_w[l], ffn_conv_b[l])
        gate, val = jnp.split(u, 2, axis=-1)
        f = (jax.nn.silu(gate) * val) @ ffn_w_down[l]
        h = h + _rms_norm(f, post_ffn_norm[l])
    return h
```

```python
from contextlib import ExitStack
import numpy as np
import concourse.bass as bass
import concourse.mybir as mybir
from concourse.bass_utils import run_bass_kernel_spmd

F32 = mybir.dt.float32
BF16 = mybir.dt.bfloat16
AF = mybir.ActivationFunctionType
ALU = mybir.AluOpType
AX = mybir.AxisListType

SAME_ENGINE_SYNC = True
NT_ALL = 32
NT_MAIN = 16
TT = 128
NFM = 22
HW = 131


class Buf:
    __slots__ = ("name", "w", "r")

    def __init__(self, name):
        self.name = name
        self.w = None
        self.r = []


class Sched:
    ENGS = ("pe", "act", "dve", "pool", "sp")

    def __init__(self, nc, stack):
        self.nc = nc
        self.stack = stack
        self.sems = {}
        self.cnt = {}
        for e in self.ENGS:
            self.sems[e] = stack.enter_context(nc.semaphore("s_" + e))
            self.cnt[e] = 0
        self.known = {e: {} for e in self.ENGS}
        self.snap = {}
        self.prog = {e: [] for e in self.ENGS}
        self.dma_sems = {}
        self.dma_cnt = {}

    def _sem_handle(self, k):
        return self.sems[k] if k in self.sems else self.dma_sems[k]

    def _need(self, eng, tok, acc):
        if tok is None:
            return
        semkey, val, src = tok
        if src == eng and (eng == "pe" or not SAME_ENGINE_SYNC):
            return
        if self.known[eng].get(semkey, 0) >= val:
            return
        if acc.get(semkey, 0) < val:
            acc[semkey] = val

    def _emit_waits(self, eng, acc):
        for semkey, val in acc.items():
            h = self._sem_handle(semkey)
            self.prog[eng].append(lambda e, h=h, val=val: e.wait_ge(h, val))
            k = self.known[eng]
            if k.get(semkey, 0) < val:
                k[semkey] = val
            if semkey in self.sems:
                sn = self.snap.get((semkey, val))
                if sn:
                    for kk, vv in sn.items():
                        if k.get(kk, 0) < vv:
                            k[kk] = vv

    def _deps(self, eng, reads, writes):
        acc = {}
        for b in reads:
            self._need(eng, b.w, acc)
        for b in writes:
            self._need(eng, b.w, acc)
            for t in b.r:
                self._need(eng, t, acc)
        self._emit_waits(eng, acc)

    def _mark(self, tok, reads, writes):
        for b in reads:
            b.r.append(tok)
            if len(b.r) > 24:
                b.r = b.r[-24:] if False else b.r
        for b in writes:
            b.w = tok
            b.r = []

    def op(self, eng, fn, reads=(), writes=()):
        self._deps(eng, reads, writes)
        self.cnt[eng] += 1
        n = self.cnt[eng]
        h = self.sems[eng]
        self.prog[eng].append(lambda e, fn=fn, h=h: fn(e).then_inc(h, 1))
        tok = (eng, n, eng)
        sn = dict(self.known[eng])
        sn[eng] = n
        self.snap[(eng, n)] = sn
        self._mark(tok, reads, writes)
        return tok

    def dma(self, queue, fn, semkey, reads=(), writes=()):
        if semkey not in self.dma_sems:
            self.dma_sems[semkey] = self.stack.enter_context(self.nc.semaphore("d_" + semkey))
            self.dma_cnt[semkey] = 0
        self._deps(queue, reads, writes)
        self.dma_cnt[semkey] += 16
        val = self.dma_cnt[semkey]
        h = self.dma_sems[semkey]
        self.prog[queue].append(lambda e, fn=fn, h=h: fn(e).then_inc(h, 16))
        tok = (semkey, val, "dma")
        self._mark(tok, reads, writes)
        return tok

    def barrier(self):
        toks = [(e, self.cnt[e], e) for e in self.ENGS if self.cnt[e] > 0]
        toks += [(k, v, "dma") for k, v in self.dma_cnt.items() if v > 0]
        for e in self.ENGS:
            acc = {}
            for t in toks:
                if t[0] == e:
                    continue
                self._need(e, t, acc)
            self._emit_waits(e, acc)

    def finish(self, final_toks):
        for t in final_toks:
            acc = {}
            self._need("sp", t, acc)
            self._emit_waits("sp", acc)
        with self.nc.Block() as block:
            @block.tensor
            def _(e):
                for f in self.prog["pe"]:
                    f(e)

            @block.scalar
            def _(e):
                for f in self.prog["act"]:
                    f(e)

            @block.vector
            def _(e):
                for f in self.prog["dve"]:
                    f(e)

            @block.gpsimd
            def _(e):
                for f in self.prog["pool"]:
                    f(e)

            @block.sync
            def _(e):
                for f in self.prog["sp"]:
                    f(e)


PC = {}
_o = 0
for _n, _w in (("cw", 32), ("cb", 8), ("mu", 14), ("w0", 4), ("a0", 4), ("kk", 4), ("ka", 4),
               ("fcw", 132), ("fcb", 44), ("g1", 8), ("g2n", 8)):
    PC[_n] = _o
    _o += _w
NPAR = _o
RC = {}
_o = 0
for _n, _w in (("dtb", 8), ("alog", 8), ("dsk", 8), ("ssdg", 512), ("lnw", 512), ("lnb", 512),
               ("pmn", 1024), ("pfn", 1024)):
    RC[_n] = _o
    _o += _w
NROW = _o
CC = {"ident": 0, "triu": 128, "trius": 256, "trils": 384, "onesblk": 512, "ones": 640}
NCONST = 768


def _cols(v, nch):
    return np.ascontiguousarray(v.reshape(nch, 128).T)


def _host_prep(inp):
    f = np.float32
    w_in = inp["w_in"][0]
    s0 = 0
    r0 = 1544
    perm = np.concatenate([
        np.arange(512, 1536),
        r0 + np.arange(0, 512),
        r0 + np.arange(576, 1088),
        r0 + np.arange(1088, 1600),
        r0 + np.arange(512, 576),
        r0 + np.arange(1600, 1664),
        r0 + np.arange(1664, 1792),
        np.arange(0, 512),
        np.arange(1536, 1544),
    ])
    win = np.ascontiguousarray(w_in[:, perm].reshape(8, 128, 3336).transpose(1, 0, 2)).reshape(128, 8 * 3336)
    wout = np.ascontiguousarray(inp["w_out"][0].reshape(8, 128, 1024).transpose(1, 0, 2)).reshape(128, 8 * 1024)
    wup = np.ascontiguousarray(inp["ffn_w_up"][0].reshape(8, 128, 5632).transpose(1, 0, 2)).reshape(128, 8 * 5632)
    wdown = np.ascontiguousarray(inp["ffn_w_down"][0].reshape(22, 128, 1024).transpose(1, 0, 2)).reshape(128, 22 * 1024)
    par = np.zeros((128, NPAR), f)
    cw = inp["ssd_conv_w"][0]
    par[:, PC["cw"]:PC["cw"] + 32] = cw.reshape(4, 8, 128).transpose(2, 1, 0).reshape(128, 32)
    par[:, PC["cb"]:PC["cb"] + 8] = _cols(inp["ssd_conv_b"][0], 8)
    mu = inp["rwkv_mu"][0]
    mup = np.concatenate([mu[0:512], mu[576:1088], mu[1088:1600], mu[512:576], mu[1600:1664], mu[1664:1792]])
    par[:, PC["mu"]:PC["mu"] + 14] = _cols(mup, 14)
    par[:, PC["w0"]:PC["w0"] + 4] = _cols(inp["rwkv_w0"][0], 4)
    par[:, PC["a0"]:PC["a0"] + 4] = _cols(inp["rwkv_a0"][0], 4)
    par[:, PC["kk"]:PC["kk"] + 4] = _cols(inp["rwkv_k_k"][0], 4)
    par[:, PC["ka"]:PC["ka"] + 4] = _cols(inp["rwkv_k_a"][0], 4)
    fw = inp["ffn_conv_w"][0]
    par[:, PC["fcw"]:PC["fcw"] + 132] = fw.reshape(3, 44, 128).transpose(2, 1, 0).reshape(128, 132)
    par[:, PC["fcb"]:PC["fcb"] + 44] = _cols(inp["ffn_conv_b"][0], 44)
    par[:, PC["g1"]:PC["g1"] + 8] = _cols(inp["pre_mix_norm"][0], 8)
    par[:, PC["g2n"]:PC["g2n"] + 8] = _cols(inp["pre_ffn_norm"][0], 8)
    rows = np.zeros((128, NROW), f)

    def put(name, v):
        rows[:, RC[name]:RC[name] + v.size] = np.broadcast_to(v.reshape(1, -1), (128, v.size))
    put("dtb", inp["ssd_dt_bias"][0])
    put("alog", inp["ssd_a_log"][0])
    put("dsk", inp["ssd_d"][0])
    put("ssdg", inp["ssd_norm"][0])
    put("lnw", inp["rwkv_ln_w"][0])
    put("lnb", inp["rwkv_ln_b"][0])
    put("pmn", inp["post_mix_norm"][0])
    put("pfn", inp["post_ffn_norm"][0])
    consts = np.zeros((128, NCONST), f)
    i = np.arange(128)
    consts[:, 0:128] = np.eye(128)
    consts[:, 128:256] = (i[:, None] <= i[None, :])
    consts[:, 256:384] = (i[:, None] < i[None, :])
    consts[:, 384:512] = (i[:, None] > i[None, :])
    consts[:, 512:640] = ((i[:, None] // 64) == (i[None, :] // 64))
    consts[:, 640:768] = 1.0
    wlora = np.concatenate([inp["rwkv_w2"][0], inp["rwkv_a2"][0]], axis=0).astype(f)
    g2 = np.ascontiguousarray(inp["rwkv_g2"][0]).astype(f)
    rk = inp["rwkv_r_k"][0]
    sel = np.zeros((128, 4, 8), f)
    for c in range(4):
        for hp in range(2):
            sel[hp * 64:(hp + 1) * 64, c, 2 * c + hp] = rk[2 * c + hp]
    shared = {"win": win, "wout": wout, "wup": wup, "wdown": wdown, "params": par, "rows": rows,
              "consts": consts, "wlora": wlora, "g2": g2, "sel": sel.reshape(128, 32)}
    return {k: np.ascontiguousarray(v, dtype=f) for k, v in shared.items()}


NSLOT = NT_MAIN + 1
DBG = 0


class _Stop(Exception):
    pass


def build_nc():
    nc = bass.Bass("TRN2", target_bir_lowering=False)
    D = {}
    for name, shape in (("x_in", [4096, 1024]), ("win", [128, 8 * 3336]), ("wout", [128, 8192]),
                        ("wup", [128, 8 * 5632]), ("wdown", [128, 22 * 1024]), ("params", [128, NPAR]),
                        ("rows", [128, NROW]), ("consts", [128, NCONST]), ("wlora", [128, 512]),
                        ("g2", [128, 512]), ("sel", [128, 32]), ("flag", [128, 1])):
        D[name] = nc.dram_tensor(name, shape, F32, kind="ExternalInput").ap()
    out = nc.dram_tensor("out", [NT_MAIN * TT, 1024], F32, kind="ExternalOutput").ap()
    hscr = nc.dram_tensor("hscr", [NSLOT * TT, 1024], F32, kind="Internal").ap()

    try:
      with ExitStack() as st:
        S = Sched(nc, st)
        dbgt = st.enter_context(nc.sbuf_tensor("sb_dbgt", [128, 64], F32))

        def CK(n, src=None):
            if DBG == n:
                if src is None:
                    S.op("dve", lambda e: e.memset(dbgt[:], 1.0), (), ())
                    tk = S.dma("sp", lambda e: e.dma_start(out=out[0:128, 0:64], in_=dbgt[:]), "dbg", (), ())
                else:
                    ap, bufs, ncol = src
                    tk = S.dma("sp", lambda e: e.dma_start(out=out[0:128, 0:ncol], in_=ap), "dbg", [b.b for b in bufs], ())
                S.finish([tk])
                raise _Stop()

        class TB:
            def __init__(self, t, name, b=None):
                self.t = t
                self.b = b if b is not None else Buf(name)

            def __getitem__(self, k):
                return self.t[k]

        class View:
            def __init__(self, ap, b):
                self.ap = ap
                self.b = b

            def __getitem__(self, k):
                return self.ap[k]

        def sb(name, shape, dt, stack=st):
            return TB(stack.enter_context(nc.sbuf_tensor("sb_" + name, shape, dt)), name)

        def psum(name, shape, dt):
            t = TB(st.enter_context(nc.psum_tensor("ps_" + name, shape, dt)), name)
            PSB.add(id(t.b))
            return t

        PSB = set()

        def rd(*xs):
            return [x.b if hasattr(x, "b") else x for x in xs]

        def _xrw(r, w):
            r = rd(*r)
            w = rd(*w)
            rr = [x for x in r if id(x) not in PSB]
            ww = list(w) + [x for x in r if id(x) in PSB and x not in w]
            return rr, ww

        def tt(eng, o, a, b, op, r, w):
            S.op(eng, lambda e: e.tensor_tensor(out=o, in0=a, in1=b, op=op), *_xrw(r, w))

        def ts(eng, o, a, s1, s2, op0, op1, r, w):
            if s2 is None:
                S.op(eng, lambda e: e.tensor_scalar(out=o, in0=a, scalar1=s1, scalar2=None, op0=op0), *_xrw(r, w))
            else:
                S.op(eng, lambda e: e.tensor_scalar(out=o, in0=a, scalar1=s1, scalar2=s2, op0=op0, op1=op1), *_xrw(r, w))

        def stt(eng, o, a, s, b, op0, op1, r, w):
            S.op(eng, lambda e: e.scalar_tensor_tensor(out=o, in0=a, scalar=s, in1=b, op0=op0, op1=op1), *_xrw(r, w))

        def act(o, a, func, r, w, bias=None, scale=None, accum=None):
            kw = {}
            if bias is not None:
                kw["bias"] = bias
            if scale is not None:
                kw["scale"] = scale
            if accum is not None:
                kw["accum_out"] = accum
            S.op("act", lambda e: e.activation(out=o, in_=a, func=func, **kw), *_xrw(r, w))

        def cp(eng, o, a, r, w):
            if eng == "act":
                act(o, a, AF.Copy, r, w)
            else:
                S.op(eng, lambda e: e.tensor_copy(out=o, in_=a), *_xrw(r, w))

        def mm(o, l, rh, start, stop, r, w):
            S.op("pe", lambda e: e.matmul(out=o, lhsT=l, rhs=rh, start=start, stop=stop, skip_group_check=True), *_xrw(r, w))

        def trn(o, a, ident, r, w):
            S.op("pe", lambda e: e.transpose(out=o, in_=a, identity=ident), *_xrw(r, w))

        def memset(eng, o, v, w):
            S.op(eng, lambda e: e.memset(o, v), (), rd(*w))

        def dma(q, o, a, key, r, w):
            return S.dma(q, lambda e: e.dma_start(out=o, in_=a), key, *_xrw(r, w))

        def v3(ap, a):
            return ap.rearrange("p (a b) -> p a b", a=a)

        par = sb("par", [128, NPAR], F32)
        cst = sb("cst", [128, NCONST], F32)
        identb = sb("identb", [128, 128], BF16)
        flag = sb("flag", [128, 1], F32)
        neghalf = sb("neghalf", [128, 512], F32)
        zeros = sb("zeros", [128, 128], F32)
        pT = psum("pT", [128, 1024], BF16)
        pTR = psum("pTR", [128, 1024], BF16)
        pool_banks = [psum("pb%d" % i, [128, 512], F32) for i in range(6)]
        rot = [0]
        rot_set = [[1, 2, 3, 4, 5]]

        def PS():
            rs = rot_set[0]
            b = pool_banks[rs[rot[0] % len(rs)]]
            rot[0] += 1
            return b

        def C(name, n=128):
            return cst[:, CC[name]:CC[name] + n]

        def P(name, c):
            return par[:, PC[name] + c:PC[name] + c + 1]

        dma("sp", par[:], D["params"], "c0", (), (par,))
        dma("sp", cst[:], D["consts"], "c2", (), (cst,))
        dma("sp", flag[:], D["flag"], "c3", (), (flag,))
        cp("dve", identb[:], C("ident"), (cst,), (identb,))
        memset("pool", neghalf[:], -0.5, (neghalf,))
        memset("pool", zeros[:], 0.0, (zeros,))
        ts("dve", par[:, PC["cw"]:PC["cw"] + 40], par[:, PC["cw"]:PC["cw"] + 40], 0.5, None, ALU.mult, None, (par,), (par,))

        def rsqrt_small(o, a, n, r, w):
            tt("pool", o, a, neghalf[:, 0:n], ALU.pow, list(r) + [neghalf], w)

        hscrB = Buf("hscr")

        with ExitStack() as stA:
            rows = sb("rows", [128, 2584], F32, stA)

            def R(name, n):
                return rows[:, RC[name]:RC[name] + n]

            dma("sp", rows[:], D["rows"][:, 0:2584], "c1", (), (rows,))
            Win = sb("Win", [128, 8, 3336], BF16, stA)
            Wout = sb("Wout", [128, 8, 1024], BF16, stA)
            Wlora = sb("Wlora", [128, 512], F32, stA)
            G2h = sb("G2h", [128, 512], BF16, stA)
            sel = sb("sel", [128, 4, 8], F32, stA)
            arow = sb("arow", [128, 8], F32, stA)
            with ExitStack() as stS:
                stage = [sb("stageA%d" % q, [128, 3336], F32, stS) for q in range(2)]
                sq = [0]

                def load_rows(src_ap, ncols, fn):
                    stg = stage[sq[0] % 2]
                    sq[0] += 1
                    dma("sp", stg[:, 0:ncols], src_ap, "wst%d" % (sq[0] % 2), (), (stg,))
                    fn(stg)
                for k in range(8):
                    load_rows(D["win"][:, k * 3336:(k + 1) * 3336], 3336,
                              lambda stg, k=k: (ts("dve", Win[:, k, :], stg[:, 0:3336], P("g1", k), None,
                                                   ALU.mult, None, (stg, par), (Win,)) if k % 2 == 0 else
                                                act(Win[:, k, :], stg[:, 0:3336], AF.Copy, (stg, par), (Win,), scale=P("g1", k))))
                ts("pool", Win[:, :, 2816:3328], Win[:, :, 2816:3328], 0.5, None, ALU.mult, None, (Win,), (Win,))
                for k in range(8):
                    load_rows(D["wout"][:, k * 1024:(k + 1) * 1024], 1024,
                              lambda stg, k=k: cp("act", Wout[:, k, :], stg[:, 0:1024], (stg,), (Wout,)))
                load_rows(D["g2"], 512, lambda stg: ts("dve", G2h[:], stg[:, 0:512], 0.5, None, ALU.mult, None, (stg,), (G2h,)))
                dma("sp", Wlora[:], D["wlora"], "c4", (), (Wlora,))
                dma("sp", sel[:].rearrange("p a b -> p (a b)"), D["sel"], "c5", (), (sel,))
                act(arow[:], R("alog", 8), AF.Exp, (rows,), (arow,))
                ts("dve", arow[:], arow[:], -1.0, None, ALU.mult, None, (arow,), (arow,))
                S.barrier()
            CK(1)

            xt1 = sb("xt", [128, 1024], F32, stA)
            xt = [xt1, xt1]
            xT = [sb("xT%d" % i, [128, 8, HW], BF16, stA) for i in range(2)]
            raw = sb("raw", [128, NFM, HW], F32, stA)
            rawB = [Buf("raw_%d" % j) for j in range(8)]
            gz = [sb("gz%d" % i, [128, 512], F32, stA) for i in range(2)]
            dtr = [sb("dtr%d" % i, [128, 8], F32, stA) for i in range(2)]
            xb = sb("xb", [128, 1024], BF16, stA)
            hb = xb
            sm = sb("sm", [128, 64], F32, stA)
            smB = [Buf("sm%d" % i) for i in range(16)]
            Pp = sb("Pp", [128, 14, 128], F32, stA)
            BB = sb("BB", [128, 2048], F32, stA)
            AC = sb("AC", [128, 1024], F32, stA)
            B2 = sb("B2", [128, 1024], F32, stA)
            B3 = sb("B3", [128, 1024], F32, stA)
            B4 = sb("B4", [128, 512], F32, stA)
            B56 = sb("B56", [128, 2048], F32, stA)
            B7 = sb("B7", [128, 1024], F32, stA)
            ht = sb("ht", [128, 1024], F32, stA)
            X1 = sb("X1", [128, 512], BF16, stA)
            X2 = sb("X2", [128, 512], BF16, stA)
            X3 = sb("X3", [128, 512], BF16, stA)
            GA = sb("GA", [128, 1024], BF16, stA)
            Btm = sb("Btm", [128, 2, 128], BF16, stA)
            STf = sb("STf", [128, 8, 64], F32, stA)
            STb = sb("STb", [128, 8, 64], BF16, stA)
            ytok = [sb("ytok%d" % q, [128, 1024], BF16, stA) for q in range(2)]
            GA0 = sb("GA0", [128, 1024], BF16, stA)
            UTfT = sb("UTfT", [128, 512], F32, stA)
            tmpST = sb("tmpST", [128, 256], F32, stA)
            mixT = sb("mixT", [128, 8, 128], BF16, stA)
            lin = sb("lin", [128, 128], F32, stA)
            sg = sb("sg", [128, 128], BF16, stA)
            btm = sb("btm", [128, 512], BF16, stA)
            ktm = sb("ktm", [128, 512], BF16, stA)
            vtm = sb("vtm", [128, 512], BF16, stA)
            ABs = [sb("ABs%d" % c, [128, 2, 256], BF16, stA) for c in range(4)]
            AKs = [sb("AKs%d" % c, [128, 2, 256], BF16, stA) for c in range(4)]
            PPm = [[sb("PPm%d_%d" % (q, c), [128, 2, 2, 128], BF16, stA) for c in range(4)] for q in range(2)]
            Pm = [[View(PPm[q][c][:, 0, :, :], PPm[q][c].b) for c in range(4)] for q in range(2)]
            PTm = [[View(PPm[q][c][:, 1, :, :], PPm[q][c].b) for c in range(4)] for q in range(2)]
            UT = sb("UT", [128, 8, 64], BF16, stA)
            STr = sb("STr", [128, 4, 64], F32, stA)
            STrb = sb("STrb", [128, 4, 64], BF16, stA)
            maskSI = sb("maskSI", [128, 256], F32, stA)
            hw0 = sb("hw0", [128, 8], F32, stA)
            tiny = sb("tiny", [128, 1], F32, stA)
            memset("pool", tiny[:], 1e-24, (tiny,))
            bnsb = sb("bnsb", [128, 8], F32, stA)
            gn = sb("gn", [128, 40], F32, stA)
            hs = sb("hs", [128, 8], F32, stA)
            hsm = Buf("hsm")
            dlt = View(v3(B56[:, 0:1792], 14), B56.b)
            th = View(v3(BB[:, 0:1024], 8), BB.b)
            xax = View(v3(BB[:, 1024:1536], 4), BB.b)
            xs_tm = View(BB[:, 1536:2048], BB.b)
            Dm = View(v3(BB[:, 0:1024], 8), BB.b)
            lw = View(v3(BB[:, 0:512], 4), BB.b)
            av = View(v3(BB[:, 512:1024], 4), BB.b)
            cum = View(v3(BB[:, 1024:1536], 4), BB.b)
            cumex = View(v3(BB[:, 1536:2048], 4), BB.b)
            yr = View(v3(B56[:, 0:512], 8), B56.b)
            ysq = View(v3(B56[:, 512:1024], 8), B56.b)
            acc = View(v3(AC[:], 8), AC.b)
            rseg = View(v3(AC[:], 8), AC.b)
            thw = View(v3(AC[:, 0:512], 4), AC.b)
            tha = View(v3(AC[:, 512:1024], 4), AC.b)
            mn = View(AC[:], AC.b)
            dx = View(B2[:, 0:512], B2.b)
            ytmp = View(B2[:, 512:1024], B2.b)
            E2 = View(v3(B2[:, 0:512], 4), B2.b)
            tmpS = View(v3(tmpST[:], 4), tmpST.b)
            ysd = View(B3[:, 0:512], B3.b)
            scm = View(v3(B3[:, 512:768], 2), B3.b)
            kkv = View(v3(B3[:, 0:512], 4), B3.b)
            UTf = View(UTfT[:], UTfT.b)
            kk2 = View(v3(B3[:, 512:1024], 4), B3.b)
            E1x = sb("E1x", [128, 4, 129], F32, stA)
            memset("pool", E1x[:], 1.0, (E1x,))
            E1 = View(E1x[:, :, 1:129], E1x.b)
            E3 = View(E1x[:, :, 0:128], E1x.b)
            inv = View(v3(B4[:, 0:512], 4), B4.b)
            kkn = View(v3(B56[:, 0:512], 4), B56.b)
            kmod = View(v3(B56[:, 512:1024], 4), B56.b)
            t1 = View(v3(B56[:, 1024:1536], 4), B56.b)
            rk = View(v3(B56[:, 1024:1536], 4), B56.b)
            beta = View(v3(B56[:, 1536:2048], 4), B56.b)
            gsb = View(B7[:, 0:512], B7.b)
            bterm = View(v3(B7[:, 512:1024], 8), B7.b)
            Xb = View(X1[:], X1.b)
            btl = View(v3(X1[:], 4), X1.b)
            Xd = View(X2[:], X2.b)
            ktl = View(v3(X2[:], 4), X2.b)
            BCb = View(v3(X3[:], 4), X3.b)
            vb = View(v3(X3[:], 4), X3.b)
            Gm = View(v3(GA[:], 8), GA.b)
            GA1 = sb("GA1", [128, 1024], BF16, stA)
            arz = [View(GA0[:].rearrange("p (c a b) -> p c a b", c=4, a=2), GA0.b),
                   View(GA1[:].rearrange("p (c a b) -> p c a b", c=4, a=2), GA1.b)]
            GAz = [GA0, GA1]
            memset("pool", GA1[:], 0.0, (GA1,))
            memset("pool", GA0[:], 0.0, (GA0,))

            cp("dve", maskSI[:, 0:128], C("trius"), (cst,), (maskSI,))
            cp("dve", maskSI[:, 128:256], C("triu"), (cst,), (maskSI,))
            memset("dve", STf[:], 0.0, (STf,))
            memset("dve", STb[:], 0.0, (STb,))
            memset("dve", STr[:], 0.0, (STr,))
            memset("dve", STrb[:], 0.0, (STrb,))
            memset("pool", xT[1][:], 0.0, (xT[1],))
            memset("pool", xT[0][:], 0.0, (xT[0],))
            ts("dve", hw0[:, 0:4], par[:, PC["w0"]:PC["w0"] + 4], 0.5, None, ALU.mult, None, (par,), (hw0,))
            ts("dve", hw0[:, 4:8], par[:, PC["a0"]:PC["a0"] + 4], 0.5, None, ALU.mult, None, (par,), (hw0,))

            def SM(i, n=1):
                return sm[:, i:i + n]

            B0T = NT_ALL - NT_MAIN

            fstate = [0]

            def front_gen(i):
                p = i % 2
                fstate[0] = 0
                dma("sp", xt[p][:], D["x_in"][i * TT:(i + 1) * TT, :], "x0", (), (xt[p],))
                act(xb[:], xt[p][:], AF.Square, (xt[p],), (xb, smB[0]), accum=SM(0))
                ts("dve", SM(1), SM(0), 1.0 / 1024, 1e-6, ALU.mult, ALU.add, (smB[0],), (smB[1],))
                rsqrt_small(SM(2), SM(1), 1, (smB[1],), (smB[2],))
                act(xb[:], xt[p][:], AF.Copy, (xt[p], smB[2]), (xb,), scale=SM(2))
                yield
                for k in range(8):
                    trn(pT[:, k * 128:(k + 1) * 128], xb[:, k * 128:(k + 1) * 128], identb[:], (xb, identb), (pT,))
                cp("dve", xT[p][:, :, 3:HW], v3(pT[:], 8), (pT,), (xT[p],))
                cp("pool", xT[p][:, :, 0:3], xT[1 - p][:, :, 128:131], (xT[1 - p],), (xT[p],))
                if i == B0T:
                    ts("pool", xT[p][:, :, 0:3], xT[p][:, :, 0:3], flag[:], None, ALU.mult, None, (xT[p], flag), (xT[p],))
                yield
                hist = i < B0T - 1
                for bnk in range(8):
                    n = min(3, NFM - 3 * bnk)
                    if hist and bnk in (2, 3, 7):
                        fstate[0] = bnk + 1
                        continue
                    pb = PS()
                    for jj in range(n):
                        j = 3 * bnk + jj
                        for k in range(8):
                            mm(pb[:, jj * HW:(jj + 1) * HW], Win[:, k, j * 128:(j + 1) * 128], xT[p][:, k, :],
                               k == 0, k == 7, (Win, xT[p]), (pb,))
                    cp("act" if bnk % 2 == 0 else "dve", raw[:, 3 * bnk:3 * bnk + n, :], v3(pb[:, 0:n * HW], n),
                       (pb,), (rawB[bnk],))
                    fstate[0] = bnk + 1
                    yield
                full = i >= B0T - 1
                if full:
                    pz = PS()
                    for k in range(8):
                        mm(pz[:, 0:512], xT[p][:, k, 3:HW], Win[:, k, 2816:3328], k == 0, k == 7, (Win, xT[p]), (pz,))
                pd = PS()
                for k in range(8):
                    mm(pd[:, 0:8], xT[p][:, k, 3:HW], Win[:, k, 3328:3336], k == 0, k == 7, (Win, xT[p]), (pd,))
                if full:
                    act(gz[p][:], pz[:, 0:512], AF.Tanh, (pz,), (gz[p],))
                    stt("dve", gz[p][:], gz[p][:], 1.0, pz[:, 0:512], ALU.add, ALU.mult, (gz[p], pz), (gz[p],))
                tt("dve", dtr[p][:], pd[:, 0:8], R("dtb", 8), ALU.add, (pd, rows), (dtr[p],))

            estate = [0]

            def early_gen(i):
                rw = raw
                estate[0] = 0
                hist = i < B0T - 1
                for c in range(6 if hist else 8):
                    need_banks(c // 3 + 1)
                    rb = rawB[c // 3]
                    ts("dve", acc[:, c, :], rw[:, c, 3:HW], P("cw", c * 4 + 3), P("cb", c), ALU.mult, ALU.add,
                       (rb, par), (acc,))
                    for k in range(3):
                        stt("dve", acc[:, c, :], rw[:, c, k:k + 128], P("cw", c * 4 + k), acc[:, c, :], ALU.mult, ALU.add,
                            (rb, par, acc), (acc,))
                    estate[0] = c + 1
                    yield
                estate[0] = 8
                need_banks(8)
                c0, c1 = (4, 13) if hist else (0, 14)
                nc_ = c1 - c0
                RBr = rawB[4:7] if hist else rawB[2:8]
                tt("dve", dlt[:, c0:c1, :], rw[:, 8 + c0:8 + c1, 2:130], rw[:, 8 + c0:8 + c1, 3:HW], ALU.subtract, RBr, (dlt,))
                yield
                tt("dve", dlt[:, c0:c1, :], dlt[:, c0:c1, :],
                   par[:, PC["mu"] + c0:PC["mu"] + c1].unsqueeze(2).broadcast_to([128, nc_, 128]),
                   ALU.mult, (dlt, par), (dlt,))
                yield
                tt("dve", Pp[:, c0:c1, :], dlt[:, c0:c1, :], rw[:, 8 + c0:8 + c1, 3:HW], ALU.add, [dlt] + RBr, (Pp,))
                yield

            def ssd_gen(i):
                p = i % 2
                full = i >= B0T - 1
                need_front_done()
                need_conv()
                nch = 8 if full else 6
                act(th[:, 0:nch, :], acc[:, 0:nch, :], AF.Tanh, (acc,), (th,))
                stt("dve", xax[:], th[:, 0:4, :], 1.0, acc[:, 0:4, :], ALU.add, ALU.mult, (th, acc), (xax,))
                stt("dve", BCb[:, 0:nch - 4, :], th[:, 4:nch, :], 1.0, acc[:, 4:nch, :], ALU.add, ALU.mult, (th, acc), (BCb,))
                act(SM(8, 8), dtr[p][:], AF.Exp, (dtr[p],), (smB[8],))
                act(SM(8, 8), SM(8, 8), AF.Ln, (smB[8],), (smB[8],), bias=1.0)
                tt("dve", SM(16, 8), SM(8, 8), arow[:], ALU.mult, (smB[8], arow), (smB[9],))
                yield
                px = PS()
                for c in range(4):
                    mm(px[:, c * 128:(c + 1) * 128], xax[:, c, :], C("ident"), True, True, (xax, cst), (px,))
                cp("act", xs_tm[:], px[:], (px,), (xs_tm,))
                if full:
                    tt("dve", v3(Xb[:], 8), v3(xs_tm[:], 8), SM(8, 8).unsqueeze(2).broadcast_to([128, 8, 64]), ALU.mult,
                       (xs_tm, smB[8]), (Xb,))
                for g in range(2):
                    trn(pTR[:, g * 128:(g + 1) * 128], BCb[:, g, :], identb[:], (BCb, identb), (pTR,))
                cp("act", Btm[:], v3(pTR[:, 0:256], 2), (pTR,), (Btm,))
                yield
                psm = PS()
                mm(psm[:, 0:8], C("triu"), SM(16, 8), True, True, (cst, smB[9]), (psm,))
                mm(psm[:, 8:16], C("trils"), SM(16, 8), True, True, (cst, smB[9]), (psm,))
                mm(psm[:, 16:24], C("ones"), SM(16, 8), True, True, (cst, smB[9]), (psm,))
                act(SM(24, 24), psm[:, 0:24], AF.Exp, (psm,), (smB[10],))
                tt("dve", SM(48, 8), SM(8, 8), SM(32, 8), ALU.mult, (smB[8], smB[10]), (smB[11],))
                tt("dve", v3(Xd[:], 8), v3(xs_tm[:], 8), SM(48, 8).unsqueeze(2).broadcast_to([128, 8, 64]), ALU.mult,
                   (xs_tm, smB[11]), (Xd,))
                yield
                if full:
                    tt("dve", v3(dx[:], 8), v3(xs_tm[:], 8), R("dsk", 8).unsqueeze(2).broadcast_to([128, 8, 64]), ALU.mult,
                       (xs_tm, rows), (dx,))
                    psc = PS()
                    for g in range(2):
                        mm(psc[:, g * 128:(g + 1) * 128], BCb[:, g, :], BCb[:, 2 + g, :], True, True, (BCb,), (psc,))
                    tt("dve", scm[:], v3(psc[:, 0:256], 2), C("triu").unsqueeze(1).broadcast_to([128, 2, 128]), ALU.mult,
                       (psc, cst), (scm,))
                    tt("dve", rseg[:], SM(16, 8).unsqueeze(2).broadcast_to([128, 8, 128]),
                       C("triu").unsqueeze(1).broadcast_to([128, 8, 128]), ALU.mult, (smB[9], cst), (rseg,))
                    yield
                    for hh in range(2):
                        pg = PS()
                        mm(pg[:], C("trils"), rseg[:, 4 * hh:4 * hh + 4, :].rearrange("p a b -> p (a b)"), True, True,
                           (cst, rseg), (pg,))
                        act(Dm[:, 4 * hh:4 * hh + 4, :].rearrange("p a b -> p (a b)"), pg[:], AF.Exp, (pg,), (Dm,))
                    for g in range(2):
                        tt("dve", Gm[:, 4 * g:4 * g + 4, :], Dm[:, 4 * g:4 * g + 4, :],
                           scm[:, g, :].unsqueeze(1).broadcast_to([128, 4, 128]), ALU.mult, (Dm, scm), (Gm,))
                    yield
                    py1 = PS()
                    for h in range(8):
                        mm(py1[:, h * 64:(h + 1) * 64], Gm[:, h, :], Xb[:, h * 64:(h + 1) * 64], True, True, (Gm, Xb), (py1,))
                    py2 = PS()
                    for h in range(8):
                        mm(py2[:, h * 64:(h + 1) * 64], BCb[:, 2 + h // 4, :], STb[:, h, :], True, True, (BCb, STb), (py2,))
                    tt("dve", v3(ytmp[:], 8), v3(py2[:], 8), SM(24, 8).unsqueeze(2).broadcast_to([128, 8, 64]), ALU.mult,
                       (py2, smB[10]), (ytmp,))
                    tt("dve", ysd[:], ytmp[:], py1[:], ALU.add, (ytmp, py1), (ysd,))
                    tt("dve", ysd[:], ysd[:], dx[:], ALU.add, (ysd, dx), (ysd,))
                    yield
                pst = PS()
                for h in range(8):
                    mm(pst[:, h * 64:(h + 1) * 64], Btm[:, h // 4, :], Xd[:, h * 64:(h + 1) * 64], True, True, (Btm, Xd), (pst,))
                tt("dve", STf[:], STf[:], SM(40, 8).unsqueeze(2).broadcast_to([128, 8, 64]), ALU.mult, (STf, smB[10]), (STf,))
                tt("dve", STf[:].rearrange("p h d -> p (h d)"), STf[:].rearrange("p h d -> p (h d)"), pst[:], ALU.add,
                   (STf, pst), (STf,))
                if i == B0T - 1:
                    ts("dve", STf[:], STf[:], flag[:], None, ALU.mult, None, (STf, flag), (STf,))
                cp("act", STb[:], STf[:], (STf,), (STb,))
                yield
                if full:
                    tt("dve", ysd[:], ysd[:], gz[p][:], ALU.mult, (ysd, gz[p]), (ysd,))
                    for g in range(2):
                        act(ytmp[:, g * 256:(g + 1) * 256], ysd[:, g * 256:(g + 1) * 256], AF.Square, (ysd,), (ytmp, smB[12]),
                            accum=SM(56 + g))
                    ts("dve", SM(58, 2), SM(56, 2), 1.0 / 256, 1e-6, ALU.mult, ALU.add, (smB[12],), (smB[13],))
                    rsqrt_small(SM(60, 2), SM(58, 2), 2, (smB[13],), (smB[14],))
                    tt("dve", v3(ysd[:], 2), v3(ysd[:], 2), SM(60, 2).unsqueeze(2).broadcast_to([128, 2, 256]), ALU.mult,
                       (ysd, smB[14]), (ysd,))
                    tt("dve", ytok[p][:, 0:512], ysd[:], R("ssdg", 512), ALU.mult, (ysd, rows), (ytok[p],))


            def rwkv_out(i):
                p = i % 2
                full = i >= B0T - 1
                OP()
                act(lin[0:64, :], Pp[0:64, 12, :], AF.Tanh, (Pp,), (lin,))
                tt("dve", kkv[:], Pp[:, 4:8, :], par[:, PC["kk"]:PC["kk"] + 4].unsqueeze(2).broadcast_to([128, 4, 128]),
                   ALU.mult, (Pp, par), (kkv,))
                cp("act", vb[:], Pp[:, 8:12, :], (Pp,), (vb,))
                plw = PS()
                pla = PS()
                for c in range(4):
                    mm(plw[:, c * 128:(c + 1) * 128], Wlora[0:64, c * 128:(c + 1) * 128], lin[0:64, :], True, True,
                       (Wlora, lin), (plw,))
                    mm(pla[:, c * 128:(c + 1) * 128], Wlora[64:128, c * 128:(c + 1) * 128], Pp[64:128, 12, :], True, True,
                       (Wlora, Pp), (pla,))
                act(kk2[:], kkv[:], AF.Square, (kkv,), (kk2,))
                pkn = PS()
                mm(pkn[:], C("onesblk"), kk2[:].rearrange("p a b -> p (a b)"), True, True, (cst, kk2), (pkn,))
                for c in range(4):
                    trn(pTR[:, c * 128:(c + 1) * 128], vb[:, c, :], identb[:], (vb, identb), (pTR,))
                cp("act", vtm[:], pTR[:, 0:512], (pTR,), (vtm,))
                for c in range(4):
                    act(thw[:, c, :], plw[:, c * 128:(c + 1) * 128], AF.Tanh, (plw, hw0), (thw,), bias=hw0[:, c:c + 1], scale=0.5)
                    act(tha[:, c, :], pla[:, c * 128:(c + 1) * 128], AF.Tanh, (pla, hw0), (tha,), bias=hw0[:, 4 + c:5 + c], scale=0.5)
                act(inv[:].rearrange("p a b -> p (a b)"), pkn[:], AF.Ln, (pkn,), (inv,), bias=tiny[:])
                act(inv[:], inv[:], AF.Exp, (inv,), (inv,), scale=-0.5)
                ts("dve", lw[:], thw[:], -0.30326533, -0.30326533, ALU.mult, ALU.add, (thw,), (lw,))
                ts("dve", av[:], tha[:], 0.5, 0.5, ALU.mult, ALU.add, (tha,), (av,))
                tt("dve", kkn[:], kkv[:], inv[:], ALU.mult, (kkv, inv), (kkn,))
                FP()
                OP()
                for c in range(4):
                    S.op("dve", lambda e, c=c: e.tensor_tensor_scan(out=cum[:, c, :], data0=lw[:, c, :], data1=zeros[:],
                                                                      initial=0.0, op0=ALU.add, op1=ALU.add),
                         rd(lw, zeros), rd(cum))
                act(E1[:], cum[:], AF.Exp, (cum,), (E1,))
                act(E2[:], cum[:], AF.Exp, (cum,), (E2,), scale=-1.0)
                FP()
                OP()
                stt("dve", t1[:], av[:], -1.0, par[:, PC["ka"]:PC["ka"] + 4].unsqueeze(2).broadcast_to([128, 4, 128]),
                    ALU.add, ALU.mult, (av, par), (t1,))
                stt("dve", kmod[:], t1[:], 1.0, Pp[:, 4:8, :], ALU.add, ALU.mult, (t1, Pp), (kmod,))
                FP()
                tt("dve", beta[:], kkn[:], av[:], ALU.mult, (kkn, av), (beta,))
                for hp in range(2):
                    sl = slice(hp * 64, (hp + 1) * 64)
                    stt("dve", arz[hp][sl, :, 0, :], kkn[sl, :, :], -1.0, E3[sl, :, :], ALU.mult, ALU.mult, (kkn, E3), (arz[hp],))
                tt("dve", btl[:], beta[:], E2[:], ALU.mult, (beta, E2), (btl,))
                tt("dve", ktl[:], kmod[:], E2[:], ALU.mult, (kmod, E2), (ktl,))
                if full:
                    for hp in range(2):
                        sl = slice(hp * 64, (hp + 1) * 64)
                        tt("dve", arz[hp][sl, :, 1, :], Pp[sl, 0:4, :], E1[sl, :, :], ALU.mult, (Pp, E1), (arz[hp],))
                if full:
                    tt("dve", rk[:], Pp[:, 0:4, :], kmod[:], ALU.mult, (Pp, kmod), (rk,))
                if full:
                    pbn = PS()
                    for c in range(4):
                        mm(pbn[:, 0:8], rk[:, c, :], sel[:, c, :], c == 0, c == 3, (rk, sel), (pbn,))
                    cp("act", bnsb[:], pbn[:, 0:8], (pbn,), (bnsb,))
                for c in range(4):
                    trn(pTR[:, c * 128:(c + 1) * 128], btl[:, c, :], identb[:], (btl, identb), (pTR,))
                    trn(pTR[:, 512 + c * 128:512 + (c + 1) * 128], ktl[:, c, :], identb[:], (ktl, identb), (pTR,))
                cp("act", btm[:], pTR[:, 0:512], (pTR,), (btm,))
                cp("dve", ktm[:], pTR[:, 512:1024], (pTR,), (ktm,))
                FP()
                drain(ogen)
                if full:
                    act(gsb[:, 0:128], Pp[:, 13, :], AF.Tanh, (Pp,), (gsb,), scale=0.5)
                    ts("dve", sg[:], gsb[:, 0:128], 1.0, None, ALU.add, None, (gsb,), (sg,))
                    pgt = PS()
                    mm(pgt[:], sg[:], G2h[:], True, True, (sg, G2h), (pgt,))
                    cp("act", gsb[:], pgt[:], (pgt,), (gsb,))
                if full:
                    tt("dve", bterm[:], v3(vtm[:], 8), bnsb[:].unsqueeze(2).broadcast_to([128, 8, 64]), ALU.mult,
                       (vtm, bnsb), (bterm,))
                for c in range(4):
                    pab = PS()
                    pak = PS()
                    pnt = PS()
                    for hp in range(2):
                        arv = GAz[hp][:, c * 256:(c + 1) * 256]
                        mm(pab[:, hp * 256:(hp + 1) * 256], btl[:, c, :], arv, True, True, (btl, arz[hp]), (pab,))
                        mm(pak[:, hp * 256:(hp + 1) * 256], ktl[:, c, :], arv, True, True, (ktl, arz[hp]), (pak,))
                        mm(pnt[:, hp * 128:(hp + 1) * 128], arz[hp][:, c, 0, :], btl[:, c, :], True, True, (arz[hp], btl), (pnt,))
                    tt("dve", ABs[c][:], v3(pab[:], 2), maskSI[:].unsqueeze(1).broadcast_to([128, 2, 256]), ALU.mult,
                       (pab, maskSI), (ABs[c],))
                    tt("dve", AKs[c][:], v3(pak[:], 2), maskSI[:].unsqueeze(1).broadcast_to([128, 2, 256]), ALU.mult,
                       (pak, maskSI), (AKs[c],))
                    tt("dve", PTm[0][c][:], v3(pnt[:, 0:256], 2), C("trils").unsqueeze(1).broadcast_to([128, 2, 128]),
                       ALU.mult, (pnt, cst), (PTm[0][c],))
                    FP()
                    EP()
                    EP()
                pu = pool_banks[0]
                for h in range(8):
                    c, hp = h // 2, h % 2
                    mm(pu[:, h * 64:(h + 1) * 64], arz[hp][:, c, 0, :], STrb[:, c, :], True, False, (arz[hp], STrb), (pu,))
                    mm(pu[:, h * 64:(h + 1) * 64], AKs[c][:, hp, 0:128], vtm[:, h * 64:(h + 1) * 64], False, True,
                       (AKs[c], vtm), (pu,))
                cp("act", UTf[:], pu[:], (pu,), (UTf,))
                cp("dve", UT[:].rearrange("p h d -> p (h d)"), UTf[:], (UTf,), (UT,))
                FP()
                for lvl in range(7):
                    q = lvl % 2
                    pd = PS()
                    for h in range(8):
                        c, hp = h // 2, h % 2
                        Pl, PlB = (ABs[c][:, hp, 0:128], ABs[c]) if lvl == 0 else (Pm[q][c][:, hp, :], Pm[q][c])
                        mm(pd[:, h * 64:(h + 1) * 64], Pl, UT[:, h, :], True, True, (PlB, UT), (pd,))
                    tt("dve", UTf[:], UTf[:], pd[:], ALU.add, (UTf, pd), (UTf,))
                    cp("act", UT[:].rearrange("p h d -> p (h d)"), UTf[:], (UTf,), (UT,))
                    EP()
                    SP()
                    if lvl < 6:
                        for c in range(4):
                            pp = PS()
                            for hp in range(2):
                                Pl, PlB = (ABs[c][:, hp, 0:128], ABs[c]) if lvl == 0 else (Pm[q][c][:, hp, :], Pm[q][c])
                                mm(pp[:, hp * 128:(hp + 1) * 128], PTm[q][c][:, hp, :], Pl, True, True,
                                   (PTm[q][c], PlB), (pp,))
                                mm(pp[:, 256 + hp * 128:256 + (hp + 1) * 128], Pl, PTm[q][c][:, hp, :], True, True,
                                   (PTm[q][c], PlB), (pp,))
                            cp("dve" if c % 2 == 0 else "act", PPm[1 - q][c][:].rearrange("p a b c -> p (a b c)"), pp[:],
                               (pp,), (PPm[1 - q][c],))
                            if c == 1:
                                SP()
                psr = PS()
                for c in range(4):
                    o = psr[:, c * 128:(c + 1) * 128]
                    mm(o, btm[:, c * 128:(c + 1) * 128], UT[:, 2 * c:2 * c + 2, :].rearrange("p a b -> p (a b)"), True, False,
                       (btm, UT), (psr,))
                    mm(o, ktm[:, c * 128:(c + 1) * 128], vtm[:, c * 128:(c + 1) * 128], False, True, (ktm, vtm), (psr,))
                if full:
                    pyr = PS()
                    for h in range(8):
                        c, hp = h // 2, h % 2
                        o = pyr[:, h * 64:(h + 1) * 64]
                        mm(o, arz[hp][:, c, 1, :], STrb[:, c, :], True, False, (arz[hp], STrb), (pyr,))
                        mm(o, ABs[c][:, hp, 128:256], UT[:, h, :], False, False, (ABs[c], UT), (pyr,))
                        mm(o, AKs[c][:, hp, 128:256], vtm[:, h * 64:(h + 1) * 64], False, True, (AKs[c], vtm), (pyr,))
                psr3 = v3(psr[:], 4)
                tt("dve", tmpS[0:64, :, :], STr[0:64, :, :], psr3[0:64, :, 0:64], ALU.add, (STr, psr), (tmpS,))
                tt("dve", tmpS[64:128, :, :], STr[64:128, :, :], psr3[64:128, :, 64:128], ALU.add, (STr, psr), (tmpS,))
                tt("dve", STr[:], tmpS[:], E1[:, :, 127:128].broadcast_to([128, 4, 64]), ALU.mult, (tmpS, E1), (STr,))
                if i == B0T - 1:
                    ts("dve", STr[:], STr[:], flag[:], None, ALU.mult, None, (STr, flag), (STr,))
                cp("act", STrb[:], STr[:], (STr,), (STrb,))
                SP()
                if full:
                    cp("act", yr[:].rearrange("p h d -> p (h d)"), pyr[:], (pyr,), (yr,))
                    ogen[0] = out_gen(i)

            def out_gen(i):
                p = i % 2
                slot = i - (B0T - 1)
                act(ysq[:], yr[:], AF.Square, (yr,), (ysq,))
                S.op("dve", lambda e: e.tensor_reduce(out=gn[:, 0:8], in_=yr[:], axis=AX.X, op=ALU.add), rd(yr), rd(gn))
                S.op("dve", lambda e: e.tensor_reduce(out=gn[:, 8:16], in_=ysq[:], axis=AX.X, op=ALU.add), rd(ysq), rd(gn))
                ts("dve", gn[:, 0:16], gn[:, 0:16], 1.0 / 64, None, ALU.mult, None, (gn,), (gn,))
                tt("dve", gn[:, 16:24], gn[:, 0:8], gn[:, 0:8], ALU.mult, (gn,), (gn,))
                tt("dve", gn[:, 24:32], gn[:, 8:16], gn[:, 16:24], ALU.subtract, (gn,), (gn,))
                ts("dve", gn[:, 24:32], gn[:, 24:32], 64e-5, None, ALU.add, None, (gn,), (gn,))
                rsqrt_small(gn[:, 32:40], gn[:, 24:32], 8, (gn,), (gn,))
                tt("dve", yr[:], yr[:], gn[:, 0:8].unsqueeze(2).broadcast_to([128, 8, 64]), ALU.subtract, (yr, gn), (yr,))
                tt("dve", yr[:], yr[:], gn[:, 32:40].unsqueeze(2).broadcast_to([128, 8, 64]), ALU.mult, (yr, gn), (yr,))
                yr2 = yr[:].rearrange("p h d -> p (h d)")
                tt("dve", yr2, yr2, R("lnw", 512), ALU.mult, (yr, rows), (yr,))
                tt("dve", yr2, yr2, R("lnb", 512), ALU.add, (yr, rows), (yr,))
                tt("dve", yr[:], yr[:], bterm[:], ALU.add, (yr, bterm), (yr,))
                tt("dve", ytok[p][:, 512:1024], yr2, gsb[:], ALU.mult, (yr, gsb), (ytok[p],))
                yield
                for c in range(8):
                    trn(pTR[:, c * 128:(c + 1) * 128], ytok[p][:, c * 128:(c + 1) * 128], identb[:], (ytok[p], identb), (pTR,))
                cp("act", mixT[:].rearrange("p a b -> p (a b)"), pTR[:], (pTR,), (mixT,))
                yield
                pm = [PS(), PS()]
                for hf in range(2):
                    for c in range(8):
                        mm(pm[hf][:], mixT[:, c, :], Wout[:, c, hf * 512:(hf + 1) * 512], c == 0, c == 7, (mixT, Wout), (pm[hf],))
                dma("sp", ht[:], D["x_in"][i * TT:(i + 1) * TT, :], "xr", (), (ht,))
                for hf in range(2):
                    act(B7[:, hf * 512:(hf + 1) * 512], pm[hf][:], AF.Square, (pm[hf],), (B7, hsm), accum=hs[:, hf:hf + 1])
                tt("dve", hs[:, 2:3], hs[:, 0:1], hs[:, 1:2], ALU.add, (hsm,), (hsm,))
                ts("dve", hs[:, 2:3], hs[:, 2:3], 1.0 / 1024, 1e-6, ALU.mult, ALU.add, (hsm,), (hsm,))
                rsqrt_small(hs[:, 3:4], hs[:, 2:3], 1, (hsm,), (hsm,))
                for hf in range(2):
                    stt("dve", B7[:, hf * 512:(hf + 1) * 512], pm[hf][:], hs[:, 3:4], R("pmn", 1024)[:, hf * 512:(hf + 1) * 512],
                        ALU.mult, ALU.mult, (pm[hf], hsm, rows), (B7,))
                tt("dve", ht[:], ht[:], B7[:], ALU.add, (ht, B7), (ht,))
                dma("sp", hscr[slot * TT:(slot + 1) * TT, :], ht[:], "hst", (ht,), (hscrB,))
                yield

            ogen = [None]

            def OP():
                if ogen[0] is not None:
                    next(ogen[0], None)

            fgen = [None]
            egen = [None]
            sgen = [None]

            def FP():
                if fgen[0] is not None:
                    next(fgen[0], None)

            def need_banks(n):
                while fstate[0] < n:
                    if fgen[0] is None or next(fgen[0], "done") == "done":
                        break

            def need_front_done():
                if fgen[0] is not None:
                    for _ in fgen[0]:
                        pass

            def EP():
                if egen[0] is not None:
                    next(egen[0], None)

            def need_conv():
                while estate[0] < 8:
                    if egen[0] is None or next(egen[0], "done") == "done":
                        break

            def SP():
                if sgen[0] is not None:
                    next(sgen[0], None)

            def drain(g):
                if g[0] is not None:
                    for _ in g[0]:
                        pass
                g[0] = None

            fgen[0] = front_gen(0)
            drain(fgen)
            fstate[0] = 8
            egen[0] = early_gen(0)
            drain(egen)
            estate[0] = 8
            sgen[0] = ssd_gen(0)
            drain(sgen)
            CK(2)
            for i in range(NT_ALL):
                if i + 1 < NT_ALL:
                    fgen[0] = front_gen(i + 1)
                    egen[0] = early_gen(i + 1)
                    sgen[0] = ssd_gen(i + 1)
                FP()
                FP()
                rwkv_out(i)
                drain(fgen)
                drain(egen)
                drain(sgen)
                CK(5)
            drain(ogen)
            S.barrier()
            CK(6)

        rot_set[0] = [4, 5]
        TBK = 256
        NTB = NT_MAIN * TT // TBK
        with ExitStack() as stB:
            Wup = sb("Wup", [128, 8, 5632], BF16, stB)
            Wdown = sb("Wdown", [128, 22, 1024], BF16, stB)
            pfn = sb("pfn", [128, 1024], F32, stB)
            dma("sp", pfn[:], D["rows"][:, RC["pfn"]:RC["pfn"] + 1024], "c6", (), (pfn,))
            WupB = [[Buf("wup%d_%d" % (k, hf)) for hf in range(2)] for k in range(8)]
            WdB = [Buf("wd%d" % f3) for f3 in range(8)]
            with ExitStack() as stS:
                stage = [sb("stageB%d" % q, [128, 3072], F32, stS) for q in range(3)]
                sq = [0]

                def load_rows2(src_ap, ncols, fn):
                    si = sq[0] % 3
                    stg = stage[si]
                    sq[0] += 1
                    dma("sp", stg[:, 0:ncols], src_ap, "wsu%d" % si, (), (stg,))
                    fn(stg, sq[0])

                def cast_up(stg, n, k, hf):
                    o = Wup[:, k, hf * 2816:(hf + 1) * 2816]
                    e = n % 2
                    if e == 0:
                        ts("dve", o, stg[:, 0:2816], P("g2n", k), None, ALU.mult, None, (stg, par), (WupB[k][hf],))
                    else:
                        act(o, stg[:, 0:2816], AF.Copy, (stg, par), (WupB[k][hf],), scale=P("g2n", k))
                for k in range(8):
                    for hf in range(2):
                        load_rows2(D["wup"][:, k * 5632 + hf * 2816:k * 5632 + (hf + 1) * 2816], 2816,
                                   lambda stg, n, k=k, hf=hf: cast_up(stg, n, k, hf))
                for f3 in range(8):
                    n3 = min(3, 22 - 3 * f3)
                    load_rows2(D["wdown"][:, 3 * f3 * 1024:(3 * f3 + n3) * 1024], n3 * 1024,
                               lambda stg, n, f3=f3, n3=n3: cp(("act", "dve")[n % 2], Wdown[:, 3 * f3:3 * f3 + n3, :],
                                                               v3(stg[:, 0:n3 * 1024], n3), (stg,), (WdB[f3],)))
                S.barrier()
            CK(7)
            hld = [sb("hld%d" % q, [128, 2, 1024], F32, stB) for q in range(2)]
            hbB = sb("hbB", [128, 1024], BF16, stB)
            hnB = [sb("hnB%d" % q, [128, 8, 2 + TBK], BF16, stB) for q in range(2)]
            U = [sb("U%d" % q, [128, 2, 2, 2 + TBK], F32, stB) for q in range(2)]
            cacc = [[sb("cacc%d_%d" % (q, gv), [128, 2, TBK], F32, stB) for gv in range(2)] for q in range(2)]
            sgt = [sb("sgt%d" % q, [128, 2, TBK], F32, stB) for q in range(2)]
            actT = [sb("actT%d" % q, [128, 2, TBK], BF16, stB) for q in range(2)]
            carry = sb("carry", [128, 22, 2, 2], F32, stB)
            carryB = [Buf("carry%d" % f) for f in range(22)]
            fo = [sb("fo%d" % q, [128, 1024], F32, stB) for q in range(2)]
            fs = sb("fs", [128, 16], F32, stB)
            junkF = sb("junkF", [128, 1024], BF16, stB)
            fsm = Buf("fsm")
            outB = Buf("out")
            last = []
            memset("pool", carry[:], 0.0, carryB)

            def prep_sub(row0, q, sub, dst, dcol):
                dma("sp", hld[q][:, sub, :], hscr[row0:row0 + TT, :], "hld%d" % q, (hscrB,), (hld[q],))
                act(hbB[:], hld[q][:, sub, :], AF.Square, (hld[q],), (hbB, fsm), accum=fs[:, 8:9])
                ts("dve", fs[:, 9:10], fs[:, 8:9], 1.0 / 1024, 1e-6, ALU.mult, ALU.add, (fsm,), (fsm,))
                rsqrt_small(fs[:, 10:11], fs[:, 9:10], 1, (fsm,), (fsm,))
                act(hbB[:], hld[q][:, sub, :], AF.Copy, (hld[q], fsm), (hbB,), scale=fs[:, 10:11])
                for k in range(8):
                    trn(pT[:, k * 128:(k + 1) * 128], hbB[:, k * 128:(k + 1) * 128], identb[:], (hbB, identb), (pT,))
                cp("dve", dst[:, :, dcol:dcol + TT], v3(pT[:], 8), (pT,), (dst,))

            prep_sub(0, 1, 0, hnB[1], 2)
            ts("pool", hnB[1][:, :, 128:130], hnB[1][:, :, 128:130], flag[:], None, ALU.mult, None, (hnB[1], flag), (hnB[1],))
            for f in range(22):
                pc = PS()
                for gv in range(2):
                    col = gv * 2816 + f * 128
                    for k in range(8):
                        mm(pc[:, gv * 2:gv * 2 + 2], Wup[:, k, col:col + 128], hnB[1][:, k, 128:130], k == 0, k == 7,
                           (WupB[k][gv], hnB[1]), (pc,))
                cp("act", carry[:, f, :, :].rearrange("p a b -> p (a b)"), pc[:, 0:4], (pc,), (carryB[f // 2],))

            def prep_gen(tb):
                for sub in range(2):
                    prep_sub(TT + tb * TBK + sub * TT, tb % 2, sub, hnB[tb % 2], 2 + sub * TT)
                    yield

            for _ in prep_gen(0):
                pass
            for tb in range(NTB):
                q2 = tb % 2
                pgen = prep_gen(tb + 1) if tb + 1 < NTB else None
                pf = [[pool_banks[0], pool_banks[1]], [pool_banks[2], pool_banks[3]]]
                pub = [pool_banks[4], pool_banks[5]]

                def A_pe(fp):
                    for j in range(2):
                        f = 2 * fp + j
                        for gv in range(2):
                            col = gv * 2816 + f * 128
                            for k in range(8):
                                mm(pub[j][:, gv * TBK:(gv + 1) * TBK], Wup[:, k, col:col + 128], hnB[q2][:, k, 2:2 + TBK], k == 0, k == 7,
                                   (WupB[k][gv], hnB[q2]), (pub[j],))

                def A_rest(fp):
                    q = fp % 2
                    Uq = U[q]
                    for j in range(2):
                        cp("act", Uq[:, j, :, 2:2 + TBK], v3(pub[j][:], 2), (pub[j],), (Uq,))
                    f0 = 2 * fp
                    cp("dve", Uq[:, :, :, 0:2], carry[:, f0:f0 + 2, :, :], (carryB[fp],), (Uq,))
                    cp("dve", carry[:, f0:f0 + 2, :, :], Uq[:, :, :, TBK:TBK + 2], (Uq,), (carryB[fp],))
                    ca = cacc[q]
                    for j in range(2):
                        for gv in range(2):
                            ch = gv * 22 + f0 + j
                            act(ca[gv][:, j, :], Uq[:, j, gv, 2:2 + TBK], AF.Identity, (Uq, par), (ca[gv],), bias=P("fcb", ch),
                                scale=P("fcw", ch * 3 + 2))

                def B_nonpe(fp):
                    q = fp % 2
                    Uq = U[q]
                    ca = cacc[q]
                    f0 = 2 * fp
                    for k in range(2):
                        for j in range(2):
                            for gv in range(2):
                                ch = gv * 22 + f0 + j
                                stt("dve", ca[gv][:, j, :], Uq[:, j, gv, k:k + TBK], P("fcw", ch * 3 + k), ca[gv][:, j, :], ALU.mult, ALU.add,
                                    (Uq, par, ca[gv]), (ca[gv],))
                    act(sgt[q][:], ca[0][:], AF.Silu, (ca[0],), (sgt[q],))
                    tt("dve", actT[q][:], sgt[q][:], ca[1][:], ALU.mult, (sgt[q], ca[1]), (actT[q],))

                def B_pe(fp):
                    q = fp % 2
                    a3 = actT[q]
                    for j in range(2):
                        f = 2 * fp + j
                        for sub in range(2):
                            for hf in range(2):
                                mm(pf[sub][hf][:], a3[:, j, sub * TT:(sub + 1) * TT], Wdown[:, f, hf * 512:(hf + 1) * 512], f == 0, f == 21,
                                   (a3, WdB[f // 3]), (pf[sub][hf],))

                A_pe(0)
                A_rest(0)
                for fp in range(11):
                    if fp + 1 < 11:
                        A_pe(fp + 1)
                    B_nonpe(fp)
                    B_pe(fp)
                    if fp + 1 < 11:
                        A_rest(fp + 1)
                    if pgen is not None and fp in (3, 7):
                        next(pgen, None)
                for sub in range(2):
                    for hf in range(2):
                        cp("act", fo[sub][:, hf * 512:(hf + 1) * 512], pf[sub][hf][:], (pf[sub][hf],), (fo[sub],))
                if pgen is not None:
                    for _ in pgen:
                        pass
                for sub in range(2):
                    fq = fo[sub]
                    o4 = 4 * sub
                    act(junkF[:], fq[:], AF.Square, (fq,), (junkF, fsm), accum=fs[:, o4:o4 + 1])
                    ts("dve", fs[:, o4 + 2:o4 + 3], fs[:, o4:o4 + 1], 1.0 / 1024, 1e-6, ALU.mult, ALU.add, (fsm,), (fsm,))
                    rsqrt_small(fs[:, o4 + 3:o4 + 4], fs[:, o4 + 2:o4 + 3], 1, (fsm,), (fsm,))
                    stt("dve", fq[:], fq[:], fs[:, o4 + 3:o4 + 4], pfn[:], ALU.mult, ALU.mult, (fq, fsm, pfn), (fq,))
                    tt("dve", fq[:], fq[:], hld[q2][:, sub, :], ALU.add, (fq, hld[q2]), (fq,))
                    r0 = tb * TBK + sub * TT
                    last.append(dma("sp", out[r0:r0 + TT, :], fq[:], "ost%d" % sub, (fq,), (outB,)))
            S.finish(last[-2:])
    except _Stop:
        pass
    return nc


_NC_CACHE = {}


def kernel(**inputs):
    inputs = {k: np.asarray(v) for k, v in inputs.items()}
    x = inputs["x"].astype(np.float32, copy=False)
    shared = _host_prep(inputs)
    in_maps = []
    for c in range(8):
        b, s = c // 2, c % 2
        m = dict(shared)
        if s == 0:
            m["x_in"] = np.ascontiguousarray(np.concatenate([x[b, 0:2048], x[b, 0:2048]], axis=0))
        else:
            m["x_in"] = np.ascontiguousarray(x[b])
        m["flag"] = np.full((128, 1), float(s), np.float32)
        in_maps.append(m)
    if "nc" not in _NC_CACHE:
        _NC_CACHE["nc"] = build_nc()
    res = run_bass_kernel_spmd(_NC_CACHE["nc"], in_maps, core_ids=list(range(8)))
    outp = np.empty((4, 4096, 1024), np.float32)
    for c in range(8):
        b, s = c // 2, c % 2
        outp[b, s * 2048:(s + 1) * 2048] = res.results[c]["out"]
    return outp
```

```python
from contextlib import ExitStack
import numpy as np
import concourse.bass as bass
import concourse.mybir as mybir
from concourse.bass_utils import run_bass_kernel_spmd

F32 = mybir.dt.float32
BF16 = mybir.dt.bfloat16
AF = mybir.ActivationFunctionType
ALU = mybir.AluOpType
AX = mybir.AxisListType

SAME_ENGINE_SYNC = True
NT_ALL = 32
NT_MAIN = 16
TT = 128
NFM = 22
HW = 131


class Buf:
    __slots__ = ("name", "w", "r")

    def __init__(self, name):
        self.name = name
        self.w = None
        self.r = []


class Sched:
    ENGS = ("pe", "act", "dve", "pool", "sp")

    def __init__(self, nc, stack):
        self.nc = nc
        self.stack = stack
        self.sems = {}
        self.cnt = {}
        for e in self.ENGS:
            self.sems[e] = stack.enter_context(nc.semaphore("s_" + e))
            self.cnt[e] = 0
        self.known = {e: {} for e in self.ENGS}
        self.snap = {}
        self.prog = {e: [] for e in self.ENGS}
        self.dma_sems = {}
        self.dma_cnt = {}

    def _sem_handle(self, k):
        return self.sems[k] if k in self.sems else self.dma_sems[k]

    def _need(self, eng, tok, acc):
        if tok is None:
            return
        semkey, val, src = tok
        if src == eng and (eng == "pe" or not SAME_ENGINE_SYNC):
            return
        if self.known[eng].get(semkey, 0) >= val:
            return
        if acc.get(semkey, 0) < val:
            acc[semkey] = val

    def _emit_waits(self, eng, acc):
        for semkey, val in acc.items():
            h = self._sem_handle(semkey)
            self.prog[eng].append(lambda e, h=h, val=val: e.wait_ge(h, val))
            k = self.known[eng]
            if k.get(semkey, 0) < val:
                k[semkey] = val
            if semkey in self.sems:
                sn = self.snap.get((semkey, val))
                if sn:
                    for kk, vv in sn.items():
                        if k.get(kk, 0) < vv:
                            k[kk] = vv

    def _deps(self, eng, reads, writes):
        acc = {}
        for b in reads:
            self._need(eng, b.w, acc)
        for b in writes:
            self._need(eng, b.w, acc)
            for t in b.r:
                self._need(eng, t, acc)
        self._emit_waits(eng, acc)

    def _mark(self, tok, reads, writes):
        for b in reads:
            b.r.append(tok)
            if len(b.r) > 24:
                b.r = b.r[-24:] if False else b.r
        for b in writes:
            b.w = tok
            b.r = []

    def op(self, eng, fn, reads=(), writes=()):
        self._deps(eng, reads, writes)
        self.cnt[eng] += 1
        n = self.cnt[eng]
        h = self.sems[eng]
        self.prog[eng].append(lambda e, fn=fn, h=h: fn(e).then_inc(h, 1))
        tok = (eng, n, eng)
        sn = dict(self.known[eng])
        sn[eng] = n
        self.snap[(eng, n)] = sn
        self._mark(tok, reads, writes)
        return tok

    def dma(self, queue, fn, semkey, reads=(), writes=()):
        if semkey not in self.dma_sems:
            self.dma_sems[semkey] = self.stack.enter_context(self.nc.semaphore("d_" + semkey))
            self.dma_cnt[semkey] = 0
        self._deps(queue, reads, writes)
        self.dma_cnt[semkey] += 16
        val = self.dma_cnt[semkey]
        h = self.dma_sems[semkey]
        self.prog[queue].append(lambda e, fn=fn, h=h: fn(e).then_inc(h, 16))
        tok = (semkey, val, "dma")
        self._mark(tok, reads, writes)
        return tok

    def barrier(self):
        toks = [(e, self.cnt[e], e) for e in self.ENGS if self.cnt[e] > 0]
        toks += [(k, v, "dma") for k, v in self.dma_cnt.items() if v > 0]
        for e in self.ENGS:
            acc = {}
            for t in toks:
                if t[0] == e:
                    continue
                self._need(e, t, acc)
            self._emit_waits(e, acc)

    def finish(self, final_toks):
        for t in final_toks:
            acc = {}
            self._need("sp", t, acc)
            self._emit_waits("sp", acc)
        with self.nc.Block() as block:
            @block.tensor
            def _(e):
                for f in self.prog["pe"]:
                    f(e)

            @block.scalar
            def _(e):
                for f in self.prog["act"]:
                    f(e)

            @block.vector
            def _(e):
                for f in self.prog["dve"]:
                    f(e)

            @block.gpsimd
            def _(e):
                for f in self.prog["pool"]:
                    f(e)

            @block.sync
            def _(e):
                for f in self.prog["sp"]:
                    f(e)


PC = {}
_o = 0
for _n, _w in (("cw", 32), ("cb", 8), ("mu", 14), ("w0", 4), ("a0", 4), ("kk", 4), ("ka", 4),
               ("fcw", 132), ("fcb", 44), ("g1", 8), ("g2n", 8)):
    PC[_n] = _o
    _o += _w
NPAR = _o
RC = {}
_o = 0
for _n, _w in (("dtb", 8), ("alog", 8), ("dsk", 8), ("ssdg", 512), ("lnw", 512), ("lnb", 512),
               ("pmn", 1024), ("pfn", 1024)):
    RC[_n] = _o
    _o += _w
NROW = _o
CC = {"ident": 0, "triu": 128, "trius": 256, "trils": 384, "onesblk": 512, "ones": 640}
NCONST = 768


def _cols(v, nch):
    return np.ascontiguousarray(v.reshape(nch, 128).T)


def _host_prep(inp):
    f = np.float32
    w_in = inp["w_in"][0]
    s0 = 0
    r0 = 1544
    perm = np.concatenate([
        np.arange(512, 1536),
        r0 + np.arange(0, 512),
        r0 + np.arange(576, 1088),
        r0 + np.arange(1088, 1600),
        r0 + np.arange(512, 576),
        r0 + np.arange(1600, 1664),
        r0 + np.arange(1664, 1792),
        np.arange(0, 512),
        np.arange(1536, 1544),
    ])
    win = np.ascontiguousarray(w_in[:, perm].reshape(8, 128, 3336).transpose(1, 0, 2)).reshape(128, 8 * 3336)
    wout = np.ascontiguousarray(inp["w_out"][0].reshape(8, 128, 1024).transpose(1, 0, 2)).reshape(128, 8 * 1024)
    wup = np.ascontiguousarray(inp["ffn_w_up"][0].reshape(8, 128, 5632).transpose(1, 0, 2)).reshape(128, 8 * 5632)
    wdown = np.ascontiguousarray(inp["ffn_w_down"][0].reshape(22, 128, 1024).transpose(1, 0, 2)).reshape(128, 22 * 1024)
    par = np.zeros((128, NPAR), f)
    cw = inp["ssd_conv_w"][0]
    par[:, PC["cw"]:PC["cw"] + 32] = cw.reshape(4, 8, 128).transpose(2, 1, 0).reshape(128, 32)
    par[:, PC["cb"]:PC["cb"] + 8] = _cols(inp["ssd_conv_b"][0], 8)
    mu = inp["rwkv_mu"][0]
    mup = np.concatenate([mu[0:512], mu[576:1088], mu[1088:1600], mu[512:576], mu[1600:1664], mu[1664:1792]])
    par[:, PC["mu"]:PC["mu"] + 14] = _cols(mup, 14)
    par[:, PC["w0"]:PC["w0"] + 4] = _cols(inp["rwkv_w0"][0], 4)
    par[:, PC["a0"]:PC["a0"] + 4] = _cols(inp["rwkv_a0"][0], 4)
    par[:, PC["kk"]:PC["kk"] + 4] = _cols(inp["rwkv_k_k"][0], 4)
    par[:, PC["ka"]:PC["ka"] + 4] = _cols(inp["rwkv_k_a"][0], 4)
    fw = inp["ffn_conv_w"][0]
    par[:, PC["fcw"]:PC["fcw"] + 132] = fw.reshape(3, 44, 128).transpose(2, 1, 0).reshape(128, 132)
    par[:, PC["fcb"]:PC["fcb"] + 44] = _cols(inp["ffn_conv_b"][0], 44)
    par[:, PC["g1"]:PC["g1"] + 8] = _cols(inp["pre_mix_norm"][0], 8)
    par[:, PC["g2n"]:PC["g2n"] + 8] = _cols(inp["pre_ffn_norm"][0], 8)
    rows = np.zeros((128, NROW), f)

    def put(name, v):
        rows[:, RC[name]:RC[name] + v.size] = np.broadcast_to(v.reshape(1, -1), (128, v.size))
    put("dtb", inp["ssd_dt_bias"][0])
    put("alog", inp["ssd_a_log"][0])
    put("dsk", inp["ssd_d"][0])
    put("ssdg", inp["ssd_norm"][0])
    put("lnw", inp["rwkv_ln_w"][0])
    put("lnb", inp["rwkv_ln_b"][0])
    put("pmn", inp["post_mix_norm"][0])
    put("pfn", inp["post_ffn_norm"][0])
    consts = np.zeros((128, NCONST), f)
    i = np.arange(128)
    consts[:, 0:128] = np.eye(128)
    consts[:, 128:256] = (i[:, None] <= i[None, :])
    consts[:, 256:384] = (i[:, None] < i[None, :])
    consts[:, 384:512] = (i[:, None] > i[None, :])
    consts[:, 512:640] = ((i[:, None] // 64) == (i[None, :] // 64))
    consts[:, 640:768] = 1.0
    wlora = np.concatenate([inp["rwkv_w2"][0], inp["rwkv_a2"][0]], axis=0).astype(f)
    g2 = np.ascontiguousarray(inp["rwkv_g2"][0]).astype(f)
    rk = inp["rwkv_r_k"][0]
    sel = np.zeros((128, 4, 8), f)
    for c in range(4):
        for hp in range(2):
            sel[hp * 64:(hp + 1) * 64, c, 2 * c + hp] = rk[2 * c + hp]
    shared = {"win": win, "wout": wout, "wup": wup, "wdown": wdown, "params": par, "rows": rows,
              "consts": consts, "wlora": wlora, "g2": g2, "sel": sel.reshape(128, 32)}
    return {k: np.ascontiguousarray(v, dtype=f) for k, v in shared.items()}


NSLOT = NT_MAIN + 1
DBG = 0


class _Stop(Exception):
    pass


def build_nc():
    nc = bass.Bass("TRN2", target_bir_lowering=False)
    D = {}
    for name, shape in (("x_in", [4096, 1024]), ("win", [128, 8 * 3336]), ("wout", [128, 8192]),
                        ("wup", [128, 8 * 5632]), ("wdown", [128, 22 * 1024]), ("params", [128, NPAR]),
                        ("rows", [128, NROW]), ("consts", [128, NCONST]), ("wlora", [128, 512]),
                        ("g2", [128, 512]), ("sel", [128, 32]), ("flag", [128, 1])):
        D[name] = nc.dram_tensor(name, shape, F32, kind="ExternalInput").ap()
    out = nc.dram_tensor("out", [NT_MAIN * TT, 1024], F32, kind="ExternalOutput").ap()
    hscr = nc.dram_tensor("hscr", [NSLOT * TT, 1024], F32, kind="Internal").ap()

    try:
      with ExitStack() as st:
        S = Sched(nc, st)
        dbgt = st.enter_context(nc.sbuf_tensor("sb_dbgt", [128, 64], F32))

        def CK(n, src=None):
            if DBG == n:
                if src is None:
                    S.op("dve", lambda e: e.memset(dbgt[:], 1.0), (), ())
                    tk = S.dma("sp", lambda e: e.dma_start(out=out[0:128, 0:64], in_=dbgt[:]), "dbg", (), ())
                else:
                    ap, bufs, ncol = src
                    tk = S.dma("sp", lambda e: e.dma_start(out=out[0:128, 0:ncol], in_=ap), "dbg", [b.b for b in bufs], ())
                S.finish([tk])
                raise _Stop()

        class TB:
            def __init__(self, t, name, b=None):
                self.t = t
                self.b = b if b is not None else Buf(name)

            def __getitem__(self, k):
                return self.t[k]

        class View:
            def __init__(self, ap, b):
                self.ap = ap
                self.b = b

            def __getitem__(self, k):
                return self.ap[k]

        def sb(name, shape, dt, stack=st):
            return TB(stack.enter_context(nc.sbuf_tensor("sb_" + name, shape, dt)), name)

        def psum(name, shape, dt):
            t = TB(st.enter_context(nc.psum_tensor("ps_" + name, shape, dt)), name)
            PSB.add(id(t.b))
            return t

        PSB = set()

        def rd(*xs):
            return [x.b if hasattr(x, "b") else x for x in xs]

        def _xrw(r, w):
            r = rd(*r)
            w = rd(*w)
            rr = [x for x in r if id(x) not in PSB]
            ww = list(w) + [x for x in r if id(x) in PSB and x not in w]
            return rr, ww

        def tt(eng, o, a, b, op, r, w):
            S.op(eng, lambda e: e.tensor_tensor(out=o, in0=a, in1=b, op=op), *_xrw(r, w))

        def ts(eng, o, a, s1, s2, op0, op1, r, w):
            if s2 is None:
                S.op(eng, lambda e: e.tensor_scalar(out=o, in0=a, scalar1=s1, scalar2=None, op0=op0), *_xrw(r, w))
            else:
                S.op(eng, lambda e: e.tensor_scalar(out=o, in0=a, scalar1=s1, scalar2=s2, op0=op0, op1=op1), *_xrw(r, w))

        def stt(eng, o, a, s, b, op0, op1, r, w):
            S.op(eng, lambda e: e.scalar_tensor_tensor(out=o, in0=a, scalar=s, in1=b, op0=op0, op1=op1), *_xrw(r, w))

        def act(o, a, func, r, w, bias=None, scale=None, accum=None):
            kw = {}
            if bias is not None:
                kw["bias"] = bias
            if scale is not None:
                kw["scale"] = scale
            if accum is not None:
                kw["accum_out"] = accum
            S.op("act", lambda e: e.activation(out=o, in_=a, func=func, **kw), *_xrw(r, w))

        def cp(eng, o, a, r, w):
            if eng == "act":
                act(o, a, AF.Copy, r, w)
            else:
                S.op(eng, lambda e: e.tensor_copy(out=o, in_=a), *_xrw(r, w))

        def mm(o, l, rh, start, stop, r, w):
            S.op("pe", lambda e: e.matmul(out=o, lhsT=l, rhs=rh, start=start, stop=stop, skip_group_check=True), *_xrw(r, w))

        def trn(o, a, ident, r, w):
            S.op("pe", lambda e: e.transpose(out=o, in_=a, identity=ident), *_xrw(r, w))

        def memset(eng, o, v, w):
            S.op(eng, lambda e: e.memset(o, v), (), rd(*w))

        def dma(q, o, a, key, r, w):
            return S.dma(q, lambda e: e.dma_start(out=o, in_=a), key, *_xrw(r, w))

        def v3(ap, a):
            return ap.rearrange("p (a b) -> p a b", a=a)

        par = sb("par", [128, NPAR], F32)
        cst = sb("cst", [128, NCONST], F32)
        identb = sb("identb", [128, 128], BF16)
        flag = sb("flag", [128, 1], F32)
        neghalf = sb("neghalf", [128, 512], F32)
        zeros = sb("zeros", [128, 128], F32)
        pT = psum("pT", [128, 1024], BF16)
        pTR = psum("pTR", [128, 1024], BF16)
        pool_banks = [psum("pb%d" % i, [128, 512], F32) for i in range(6)]
        rot = [0]
        rot_set = [[1, 2, 3, 4, 5]]

        def PS():
            rs = rot_set[0]
            b = pool_banks[rs[rot[0] % len(rs)]]
            rot[0] += 1
            return b

        def C(name, n=128):
            return cst[:, CC[name]:CC[name] + n]

        def P(name, c):
            return par[:, PC[name] + c:PC[name] + c + 1]

        dma("sp", par[:], D["params"], "c0", (), (par,))
        dma("sp", cst[:], D["consts"], "c2", (), (cst,))
        dma("sp", flag[:], D["flag"], "c3", (), (flag,))
        cp("dve", identb[:], C("ident"), (cst,), (identb,))
        memset("pool", neghalf[:], -0.5, (neghalf,))
        memset("pool", zeros[:], 0.0, (zeros,))
        ts("dve", par[:, PC["cw"]:PC["cw"] + 40], par[:, PC["cw"]:PC["cw"] + 40], 0.5, None, ALU.mult, None, (par,), (par,))

        def rsqrt_small(o, a, n, r, w):
            tt("pool", o, a, neghalf[:, 0:n], ALU.pow, list(r) + [neghalf], w)

        hscrB = Buf("hscr")

        with ExitStack() as stA:
            rows = sb("rows", [128, 2584], F32, stA)

            def R(name, n):
                return rows[:, RC[name]:RC[name] + n]

            dma("sp", rows[:], D["rows"][:, 0:2584], "c1", (), (rows,))
            Win = sb("Win", [128, 8, 3336], BF16, stA)
            Wout = sb("Wout", [128, 8, 1024], BF16, stA)
            Wlora = sb("Wlora", [128, 512], F32, stA)
            G2h = sb("G2h", [128, 512], BF16, stA)
            sel = sb("sel", [128, 4, 8], F32, stA)
            arow = sb("arow", [128, 8], F32, stA)
            with ExitStack() as stS:
                stage = [sb("stageA%d" % q, [128, 3336], F32, stS) for q in range(2)]
                sq = [0]

                def load_rows(src_ap, ncols, fn):
                    stg = stage[sq[0] % 2]
                    sq[0] += 1
                    dma("sp", stg[:, 0:ncols], src_ap, "wst%d" % (sq[0] % 2), (), (stg,))
                    fn(stg)
                for k in range(8):
                    load_rows(D["win"][:, k * 3336:(k + 1) * 3336], 3336,
                              lambda stg, k=k: (ts("dve", Win[:, k, :], stg[:, 0:3336], P("g1", k), None,
                                                   ALU.mult, None, (stg, par), (Win,)) if k % 2 == 0 else
                                                act(Win[:, k, :], stg[:, 0:3336], AF.Copy, (stg, par), (Win,), scale=P("g1", k))))
                ts("pool", Win[:, :, 2816:3328], Win[:, :, 2816:3328], 0.5, None, ALU.mult, None, (Win,), (Win,))
                for k in range(8):
                    load_rows(D["wout"][:, k * 1024:(k + 1) * 1024], 1024,
                              lambda stg, k=k: cp("act", Wout[:, k, :], stg[:, 0:1024], (stg,), (Wout,)))
                load_rows(D["g2"], 512, lambda stg: ts("dve", G2h[:], stg[:, 0:512], 0.5, None, ALU.mult, None, (stg,), (G2h,)))
                dma("sp", Wlora[:], D["wlora"], "c4", (), (Wlora,))
                dma("sp", sel[:].rearrange("p a b -> p (a b)"), D["sel"], "c5", (), (sel,))
                act(arow[:], R("alog", 8), AF.Exp, (rows,), (arow,))
                ts("dve", arow[:], arow[:], -1.0, None, ALU.mult, None, (arow,), (arow,))
                S.barrier()
            CK(1)

            xt1 = sb("xt", [128, 1024], F32, stA)
            xt = [xt1, xt1]
            xT = [sb("xT%d" % i, [128, 8, HW], BF16, stA) for i in range(2)]
            raw = sb("raw", [128, NFM, HW], F32, stA)
            rawB = [Buf("raw_%d" % j) for j in range(8)]
            gz = [sb("gz%d" % i, [128, 512], F32, stA) for i in range(2)]
            dtr = [sb("dtr%d" % i, [128, 8], F32, stA) for i in range(2)]
            xb = sb("xb", [128, 1024], BF16, stA)
            hb = xb
            sm = sb("sm", [128, 64], F32, stA)
            smB = [Buf("sm%d" % i) for i in range(16)]
            Pp = sb("Pp", [128, 14, 128], F32, stA)
            BB = sb("BB", [128, 2048], F32, stA)
            AC = sb("AC", [128, 1024], F32, stA)
            B2 = sb("B2", [128, 1024], F32, stA)
            B3 = sb("B3", [128, 1024], F32, stA)
            B4 = sb("B4", [128, 512], F32, stA)
            B56 = sb("B56", [128, 2048], F32, stA)
            B7 = sb("B7", [128, 1024], F32, stA)
            ht = sb("ht", [128, 1024], F32, stA)
            X1 = sb("X1", [128, 512], BF16, stA)
            X2 = sb("X2", [128, 512], BF16, stA)
            X3 = sb("X3", [128, 512], BF16, stA)
            GA = sb("GA", [128, 1024], BF16, stA)
            Btm = sb("Btm", [128, 2, 128], BF16, stA)
            STf = sb("STf", [128, 8, 64], F32, stA)
            STb = sb("STb", [128, 8, 64], BF16, stA)
            ytok = [sb("ytok%d" % q, [128, 1024], BF16, stA) for q in range(2)]
            GA0 = sb("GA0", [128, 1024], BF16, stA)
            UTfT = sb("UTfT", [128, 512], F32, stA)
            tmpST = sb("tmpST", [128, 256], F32, stA)
            mixT = sb("mixT", [128, 8, 128], BF16, stA)
            lin = sb("lin", [128, 128], F32, stA)
            sg = sb("sg", [128, 128], BF16, stA)
            btm = sb("btm", [128, 512], BF16, stA)
            ktm = sb("ktm", [128, 512], BF16, stA)
            vtm = sb("vtm", [128, 512], BF16, stA)
            ABs = [sb("ABs%d" % c, [128, 2, 256], BF16, stA) for c in range(4)]
            AKs = [sb("AKs%d" % c, [128, 2, 256], BF16, stA) for c in range(4)]
            PPm = [[sb("PPm%d_%d" % (q, c), [128, 2, 2, 128], BF16, stA) for c in range(4)] for q in range(2)]
            Pm = [[View(PPm[q][c][:, 0, :, :], PPm[q][c].b) for c in range(4)] for q in range(2)]
            PTm = [[View(PPm[q][c][:, 1, :, :], PPm[q][c].b) for c in range(4)] for q in range(2)]
            UT = sb("UT", [128, 8, 64], BF16, stA)
            STr = sb("STr", [128, 4, 64], F32, stA)
            STrb = sb("STrb", [128, 4, 64], BF16, stA)
            maskSI = sb("maskSI", [128, 256], F32, stA)
            hw0 = sb("hw0", [128, 8], F32, stA)
            tiny = sb("tiny", [128, 1], F32, stA)
            memset("pool", tiny[:], 1e-24, (tiny,))
            bnsb = sb("bnsb", [128, 8], F32, stA)
            gn = sb("gn", [128, 40], F32, stA)
            hs = sb("hs", [128, 8], F32, stA)
            hsm = Buf("hsm")
            dlt = View(v3(B56[:, 0:1792], 14), B56.b)
            th = View(v3(BB[:, 0:1024], 8), BB.b)
            xax = View(v3(BB[:, 1024:1536], 4), BB.b)
            xs_tm = View(BB[:, 1536:2048], BB.b)
            Dm = View(v3(BB[:, 0:1024], 8), BB.b)
            lw = View(v3(BB[:, 0:512], 4), BB.b)
            av = View(v3(BB[:, 512:1024], 4), BB.b)
            cum = View(v3(BB[:, 1024:1536], 4), BB.b)
            cumex = View(v3(BB[:, 1536:2048], 4), BB.b)
            yr = View(v3(B56[:, 0:512], 8), B56.b)
            ysq = View(v3(B56[:, 512:1024], 8), B56.b)
            acc = View(v3(AC[:], 8), AC.b)
            rseg = View(v3(AC[:], 8), AC.b)
            thw = View(v3(AC[:, 0:512], 4), AC.b)
            tha = View(v3(AC[:, 512:1024], 4), AC.b)
            mn = View(AC[:], AC.b)
            dx = View(B2[:, 0:512], B2.b)
            ytmp = View(B2[:, 512:1024], B2.b)
            E2 = View(v3(B2[:, 0:512], 4), B2.b)
            tmpS = View(v3(tmpST[:], 4), tmpST.b)
            ysd = View(B3[:, 0:512], B3.b)
            scm = View(v3(B3[:, 512:768], 2), B3.b)
            kkv = View(v3(B3[:, 0:512], 4), B3.b)
            UTf = View(UTfT[:], UTfT.b)
            kk2 = View(v3(B3[:, 512:1024], 4), B3.b)
            E1x = sb("E1x", [128, 4, 129], F32, stA)
            memset("pool", E1x[:], 1.0, (E1x,))
            E1 = View(E1x[:, :, 1:129], E1x.b)
            E3 = View(E1x[:, :, 0:128], E1x.b)
            inv = View(v3(B4[:, 0:512], 4), B4.b)
            kkn = View(v3(B56[:, 0:512], 4), B56.b)
            kmod = View(v3(B56[:, 512:1024], 4), B56.b)
            t1 = View(v3(B56[:, 1024:1536], 4), B56.b)
            rk = View(v3(B56[:, 1024:1536], 4), B56.b)
            beta = View(v3(B56[:, 1536:2048], 4), B56.b)
            gsb = View(B7[:, 0:512], B7.b)
            bterm = View(v3(B7[:, 512:1024], 8), B7.b)
            Xb = View(X1[:], X1.b)
            btl = View(v3(X1[:], 4), X1.b)
            Xd = View(X2[:], X2.b)
            ktl = View(v3(X2[:], 4), X2.b)
            BCb = View(v3(X3[:], 4), X3.b)
            vb = View(v3(X3[:], 4), X3.b)
            Gm = View(v3(GA[:], 8), GA.b)
            GA1 = sb("GA1", [128, 1024], BF16, stA)
            arz = [View(GA0[:].rearrange("p (c a b) -> p c a b", c=4, a=2), GA0.b),
                   View(GA1[:].rearrange("p (c a b) -> p c a b", c=4, a=2), GA1.b)]
            GAz = [GA0, GA1]
            memset("pool", GA1[:], 0.0, (GA1,))
            memset("pool", GA0[:], 0.0, (GA0,))

            cp("dve", maskSI[:, 0:128], C("trius"), (cst,), (maskSI,))
            cp("dve", maskSI[:, 128:256], C("triu"), (cst,), (maskSI,))
            memset("dve", STf[:], 0.0, (STf,))
            memset("dve", STb[:], 0.0, (STb,))
            memset("dve", STr[:], 0.0, (STr,))
            memset("dve", STrb[:], 0.0, (STrb,))
            memset("pool", xT[1][:], 0.0, (xT[1],))
            memset("pool", xT[0][:], 0.0, (xT[0],))
            ts("dve", hw0[:, 0:4], par[:, PC["w0"]:PC["w0"] + 4], 0.5, None, ALU.mult, None, (par,), (hw0,))
            ts("dve", hw0[:, 4:8], par[:, PC["a0"]:PC["a0"] + 4], 0.5, None, ALU.mult, None, (par,), (hw0,))

            def SM(i, n=1):
                return sm[:, i:i + n]

            B0T = NT_ALL - NT_MAIN

            fstate = [0]

            def front_gen(i):
                p = i % 2
                fstate[0] = 0
                dma("sp", xt[p][:], D["x_in"][i * TT:(i + 1) * TT, :], "x0", (), (xt[p],))
                act(xb[:], xt[p][:], AF.Square, (xt[p],), (xb, smB[0]), accum=SM(0))
                ts("dve", SM(1), SM(0), 1.0 / 1024, 1e-6, ALU.mult, ALU.add, (smB[0],), (smB[1],))
                rsqrt_small(SM(2), SM(1), 1, (smB[1],), (smB[2],))
                act(xb[:], xt[p][:], AF.Copy, (xt[p], smB[2]), (xb,), scale=SM(2))
                yield
                for k in range(8):
                    trn(pT[:, k * 128:(k + 1) * 128], xb[:, k * 128:(k + 1) * 128], identb[:], (xb, identb), (pT,))
                cp("dve", xT[p][:, :, 3:HW], v3(pT[:], 8), (pT,), (xT[p],))
                cp("pool", xT[p][:, :, 0:3], xT[1 - p][:, :, 128:131], (xT[1 - p],), (xT[p],))
                if i == B0T:
                    ts("pool", xT[p][:, :, 0:3], xT[p][:, :, 0:3], flag[:], None, ALU.mult, None, (xT[p], flag), (xT[p],))
                yield
                hist = i < B0T - 1
                for bnk in range(8):
                    n = min(3, NFM - 3 * bnk)
                    if hist and bnk in (2, 3, 7):
                        fstate[0] = bnk + 1
                        continue
                    pb = PS()
                    for jj in range(n):
                        j = 3 * bnk + jj
                        for k in range(8):
                            mm(pb[:, jj * HW:(jj + 1) * HW], Win[:, k, j * 128:(j + 1) * 128], xT[p][:, k, :],
                               k == 0, k == 7, (Win, xT[p]), (pb,))
                    cp("act" if bnk % 2 == 0 else "dve", raw[:, 3 * bnk:3 * bnk + n, :], v3(pb[:, 0:n * HW], n),
                       (pb,), (rawB[bnk],))
                    fstate[0] = bnk + 1
                    yield
                full = i >= B0T - 1
                if full:
                    pz = PS()
                    for k in range(8):
                        mm(pz[:, 0:512], xT[p][:, k, 3:HW], Win[:, k, 2816:3328], k == 0, k == 7, (Win, xT[p]), (pz,))
                pd = PS()
                for k in range(8):
                    mm(pd[:, 0:8], xT[p][:, k, 3:HW], Win[:, k, 3328:3336], k == 0, k == 7, (Win, xT[p]), (pd,))
                if full:
                    act(gz[p][:], pz[:, 0:512], AF.Tanh, (pz,), (gz[p],))
                    stt("dve", gz[p][:], gz[p][:], 1.0, pz[:, 0:512], ALU.add, ALU.mult, (gz[p], pz), (gz[p],))
                tt("dve", dtr[p][:], pd[:, 0:8], R("dtb", 8), ALU.add, (pd, rows), (dtr[p],))

            estate = [0]

            def early_gen(i):
                rw = raw
                estate[0] = 0
                hist = i < B0T - 1
                for c in range(6 if hist else 8):
                    need_banks(c // 3 + 1)
                    rb = rawB[c // 3]
                    ts("dve", acc[:, c, :], rw[:, c, 3:HW], P("cw", c * 4 + 3), P("cb", c), ALU.mult, ALU.add,
                       (rb, par), (acc,))
                    for k in range(3):
                        stt("dve", acc[:, c, :], rw[:, c, k:k + 128], P("cw", c * 4 + k), acc[:, c, :], ALU.mult, ALU.add,
                            (rb, par, acc), (acc,))
                    estate[0] = c + 1
                    yield
                estate[0] = 8
                need_banks(8)
                c0, c1 = (4, 13) if hist else (0, 14)
                nc_ = c1 - c0
                RBr = rawB[4:7] if hist else rawB[2:8]
                tt("dve", dlt[:, c0:c1, :], rw[:, 8 + c0:8 + c1, 2:130], rw[:, 8 + c0:8 + c1, 3:HW], ALU.subtract, RBr, (dlt,))
                yield
                tt("dve", dlt[:, c0:c1, :], dlt[:, c0:c1, :],
                   par[:, PC["mu"] + c0:PC["mu"] + c1].unsqueeze(2).broadcast_to([128, nc_, 128]),
                   ALU.mult, (dlt, par), (dlt,))
                yield
                tt("dve", Pp[:, c0:c1, :], dlt[:, c0:c1, :], rw[:, 8 + c0:8 + c1, 3:HW], ALU.add, [dlt] + RBr, (Pp,))
                yield

            def ssd_gen(i):
                p = i % 2
                full = i >= B0T - 1
                need_front_done()
                need_conv()
                nch = 8 if full else 6
                act(th[:, 0:nch, :], acc[:, 0:nch, :], AF.Tanh, (acc,), (th,))
                stt("dve", xax[:], th[:, 0:4, :], 1.0, acc[:, 0:4, :], ALU.add, ALU.mult, (th, acc), (xax,))
                stt("dve", BCb[:, 0:nch - 4, :], th[:, 4:nch, :], 1.0, acc[:, 4:nch, :], ALU.add, ALU.mult, (th, acc), (BCb,))
                act(SM(8, 8), dtr[p][:], AF.Exp, (dtr[p],), (smB[8],))
                act(SM(8, 8), SM(8, 8), AF.Ln, (smB[8],), (smB[8],), bias=1.0)
                tt("dve", SM(16, 8), SM(8, 8), arow[:], ALU.mult, (smB[8], arow), (smB[9],))
                yield
                px = PS()
                for c in range(4):
                    mm(px[:, c * 128:(c + 1) * 128], xax[:, c, :], C("ident"), True, True, (xax, cst), (px,))
                cp("act", xs_tm[:], px[:], (px,), (xs_tm,))
                if full:
                    tt("dve", v3(Xb[:], 8), v3(xs_tm[:], 8), SM(8, 8).unsqueeze(2).broadcast_to([128, 8, 64]), ALU.mult,
                       (xs_tm, smB[8]), (Xb,))
                for g in range(2):
                    trn(pTR[:, g * 128:(g + 1) * 128], BCb[:, g, :], identb[:], (BCb, identb), (pTR,))
                cp("act", Btm[:], v3(pTR[:, 0:256], 2), (pTR,), (Btm,))
                yield
                psm = PS()
                mm(psm[:, 0:8], C("triu"), SM(16, 8), True, True, (cst, smB[9]), (psm,))
                mm(psm[:, 8:16], C("trils"), SM(16, 8), True, True, (cst, smB[9]), (psm,))
                mm(psm[:, 16:24], C("ones"), SM(16, 8), True, True, (cst, smB[9]), (psm,))
                act(SM(24, 24), psm[:, 0:24], AF.Exp, (psm,), (smB[10],))
                tt("dve", SM(48, 8), SM(8, 8), SM(32, 8), ALU.mult, (smB[8], smB[10]), (smB[11],))
                tt("dve", v3(Xd[:], 8), v3(xs_tm[:], 8), SM(48, 8).unsqueeze(2).broadcast_to([128, 8, 64]), ALU.mult,
                   (xs_tm, smB[11]), (Xd,))
                yield
                if full:
                    tt("dve", v3(dx[:], 8), v3(xs_tm[:], 8), R("dsk", 8).unsqueeze(2).broadcast_to([128, 8, 64]), ALU.mult,
                       (xs_tm, rows), (dx,))
                    psc = PS()
                    for g in range(2):
                        mm(psc[:, g * 128:(g + 1) * 128], BCb[:, g, :], BCb[:, 2 + g, :], True, True, (BCb,), (psc,))
                    tt("dve", scm[:], v3(psc[:, 0:256], 2), C("triu").unsqueeze(1).broadcast_to([128, 2, 128]), ALU.mult,
                       (psc, cst), (scm,))
                    tt("dve", rseg[:], SM(16, 8).unsqueeze(2).broadcast_to([128, 8, 128]),
                       C("triu").unsqueeze(1).broadcast_to([128, 8, 128]), ALU.mult, (smB[9], cst), (rseg,))
                    yield
                    for hh in range(2):
                        pg = PS()
                        mm(pg[:], C("trils"), rseg[:, 4 * hh:4 * hh + 4, :].rearrange("p a b -> p (a b)"), True, True,
                           (cst, rseg), (pg,))
                        act(Dm[:, 4 * hh:4 * hh + 4, :].rearrange("p a b -> p (a b)"), pg[:], AF.Exp, (pg,), (Dm,))
                    for g in range(2):
                        tt("dve", Gm[:, 4 * g:4 * g + 4, :], Dm[:, 4 * g:4 * g + 4, :],
                           scm[:, g, :].unsqueeze(1).broadcast_to([128, 4, 128]), ALU.mult, (Dm, scm), (Gm,))
                    yield
                    py1 = PS()
                    for h in range(8):
                        mm(py1[:, h * 64:(h + 1) * 64], Gm[:, h, :], Xb[:, h * 64:(h + 1) * 64], True, True, (Gm, Xb), (py1,))
                    py2 = PS()
                    for h in range(8):
                        mm(py2[:, h * 64:(h + 1) * 64], BCb[:, 2 + h // 4, :], STb[:, h, :], True, True, (BCb, STb), (py2,))
                    tt("dve", v3(ytmp[:], 8), v3(py2[:], 8), SM(24, 8).unsqueeze(2).broadcast_to([128, 8, 64]), ALU.mult,
                       (py2, smB[10]), (ytmp,))
                    tt("dve", ysd[:], ytmp[:], py1[:], ALU.add, (ytmp, py1), (ysd,))
                    tt("dve", ysd[:], ysd[:], dx[:], ALU.add, (ysd, dx), (ysd,))
                    yield
                pst = PS()
                for h in range(8):
                    mm(pst[:, h * 64:(h + 1) * 64], Btm[:, h // 4, :], Xd[:, h * 64:(h + 1) * 64], True, True, (Btm, Xd), (pst,))
                tt("dve", STf[:], STf[:], SM(40, 8).unsqueeze(2).broadcast_to([128, 8, 64]), ALU.mult, (STf, smB[10]), (STf,))
                tt("dve", STf[:].rearrange("p h d -> p (h d)"), STf[:].rearrange("p h d -> p (h d)"), pst[:], ALU.add,
                   (STf, pst), (STf,))
                if i == B0T - 1:
                    ts("dve", STf[:], STf[:], flag[:], None, ALU.mult, None, (STf, flag), (STf,))
                cp("act", STb[:], STf[:], (STf,), (STb,))
                yield
                if full:
                    tt("dve", ysd[:], ysd[:], gz[p][:], ALU.mult, (ysd, gz[p]), (ysd,))
                    for g in range(2):
                        act(ytmp[:, g * 256:(g + 1) * 256], ysd[:, g * 256:(g + 1) * 256], AF.Square, (ysd,), (ytmp, smB[12]),
                            accum=SM(56 + g))
                    ts("dve", SM(58, 2), SM(56, 2), 1.0 / 256, 1e-6, ALU.mult, ALU.add, (smB[12],), (smB[13],))
                    rsqrt_small(SM(60, 2), SM(58, 2), 2, (smB[13],), (smB[14],))
                    tt("dve", v3(ysd[:], 2), v3(ysd[:], 2), SM(60, 2).unsqueeze(2).broadcast_to([128, 2, 256]), ALU.mult,
                       (ysd, smB[14]), (ysd,))
                    tt("dve", ytok[p][:, 0:512], ysd[:], R("ssdg", 512), ALU.mult, (ysd, rows), (ytok[p],))


            def rwkv_out(i):
                p = i % 2
                full = i >= B0T - 1
                OP()
                act(lin[0:64, :], Pp[0:64, 12, :], AF.Tanh, (Pp,), (lin,))
                tt("dve", kkv[:], Pp[:, 4:8, :], par[:, PC["kk"]:PC["kk"] + 4].unsqueeze(2).broadcast_to([128, 4, 128]),
                   ALU.mult, (Pp, par), (kkv,))
                cp("act", vb[:], Pp[:, 8:12, :], (Pp,), (vb,))
                plw = PS()
                pla = PS()
                for c in range(4):
                    mm(plw[:, c * 128:(c + 1) * 128], Wlora[0:64, c * 128:(c + 1) * 128], lin[0:64, :], True, True,
                       (Wlora, lin), (plw,))
                    mm(pla[:, c * 128:(c + 1) * 128], Wlora[64:128, c * 128:(c + 1) * 128], Pp[64:128, 12, :], True, True,
                       (Wlora, Pp), (pla,))
                act(kk2[:], kkv[:], AF.Square, (kkv,), (kk2,))
                pkn = PS()
                mm(pkn[:], C("onesblk"), kk2[:].rearrange("p a b -> p (a b)"), True, True, (cst, kk2), (pkn,))
                for c in range(4):
                    act(thw[:, c, :], plw[:, c * 128:(c + 1) * 128], AF.Tanh, (plw, hw0), (thw,), bias=hw0[:, c:c + 1], scale=0.5)
                    act(tha[:, c, :], pla[:, c * 128:(c + 1) * 128], AF.Tanh, (pla, hw0), (tha,), bias=hw0[:, 4 + c:5 + c], scale=0.5)
                act(inv[:].rearrange("p a b -> p (a b)"), pkn[:], AF.Ln, (pkn,), (inv,), bias=tiny[:])
                act(inv[:], inv[:], AF.Exp, (inv,), (inv,), scale=-0.5)
                ts("dve", lw[:], thw[:], -0.30326533, -0.30326533, ALU.mult, ALU.add, (thw,), (lw,))
                ts("dve", av[:], tha[:], 0.5, 0.5, ALU.mult, ALU.add, (tha,), (av,))
                tt("dve", kkn[:], kkv[:], inv[:], ALU.mult, (kkv, inv), (kkn,))
                FP()
                OP()
                for c in range(4):
                    S.op("dve", lambda e, c=c: e.tensor_tensor_scan(out=cum[:, c, :], data0=lw[:, c, :], data1=zeros[:],
                                                                      initial=0.0, op0=ALU.add, op1=ALU.add),
                         rd(lw, zeros), rd(cum))
                act(E1[:], cum[:], AF.Exp, (cum,), (E1,))
                act(E2[:], cum[:], AF.Exp, (cum,), (E2,), scale=-1.0)
                FP()
                OP()
                stt("dve", t1[:], av[:], -1.0, par[:, PC["ka"]:PC["ka"] + 4].unsqueeze(2).broadcast_to([128, 4, 128]),
                    ALU.add, ALU.mult, (av, par), (t1,))
                stt("dve", kmod[:], t1[:], 1.0, Pp[:, 4:8, :], ALU.add, ALU.mult, (t1, Pp), (kmod,))
                FP()
                tt("dve", beta[:], kkn[:], av[:], ALU.mult, (kkn, av), (beta,))
                for hp in range(2):
                    sl = slice(hp * 64, (hp + 1) * 64)
                    stt("dve", arz[hp][sl, :, 0, :], kkn[sl, :, :], -1.0, E3[sl, :, :], ALU.mult, ALU.mult, (kkn, E3), (arz[hp],))
                tt("dve", btl[:], beta[:], E2[:], ALU.mult, (beta, E2), (btl,))
                tt("dve", ktl[:], kmod[:], E2[:], ALU.mult, (kmod, E2), (ktl,))
                if full:
                    for hp in range(2):
                        sl = slice(hp * 64, (hp + 1) * 64)
                        tt("dve", arz[hp][sl, :, 1, :], Pp[sl, 0:4, :], E1[sl, :, :], ALU.mult, (Pp, E1), (arz[hp],))
                if full:
                    tt("dve", rk[:], Pp[:, 0:4, :], kmod[:], ALU.mult, (Pp, kmod), (rk,))
                if full:
                    pbn = PS()
                    for c in range(4):
                        mm(pbn[:, 0:8], rk[:, c, :], sel[:, c, :], c == 0, c == 3, (rk, sel), (pbn,))
                    cp("act", bnsb[:], pbn[:, 0:8], (pbn,), (bnsb,))
                for c in range(4):
                    trn(pTR[:, c * 128:(c + 1) * 128], vb[:, c, :], identb[:], (vb, identb), (pTR,))
                cp("act", vtm[:], pTR[:, 0:512], (pTR,), (vtm,))
                FP()
                drain(ogen)
                if full:
                    act(gsb[:, 0:128], Pp[:, 13, :], AF.Tanh, (Pp,), (gsb,), scale=0.5)
                    ts("dve", sg[:], gsb[:, 0:128], 1.0, None, ALU.add, None, (gsb,), (sg,))
                    pgt = PS()
                    mm(pgt[:], sg[:], G2h[:], True, True, (sg, G2h), (pgt,))
                    cp("act", gsb[:], pgt[:], (pgt,), (gsb,))
                if full:
                    tt("dve", bterm[:], v3(vtm[:], 8), bnsb[:].unsqueeze(2).broadcast_to([128, 8, 64]), ALU.mult,
                       (vtm, bnsb), (bterm,))
                for c in range(4):
                    pab = PS()
                    pak = PS()
                    pnt = PS()
                    for hp in range(2):
                        arv = GAz[hp][:, c * 256:(c + 1) * 256]
                        mm(pab[:, hp * 256:(hp + 1) * 256], btl[:, c, :], arv, True, True, (btl, arz[hp]), (pab,))
                        mm(pak[:, hp * 256:(hp + 1) * 256], ktl[:, c, :], arv, True, True, (ktl, arz[hp]), (pak,))
                        mm(pnt[:, hp * 128:(hp + 1) * 128], arz[hp][:, c, 0, :], btl[:, c, :], True, True, (arz[hp], btl), (pnt,))
                    tt("dve", ABs[c][:], v3(pab[:], 2), maskSI[:].unsqueeze(1).broadcast_to([128, 2, 256]), ALU.mult,
                       (pab, maskSI), (ABs[c],))
                    tt("dve", AKs[c][:], v3(pak[:], 2), maskSI[:].unsqueeze(1).broadcast_to([128, 2, 256]), ALU.mult,
                       (pak, maskSI), (AKs[c],))
                    tt("dve", PTm[0][c][:], v3(pnt[:, 0:256], 2), C("trils").unsqueeze(1).broadcast_to([128, 2, 128]),
                       ALU.mult, (pnt, cst), (PTm[0][c],))
                    FP()
                    EP()
                    EP()
                for c in range(4):
                    trn(pTR[:, c * 128:(c + 1) * 128], btl[:, c, :], identb[:], (btl, identb), (pTR,))
                    trn(pTR[:, 512 + c * 128:512 + (c + 1) * 128], ktl[:, c, :], identb[:], (ktl, identb), (pTR,))
                cp("act", btm[:], pTR[:, 0:512], (pTR,), (btm,))
                cp("dve", ktm[:], pTR[:, 512:1024], (pTR,), (ktm,))
                pu = pool_banks[0]
                for h in range(8):
                    c, hp = h // 2, h % 2
                    mm(pu[:, h * 64:(h + 1) * 64], arz[hp][:, c, 0, :], STrb[:, c, :], True, False, (arz[hp], STrb), (pu,))
                    mm(pu[:, h * 64:(h + 1) * 64], AKs[c][:, hp, 0:128], vtm[:, h * 64:(h + 1) * 64], False, True,
                       (AKs[c], vtm), (pu,))
                cp("act", UTf[:], pu[:], (pu,), (UTf,))
                cp("dve", UT[:].rearrange("p h d -> p (h d)"), UTf[:], (UTf,), (UT,))
                FP()
                for lvl in range(7):
                    q = lvl % 2
                    pd = PS()
                    for h in range(8):
                        c, hp = h // 2, h % 2
                        Pl, PlB = (ABs[c][:, hp, 0:128], ABs[c]) if lvl == 0 else (Pm[q][c][:, hp, :], Pm[q][c])
                        mm(pd[:, h * 64:(h + 1) * 64], Pl, UT[:, h, :], True, True, (PlB, UT), (pd,))
                    tt("dve", UTf[:], UTf[:], pd[:], ALU.add, (UTf, pd), (UTf,))
                    cp("act", UT[:].rearrange("p h d -> p (h d)"), UTf[:], (UTf,), (UT,))
                    EP()
                    SP()
                    if lvl < 6:
                        for c in range(4):
                            pp = PS()
                            for hp in range(2):
                                Pl, PlB = (ABs[c][:, hp, 0:128], ABs[c]) if lvl == 0 else (Pm[q][c][:, hp, :], Pm[q][c])
                                mm(pp[:, hp * 128:(hp + 1) * 128], PTm[q][c][:, hp, :], Pl, True, True,
                                   (PTm[q][c], PlB), (pp,))
                                mm(pp[:, 256 + hp * 128:256 + (hp + 1) * 128], Pl, PTm[q][c][:, hp, :], True, True,
                                   (PTm[q][c], PlB), (pp,))
                            cp("dve" if c % 2 == 0 else "act", PPm[1 - q][c][:].rearrange("p a b c -> p (a b c)"), pp[:],
                               (pp,), (PPm[1 - q][c],))
                            if c == 1:
                                SP()
                psr = PS()
                for c in range(4):
                    o = psr[:, c * 128:(c + 1) * 128]
                    mm(o, btm[:, c * 128:(c + 1) * 128], UT[:, 2 * c:2 * c + 2, :].rearrange("p a b -> p (a b)"), True, False,
                       (btm, UT), (psr,))
                    mm(o, ktm[:, c * 128:(c + 1) * 128], vtm[:, c * 128:(c + 1) * 128], False, True, (ktm, vtm), (psr,))
                if full:
                    pyr = PS()
                    for h in range(8):
                        c, hp = h // 2, h % 2
                        o = pyr[:, h * 64:(h + 1) * 64]
                        mm(o, arz[hp][:, c, 1, :], STrb[:, c, :], True, False, (arz[hp], STrb), (pyr,))
                        mm(o, ABs[c][:, hp, 128:256], UT[:, h, :], False, False, (ABs[c], UT), (pyr,))
                        mm(o, AKs[c][:, hp, 128:256], vtm[:, h * 64:(h + 1) * 64], False, True, (AKs[c], vtm), (pyr,))
                psr3 = v3(psr[:], 4)
                tt("dve", tmpS[0:64, :, :], STr[0:64, :, :], psr3[0:64, :, 0:64], ALU.add, (STr, psr), (tmpS,))
                tt("dve", tmpS[64:128, :, :], STr[64:128, :, :], psr3[64:128, :, 64:128], ALU.add, (STr, psr), (tmpS,))
                tt("dve", STr[:], tmpS[:], E1[:, :, 127:128].broadcast_to([128, 4, 64]), ALU.mult, (tmpS, E1), (STr,))
                if i == B0T - 1:
                    ts("dve", STr[:], STr[:], flag[:], None, ALU.mult, None, (STr, flag), (STr,))
                cp("act", STrb[:], STr[:], (STr,), (STrb,))
                SP()
                if full:
                    cp("act", yr[:].rearrange("p h d -> p (h d)"), pyr[:], (pyr,), (yr,))
                    ogen[0] = out_gen(i)

            def out_gen(i):
                p = i % 2
                slot = i - (B0T - 1)
                act(ysq[:], yr[:], AF.Square, (yr,), (ysq,))
                S.op("dve", lambda e: e.tensor_reduce(out=gn[:, 0:8], in_=yr[:], axis=AX.X, op=ALU.add), rd(yr), rd(gn))
                S.op("dve", lambda e: e.tensor_reduce(out=gn[:, 8:16], in_=ysq[:], axis=AX.X, op=ALU.add), rd(ysq), rd(gn))
                ts("dve", gn[:, 0:16], gn[:, 0:16], 1.0 / 64, None, ALU.mult, None, (gn,), (gn,))
                tt("dve", gn[:, 16:24], gn[:, 0:8], gn[:, 0:8], ALU.mult, (gn,), (gn,))
                tt("dve", gn[:, 24:32], gn[:, 8:16], gn[:, 16:24], ALU.subtract, (gn,), (gn,))
                ts("dve", gn[:, 24:32], gn[:, 24:32], 64e-5, None, ALU.add, None, (gn,), (gn,))
                rsqrt_small(gn[:, 32:40], gn[:, 24:32], 8, (gn,), (gn,))
                tt("dve", yr[:], yr[:], gn[:, 0:8].unsqueeze(2).broadcast_to([128, 8, 64]), ALU.subtract, (yr, gn), (yr,))
                tt("dve", yr[:], yr[:], gn[:, 32:40].unsqueeze(2).broadcast_to([128, 8, 64]), ALU.mult, (yr, gn), (yr,))
                yr2 = yr[:].rearrange("p h d -> p (h d)")
                tt("dve", yr2, yr2, R("lnw", 512), ALU.mult, (yr, rows), (yr,))
                tt("dve", yr2, yr2, R("lnb", 512), ALU.add, (yr, rows), (yr,))
                tt("dve", yr[:], yr[:], bterm[:], ALU.add, (yr, bterm), (yr,))
                tt("dve", ytok[p][:, 512:1024], yr2, gsb[:], ALU.mult, (yr, gsb), (ytok[p],))
                yield
                for c in range(8):
                    trn(pTR[:, c * 128:(c + 1) * 128], ytok[p][:, c * 128:(c + 1) * 128], identb[:], (ytok[p], identb), (pTR,))
                cp("act", mixT[:].rearrange("p a b -> p (a b)"), pTR[:], (pTR,), (mixT,))
                yield
                pm = [PS(), PS()]
                for hf in range(2):
                    for c in range(8):
                        mm(pm[hf][:], mixT[:, c, :], Wout[:, c, hf * 512:(hf + 1) * 512], c == 0, c == 7, (mixT, Wout), (pm[hf],))
                dma("sp", ht[:], D["x_in"][i * TT:(i + 1) * TT, :], "xr", (), (ht,))
                for hf in range(2):
                    act(B7[:, hf * 512:(hf + 1) * 512], pm[hf][:], AF.Square, (pm[hf],), (B7, hsm), accum=hs[:, hf:hf + 1])
                tt("dve", hs[:, 2:3], hs[:, 0:1], hs[:, 1:2], ALU.add, (hsm,), (hsm,))
                ts("dve", hs[:, 2:3], hs[:, 2:3], 1.0 / 1024, 1e-6, ALU.mult, ALU.add, (hsm,), (hsm,))
                rsqrt_small(hs[:, 3:4], hs[:, 2:3], 1, (hsm,), (hsm,))
                for hf in range(2):
                    stt("dve", B7[:, hf * 512:(hf + 1) * 512], pm[hf][:], hs[:, 3:4], R("pmn", 1024)[:, hf * 512:(hf + 1) * 512],
                        ALU.mult, ALU.mult, (pm[hf], hsm, rows), (B7,))
                tt("dve", ht[:], ht[:], B7[:], ALU.add, (ht, B7), (ht,))
                dma("sp", hscr[slot * TT:(slot + 1) * TT, :], ht[:], "hst", (ht,), (hscrB,))
                yield

            ogen = [None]

            def OP():
                if ogen[0] is not None:
                    next(ogen[0], None)

            fgen = [None]
            egen = [None]
            sgen = [None]

            def FP():
                if fgen[0] is not None:
                    next(fgen[0], None)

            def need_banks(n):
                while fstate[0] < n:
                    if fgen[0] is None or next(fgen[0], "done") == "done":
                        break

            def need_front_done():
                if fgen[0] is not None:
                    for _ in fgen[0]:
                        pass

            def EP():
                if egen[0] is not None:
                    next(egen[0], None)

            def need_conv():
                while estate[0] < 8:
                    if egen[0] is None or next(egen[0], "done") == "done":
                        break

            def SP():
                if sgen[0] is not None:
                    next(sgen[0], None)

            def drain(g):
                if g[0] is not None:
                    for _ in g[0]:
                        pass
                g[0] = None

            fgen[0] = front_gen(0)
            drain(fgen)
            fstate[0] = 8
            egen[0] = early_gen(0)
            drain(egen)
            estate[0] = 8
            sgen[0] = ssd_gen(0)
            drain(sgen)
            CK(2)
            for i in range(NT_ALL):
                if i + 1 < NT_ALL:
                    fgen[0] = front_gen(i + 1)
                    egen[0] = early_gen(i + 1)
                    sgen[0] = ssd_gen(i + 1)
                FP()
                FP()
                rwkv_out(i)
                drain(fgen)
                drain(egen)
                drain(sgen)
                CK(5)
            drain(ogen)
            S.barrier()
            CK(6)

        rot_set[0] = [4, 5]
        TBK = 256
        NTB = NT_MAIN * TT // TBK
        with ExitStack() as stB:
            Wup = sb("Wup", [128, 8, 5632], BF16, stB)
            Wdown = sb("Wdown", [128, 22, 1024], BF16, stB)
            pfn = sb("pfn", [128, 1024], F32, stB)
            dma("sp", pfn[:], D["rows"][:, RC["pfn"]:RC["pfn"] + 1024], "c6", (), (pfn,))
            WupB = [[Buf("wup%d_%d" % (k, hf)) for hf in range(2)] for k in range(8)]
            WdB = [Buf("wd%d" % f3) for f3 in range(8)]
            with ExitStack() as stS:
                stage = [sb("stageB%d" % q, [128, 3072], F32, stS) for q in range(3)]
                sq = [0]

                def load_rows2(src_ap, ncols, fn):
                    si = sq[0] % 3
                    stg = stage[si]
                    sq[0] += 1
                    dma("sp", stg[:, 0:ncols], src_ap, "wsu%d" % si, (), (stg,))
                    fn(stg, sq[0])

                def cast_up(stg, n, k, hf):
                    o = Wup[:, k, hf * 2816:(hf + 1) * 2816]
                    e = n % 2
                    if e == 0:
                        ts("dve", o, stg[:, 0:2816], P("g2n", k), None, ALU.mult, None, (stg, par), (WupB[k][hf],))
                    else:
                        act(o, stg[:, 0:2816], AF.Copy, (stg, par), (WupB[k][hf],), scale=P("g2n", k))
                for k in range(8):
                    for hf in range(2):
                        load_rows2(D["wup"][:, k * 5632 + hf * 2816:k * 5632 + (hf + 1) * 2816], 2816,
                                   lambda stg, n, k=k, hf=hf: cast_up(stg, n, k, hf))
                for f3 in range(8):
                    n3 = min(3, 22 - 3 * f3)
                    load_rows2(D["wdown"][:, 3 * f3 * 1024:(3 * f3 + n3) * 1024], n3 * 1024,
                               lambda stg, n, f3=f3, n3=n3: cp(("act", "dve")[n % 2], Wdown[:, 3 * f3:3 * f3 + n3, :],
                                                               v3(stg[:, 0:n3 * 1024], n3), (stg,), (WdB[f3],)))
                S.barrier()
            CK(7)
            hld = [sb("hld%d" % q, [128, 2, 1024], F32, stB) for q in range(2)]
            hbB = sb("hbB", [128, 1024], BF16, stB)
            hnB = [sb("hnB%d" % q, [128, 8, 2 + TBK], BF16, stB) for q in range(2)]
            U = [sb("U%d" % q, [128, 2, 2, 2 + TBK], F32, stB) for q in range(2)]
            cacc = [[sb("cacc%d_%d" % (q, gv), [128, 2, TBK], F32, stB) for gv in range(2)] for q in range(2)]
            sgt = [sb("sgt%d" % q, [128, 2, TBK], F32, stB) for q in range(2)]
            actT = [sb("actT%d" % q, [128, 2, TBK], BF16, stB) for q in range(2)]
            carry = sb("carry", [128, 22, 2, 2], F32, stB)
            carryB = [Buf("carry%d" % f) for f in range(22)]
            fo = [sb("fo%d" % q, [128, 1024], F32, stB) for q in range(2)]
            fs = sb("fs", [128, 16], F32, stB)
            junkF = sb("junkF", [128, 1024], BF16, stB)
            fsm = Buf("fsm")
            outB = Buf("out")
            last = []
            memset("pool", carry[:], 0.0, carryB)

            def prep_sub(row0, q, sub, dst, dcol):
                dma("sp", hld[q][:, sub, :], hscr[row0:row0 + TT, :], "hld%d" % q, (hscrB,), (hld[q],))
                act(hbB[:], hld[q][:, sub, :], AF.Square, (hld[q],), (hbB, fsm), accum=fs[:, 8:9])
                ts("dve", fs[:, 9:10], fs[:, 8:9], 1.0 / 1024, 1e-6, ALU.mult, ALU.add, (fsm,), (fsm,))
                rsqrt_small(fs[:, 10:11], fs[:, 9:10], 1, (fsm,), (fsm,))
                act(hbB[:], hld[q][:, sub, :], AF.Copy, (hld[q], fsm), (hbB,), scale=fs[:, 10:11])
                for k in range(8):
                    trn(pT[:, k * 128:(k + 1) * 128], hbB[:, k * 128:(k + 1) * 128], identb[:], (hbB, identb), (pT,))
                cp("dve", dst[:, :, dcol:dcol + TT], v3(pT[:], 8), (pT,), (dst,))

            prep_sub(0, 1, 0, hnB[1], 2)
            ts("pool", hnB[1][:, :, 128:130], hnB[1][:, :, 128:130], flag[:], None, ALU.mult, None, (hnB[1], flag), (hnB[1],))
            for f in range(22):
                pc = PS()
                for gv in range(2):
                    col = gv * 2816 + f * 128
                    for k in range(8):
                        mm(pc[:, gv * 2:gv * 2 + 2], Wup[:, k, col:col + 128], hnB[1][:, k, 128:130], k == 0, k == 7,
                           (WupB[k][gv], hnB[1]), (pc,))
                cp("act", carry[:, f, :, :].rearrange("p a b -> p (a b)"), pc[:, 0:4], (pc,), (carryB[f // 2],))

            def prep_gen(tb):
                for sub in range(2):
                    prep_sub(TT + tb * TBK + sub * TT, tb % 2, sub, hnB[tb % 2], 2 + sub * TT)
                    yield

            for _ in prep_gen(0):
                pass
            for tb in range(NTB):
                q2 = tb % 2
                pgen = prep_gen(tb + 1) if tb + 1 < NTB else None
                pf = [[pool_banks[0], pool_banks[1]], [pool_banks[2], pool_banks[3]]]
                pub = [pool_banks[4], pool_banks[5]]

                def A_pe(fp):
                    for j in range(2):
                        f = 2 * fp + j
                        for gv in range(2):
                            col = gv * 2816 + f * 128
                            for k in range(8):
                                mm(pub[j][:, gv * TBK:(gv + 1) * TBK], Wup[:, k, col:col + 128], hnB[q2][:, k, 2:2 + TBK], k == 0, k == 7,
                                   (WupB[k][gv], hnB[q2]), (pub[j],))

                def A_rest(fp):
                    q = fp % 2
                    Uq = U[q]
                    for j in range(2):
                        cp("act", Uq[:, j, :, 2:2 + TBK], v3(pub[j][:], 2), (pub[j],), (Uq,))
                    f0 = 2 * fp
                    cp("dve", Uq[:, :, :, 0:2], carry[:, f0:f0 + 2, :, :], (carryB[fp],), (Uq,))
                    cp("dve", carry[:, f0:f0 + 2, :, :], Uq[:, :, :, TBK:TBK + 2], (Uq,), (carryB[fp],))
                    ca = cacc[q]
                    for j in range(2):
                        for gv in range(2):
                            ch = gv * 22 + f0 + j
                            act(ca[gv][:, j, :], Uq[:, j, gv, 2:2 + TBK], AF.Identity, (Uq, par), (ca[gv],), bias=P("fcb", ch),
                                scale=P("fcw", ch * 3 + 2))

                def B_nonpe(fp):
                    q = fp % 2
                    Uq = U[q]
                    ca = cacc[q]
                    f0 = 2 * fp
                    for k in range(2):
                        for j in range(2):
                            for gv in range(2):
                                ch = gv * 22 + f0 + j
                                stt("dve", ca[gv][:, j, :], Uq[:, j, gv, k:k + TBK], P("fcw", ch * 3 + k), ca[gv][:, j, :], ALU.mult, ALU.add,
                                    (Uq, par, ca[gv]), (ca[gv],))
                    act(sgt[q][:], ca[0][:], AF.Silu, (ca[0],), (sgt[q],))
                    tt("dve", actT[q][:], sgt[q][:], ca[1][:], ALU.mult, (sgt[q], ca[1]), (actT[q],))

                def B_pe(fp):
                    q = fp % 2
                    a3 = actT[q]
                    for j in range(2):
                        f = 2 * fp + j
                        for sub in range(2):
                            for hf in range(2):
                                mm(pf[sub][hf][:], a3[:, j, sub * TT:(sub + 1) * TT], Wdown[:, f, hf * 512:(hf + 1) * 512], f == 0, f == 21,
                                   (a3, WdB[f // 3]), (pf[sub][hf],))

                A_pe(0)
                A_rest(0)
                for fp in range(11):
                    if fp + 1 < 11:
                        A_pe(fp + 1)
                    B_nonpe(fp)
                    B_pe(fp)
                    if fp + 1 < 11:
                        A_rest(fp + 1)
                    if pgen is not None and fp in (3, 7):
                        next(pgen, None)
                for sub in range(2):
                    for hf in range(2):
                        cp("act", fo[sub][:, hf * 512:(hf + 1) * 512], pf[sub][hf][:], (pf[sub][hf],), (fo[sub],))
                if pgen is not None:
                    for _ in pgen:
                        pass
                for sub in range(2):
                    fq = fo[sub]
                    o4 = 4 * sub
                    act(junkF[:], fq[:], AF.Square, (fq,), (junkF, fsm), accum=fs[:, o4:o4 + 1])
                    ts("dve", fs[:, o4 + 2:o4 + 3], fs[:, o4:o4 + 1], 1.0 / 1024, 1e-6, ALU.mult, ALU.add, (fsm,), (fsm,))
                    rsqrt_small(fs[:, o4 + 3:o4 + 4], fs[:, o4 + 2:o4 + 3], 1, (fsm,), (fsm,))
                    stt("dve", fq[:], fq[:], fs[:, o4 + 3:o4 + 4], pfn[:], ALU.mult, ALU.mult, (fq, fsm, pfn), (fq,))
                    tt("dve", fq[:], fq[:], hld[q2][:, sub, :], ALU.add, (fq, hld[q2]), (fq,))
                    r0 = tb * TBK + sub * TT
                    last.append(dma("sp", out[r0:r0 + TT, :], fq[:], "ost%d" % sub, (fq,), (outB,)))
            S.finish(last[-2:])
    except _Stop:
        pass
    return nc


_NC_CACHE = {}


def kernel(**inputs):
    inputs = {k: np.asarray(v) for k, v in inputs.items()}
    x = inputs["x"].astype(np.float32, copy=False)
    shared = _host_prep(inputs)
    in_maps = []
    for c in range(8):
        b, s = c // 2, c % 2
        m = dict(shared)
        if s == 0:
            m["x_in"] = np.ascontiguousarray(np.concatenate([x[b, 0:2048], x[b, 0:2048]], axis=0))
        else:
            m["x_in"] = np.ascontiguousarray(x[b])
        m["flag"] = np.full((128, 1), float(s), np.float32)
        in_maps.append(m)
    if "nc" not in _NC_CACHE:
        _NC_CACHE["nc"] = build_nc()
    res = run_bass_kernel_spmd(_NC_CACHE["nc"], in_maps, core_ids=list(range(8)))
    outp = np.empty((4, 4096, 1024), np.float32)
    for c in range(8):
        b, s = c // 2, c % 2
        outp[b, s * 2048:(s + 1) * 2048] = res.results[c]["out"]
    return outp
```

```python
from contextlib import ExitStack
import numpy as np
import concourse.bass as bass
import concourse.mybir as mybir
from concourse.bass_utils import run_bass_kernel_spmd

F32 = mybir.dt.float32
BF16 = mybir.dt.bfloat16
AF = mybir.ActivationFunctionType
ALU = mybir.AluOpType
AX = mybir.AxisListType

SAME_ENGINE_SYNC = True
NT_ALL = 32
NT_MAIN = 16
TT = 128
NFM = 22
HW = 131


class Buf:
    __slots__ = ("name", "w", "r")

    def __init__(self, name):
        self.name = name
        self.w = None
        self.r = []


class Sched:
    ENGS = ("pe", "act", "dve", "pool", "sp")

    def __init__(self, nc, stack):
        self.nc = nc
        self.stack = stack
        self.sems = {}
        self.cnt = {}
        for e in self.ENGS:
            self.sems[e] = stack.enter_context(nc.semaphore("s_" + e))
            self.cnt[e] = 0
        self.known = {e: {} for e in self.ENGS}
        self.snap = {}
        self.prog = {e: [] for e in self.ENGS}
        self.dma_sems = {}
        self.dma_cnt = {}

    def _sem_handle(self, k):
        return self.sems[k] if k in self.sems else self.dma_sems[k]

    def _need(self, eng, tok, acc):
        if tok is None:
            return
        semkey, val, src = tok
        if src == eng and (eng == "pe" or not SAME_ENGINE_SYNC):
            return
        if self.known[eng].get(semkey, 0) >= val:
            return
        if acc.get(semkey, 0) < val:
            acc[semkey] = val

    def _emit_waits(self, eng, acc):
        for semkey, val in acc.items():
            h = self._sem_handle(semkey)
            self.prog[eng].append(lambda e, h=h, val=val: e.wait_ge(h, val))
            k = self.known[eng]
            if k.get(semkey, 0) < val:
                k[semkey] = val
            if semkey in self.sems:
                sn = self.snap.get((semkey, val))
                if sn:
                    for kk, vv in sn.items():
                        if k.get(kk, 0) < vv:
                            k[kk] = vv

    def _deps(self, eng, reads, writes):
        acc = {}
        for b in reads:
            self._need(eng, b.w, acc)
        for b in writes:
            self._need(eng, b.w, acc)
            for t in b.r:
                self._need(eng, t, acc)
        self._emit_waits(eng, acc)

    def _mark(self, tok, reads, writes):
        for b in reads:
            b.r.append(tok)
            if len(b.r) > 24:
                b.r = b.r[-24:] if False else b.r
        for b in writes:
            b.w = tok
            b.r = []

    def op(self, eng, fn, reads=(), writes=()):
        self._deps(eng, reads, writes)
        self.cnt[eng] += 1
        n = self.cnt[eng]
        h = self.sems[eng]
        self.prog[eng].append(lambda e, fn=fn, h=h: fn(e).then_inc(h, 1))
        tok = (eng, n, eng)
        sn = dict(self.known[eng])
        sn[eng] = n
        self.snap[(eng, n)] = sn
        self._mark(tok, reads, writes)
        return tok

    def dma(self, queue, fn, semkey, reads=(), writes=()):
        if semkey not in self.dma_sems:
            self.dma_sems[semkey] = self.stack.enter_context(self.nc.semaphore("d_" + semkey))
            self.dma_cnt[semkey] = 0
        self._deps(queue, reads, writes)
        self.dma_cnt[semkey] += 16
        val = self.dma_cnt[semkey]
        h = self.dma_sems[semkey]
        self.prog[queue].append(lambda e, fn=fn, h=h: fn(e).then_inc(h, 16))
        tok = (semkey, val, "dma")
        self._mark(tok, reads, writes)
        return tok

    def barrier(self):
        toks = [(e, self.cnt[e], e) for e in self.ENGS if self.cnt[e] > 0]
        toks += [(k, v, "dma") for k, v in self.dma_cnt.items() if v > 0]
        for e in self.ENGS:
            acc = {}
            for t in toks:
                if t[0] == e:
                    continue
                self._need(e, t, acc)
            self._emit_waits(e, acc)

    def finish(self, final_toks):
        for t in final_toks:
            acc = {}
            self._need("sp", t, acc)
            self._emit_waits("sp", acc)
        with self.nc.Block() as block:
            @block.tensor
            def _(e):
                for f in self.prog["pe"]:
                    f(e)

            @block.scalar
            def _(e):
                for f in self.prog["act"]:
                    f(e)

            @block.vector
            def _(e):
                for f in self.prog["dve"]:
                    f(e)

            @block.gpsimd
            def _(e):
                for f in self.prog["pool"]:
                    f(e)

            @block.sync
            def _(e):
                for f in self.prog["sp"]:
                    f(e)


PC = {}
_o = 0
for _n, _w in (("cw", 32), ("cb", 8), ("mu", 14), ("w0", 4), ("a0", 4), ("kk", 4), ("ka", 4),
               ("fcw", 132), ("fcb", 44), ("g1", 8), ("g2n", 8)):
    PC[_n] = _o
    _o += _w
NPAR = _o
RC = {}
_o = 0
for _n, _w in (("dtb", 8), ("alog", 8), ("dsk", 8), ("ssdg", 512), ("lnw", 512), ("lnb", 512),
               ("pmn", 1024), ("pfn", 1024)):
    RC[_n] = _o
    _o += _w
NROW = _o
CC = {"ident": 0, "triu": 128, "trius": 256, "trils": 384, "onesblk": 512, "ones": 640}
NCONST = 768


def _cols(v, nch):
    return np.ascontiguousarray(v.reshape(nch, 128).T)


def _host_prep(inp):
    f = np.float32
    w_in = inp["w_in"][0]
    s0 = 0
    r0 = 1544
    perm = np.concatenate([
        np.arange(512, 1536),
        r0 + np.arange(0, 512),
        r0 + np.arange(576, 1088),
        r0 + np.arange(1088, 1600),
        r0 + np.arange(512, 576),
        r0 + np.arange(1600, 1664),
        r0 + np.arange(1664, 1792),
        np.arange(0, 512),
        np.arange(1536, 1544),
    ])
    win = np.ascontiguousarray(w_in[:, perm].reshape(8, 128, 3336).transpose(1, 0, 2)).reshape(128, 8 * 3336)
    wout = np.ascontiguousarray(inp["w_out"][0].reshape(8, 128, 1024).transpose(1, 0, 2)).reshape(128, 8 * 1024)
    wup = np.ascontiguousarray(inp["ffn_w_up"][0].reshape(8, 128, 5632).transpose(1, 0, 2)).reshape(128, 8 * 5632)
    wdown = np.ascontiguousarray(inp["ffn_w_down"][0].reshape(22, 128, 1024).transpose(1, 0, 2)).reshape(128, 22 * 1024)
    par = np.zeros((128, NPAR), f)
    cw = inp["ssd_conv_w"][0]
    par[:, PC["cw"]:PC["cw"] + 32] = cw.reshape(4, 8, 128).transpose(2, 1, 0).reshape(128, 32)
    par[:, PC["cb"]:PC["cb"] + 8] = _cols(inp["ssd_conv_b"][0], 8)
    mu = inp["rwkv_mu"][0]
    mup = np.concatenate([mu[0:512], mu[576:1088], mu[1088:1600], mu[512:576], mu[1600:1664], mu[1664:1792]])
    par[:, PC["mu"]:PC["mu"] + 14] = _cols(mup, 14)
    par[:, PC["w0"]:PC["w0"] + 4] = _cols(inp["rwkv_w0"][0], 4)
    par[:, PC["a0"]:PC["a0"] + 4] = _cols(inp["rwkv_a0"][0], 4)
    par[:, PC["kk"]:PC["kk"] + 4] = _cols(inp["rwkv_k_k"][0], 4)
    par[:, PC["ka"]:PC["ka"] + 4] = _cols(inp["rwkv_k_a"][0], 4)
    fw = inp["ffn_conv_w"][0]
    par[:, PC["fcw"]:PC["fcw"] + 132] = fw.reshape(3, 44, 128).transpose(2, 1, 0).reshape(128, 132)
    par[:, PC["fcb"]:PC["fcb"] + 44] = _cols(inp["ffn_conv_b"][0], 44)
    par[:, PC["g1"]:PC["g1"] + 8] = _cols(inp["pre_mix_norm"][0], 8)
    par[:, PC["g2n"]:PC["g2n"] + 8] = _cols(inp["pre_ffn_norm"][0], 8)
    rows = np.zeros((128, NROW), f)

    def put(name, v):
        rows[:, RC[name]:RC[name] + v.size] = np.broadcast_to(v.reshape(1, -1), (128, v.size))
    put("dtb", inp["ssd_dt_bias"][0])
    put("alog", inp["ssd_a_log"][0])
    put("dsk", inp["ssd_d"][0])
    put("ssdg", inp["ssd_norm"][0])
    put("lnw", inp["rwkv_ln_w"][0])
    put("lnb", inp["rwkv_ln_b"][0])
    put("pmn", inp["post_mix_norm"][0])
    put("pfn", inp["post_ffn_norm"][0])
    consts = np.zeros((128, NCONST), f)
    i = np.arange(128)
    consts[:, 0:128] = np.eye(128)
    consts[:, 128:256] = (i[:, None] <= i[None, :])
    consts[:, 256:384] = (i[:, None] < i[None, :])
    consts[:, 384:512] = (i[:, None] > i[None, :])
    consts[:, 512:640] = ((i[:, None] // 64) == (i[None, :] // 64))
    consts[:, 640:768] = 1.0
    wlora = np.concatenate([inp["rwkv_w2"][0], inp["rwkv_a2"][0]], axis=0).astype(f)
    g2 = np.ascontiguousarray(inp["rwkv_g2"][0]).astype(f)
    rk = inp["rwkv_r_k"][0]
    sel = np.zeros((128, 4, 8), f)
    for c in range(4):
        for hp in range(2):
            sel[hp * 64:(hp + 1) * 64, c, 2 * c + hp] = rk[2 * c + hp]
    shared = {"win": win, "wout": wout, "wup": wup, "wdown": wdown, "params": par, "rows": rows,
              "consts": consts, "wlora": wlora, "g2": g2, "sel": sel.reshape(128, 32)}
    return {k: np.ascontiguousarray(v, dtype=f) for k, v in shared.items()}


NSLOT = NT_MAIN + 1
DBG = 0


class _Stop(Exception):
    pass


def build_nc():
    nc = bass.Bass("TRN2", target_bir_lowering=False)
    D = {}
    for name, shape in (("x_in", [4096, 1024]), ("win", [128, 8 * 3336]), ("wout", [128, 8192]),
                        ("wup", [128, 8 * 5632]), ("wdown", [128, 22 * 1024]), ("params", [128, NPAR]),
                        ("rows", [128, NROW]), ("consts", [128, NCONST]), ("wlora", [128, 512]),
                        ("g2", [128, 512]), ("sel", [128, 32]), ("flag", [128, 1])):
        D[name] = nc.dram_tensor(name, shape, F32, kind="ExternalInput").ap()
    out = nc.dram_tensor("out", [NT_MAIN * TT, 1024], F32, kind="ExternalOutput").ap()
    hscr = nc.dram_tensor("hscr", [NSLOT * TT, 1024], F32, kind="Internal").ap()

    try:
      with ExitStack() as st:
        S = Sched(nc, st)
        dbgt = st.enter_context(nc.sbuf_tensor("sb_dbgt", [128, 64], F32))

        def CK(n, src=None):
            if DBG == n:
                if src is None:
                    S.op("dve", lambda e: e.memset(dbgt[:], 1.0), (), ())
                    tk = S.dma("sp", lambda e: e.dma_start(out=out[0:128, 0:64], in_=dbgt[:]), "dbg", (), ())
                else:
                    ap, bufs, ncol = src
                    tk = S.dma("sp", lambda e: e.dma_start(out=out[0:128, 0:ncol], in_=ap), "dbg", [b.b for b in bufs], ())
                S.finish([tk])
                raise _Stop()

        class TB:
            def __init__(self, t, name, b=None):
                self.t = t
                self.b = b if b is not None else Buf(name)

            def __getitem__(self, k):
                return self.t[k]

        class View:
            def __init__(self, ap, b):
                self.ap = ap
                self.b = b

            def __getitem__(self, k):
                return self.ap[k]

        def sb(name, shape, dt, stack=st):
            return TB(stack.enter_context(nc.sbuf_tensor("sb_" + name, shape, dt)), name)

        def psum(name, shape, dt):
            t = TB(st.enter_context(nc.psum_tensor("ps_" + name, shape, dt)), name)
            PSB.add(id(t.b))
            return t

        PSB = set()

        def rd(*xs):
            return [x.b if hasattr(x, "b") else x for x in xs]

        def _xrw(r, w):
            r = rd(*r)
            w = rd(*w)
            rr = [x for x in r if id(x) not in PSB]
            ww = list(w) + [x for x in r if id(x) in PSB and x not in w]
            return rr, ww

        def tt(eng, o, a, b, op, r, w):
            S.op(eng, lambda e: e.tensor_tensor(out=o, in0=a, in1=b, op=op), *_xrw(r, w))

        def ts(eng, o, a, s1, s2, op0, op1, r, w):
            if s2 is None:
                S.op(eng, lambda e: e.tensor_scalar(out=o, in0=a, scalar1=s1, scalar2=None, op0=op0), *_xrw(r, w))
            else:
                S.op(eng, lambda e: e.tensor_scalar(out=o, in0=a, scalar1=s1, scalar2=s2, op0=op0, op1=op1), *_xrw(r, w))

        def stt(eng, o, a, s, b, op0, op1, r, w):
            S.op(eng, lambda e: e.scalar_tensor_tensor(out=o, in0=a, scalar=s, in1=b, op0=op0, op1=op1), *_xrw(r, w))

        def act(o, a, func, r, w, bias=None, scale=None, accum=None):
            kw = {}
            if bias is not None:
                kw["bias"] = bias
            if scale is not None:
                kw["scale"] = scale
            if accum is not None:
                kw["accum_out"] = accum
            S.op("act", lambda e: e.activation(out=o, in_=a, func=func, **kw), *_xrw(r, w))

        def cp(eng, o, a, r, w):
            if eng == "act":
                act(o, a, AF.Copy, r, w)
            else:
                S.op(eng, lambda e: e.tensor_copy(out=o, in_=a), *_xrw(r, w))

        def mm(o, l, rh, start, stop, r, w):
            S.op("pe", lambda e: e.matmul(out=o, lhsT=l, rhs=rh, start=start, stop=stop, skip_group_check=True), *_xrw(r, w))

        def trn(o, a, ident, r, w):
            S.op("pe", lambda e: e.transpose(out=o, in_=a, identity=ident), *_xrw(r, w))

        def memset(eng, o, v, w):
            S.op(eng, lambda e: e.memset(o, v), (), rd(*w))

        def dma(q, o, a, key, r, w):
            return S.dma(q, lambda e: e.dma_start(out=o, in_=a), key, *_xrw(r, w))

        def v3(ap, a):
            return ap.rearrange("p (a b) -> p a b", a=a)

        par = sb("par", [128, NPAR], F32)
        cst = sb("cst", [128, NCONST], F32)
        identb = sb("identb", [128, 128], BF16)
        flag = sb("flag", [128, 1], F32)
        neghalf = sb("neghalf", [128, 512], F32)
        zeros = sb("zeros", [128, 128], F32)
        pT = psum("pT", [128, 1024], BF16)
        pTR = psum("pTR", [128, 1024], BF16)
        pool_banks = [psum("pb%d" % i, [128, 512], F32) for i in range(6)]
        rot = [0]
        rot_set = [[1, 2, 3, 4, 5]]

        def PS():
            rs = rot_set[0]
            b = pool_banks[rs[rot[0] % len(rs)]]
            rot[0] += 1
            return b

        def C(name, n=128):
            return cst[:, CC[name]:CC[name] + n]

        def P(name, c):
            return par[:, PC[name] + c:PC[name] + c + 1]

        dma("sp", par[:], D["params"], "c0", (), (par,))
        dma("sp", cst[:], D["consts"], "c2", (), (cst,))
        dma("sp", flag[:], D["flag"], "c3", (), (flag,))
        cp("dve", identb[:], C("ident"), (cst,), (identb,))
        memset("pool", neghalf[:], -0.5, (neghalf,))
        memset("pool", zeros[:], 0.0, (zeros,))
        ts("dve", par[:, PC["cw"]:PC["cw"] + 40], par[:, PC["cw"]:PC["cw"] + 40], 0.5, None, ALU.mult, None, (par,), (par,))

        def rsqrt_small(o, a, n, r, w):
            tt("pool", o, a, neghalf[:, 0:n], ALU.pow, list(r) + [neghalf], w)

        hscrB = Buf("hscr")

        with ExitStack() as stA:
            rows = sb("rows", [128, 2584], F32, stA)

            def R(name, n):
                return rows[:, RC[name]:RC[name] + n]

            dma("sp", rows[:], D["rows"][:, 0:2584], "c1", (), (rows,))
            Win = sb("Win", [128, 8, 3336], BF16, stA)
            Wout = sb("Wout", [128, 8, 1024], BF16, stA)
            Wlora = sb("Wlora", [128, 512], F32, stA)
            G2h = sb("G2h", [128, 512], BF16, stA)
            sel = sb("sel", [128, 4, 8], F32, stA)
            arow = sb("arow", [128, 8], F32, stA)
            with ExitStack() as stS:
                stage = [sb("stageA%d" % q, [128, 3336], F32, stS) for q in range(2)]
                sq = [0]

                def load_rows(src_ap, ncols, fn):
                    stg = stage[sq[0] % 2]
                    sq[0] += 1
                    dma("sp", stg[:, 0:ncols], src_ap, "wst%d" % (sq[0] % 2), (), (stg,))
                    fn(stg)
                for k in range(8):
                    load_rows(D["win"][:, k * 3336:(k + 1) * 3336], 3336,
                              lambda stg, k=k: (ts("dve", Win[:, k, :], stg[:, 0:3336], P("g1", k), None,
                                                   ALU.mult, None, (stg, par), (Win,)) if k % 2 == 0 else
                                                act(Win[:, k, :], stg[:, 0:3336], AF.Copy, (stg, par), (Win,), scale=P("g1", k))))
                ts("pool", Win[:, :, 2816:3328], Win[:, :, 2816:3328], 0.5, None, ALU.mult, None, (Win,), (Win,))
                for k in range(8):
                    load_rows(D["wout"][:, k * 1024:(k + 1) * 1024], 1024,
                              lambda stg, k=k: cp("act", Wout[:, k, :], stg[:, 0:1024], (stg,), (Wout,)))
                load_rows(D["g2"], 512, lambda stg: ts("dve", G2h[:], stg[:, 0:512], 0.5, None, ALU.mult, None, (stg,), (G2h,)))
                dma("sp", Wlora[:], D["wlora"], "c4", (), (Wlora,))
                dma("sp", sel[:].rearrange("p a b -> p (a b)"), D["sel"], "c5", (), (sel,))
                act(arow[:], R("alog", 8), AF.Exp, (rows,), (arow,))
                ts("dve", arow[:], arow[:], -1.0, None, ALU.mult, None, (arow,), (arow,))
                S.barrier()
            CK(1)

            xt1 = sb("xt", [128, 1024], F32, stA)
            xt = [xt1, xt1]
            xT = [sb("xT%d" % i, [128, 8, HW], BF16, stA) for i in range(2)]
            raw = sb("raw", [128, NFM, HW], F32, stA)
            rawB = [Buf("raw_%d" % j) for j in range(8)]
            gz = [sb("gz%d" % i, [128, 512], F32, stA) for i in range(2)]
            dtr = [sb("dtr%d" % i, [128, 8], F32, stA) for i in range(2)]
            xb = sb("xb", [128, 1024], BF16, stA)
            hb = xb
            sm = sb("sm", [128, 64], F32, stA)
            smB = [Buf("sm%d" % i) for i in range(16)]
            Pp = sb("Pp", [128, 14, 128], F32, stA)
            BB = sb("BB", [128, 2048], F32, stA)
            AC = sb("AC", [128, 1024], F32, stA)
            B2 = sb("B2", [128, 1024], F32, stA)
            B3 = sb("B3", [128, 1024], F32, stA)
            B4 = sb("B4", [128, 512], F32, stA)
            B56 = sb("B56", [128, 2048], F32, stA)
            B7 = sb("B7", [128, 1024], F32, stA)
            ht = sb("ht", [128, 1024], F32, stA)
            X1 = sb("X1", [128, 512], BF16, stA)
            X2 = sb("X2", [128, 512], BF16, stA)
            X3 = sb("X3", [128, 512], BF16, stA)
            GA = sb("GA", [128, 1024], BF16, stA)
            Btm = sb("Btm", [128, 2, 128], BF16, stA)
            STf = sb("STf", [128, 8, 64], F32, stA)
            STb = sb("STb", [128, 8, 64], BF16, stA)
            ytok = [sb("ytok%d" % q, [128, 1024], BF16, stA) for q in range(2)]
            GA0 = sb("GA0", [128, 1024], BF16, stA)
            UTfT = sb("UTfT", [128, 512], F32, stA)
            tmpST = sb("tmpST", [128, 256], F32, stA)
            mixT = sb("mixT", [128, 8, 128], BF16, stA)
            lin = sb("lin", [128, 128], F32, stA)
            sg = sb("sg", [128, 128], BF16, stA)
            btm = sb("btm", [128, 512], BF16, stA)
            ktm = sb("ktm", [128, 512], BF16, stA)
            vtm = sb("vtm", [128, 512], BF16, stA)
            ABs = [sb("ABs%d" % c, [128, 2, 256], BF16, stA) for c in range(4)]
            AKs = [sb("AKs%d" % c, [128, 2, 256], BF16, stA) for c in range(4)]
            PPm = [[sb("PPm%d_%d" % (q, c), [128, 2, 2, 128], BF16, stA) for c in range(4)] for q in range(2)]
            Pm = [[View(PPm[q][c][:, 0, :, :], PPm[q][c].b) for c in range(4)] for q in range(2)]
            PTm = [[View(PPm[q][c][:, 1, :, :], PPm[q][c].b) for c in range(4)] for q in range(2)]
            UT = sb("UT", [128, 8, 64], BF16, stA)
            STr = sb("STr", [128, 4, 64], F32, stA)
            STrb = sb("STrb", [128, 4, 64], BF16, stA)
            maskSI = sb("maskSI", [128, 256], F32, stA)
            hw0 = sb("hw0", [128, 8], F32, stA)
            tiny = sb("tiny", [128, 1], F32, stA)
            memset("pool", tiny[:], 1e-24, (tiny,))
            bnsb = sb("bnsb", [128, 8], F32, stA)
            gn = sb("gn", [128, 40], F32, stA)
            hs = sb("hs", [128, 8], F32, stA)
            hsm = Buf("hsm")
            dlt = View(v3(B56[:, 0:1792], 14), B56.b)
            th = View(v3(BB[:, 0:1024], 8), BB.b)
            xax = View(v3(BB[:, 1024:1536], 4), BB.b)
            xs_tm = View(BB[:, 1536:2048], BB.b)
            Dm = View(v3(BB[:, 0:1024], 8), BB.b)
            lw = View(v3(BB[:, 0:512], 4), BB.b)
            av = View(v3(BB[:, 512:1024], 4), BB.b)
            cum = View(v3(BB[:, 1024:1536], 4), BB.b)
            cumex = View(v3(BB[:, 1536:2048], 4), BB.b)
            yr = View(v3(B56[:, 0:512], 8), B56.b)
            ysq = View(v3(B56[:, 512:1024], 8), B56.b)
            acc = View(v3(AC[:], 8), AC.b)
            rseg = View(v3(AC[:], 8), AC.b)
            thw = View(v3(AC[:, 0:512], 4), AC.b)
            tha = View(v3(AC[:, 512:1024], 4), AC.b)
            mn = View(AC[:], AC.b)
            dx = View(B2[:, 0:512], B2.b)
            ytmp = View(B2[:, 512:1024], B2.b)
            E2 = View(v3(B2[:, 0:512], 4), B2.b)
            tmpS = View(v3(tmpST[:], 4), tmpST.b)
            ysd = View(B3[:, 0:512], B3.b)
            scm = View(v3(B3[:, 512:768], 2), B3.b)
            kkv = View(v3(B3[:, 0:512], 4), B3.b)
            UTf = View(UTfT[:], UTfT.b)
            kk2 = View(v3(B3[:, 512:1024], 4), B3.b)
            E1x = sb("E1x", [128, 4, 129], F32, stA)
            memset("pool", E1x[:], 1.0, (E1x,))
            E1 = View(E1x[:, :, 1:129], E1x.b)
            E3 = View(E1x[:, :, 0:128], E1x.b)
            inv = View(v3(B4[:, 0:512], 4), B4.b)
            kkn = View(v3(B56[:, 0:512], 4), B56.b)
            kmod = View(v3(B56[:, 512:1024], 4), B56.b)
            t1 = View(v3(B56[:, 1024:1536], 4), B56.b)
            rk = View(v3(B56[:, 1024:1536], 4), B56.b)
            beta = View(v3(B56[:, 1536:2048], 4), B56.b)
            gsb = View(B7[:, 0:512], B7.b)
            bterm = View(v3(B7[:, 512:1024], 8), B7.b)
            Xb = View(X1[:], X1.b)
            btl = View(v3(X1[:], 4), X1.b)
            Xd = View(X2[:], X2.b)
            ktl = View(v3(X2[:], 4), X2.b)
            BCb = View(v3(X3[:], 4), X3.b)
            vb = View(v3(X3[:], 4), X3.b)
            Gm = View(v3(GA[:], 8), GA.b)
            GA1 = sb("GA1", [128, 1024], BF16, stA)
            arz = [View(GA0[:].rearrange("p (c a b) -> p c a b", c=4, a=2), GA0.b),
                   View(GA1[:].rearrange("p (c a b) -> p c a b", c=4, a=2), GA1.b)]
            GAz = [GA0, GA1]
            memset("pool", GA1[:], 0.0, (GA1,))
            memset("pool", GA0[:], 0.0, (GA0,))

            cp("dve", maskSI[:, 0:128], C("trius"), (cst,), (maskSI,))
            cp("dve", maskSI[:, 128:256], C("triu"), (cst,), (maskSI,))
            memset("dve", STf[:], 0.0, (STf,))
            memset("dve", STb[:], 0.0, (STb,))
            memset("dve", STr[:], 0.0, (STr,))
            memset("dve", STrb[:], 0.0, (STrb,))
            memset("pool", xT[1][:], 0.0, (xT[1],))
            memset("pool", xT[0][:], 0.0, (xT[0],))
            ts("dve", hw0[:, 0:4], par[:, PC["w0"]:PC["w0"] + 4], 0.5, None, ALU.mult, None, (par,), (hw0,))
            ts("dve", hw0[:, 4:8], par[:, PC["a0"]:PC["a0"] + 4], 0.5, None, ALU.mult, None, (par,), (hw0,))

            def SM(i, n=1):
                return sm[:, i:i + n]

            B0T = NT_ALL - NT_MAIN

            fstate = [0]

            def front_gen(i):
                p = i % 2
                fstate[0] = 0
                dma("sp", xt[p][:], D["x_in"][i * TT:(i + 1) * TT, :], "x0", (), (xt[p],))
                act(xb[:], xt[p][:], AF.Square, (xt[p],), (xb, smB[0]), accum=SM(0))
                ts("dve", SM(1), SM(0), 1.0 / 1024, 1e-6, ALU.mult, ALU.add, (smB[0],), (smB[1],))
                rsqrt_small(SM(2), SM(1), 1, (smB[1],), (smB[2],))
                act(xb[:], xt[p][:], AF.Copy, (xt[p], smB[2]), (xb,), scale=SM(2))
                yield
                for k in range(8):
                    trn(pT[:, k * 128:(k + 1) * 128], xb[:, k * 128:(k + 1) * 128], identb[:], (xb, identb), (pT,))
                cp("dve", xT[p][:, :, 3:HW], v3(pT[:], 8), (pT,), (xT[p],))
                cp("pool", xT[p][:, :, 0:3], xT[1 - p][:, :, 128:131], (xT[1 - p],), (xT[p],))
                if i == B0T:
                    ts("pool", xT[p][:, :, 0:3], xT[p][:, :, 0:3], flag[:], None, ALU.mult, None, (xT[p], flag), (xT[p],))
                yield
                hist = i < B0T - 1
                for bnk in range(8):
                    n = min(3, NFM - 3 * bnk)
                    if hist and bnk in (2, 3, 7):
                        fstate[0] = bnk + 1
                        continue
                    pb = PS()
                    for jj in range(n):
                        j = 3 * bnk + jj
                        for k in range(8):
                            mm(pb[:, jj * HW:(jj + 1) * HW], Win[:, k, j * 128:(j + 1) * 128], xT[p][:, k, :],
                               k == 0, k == 7, (Win, xT[p]), (pb,))
                    cp("act" if bnk % 2 == 0 else "dve", raw[:, 3 * bnk:3 * bnk + n, :], v3(pb[:, 0:n * HW], n),
                       (pb,), (rawB[bnk],))
                    fstate[0] = bnk + 1
                    yield
                full = i >= B0T - 1
                if full:
                    pz = PS()
                    for k in range(8):
                        mm(pz[:, 0:512], xT[p][:, k, 3:HW], Win[:, k, 2816:3328], k == 0, k == 7, (Win, xT[p]), (pz,))
                pd = PS()
                for k in range(8):
                    mm(pd[:, 0:8], xT[p][:, k, 3:HW], Win[:, k, 3328:3336], k == 0, k == 7, (Win, xT[p]), (pd,))
                if full:
                    act(gz[p][:], pz[:, 0:512], AF.Tanh, (pz,), (gz[p],))
                    stt("dve", gz[p][:], gz[p][:], 1.0, pz[:, 0:512], ALU.add, ALU.mult, (gz[p], pz), (gz[p],))
                tt("dve", dtr[p][:], pd[:, 0:8], R("dtb", 8), ALU.add, (pd, rows), (dtr[p],))

            estate = [0]

            def early_gen(i):
                rw = raw
                estate[0] = 0
                hist = i < B0T - 1
                for c in range(6 if hist else 8):
                    need_banks(c // 3 + 1)
                    rb = rawB[c // 3]
                    ts("dve", acc[:, c, :], rw[:, c, 3:HW], P("cw", c * 4 + 3), P("cb", c), ALU.mult, ALU.add,
                       (rb, par), (acc,))
                    for k in range(3):
                        stt("dve", acc[:, c, :], rw[:, c, k:k + 128], P("cw", c * 4 + k), acc[:, c, :], ALU.mult, ALU.add,
                            (rb, par, acc), (acc,))
                    estate[0] = c + 1
                    yield
                estate[0] = 8
                need_banks(8)
                c0, c1 = (4, 13) if hist else (0, 14)
                nc_ = c1 - c0
                RBr = rawB[4:7] if hist else rawB[2:8]
                tt("dve", dlt[:, c0:c1, :], rw[:, 8 + c0:8 + c1, 2:130], rw[:, 8 + c0:8 + c1, 3:HW], ALU.subtract, RBr, (dlt,))
                yield
                tt("dve", dlt[:, c0:c1, :], dlt[:, c0:c1, :],
                   par[:, PC["mu"] + c0:PC["mu"] + c1].unsqueeze(2).broadcast_to([128, nc_, 128]),
                   ALU.mult, (dlt, par), (dlt,))
                yield
                tt("dve", Pp[:, c0:c1, :], dlt[:, c0:c1, :], rw[:, 8 + c0:8 + c1, 3:HW], ALU.add, [dlt] + RBr, (Pp,))
                yield

            def ssd_gen(i):
                p = i % 2
                full = i >= B0T - 1
                need_front_done()
                need_conv()
                nch = 8 if full else 6
                act(th[:, 0:nch, :], acc[:, 0:nch, :], AF.Tanh, (acc,), (th,))
                stt("dve", xax[:], th[:, 0:4, :], 1.0, acc[:, 0:4, :], ALU.add, ALU.mult, (th, acc), (xax,))
                stt("dve", BCb[:, 0:nch - 4, :], th[:, 4:nch, :], 1.0, acc[:, 4:nch, :], ALU.add, ALU.mult, (th, acc), (BCb,))
                act(SM(8, 8), dtr[p][:], AF.Exp, (dtr[p],), (smB[8],))
                act(SM(8, 8), SM(8, 8), AF.Ln, (smB[8],), (smB[8],), bias=1.0)
                tt("dve", SM(16, 8), SM(8, 8), arow[:], ALU.mult, (smB[8], arow), (smB[9],))
                yield
                px = PS()
                for c in range(4):
                    mm(px[:, c * 128:(c + 1) * 128], xax[:, c, :], C("ident"), True, True, (xax, cst), (px,))
                cp("act", xs_tm[:], px[:], (px,), (xs_tm,))
                if full:
                    tt("dve", v3(Xb[:], 8), v3(xs_tm[:], 8), SM(8, 8).unsqueeze(2).broadcast_to([128, 8, 64]), ALU.mult,
                       (xs_tm, smB[8]), (Xb,))
                for g in range(2):
                    trn(pTR[:, g * 128:(g + 1) * 128], BCb[:, g, :], identb[:], (BCb, identb), (pTR,))
                cp("act", Btm[:], v3(pTR[:, 0:256], 2), (pTR,), (Btm,))
                yield
                psm = PS()
                mm(psm[:, 0:8], C("triu"), SM(16, 8), True, True, (cst, smB[9]), (psm,))
                mm(psm[:, 8:16], C("trils"), SM(16, 8), True, True, (cst, smB[9]), (psm,))
                mm(psm[:, 16:24], C("ones"), SM(16, 8), True, True, (cst, smB[9]), (psm,))
                act(SM(24, 24), psm[:, 0:24], AF.Exp, (psm,), (smB[10],))
                tt("dve", SM(48, 8), SM(8, 8), SM(32, 8), ALU.mult, (smB[8], smB[10]), (smB[11],))
                tt("dve", v3(Xd[:], 8), v3(xs_tm[:], 8), SM(48, 8).unsqueeze(2).broadcast_to([128, 8, 64]), ALU.mult,
                   (xs_tm, smB[11]), (Xd,))
                yield
                if full:
                    tt("dve", v3(dx[:], 8), v3(xs_tm[:], 8), R("dsk", 8).unsqueeze(2).broadcast_to([128, 8, 64]), ALU.mult,
                       (xs_tm, rows), (dx,))
                    psc = PS()
                    for g in range(2):
                        mm(psc[:, g * 128:(g + 1) * 128], BCb[:, g, :], BCb[:, 2 + g, :], True, True, (BCb,), (psc,))
                    tt("dve", scm[:], v3(psc[:, 0:256], 2), C("triu").unsqueeze(1).broadcast_to([128, 2, 128]), ALU.mult,
                       (psc, cst), (scm,))
                    tt("dve", rseg[:], SM(16, 8).unsqueeze(2).broadcast_to([128, 8, 128]),
                       C("triu").unsqueeze(1).broadcast_to([128, 8, 128]), ALU.mult, (smB[9], cst), (rseg,))
                    yield
                    for hh in range(2):
                        pg = PS()
                        mm(pg[:], C("trils"), rseg[:, 4 * hh:4 * hh + 4, :].rearrange("p a b -> p (a b)"), True, True,
                           (cst, rseg), (pg,))
                        act(Dm[:, 4 * hh:4 * hh + 4, :].rearrange("p a b -> p (a b)"), pg[:], AF.Exp, (pg,), (Dm,))
                    for g in range(2):
                        tt("dve", Gm[:, 4 * g:4 * g + 4, :], Dm[:, 4 * g:4 * g + 4, :],
                           scm[:, g, :].unsqueeze(1).broadcast_to([128, 4, 128]), ALU.mult, (Dm, scm), (Gm,))
                    yield
                    py1 = PS()
                    for h in range(8):
                        mm(py1[:, h * 64:(h + 1) * 64], Gm[:, h, :], Xb[:, h * 64:(h + 1) * 64], True, True, (Gm, Xb), (py1,))
                    py2 = PS()
                    for h in range(8):
                        mm(py2[:, h * 64:(h + 1) * 64], BCb[:, 2 + h // 4, :], STb[:, h, :], True, True, (BCb, STb), (py2,))
                    tt("dve", v3(ytmp[:], 8), v3(py2[:], 8), SM(24, 8).unsqueeze(2).broadcast_to([128, 8, 64]), ALU.mult,
                       (py2, smB[10]), (ytmp,))
                    tt("dve", ysd[:], ytmp[:], py1[:], ALU.add, (ytmp, py1), (ysd,))
                    tt("dve", ysd[:], ysd[:], dx[:], ALU.add, (ysd, dx), (ysd,))
                    yield
                pst = PS()
                for h in range(8):
                    mm(pst[:, h * 64:(h + 1) * 64], Btm[:, h // 4, :], Xd[:, h * 64:(h + 1) * 64], True, True, (Btm, Xd), (pst,))
                tt("dve", STf[:], STf[:], SM(40, 8).unsqueeze(2).broadcast_to([128, 8, 64]), ALU.mult, (STf, smB[10]), (STf,))
                tt("dve", STf[:].rearrange("p h d -> p (h d)"), STf[:].rearrange("p h d -> p (h d)"), pst[:], ALU.add,
                   (STf, pst), (STf,))
                if i == B0T - 1:
                    ts("dve", STf[:], STf[:], flag[:], None, ALU.mult, None, (STf, flag), (STf,))
                cp("act", STb[:], STf[:], (STf,), (STb,))
                yield
                if full:
                    tt("dve", ysd[:], ysd[:], gz[p][:], ALU.mult, (ysd, gz[p]), (ysd,))
                    for g in range(2):
                        act(ytmp[:, g * 256:(g + 1) * 256], ysd[:, g * 256:(g + 1) * 256], AF.Square, (ysd,), (ytmp, smB[12]),
                            accum=SM(56 + g))
                    ts("dve", SM(58, 2), SM(56, 2), 1.0 / 256, 1e-6, ALU.mult, ALU.add, (smB[12],), (smB[13],))
                    rsqrt_small(SM(60, 2), SM(58, 2), 2, (smB[13],), (smB[14],))
                    tt("dve", v3(ysd[:], 2), v3(ysd[:], 2), SM(60, 2).unsqueeze(2).broadcast_to([128, 2, 256]), ALU.mult,
                       (ysd, smB[14]), (ysd,))
                    tt("dve", ytok[p][:, 0:512], ysd[:], R("ssdg", 512), ALU.mult, (ysd, rows), (ytok[p],))


            def rwkv_out(i):
                p = i % 2
                full = i >= B0T - 1
                OP()
                act(lin[0:64, :], Pp[0:64, 12, :], AF.Tanh, (Pp,), (lin,))
                tt("dve", kkv[:], Pp[:, 4:8, :], par[:, PC["kk"]:PC["kk"] + 4].unsqueeze(2).broadcast_to([128, 4, 128]),
                   ALU.mult, (Pp, par), (kkv,))
                cp("act", vb[:], Pp[:, 8:12, :], (Pp,), (vb,))
                plw = PS()
                pla = PS()
                for c in range(4):
                    mm(plw[:, c * 128:(c + 1) * 128], Wlora[0:64, c * 128:(c + 1) * 128], lin[0:64, :], True, True,
                       (Wlora, lin), (plw,))
                    mm(pla[:, c * 128:(c + 1) * 128], Wlora[64:128, c * 128:(c + 1) * 128], Pp[64:128, 12, :], True, True,
                       (Wlora, Pp), (pla,))
                act(kk2[:], kkv[:], AF.Square, (kkv,), (kk2,))
                pkn = PS()
                mm(pkn[:], C("onesblk"), kk2[:].rearrange("p a b -> p (a b)"), True, True, (cst, kk2), (pkn,))
                for c in range(4):
                    act(thw[:, c, :], plw[:, c * 128:(c + 1) * 128], AF.Tanh, (plw, hw0), (thw,), bias=hw0[:, c:c + 1], scale=0.5)
                    act(tha[:, c, :], pla[:, c * 128:(c + 1) * 128], AF.Tanh, (pla, hw0), (tha,), bias=hw0[:, 4 + c:5 + c], scale=0.5)
                act(inv[:].rearrange("p a b -> p (a b)"), pkn[:], AF.Ln, (pkn,), (inv,), bias=tiny[:])
                act(inv[:], inv[:], AF.Exp, (inv,), (inv,), scale=-0.5)
                ts("dve", lw[:], thw[:], -0.30326533, -0.30326533, ALU.mult, ALU.add, (thw,), (lw,))
                ts("dve", av[:], tha[:], 0.5, 0.5, ALU.mult, ALU.add, (tha,), (av,))
                tt("dve", kkn[:], kkv[:], inv[:], ALU.mult, (kkv, inv), (kkn,))
                FP()
                OP()
                for c in range(4):
                    S.op("dve", lambda e, c=c: e.tensor_tensor_scan(out=cum[:, c, :], data0=lw[:, c, :], data1=zeros[:],
                                                                      initial=0.0, op0=ALU.add, op1=ALU.add),
                         rd(lw, zeros), rd(cum))
                act(E1[:], cum[:], AF.Exp, (cum,), (E1,))
                act(E2[:], cum[:], AF.Exp, (cum,), (E2,), scale=-1.0)
                FP()
                OP()
                stt("dve", t1[:], av[:], -1.0, par[:, PC["ka"]:PC["ka"] + 4].unsqueeze(2).broadcast_to([128, 4, 128]),
                    ALU.add, ALU.mult, (av, par), (t1,))
                stt("dve", kmod[:], t1[:], 1.0, Pp[:, 4:8, :], ALU.add, ALU.mult, (t1, Pp), (kmod,))
                FP()
                tt("dve", beta[:], kkn[:], av[:], ALU.mult, (kkn, av), (beta,))
                for hp in range(2):
                    sl = slice(hp * 64, (hp + 1) * 64)
                    stt("dve", arz[hp][sl, :, 0, :], kkn[sl, :, :], -1.0, E3[sl, :, :], ALU.mult, ALU.mult, (kkn, E3), (arz[hp],))
                tt("dve", btl[:], beta[:], E2[:], ALU.mult, (beta, E2), (btl,))
                tt("dve", ktl[:], kmod[:], E2[:], ALU.mult, (kmod, E2), (ktl,))
                if full:
                    for hp in range(2):
                        sl = slice(hp * 64, (hp + 1) * 64)
                        tt("dve", arz[hp][sl, :, 1, :], Pp[sl, 0:4, :], E1[sl, :, :], ALU.mult, (Pp, E1), (arz[hp],))
                if full:
                    tt("dve", rk[:], Pp[:, 0:4, :], kmod[:], ALU.mult, (Pp, kmod), (rk,))
                if full:
                    pbn = PS()
                    for c in range(4):
                        mm(pbn[:, 0:8], rk[:, c, :], sel[:, c, :], c == 0, c == 3, (rk, sel), (pbn,))
                    cp("act", bnsb[:], pbn[:, 0:8], (pbn,), (bnsb,))
                for c in range(4):
                    trn(pTR[:, c * 128:(c + 1) * 128], vb[:, c, :], identb[:], (vb, identb), (pTR,))
                cp("act", vtm[:], pTR[:, 0:512], (pTR,), (vtm,))
                FP()
                drain(ogen)
                if full:
                    act(gsb[:, 0:128], Pp[:, 13, :], AF.Tanh, (Pp,), (gsb,), scale=0.5)
                    ts("dve", sg[:], gsb[:, 0:128], 1.0, None, ALU.add, None, (gsb,), (sg,))
                    pgt = PS()
                    mm(pgt[:], sg[:], G2h[:], True, True, (sg, G2h), (pgt,))
                    cp("act", gsb[:], pgt[:], (pgt,), (gsb,))
                if full:
                    tt("dve", bterm[:], v3(vtm[:], 8), bnsb[:].unsqueeze(2).broadcast_to([128, 8, 64]), ALU.mult,
                       (vtm, bnsb), (bterm,))
                for c in range(4):
                    pab = PS()
                    pak = PS()
                    pnt = PS()
                    for hp in range(2):
                        arv = GAz[hp][:, c * 256:(c + 1) * 256]
                        mm(pab[:, hp * 256:(hp + 1) * 256], btl[:, c, :], arv, True, True, (btl, arz[hp]), (pab,))
                        mm(pak[:, hp * 256:(hp + 1) * 256], ktl[:, c, :], arv, True, True, (ktl, arz[hp]), (pak,))
                        mm(pnt[:, hp * 128:(hp + 1) * 128], arz[hp][:, c, 0, :], btl[:, c, :], True, True, (arz[hp], btl), (pnt,))
                    tt("dve", ABs[c][:], v3(pab[:], 2), maskSI[:].unsqueeze(1).broadcast_to([128, 2, 256]), ALU.mult,
                       (pab, maskSI), (ABs[c],))
                    tt("dve", AKs[c][:], v3(pak[:], 2), maskSI[:].unsqueeze(1).broadcast_to([128, 2, 256]), ALU.mult,
                       (pak, maskSI), (AKs[c],))
                    tt("dve", PTm[0][c][:], v3(pnt[:, 0:256], 2), C("trils").unsqueeze(1).broadcast_to([128, 2, 128]),
                       ALU.mult, (pnt, cst), (PTm[0][c],))
                    FP()
                    EP()
                    EP()
                pu = pool_banks[0]
                for h in range(8):
                    c, hp = h // 2, h % 2
                    mm(pu[:, h * 64:(h + 1) * 64], arz[hp][:, c, 0, :], STrb[:, c, :], True, False, (arz[hp], STrb), (pu,))
                    mm(pu[:, h * 64:(h + 1) * 64], AKs[c][:, hp, 0:128], vtm[:, h * 64:(h + 1) * 64], False, True,
                       (AKs[c], vtm), (pu,))
                cp("act", UTf[:], pu[:], (pu,), (UTf,))
                cp("dve", UT[:].rearrange("p h d -> p (h d)"), UTf[:], (UTf,), (UT,))
                for c in range(4):
                    trn(pTR[:, c * 128:(c + 1) * 128], btl[:, c, :], identb[:], (btl, identb), (pTR,))
                    trn(pTR[:, 512 + c * 128:512 + (c + 1) * 128], ktl[:, c, :], identb[:], (ktl, identb), (pTR,))
                cp("act", btm[:], pTR[:, 0:512], (pTR,), (btm,))
                cp("dve", ktm[:], pTR[:, 512:1024], (pTR,), (ktm,))
                FP()
                for lvl in range(7):
                    q = lvl % 2
                    pd = PS()
                    for h in range(8):
                        c, hp = h // 2, h % 2
                        Pl, PlB = (ABs[c][:, hp, 0:128], ABs[c]) if lvl == 0 else (Pm[q][c][:, hp, :], Pm[q][c])
                        mm(pd[:, h * 64:(h + 1) * 64], Pl, UT[:, h, :], True, True, (PlB, UT), (pd,))
                    tt("dve", UTf[:], UTf[:], pd[:], ALU.add, (UTf, pd), (UTf,))
                    cp("act", UT[:].rearrange("p h d -> p (h d)"), UTf[:], (UTf,), (UT,))
                    EP()
                    SP()
                    if lvl < 6:
                        for c in range(4):
                            pp = PS()
                            for hp in range(2):
                                Pl, PlB = (ABs[c][:, hp, 0:128], ABs[c]) if lvl == 0 else (Pm[q][c][:, hp, :], Pm[q][c])
                                mm(pp[:, hp * 128:(hp + 1) * 128], PTm[q][c][:, hp, :], Pl, True, True,
                                   (PTm[q][c], PlB), (pp,))
                                mm(pp[:, 256 + hp * 128:256 + (hp + 1) * 128], Pl, PTm[q][c][:, hp, :], True, True,
                                   (PTm[q][c], PlB), (pp,))
                            cp("dve" if c % 2 == 0 else "act", PPm[1 - q][c][:].rearrange("p a b c -> p (a b c)"), pp[:],
                               (pp,), (PPm[1 - q][c],))
                            if c == 1:
                                SP()
                psr = PS()
                for c in range(4):
                    o = psr[:, c * 128:(c + 1) * 128]
                    mm(o, btm[:, c * 128:(c + 1) * 128], UT[:, 2 * c:2 * c + 2, :].rearrange("p a b -> p (a b)"), True, False,
                       (btm, UT), (psr,))
                    mm(o, ktm[:, c * 128:(c + 1) * 128], vtm[:, c * 128:(c + 1) * 128], False, True, (ktm, vtm), (psr,))
                if full:
                    pyr = PS()
                    for h in range(8):
                        c, hp = h // 2, h % 2
                        o = pyr[:, h * 64:(h + 1) * 64]
                        mm(o, arz[hp][:, c, 1, :], STrb[:, c, :], True, False, (arz[hp], STrb), (pyr,))
                        mm(o, ABs[c][:, hp, 128:256], UT[:, h, :], False, False, (ABs[c], UT), (pyr,))
                        mm(o, AKs[c][:, hp, 128:256], vtm[:, h * 64:(h + 1) * 64], False, True, (AKs[c], vtm), (pyr,))
                psr3 = v3(psr[:], 4)
                tt("dve", tmpS[0:64, :, :], STr[0:64, :, :], psr3[0:64, :, 0:64], ALU.add, (STr, psr), (tmpS,))
                tt("dve", tmpS[64:128, :, :], STr[64:128, :, :], psr3[64:128, :, 64:128], ALU.add, (STr, psr), (tmpS,))
                tt("dve", STr[:], tmpS[:], E1[:, :, 127:128].broadcast_to([128, 4, 64]), ALU.mult, (tmpS, E1), (STr,))
                if i == B0T - 1:
                    ts("dve", STr[:], STr[:], flag[:], None, ALU.mult, None, (STr, flag), (STr,))
                cp("act", STrb[:], STr[:], (STr,), (STrb,))
                SP()
                if full:
                    cp("act", yr[:].rearrange("p h d -> p (h d)"), pyr[:], (pyr,), (yr,))
                    ogen[0] = out_gen(i)

            def out_gen(i):
                p = i % 2
                slot = i - (B0T - 1)
                act(ysq[:], yr[:], AF.Square, (yr,), (ysq,))
                S.op("dve", lambda e: e.tensor_reduce(out=gn[:, 0:8], in_=yr[:], axis=AX.X, op=ALU.add), rd(yr), rd(gn))
                S.op("dve", lambda e: e.tensor_reduce(out=gn[:, 8:16], in_=ysq[:], axis=AX.X, op=ALU.add), rd(ysq), rd(gn))
                ts("dve", gn[:, 0:16], gn[:, 0:16], 1.0 / 64, None, ALU.mult, None, (gn,), (gn,))
                tt("dve", gn[:, 16:24], gn[:, 0:8], gn[:, 0:8], ALU.mult, (gn,), (gn,))
                tt("dve", gn[:, 24:32], gn[:, 8:16], gn[:, 16:24], ALU.subtract, (gn,), (gn,))
                ts("dve", gn[:, 24:32], gn[:, 24:32], 64e-5, None, ALU.add, None, (gn,), (gn,))
                rsqrt_small(gn[:, 32:40], gn[:, 24:32], 8, (gn,), (gn,))
                tt("dve", yr[:], yr[:], gn[:, 0:8].unsqueeze(2).broadcast_to([128, 8, 64]), ALU.subtract, (yr, gn), (yr,))
                tt("dve", yr[:], yr[:], gn[:, 32:40].unsqueeze(2).broadcast_to([128, 8, 64]), ALU.mult, (yr, gn), (yr,))
                yr2 = yr[:].rearrange("p h d -> p (h d)")
                tt("dve", yr2, yr2, R("lnw", 512), ALU.mult, (yr, rows), (yr,))
                tt("dve", yr2, yr2, R("lnb", 512), ALU.add, (yr, rows), (yr,))
                tt("dve", yr[:], yr[:], bterm[:], ALU.add, (yr, bterm), (yr,))
                tt("dve", ytok[p][:, 512:1024], yr2, gsb[:], ALU.mult, (yr, gsb), (ytok[p],))
                yield
                for c in range(8):
                    trn(pTR[:, c * 128:(c + 1) * 128], ytok[p][:, c * 128:(c + 1) * 128], identb[:], (ytok[p], identb), (pTR,))
                cp("act", mixT[:].rearrange("p a b -> p (a b)"), pTR[:], (pTR,), (mixT,))
                yield
                pm = [PS(), PS()]
                for hf in range(2):
                    for c in range(8):
                        mm(pm[hf][:], mixT[:, c, :], Wout[:, c, hf * 512:(hf + 1) * 512], c == 0, c == 7, (mixT, Wout), (pm[hf],))
                dma("sp", ht[:], D["x_in"][i * TT:(i + 1) * TT, :], "xr", (), (ht,))
                for hf in range(2):
                    act(B7[:, hf * 512:(hf + 1) * 512], pm[hf][:], AF.Square, (pm[hf],), (B7, hsm), accum=hs[:, hf:hf + 1])
                tt("dve", hs[:, 2:3], hs[:, 0:1], hs[:, 1:2], ALU.add, (hsm,), (hsm,))
                ts("dve", hs[:, 2:3], hs[:, 2:3], 1.0 / 1024, 1e-6, ALU.mult, ALU.add, (hsm,), (hsm,))
                rsqrt_small(hs[:, 3:4], hs[:, 2:3], 1, (hsm,), (hsm,))
                for hf in range(2):
                    stt("dve", B7[:, hf * 512:(hf + 1) * 512], pm[hf][:], hs[:, 3:4], R("pmn", 1024)[:, hf * 512:(hf + 1) * 512],
                        ALU.mult, ALU.mult, (pm[hf], hsm, rows), (B7,))
                tt("dve", ht[:], ht[:], B7[:], ALU.add, (ht, B7), (ht,))
                dma("sp", hscr[slot * TT:(slot + 1) * TT, :], ht[:], "hst", (ht,), (hscrB,))
                yield

            ogen = [None]

            def OP():
                if ogen[0] is not None:
                    next(ogen[0], None)

            fgen = [None]
            egen = [None]
            sgen = [None]

            def FP():
                if fgen[0] is not None:
                    next(fgen[0], None)

            def need_banks(n):
                while fstate[0] < n:
                    if fgen[0] is None or next(fgen[0], "done") == "done":
                        break

            def need_front_done():
                if fgen[0] is not None:
                    for _ in fgen[0]:
                        pass

            def EP():
                if egen[0] is not None:
                    next(egen[0], None)

            def need_conv():
                while estate[0] < 8:
                    if egen[0] is None or next(egen[0], "done") == "done":
                        break

            def SP():
                if sgen[0] is not None:
                    next(sgen[0], None)

            def drain(g):
                if g[0] is not None:
                    for _ in g[0]:
                        pass
                g[0] = None

            fgen[0] = front_gen(0)
            drain(fgen)
            fstate[0] = 8
            egen[0] = early_gen(0)
            drain(egen)
            estate[0] = 8
            sgen[0] = ssd_gen(0)
            drain(sgen)
            CK(2)
            for i in range(NT_ALL):
                if i + 1 < NT_ALL:
                    fgen[0] = front_gen(i + 1)
                    egen[0] = early_gen(i + 1)
                    sgen[0] = ssd_gen(i + 1)
                FP()
                FP()
                rwkv_out(i)
                drain(fgen)
                drain(egen)
                drain(sgen)
                CK(5)
            drain(ogen)
            S.barrier()
            CK(6)

        rot_set[0] = [4, 5]
        TBK = 256
        NTB = NT_MAIN * TT // TBK
        with ExitStack() as stB:
            Wup = sb("Wup", [128, 8, 5632], BF16, stB)
            Wdown = sb("Wdown", [128, 22, 1024], BF16, stB)
            pfn = sb("pfn", [128, 1024], F32, stB)
            dma("sp", pfn[:], D["rows"][:, RC["pfn"]:RC["pfn"] + 1024], "c6", (), (pfn,))
            WupB = [[Buf("wup%d_%d" % (k, hf)) for hf in range(2)] for k in range(8)]
            WdB = [Buf("wd%d" % f3) for f3 in range(8)]
            with ExitStack() as stS:
                stage = [sb("stageB%d" % q, [128, 3072], F32, stS) for q in range(3)]
                sq = [0]

                def load_rows2(src_ap, ncols, fn):
                    si = sq[0] % 3
                    stg = stage[si]
                    sq[0] += 1
                    dma("sp", stg[:, 0:ncols], src_ap, "wsu%d" % si, (), (stg,))
                    fn(stg, sq[0])

                def cast_up(stg, n, k, hf):
                    o = Wup[:, k, hf * 2816:(hf + 1) * 2816]
                    e = n % 2
                    if e == 0:
                        ts("dve", o, stg[:, 0:2816], P("g2n", k), None, ALU.mult, None, (stg, par), (WupB[k][hf],))
                    else:
                        act(o, stg[:, 0:2816], AF.Copy, (stg, par), (WupB[k][hf],), scale=P("g2n", k))
                for k in range(8):
                    for hf in range(2):
                        load_rows2(D["wup"][:, k * 5632 + hf * 2816:k * 5632 + (hf + 1) * 2816], 2816,
                                   lambda stg, n, k=k, hf=hf: cast_up(stg, n, k, hf))
                for f3 in range(8):
                    n3 = min(3, 22 - 3 * f3)
                    load_rows2(D["wdown"][:, 3 * f3 * 1024:(3 * f3 + n3) * 1024], n3 * 1024,
                               lambda stg, n, f3=f3, n3=n3: cp(("act", "dve")[n % 2], Wdown[:, 3 * f3:3 * f3 + n3, :],
                                                               v3(stg[:, 0:n3 * 1024], n3), (stg,), (WdB[f3],)))
                S.barrier()
            CK(7)
            hld = [sb("hld%d" % q, [128, 2, 1024], F32, stB) for q in range(2)]
            hbB = sb("hbB", [128, 1024], BF16, stB)
            hnB = [sb("hnB%d" % q, [128, 8, 2 + TBK], BF16, stB) for q in range(2)]
            U = [sb("U%d" % q, [128, 2, 2, 2 + TBK], F32, stB) for q in range(2)]
            cacc = [[sb("cacc%d_%d" % (q, gv), [128, 2, TBK], F32, stB) for gv in range(2)] for q in range(2)]
            sgt = [sb("sgt%d" % q, [128, 2, TBK], F32, stB) for q in range(2)]
            actT = [sb("actT%d" % q, [128, 2, TBK], BF16, stB) for q in range(2)]
            carry = sb("carry", [128, 22, 2, 2], F32, stB)
            carryB = [Buf("carry%d" % f) for f in range(22)]
            fo = [sb("fo%d" % q, [128, 1024], F32, stB) for q in range(2)]
            fs = sb("fs", [128, 16], F32, stB)
            junkF = sb("junkF", [128, 1024], BF16, stB)
            fsm = Buf("fsm")
            outB = Buf("out")
            last = []
            memset("pool", carry[:], 0.0, carryB)

            def prep_sub(row0, q, sub, dst, dcol):
                dma("sp", hld[q][:, sub, :], hscr[row0:row0 + TT, :], "hld%d" % q, (hscrB,), (hld[q],))
                act(hbB[:], hld[q][:, sub, :], AF.Square, (hld[q],), (hbB, fsm), accum=fs[:, 8:9])
                ts("dve", fs[:, 9:10], fs[:, 8:9], 1.0 / 1024, 1e-6, ALU.mult, ALU.add, (fsm,), (fsm,))
                rsqrt_small(fs[:, 10:11], fs[:, 9:10], 1, (fsm,), (fsm,))
                act(hbB[:], hld[q][:, sub, :], AF.Copy, (hld[q], fsm), (hbB,), scale=fs[:, 10:11])
                for k in range(8):
                    trn(pT[:, k * 128:(k + 1) * 128], hbB[:, k * 128:(k + 1) * 128], identb[:], (hbB, identb), (pT,))
                cp("dve", dst[:, :, dcol:dcol + TT], v3(pT[:], 8), (pT,), (dst,))

            prep_sub(0, 1, 0, hnB[1], 2)
            ts("pool", hnB[1][:, :, 128:130], hnB[1][:, :, 128:130], flag[:], None, ALU.mult, None, (hnB[1], flag), (hnB[1],))
            for f in range(22):
                pc = PS()
                for gv in range(2):
                    col = gv * 2816 + f * 128
                    for k in range(8):
                        mm(pc[:, gv * 2:gv * 2 + 2], Wup[:, k, col:col + 128], hnB[1][:, k, 128:130], k == 0, k == 7,
                           (WupB[k][gv], hnB[1]), (pc,))
                cp("act", carry[:, f, :, :].rearrange("p a b -> p (a b)"), pc[:, 0:4], (pc,), (carryB[f // 2],))

            def prep_gen(tb):
                for sub in range(2):
                    prep_sub(TT + tb * TBK + sub * TT, tb % 2, sub, hnB[tb % 2], 2 + sub * TT)
                    yield

            for _ in prep_gen(0):
                pass
            for tb in range(NTB):
                q2 = tb % 2
                pgen = prep_gen(tb + 1) if tb + 1 < NTB else None
                pf = [[pool_banks[0], pool_banks[1]], [pool_banks[2], pool_banks[3]]]
                pub = [pool_banks[4], pool_banks[5]]

                def A_pe(fp):
                    for j in range(2):
                        f = 2 * fp + j
                        for gv in range(2):
                            col = gv * 2816 + f * 128
                            for k in range(8):
                                mm(pub[j][:, gv * TBK:(gv + 1) * TBK], Wup[:, k, col:col + 128], hnB[q2][:, k, 2:2 + TBK], k == 0, k == 7,
                                   (WupB[k][gv], hnB[q2]), (pub[j],))

                def A_rest(fp):
                    q = fp % 2
                    Uq = U[q]
                    for j in range(2):
                        cp("act", Uq[:, j, :, 2:2 + TBK], v3(pub[j][:], 2), (pub[j],), (Uq,))
                    f0 = 2 * fp
                    cp("dve", Uq[:, :, :, 0:2], carry[:, f0:f0 + 2, :, :], (carryB[fp],), (Uq,))
                    cp("dve", carry[:, f0:f0 + 2, :, :], Uq[:, :, :, TBK:TBK + 2], (Uq,), (carryB[fp],))
                    ca = cacc[q]
                    for j in range(2):
                        for gv in range(2):
                            ch = gv * 22 + f0 + j
                            act(ca[gv][:, j, :], Uq[:, j, gv, 2:2 + TBK], AF.Identity, (Uq, par), (ca[gv],), bias=P("fcb", ch),
                                scale=P("fcw", ch * 3 + 2))

                def B_nonpe(fp):
                    q = fp % 2
                    Uq = U[q]
                    ca = cacc[q]
                    f0 = 2 * fp
                    for k in range(2):
                        for j in range(2):
                            for gv in range(2):
                                ch = gv * 22 + f0 + j
                                stt("dve", ca[gv][:, j, :], Uq[:, j, gv, k:k + TBK], P("fcw", ch * 3 + k), ca[gv][:, j, :], ALU.mult, ALU.add,
                                    (Uq, par, ca[gv]), (ca[gv],))
                    act(sgt[q][:], ca[0][:], AF.Silu, (ca[0],), (sgt[q],))
                    tt("dve", actT[q][:], sgt[q][:], ca[1][:], ALU.mult, (sgt[q], ca[1]), (actT[q],))

                def B_pe(fp):
                    q = fp % 2
                    a3 = actT[q]
                    for j in range(2):
                        f = 2 * fp + j
                        for sub in range(2):
                            for hf in range(2):
                                mm(pf[sub][hf][:], a3[:, j, sub * TT:(sub + 1) * TT], Wdown[:, f, hf * 512:(hf + 1) * 512], f == 0, f == 21,
                                   (a3, WdB[f // 3]), (pf[sub][hf],))

                A_pe(0)
                A_rest(0)
                for fp in range(11):
                    if fp + 1 < 11:
                        A_pe(fp + 1)
                    B_nonpe(fp)
                    B_pe(fp)
                    if fp + 1 < 11:
                        A_rest(fp + 1)
                    if pgen is not None and fp in (3, 7):
                        next(pgen, None)
                for sub in range(2):
                    for hf in range(2):
                        cp("act", fo[sub][:, hf * 512:(hf + 1) * 512], pf[sub][hf][:], (pf[sub][hf],), (fo[sub],))
                if pgen is not None:
                    for _ in pgen:
                        pass
                for sub in range(2):
                    fq = fo[sub]
                    o4 = 4 * sub
                    act(junkF[:], fq[:], AF.Square, (fq,), (junkF, fsm), accum=fs[:, o4:o4 + 1])
                    ts("dve", fs[:, o4 + 2:o4 + 3], fs[:, o4:o4 + 1], 1.0 / 1024, 1e-6, ALU.mult, ALU.add, (fsm,), (fsm,))
                    rsqrt_small(fs[:, o4 + 3:o4 + 4], fs[:, o4 + 2:o4 + 3], 1, (fsm,), (fsm,))
                    stt("dve", fq[:], fq[:], fs[:, o4 + 3:o4 + 4], pfn[:], ALU.mult, ALU.mult, (fq, fsm, pfn), (fq,))
                    tt("dve", fq[:], fq[:], hld[q2][:, sub, :], ALU.add, (fq, hld[q2]), (fq,))
                    r0 = tb * TBK + sub * TT
                    last.append(dma("sp", out[r0:r0 + TT, :], fq[:], "ost%d" % sub, (fq,), (outB,)))
            S.finish(last[-2:])
    except _Stop:
        pass
    return nc


_NC_CACHE = {}


def kernel(**inputs):
    inputs = {k: np.asarray(v) for k, v in inputs.items()}
    x = inputs["x"].astype(np.float32, copy=False)
    shared = _host_prep(inputs)
    in_maps = []
    for c in range(8):
        b, s = c // 2, c % 2
        m = dict(shared)
        if s == 0:
            m["x_in"] = np.ascontiguousarray(np.concatenate([x[b, 0:2048], x[b, 0:2048]], axis=0))
        else:
            m["x_in"] = np.ascontiguousarray(x[b])
        m["flag"] = np.full((128, 1), float(s), np.float32)
        in_maps.append(m)
    if "nc" not in _NC_CACHE:
        _NC_CACHE["nc"] = build_nc()
    res = run_bass_kernel_spmd(_NC_CACHE["nc"], in_maps, core_ids=list(range(8)))
    outp = np.empty((4, 4096, 1024), np.float32)
    for c in range(8):
        b, s = c // 2, c % 2
        outp[b, s * 2048:(s + 1) * 2048] = res.results[c]["out"]
    return outp
```

```python
from contextlib import ExitStack
import numpy as np
import concourse.bass as bass
import concourse.mybir as mybir
from concourse.bass_utils import run_bass_kernel_spmd

F32 = mybir.dt.float32
BF16 = mybir.dt.bfloat16
AF = mybir.ActivationFunctionType
ALU = mybir.AluOpType
AX = mybir.AxisListType

SAME_ENGINE_SYNC = True
NT_ALL = 32
NT_MAIN = 16
TT = 128
NFM = 22
HW = 131


class Buf:
    __slots__ = ("name", "w", "r")

    def __init__(self, name):
        self.name = name
        self.w = None
        self.r = []


class Sched:
    ENGS = ("pe", "act", "dve", "pool", "sp")

    def __init__(self, nc, stack):
        self.nc = nc
        self.stack = stack
        self.sems = {}
        self.cnt = {}
        for e in self.ENGS:
            self.sems[e] = stack.enter_context(nc.semaphore("s_" + e))
            self.cnt[e] = 0
        self.known = {e: {} for e in self.ENGS}
        self.snap = {}
        self.prog = {e: [] for e in self.ENGS}
        self.dma_sems = {}
        self.dma_cnt = {}

    def _sem_handle(self, k):
        return self.sems[k] if k in self.sems else self.dma_sems[k]

    def _need(self, eng, tok, acc):
        if tok is None:
            return
        semkey, val, src = tok
        if src == eng and (eng == "pe" or not SAME_ENGINE_SYNC):
            return
        if self.known[eng].get(semkey, 0) >= val:
            return
        if acc.get(semkey, 0) < val:
            acc[semkey] = val

    def _emit_waits(self, eng, acc):
        for semkey, val in acc.items():
            h = self._sem_handle(semkey)
            self.prog[eng].append(lambda e, h=h, val=val: e.wait_ge(h, val))
            k = self.known[eng]
            if k.get(semkey, 0) < val:
                k[semkey] = val
            if semkey in self.sems:
                sn = self.snap.get((semkey, val))
                if sn:
                    for kk, vv in sn.items():
                        if k.get(kk, 0) < vv:
                            k[kk] = vv

    def _deps(self, eng, reads, writes):
        acc = {}
        for b in reads:
            self._need(eng, b.w, acc)
        for b in writes:
            self._need(eng, b.w, acc)
            for t in b.r:
                self._need(eng, t, acc)
        self._emit_waits(eng, acc)

    def _mark(self, tok, reads, writes):
        for b in reads:
            b.r.append(tok)
            if len(b.r) > 24:
                b.r = b.r[-24:] if False else b.r
        for b in writes:
            b.w = tok
            b.r = []

    def op(self, eng, fn, reads=(), writes=()):
        self._deps(eng, reads, writes)
        self.cnt[eng] += 1
        n = self.cnt[eng]
        h = self.sems[eng]
        self.prog[eng].append(lambda e, fn=fn, h=h: fn(e).then_inc(h, 1))
        tok = (eng, n, eng)
        sn = dict(self.known[eng])
        sn[eng] = n
        self.snap[(eng, n)] = sn
        self._mark(tok, reads, writes)
        return tok

    def dma(self, queue, fn, semkey, reads=(), writes=()):
        if semkey not in self.dma_sems:
            self.dma_sems[semkey] = self.stack.enter_context(self.nc.semaphore("d_" + semkey))
            self.dma_cnt[semkey] = 0
        self._deps(queue, reads, writes)
        self.dma_cnt[semkey] += 16
        val = self.dma_cnt[semkey]
        h = self.dma_sems[semkey]
        self.prog[queue].append(lambda e, fn=fn, h=h: fn(e).then_inc(h, 16))
        tok = (semkey, val, "dma")
        self._mark(tok, reads, writes)
        return tok

    def barrier(self):
        toks = [(e, self.cnt[e], e) for e in self.ENGS if self.cnt[e] > 0]
        toks += [(k, v, "dma") for k, v in self.dma_cnt.items() if v > 0]
        for e in self.ENGS:
            acc = {}
            for t in toks:
                if t[0] == e:
                    continue
                self._need(e, t, acc)
            self._emit_waits(e, acc)

    def finish(self, final_toks):
        for t in final_toks:
            acc = {}
            self._need("sp", t, acc)
            self._emit_waits("sp", acc)
        with self.nc.Block() as block:
            @block.tensor
            def _(e):
                for f in self.prog["pe"]:
                    f(e)

            @block.scalar
            def _(e):
                for f in self.prog["act"]:
                    f(e)

            @block.vector
            def _(e):
                for f in self.prog["dve"]:
                    f(e)

            @block.gpsimd
            def _(e):
                for f in self.prog["pool"]:
                    f(e)

            @block.sync
            def _(e):
                for f in self.prog["sp"]:
                    f(e)


PC = {}
_o = 0
for _n, _w in (("cw", 32), ("cb", 8), ("mu", 14), ("w0", 4), ("a0", 4), ("kk", 4), ("ka", 4),
               ("fcw", 132), ("fcb", 44), ("g1", 8), ("g2n", 8)):
    PC[_n] = _o
    _o += _w
NPAR = _o
RC = {}
_o = 0
for _n, _w in (("dtb", 8), ("alog", 8), ("dsk", 8), ("ssdg", 512), ("lnw", 512), ("lnb", 512),
               ("pmn", 1024), ("pfn", 1024)):
    RC[_n] = _o
    _o += _w
NROW = _o
CC = {"ident": 0, "triu": 128, "trius": 256, "trils": 384, "onesblk": 512, "ones": 640}
NCONST = 768


def _cols(v, nch):
    return np.ascontiguousarray(v.reshape(nch, 128).T)


def _host_prep(inp):
    f = np.float32
    w_in = inp["w_in"][0]
    s0 = 0
    r0 = 1544
    perm = np.concatenate([
        np.arange(512, 1536),
        r0 + np.arange(0, 512),
        r0 + np.arange(576, 1088),
        r0 + np.arange(1088, 1600),
        r0 + np.arange(512, 576),
        r0 + np.arange(1600, 1664),
        r0 + np.arange(1664, 1792),
        np.arange(0, 512),
        np.arange(1536, 1544),
    ])
    win = np.ascontiguousarray(w_in[:, perm].reshape(8, 128, 3336).transpose(1, 0, 2)).reshape(128, 8 * 3336)
    wout = np.ascontiguousarray(inp["w_out"][0].reshape(8, 128, 1024).transpose(1, 0, 2)).reshape(128, 8 * 1024)
    wup = np.ascontiguousarray(inp["ffn_w_up"][0].reshape(8, 128, 5632).transpose(1, 0, 2)).reshape(128, 8 * 5632)
    wdown = np.ascontiguousarray(inp["ffn_w_down"][0].reshape(22, 128, 1024).transpose(1, 0, 2)).reshape(128, 22 * 1024)
    par = np.zeros((128, NPAR), f)
    cw = inp["ssd_conv_w"][0]
    par[:, PC["cw"]:PC["cw"] + 32] = cw.reshape(4, 8, 128).transpose(2, 1, 0).reshape(128, 32)
    par[:, PC["cb"]:PC["cb"] + 8] = _cols(inp["ssd_conv_b"][0], 8)
    mu = inp["rwkv_mu"][0]
    mup = np.concatenate([mu[0:512], mu[576:1088], mu[1088:1600], mu[512:576], mu[1600:1664], mu[1664:1792]])
    par[:, PC["mu"]:PC["mu"] + 14] = _cols(mup, 14)
    par[:, PC["w0"]:PC["w0"] + 4] = _cols(inp["rwkv_w0"][0], 4)
    par[:, PC["a0"]:PC["a0"] + 4] = _cols(inp["rwkv_a0"][0], 4)
    par[:, PC["kk"]:PC["kk"] + 4] = _cols(inp["rwkv_k_k"][0], 4)
    par[:, PC["ka"]:PC["ka"] + 4] = _cols(inp["rwkv_k_a"][0], 4)
    fw = inp["ffn_conv_w"][0]
    par[:, PC["fcw"]:PC["fcw"] + 132] = fw.reshape(3, 44, 128).transpose(2, 1, 0).reshape(128, 132)
    par[:, PC["fcb"]:PC["fcb"] + 44] = _cols(inp["ffn_conv_b"][0], 44)
    par[:, PC["g1"]:PC["g1"] + 8] = _cols(inp["pre_mix_norm"][0], 8)
    par[:, PC["g2n"]:PC["g2n"] + 8] = _cols(inp["pre_ffn_norm"][0], 8)
    rows = np.zeros((128, NROW), f)

    def put(name, v):
        rows[:, RC[name]:RC[name] + v.size] = np.broadcast_to(v.reshape(1, -1), (128, v.size))
    put("dtb", inp["ssd_dt_bias"][0])
    put("alog", inp["ssd_a_log"][0])
    put("dsk", inp["ssd_d"][0])
    put("ssdg", inp["ssd_norm"][0])
    put("lnw", inp["rwkv_ln_w"][0])
    put("lnb", inp["rwkv_ln_b"][0])
    put("pmn", inp["post_mix_norm"][0])
    put("pfn", inp["post_ffn_norm"][0])
    consts = np.zeros((128, NCONST), f)
    i = np.arange(128)
    consts[:, 0:128] = np.eye(128)
    consts[:, 128:256] = (i[:, None] <= i[None, :])
    consts[:, 256:384] = (i[:, None] < i[None, :])
    consts[:, 384:512] = (i[:, None] > i[None, :])
    consts[:, 512:640] = ((i[:, None] // 64) == (i[None, :] // 64))
    consts[:, 640:768] = 1.0
    wlora = np.concatenate([inp["rwkv_w2"][0], inp["rwkv_a2"][0]], axis=0).astype(f)
    g2 = np.ascontiguousarray(inp["rwkv_g2"][0]).astype(f)
    rk = inp["rwkv_r_k"][0]
    sel = np.zeros((128, 4, 8), f)
    for c in range(4):
        for hp in range(2):
            sel[hp * 64:(hp + 1) * 64, c, 2 * c + hp] = rk[2 * c + hp]
    shared = {"win": win, "wout": wout, "wup": wup, "wdown": wdown, "params": par, "rows": rows,
              "consts": consts, "wlora": wlora, "g2": g2, "sel": sel.reshape(128, 32)}
    return {k: np.ascontiguousarray(v, dtype=f) for k, v in shared.items()}


NSLOT = NT_MAIN + 1
DBG = 0


class _Stop(Exception):
    pass


def build_nc():
    nc = bass.Bass("TRN2", target_bir_lowering=False)
    D = {}
    for name, shape in (("x_in", [4096, 1024]), ("win", [128, 8 * 3336]), ("wout", [128, 8192]),
                        ("wup", [128, 8 * 5632]), ("wdown", [128, 22 * 1024]), ("params", [128, NPAR]),
                        ("rows", [128, NROW]), ("consts", [128, NCONST]), ("wlora", [128, 512]),
                        ("g2", [128, 512]), ("sel", [128, 32]), ("flag", [128, 1])):
        D[name] = nc.dram_tensor(name, shape, F32, kind="ExternalInput").ap()
    out = nc.dram_tensor("out", [NT_MAIN * TT, 1024], F32, kind="ExternalOutput").ap()
    hscr = nc.dram_tensor("hscr", [NSLOT * TT, 1024], F32, kind="Internal").ap()

    try:
      with ExitStack() as st:
        S = Sched(nc, st)
        dbgt = st.enter_context(nc.sbuf_tensor("sb_dbgt", [128, 64], F32))

        def CK(n, src=None):
            if DBG == n:
                if src is None:
                    S.op("dve", lambda e: e.memset(dbgt[:], 1.0), (), ())
                    tk = S.dma("sp", lambda e: e.dma_start(out=out[0:128, 0:64], in_=dbgt[:]), "dbg", (), ())
                else:
                    ap, bufs, ncol = src
                    tk = S.dma("sp", lambda e: e.dma_start(out=out[0:128, 0:ncol], in_=ap), "dbg", [b.b for b in bufs], ())
                S.finish([tk])
                raise _Stop()

        class TB:
            def __init__(self, t, name, b=None):
                self.t = t
                self.b = b if b is not None else Buf(name)

            def __getitem__(self, k):
                return self.t[k]

        class View:
            def __init__(self, ap, b):
                self.ap = ap
                self.b = b

            def __getitem__(self, k):
                return self.ap[k]

        def sb(name, shape, dt, stack=st):
            return TB(stack.enter_context(nc.sbuf_tensor("sb_" + name, shape, dt)), name)

        def psum(name, shape, dt):
            t = TB(st.enter_context(nc.psum_tensor("ps_" + name, shape, dt)), name)
            PSB.add(id(t.b))
            return t

        PSB = set()

        def rd(*xs):
            return [x.b if hasattr(x, "b") else x for x in xs]

        def _xrw(r, w):
            r = rd(*r)
            w = rd(*w)
            rr = [x for x in r if id(x) not in PSB]
            ww = list(w) + [x for x in r if id(x) in PSB and x not in w]
            return rr, ww

        def tt(eng, o, a, b, op, r, w):
            S.op(eng, lambda e: e.tensor_tensor(out=o, in0=a, in1=b, op=op), *_xrw(r, w))

        def ts(eng, o, a, s1, s2, op0, op1, r, w):
            if s2 is None:
                S.op(eng, lambda e: e.tensor_scalar(out=o, in0=a, scalar1=s1, scalar2=None, op0=op0), *_xrw(r, w))
            else:
                S.op(eng, lambda e: e.tensor_scalar(out=o, in0=a, scalar1=s1, scalar2=s2, op0=op0, op1=op1), *_xrw(r, w))

        def stt(eng, o, a, s, b, op0, op1, r, w):
            S.op(eng, lambda e: e.scalar_tensor_tensor(out=o, in0=a, scalar=s, in1=b, op0=op0, op1=op1), *_xrw(r, w))

        def act(o, a, func, r, w, bias=None, scale=None, accum=None):
            kw = {}
            if bias is not None:
                kw["bias"] = bias
            if scale is not None:
                kw["scale"] = scale
            if accum is not None:
                kw["accum_out"] = accum
            S.op("act", lambda e: e.activation(out=o, in_=a, func=func, **kw), *_xrw(r, w))

        def cp(eng, o, a, r, w):
            if eng == "act":
                act(o, a, AF.Copy, r, w)
            else:
                S.op(eng, lambda e: e.tensor_copy(out=o, in_=a), *_xrw(r, w))

        def mm(o, l, rh, start, stop, r, w):
            S.op("pe", lambda e: e.matmul(out=o, lhsT=l, rhs=rh, start=start, stop=stop, skip_group_check=True), *_xrw(r, w))

        def trn(o, a, ident, r, w):
            S.op("pe", lambda e: e.transpose(out=o, in_=a, identity=ident), *_xrw(r, w))

        def memset(eng, o, v, w):
            S.op(eng, lambda e: e.memset(o, v), (), rd(*w))

        def dma(q, o, a, key, r, w):
            return S.dma(q, lambda e: e.dma_start(out=o, in_=a), key, *_xrw(r, w))

        def v3(ap, a):
            return ap.rearrange("p (a b) -> p a b", a=a)

        par = sb("par", [128, NPAR], F32)
        cst = sb("cst", [128, NCONST], F32)
        identb = sb("identb", [128, 128], BF16)
        flag = sb("flag", [128, 1], F32)
        neghalf = sb("neghalf", [128, 512], F32)
        zeros = sb("zeros", [128, 128], F32)
        pT = psum("pT", [128, 1024], BF16)
        pTR = psum("pTR", [128, 1024], BF16)
        pool_banks = [psum("pb%d" % i, [128, 512], F32) for i in range(6)]
        rot = [0]
        rot_set = [[1, 2, 3, 4, 5]]

        def PS():
            rs = rot_set[0]
            b = pool_banks[rs[rot[0] % len(rs)]]
            rot[0] += 1
            return b

        def C(name, n=128):
            return cst[:, CC[name]:CC[name] + n]

        def P(name, c):
            return par[:, PC[name] + c:PC[name] + c + 1]

        dma("sp", par[:], D["params"], "c0", (), (par,))
        dma("sp", cst[:], D["consts"], "c2", (), (cst,))
        dma("sp", flag[:], D["flag"], "c3", (), (flag,))
        cp("dve", identb[:], C("ident"), (cst,), (identb,))
        memset("pool", neghalf[:], -0.5, (neghalf,))
        memset("pool", zeros[:], 0.0, (zeros,))
        ts("dve", par[:, PC["cw"]:PC["cw"] + 40], par[:, PC["cw"]:PC["cw"] + 40], 0.5, None, ALU.mult, None, (par,), (par,))

        def rsqrt_small(o, a, n, r, w):
            tt("pool", o, a, neghalf[:, 0:n], ALU.pow, list(r) + [neghalf], w)

        hscrB = Buf("hscr")

        with ExitStack() as stA:
            rows = sb("rows", [128, 2584], F32, stA)

            def R(name, n):
                return rows[:, RC[name]:RC[name] + n]

            dma("sp", rows[:], D["rows"][:, 0:2584], "c1", (), (rows,))
            Win = sb("Win", [128, 8, 3336], BF16, stA)
            Wout = sb("Wout", [128, 8, 1024], BF16, stA)
            Wlora = sb("Wlora", [128, 512], F32, stA)
            G2h = sb("G2h", [128, 512], BF16, stA)
            sel = sb("sel", [128, 4, 8], F32, stA)
            arow = sb("arow", [128, 8], F32, stA)
            with ExitStack() as stS:
                stage = [sb("stageA%d" % q, [128, 3336], F32, stS) for q in range(2)]
                sq = [0]

                def load_rows(src_ap, ncols, fn):
                    stg = stage[sq[0] % 2]
                    sq[0] += 1
                    dma("sp", stg[:, 0:ncols], src_ap, "wst%d" % (sq[0] % 2), (), (stg,))
                    fn(stg)
                for k in range(8):
                    load_rows(D["win"][:, k * 3336:(k + 1) * 3336], 3336,
                              lambda stg, k=k: (ts("dve", Win[:, k, :], stg[:, 0:3336], P("g1", k), None,
                                                   ALU.mult, None, (stg, par), (Win,)) if k % 2 == 0 else
                                                act(Win[:, k, :], stg[:, 0:3336], AF.Copy, (stg, par), (Win,), scale=P("g1", k))))
                ts("pool", Win[:, :, 2816:3328], Win[:, :, 2816:3328], 0.5, None, ALU.mult, None, (Win,), (Win,))
                for k in range(8):
                    load_rows(D["wout"][:, k * 1024:(k + 1) * 1024], 1024,
                              lambda stg, k=k: cp("act", Wout[:, k, :], stg[:, 0:1024], (stg,), (Wout,)))
                load_rows(D["g2"], 512, lambda stg: ts("dve", G2h[:], stg[:, 0:512], 0.5, None, ALU.mult, None, (stg,), (G2h,)))
                dma("sp", Wlora[:], D["wlora"], "c4", (), (Wlora,))
                dma("sp", sel[:].rearrange("p a b -> p (a b)"), D["sel"], "c5", (), (sel,))
                act(arow[:], R("alog", 8), AF.Exp, (rows,), (arow,))
                ts("dve", arow[:], arow[:], -1.0, None, ALU.mult, None, (arow,), (arow,))
                S.barrier()
            CK(1)

            xt1 = sb("xt", [128, 1024], F32, stA)
            xt = [xt1, xt1]
            xT = [sb("xT%d" % i, [128, 8, HW], BF16, stA) for i in range(2)]
            raw = sb("raw", [128, NFM, HW], F32, stA)
            rawB = [Buf("raw_%d" % j) for j in range(8)]
            gz = [sb("gz%d" % i, [128, 512], F32, stA) for i in range(2)]
            dtr = [sb("dtr%d" % i, [128, 8], F32, stA) for i in range(2)]
            xb = sb("xb", [128, 1024], BF16, stA)
            hb = xb
            sm = sb("sm", [128, 64], F32, stA)
            smB = [Buf("sm%d" % i) for i in range(16)]
            Pp = sb("Pp", [128, 14, 128], F32, stA)
            BB = sb("BB", [128, 2048], F32, stA)
            AC = sb("AC", [128, 1024], F32, stA)
            B2 = sb("B2", [128, 1024], F32, stA)
            B3 = sb("B3", [128, 1024], F32, stA)
            B4 = sb("B4", [128, 512], F32, stA)
            B56 = sb("B56", [128, 2048], F32, stA)
            B7 = sb("B7", [128, 1024], F32, stA)
            ht = sb("ht", [128, 1024], F32, stA)
            X1 = sb("X1", [128, 512], BF16, stA)
            X2 = sb("X2", [128, 512], BF16, stA)
            X3 = sb("X3", [128, 512], BF16, stA)
            GA = sb("GA", [128, 1024], BF16, stA)
            Btm = sb("Btm", [128, 2, 128], BF16, stA)
            STf = sb("STf", [128, 8, 64], F32, stA)
            STb = sb("STb", [128, 8, 64], BF16, stA)
            ytok = [sb("ytok%d" % q, [128, 1024], BF16, stA) for q in range(2)]
            GA0 = sb("GA0", [128, 1024], BF16, stA)
            UTfT = sb("UTfT", [128, 512], F32, stA)
            tmpST = sb("tmpST", [128, 256], F32, stA)
            mixT = sb("mixT", [128, 8, 128], BF16, stA)
            lin = sb("lin", [128, 128], F32, stA)
            sg = sb("sg", [128, 128], BF16, stA)
            btm = sb("btm", [128, 512], BF16, stA)
            ktm = sb("ktm", [128, 512], BF16, stA)
            vtm = sb("vtm", [128, 512], BF16, stA)
            ABs = [sb("ABs%d" % c, [128, 2, 256], BF16, stA) for c in range(4)]
            AKs = [sb("AKs%d" % c, [128, 2, 256], BF16, stA) for c in range(4)]
            PPm = [[sb("PPm%d_%d" % (q, c), [128, 2, 2, 128], BF16, stA) for c in range(4)] for q in range(2)]
            Pm = [[View(PPm[q][c][:, 0, :, :], PPm[q][c].b) for c in range(4)] for q in range(2)]
            PTm = [[View(PPm[q][c][:, 1, :, :], PPm[q][c].b) for c in range(4)] for q in range(2)]
            UT = sb("UT", [128, 8, 64], BF16, stA)
            STr = sb("STr", [128, 4, 64], F32, stA)
            STrb = sb("STrb", [128, 4, 64], BF16, stA)
            maskSI = sb("maskSI", [128, 256], F32, stA)
            hw0 = sb("hw0", [128, 8], F32, stA)
            tiny = sb("tiny", [128, 1], F32, stA)
            memset("pool", tiny[:], 1e-24, (tiny,))
            bnsb = sb("bnsb", [128, 8], F32, stA)
            gn = sb("gn", [128, 40], F32, stA)
            hs = sb("hs", [128, 8], F32, stA)
            hsm = Buf("hsm")
            dlt = View(v3(B56[:, 0:1792], 14), B56.b)
            th = View(v3(BB[:, 0:1024], 8), BB.b)
            xax = View(v3(BB[:, 1024:1536], 4), BB.b)
            xs_tm = View(BB[:, 1536:2048], BB.b)
            Dm = View(v3(BB[:, 0:1024], 8), BB.b)
            lw = View(v3(BB[:, 0:512], 4), BB.b)
            av = View(v3(BB[:, 512:1024], 4), BB.b)
            cum = View(v3(BB[:, 1024:1536], 4), BB.b)
            cumex = View(v3(BB[:, 1536:2048], 4), BB.b)
            yr = View(v3(B56[:, 0:512], 8), B56.b)
            ysq = View(v3(B56[:, 512:1024], 8), B56.b)
            acc = View(v3(AC[:], 8), AC.b)
            rseg = View(v3(AC[:], 8), AC.b)
            thw = View(v3(AC[:, 0:512], 4), AC.b)
            tha = View(v3(AC[:, 512:1024], 4), AC.b)
            mn = View(AC[:], AC.b)
            dx = View(B2[:, 0:512], B2.b)
            ytmp = View(B2[:, 512:1024], B2.b)
            E2 = View(v3(B2[:, 0:512], 4), B2.b)
            tmpS = View(v3(tmpST[:], 4), tmpST.b)
            ysd = View(B3[:, 0:512], B3.b)
            scm = View(v3(B3[:, 512:768], 2), B3.b)
            kkv = View(v3(B3[:, 0:512], 4), B3.b)
            UTf = View(UTfT[:], UTfT.b)
            kk2 = View(v3(B3[:, 512:1024], 4), B3.b)
            E1x = sb("E1x", [128, 4, 129], F32, stA)
            memset("pool", E1x[:], 1.0, (E1x,))
            E1 = View(E1x[:, :, 1:129], E1x.b)
            E3 = View(E1x[:, :, 0:128], E1x.b)
            inv = View(v3(B4[:, 0:512], 4), B4.b)
            kkn = View(v3(B56[:, 0:512], 4), B56.b)
            kmod = View(v3(B56[:, 512:1024], 4), B56.b)
            t1 = View(v3(B56[:, 1024:1536], 4), B56.b)
            rk = View(v3(B56[:, 1024:1536], 4), B56.b)
            beta = View(v3(B56[:, 1536:2048], 4), B56.b)
            gsb = View(B7[:, 0:512], B7.b)
            bterm = View(v3(B7[:, 512:1024], 8), B7.b)
            Xb = View(X1[:], X1.b)
            btl = View(v3(X1[:], 4), X1.b)
            Xd = View(X2[:], X2.b)
            ktl = View(v3(X2[:], 4), X2.b)
            BCb = View(v3(X3[:], 4), X3.b)
            vb = View(v3(X3[:], 4), X3.b)
            Gm = View(v3(GA[:], 8), GA.b)
            GA1 = sb("GA1", [128, 1024], BF16, stA)
            arz = [View(GA0[:].rearrange("p (c a b) -> p c a b", c=4, a=2), GA0.b),
                   View(GA1[:].rearrange("p (c a b) -> p c a b", c=4, a=2), GA1.b)]
            GAz = [GA0, GA1]
            memset("pool", GA1[:], 0.0, (GA1,))
            memset("pool", GA0[:], 0.0, (GA0,))

            cp("dve", maskSI[:, 0:128], C("trius"), (cst,), (maskSI,))
            cp("dve", maskSI[:, 128:256], C("triu"), (cst,), (maskSI,))
            memset("dve", STf[:], 0.0, (STf,))
            memset("dve", STb[:], 0.0, (STb,))
            memset("dve", STr[:], 0.0, (STr,))
            memset("dve", STrb[:], 0.0, (STrb,))
            memset("pool", xT[1][:], 0.0, (xT[1],))
            memset("pool", xT[0][:], 0.0, (xT[0],))
            ts("dve", hw0[:, 0:4], par[:, PC["w0"]:PC["w0"] + 4], 0.5, None, ALU.mult, None, (par,), (hw0,))
            ts("dve", hw0[:, 4:8], par[:, PC["a0"]:PC["a0"] + 4], 0.5, None, ALU.mult, None, (par,), (hw0,))

            def SM(i, n=1):
                return sm[:, i:i + n]

            B0T = NT_ALL - NT_MAIN

            fstate = [0]

            def front_gen(i):
                p = i % 2
                fstate[0] = 0
                dma("sp", xt[p][:], D["x_in"][i * TT:(i + 1) * TT, :], "x0", (), (xt[p],))
                act(xb[:], xt[p][:], AF.Square, (xt[p],), (xb, smB[0]), accum=SM(0))
                ts("dve", SM(1), SM(0), 1.0 / 1024, 1e-6, ALU.mult, ALU.add, (smB[0],), (smB[1],))
                rsqrt_small(SM(2), SM(1), 1, (smB[1],), (smB[2],))
                act(xb[:], xt[p][:], AF.Copy, (xt[p], smB[2]), (xb,), scale=SM(2))
                yield
                for k in range(8):
                    trn(pT[:, k * 128:(k + 1) * 128], xb[:, k * 128:(k + 1) * 128], identb[:], (xb, identb), (pT,))
                cp("dve", xT[p][:, :, 3:HW], v3(pT[:], 8), (pT,), (xT[p],))
                cp("pool", xT[p][:, :, 0:3], xT[1 - p][:, :, 128:131], (xT[1 - p],), (xT[p],))
                if i == B0T:
                    ts("pool", xT[p][:, :, 0:3], xT[p][:, :, 0:3], flag[:], None, ALU.mult, None, (xT[p], flag), (xT[p],))
                yield
                hist = i < B0T - 1
                for bnk in range(8):
                    n = min(3, NFM - 3 * bnk)
                    if hist and bnk in (2, 3, 7):
                        fstate[0] = bnk + 1
                        continue
                    pb = PS()
                    for jj in range(n):
                        j = 3 * bnk + jj
                        for k in range(8):
                            mm(pb[:, jj * HW:(jj + 1) * HW], Win[:, k, j * 128:(j + 1) * 128], xT[p][:, k, :],
                               k == 0, k == 7, (Win, xT[p]), (pb,))
                    cp("act" if bnk % 2 == 0 else "dve", raw[:, 3 * bnk:3 * bnk + n, :], v3(pb[:, 0:n * HW], n),
                       (pb,), (rawB[bnk],))
                    fstate[0] = bnk + 1
                    yield
                full = i >= B0T - 1
                if full:
                    pz = PS()
                    for k in range(8):
                        mm(pz[:, 0:512], xT[p][:, k, 3:HW], Win[:, k, 2816:3328], k == 0, k == 7, (Win, xT[p]), (pz,))
                pd = PS()
                for k in range(8):
                    mm(pd[:, 0:8], xT[p][:, k, 3:HW], Win[:, k, 3328:3336], k == 0, k == 7, (Win, xT[p]), (pd,))
                if full:
                    act(gz[p][:], pz[:, 0:512], AF.Tanh, (pz,), (gz[p],))
                    stt("dve", gz[p][:], gz[p][:], 1.0, pz[:, 0:512], ALU.add, ALU.mult, (gz[p], pz), (gz[p],))
                tt("dve", dtr[p][:], pd[:, 0:8], R("dtb", 8), ALU.add, (pd, rows), (dtr[p],))

            estate = [0]

            def early_gen(i):
                rw = raw
                estate[0] = 0
                hist = i < B0T - 1
                for c in range(6 if hist else 8):
                    need_banks(c // 3 + 1)
                    rb = rawB[c // 3]
                    ts("dve", acc[:, c, :], rw[:, c, 3:HW], P("cw", c * 4 + 3), P("cb", c), ALU.mult, ALU.add,
                       (rb, par), (acc,))
                    for k in range(3):
                        stt("dve", acc[:, c, :], rw[:, c, k:k + 128], P("cw", c * 4 + k), acc[:, c, :], ALU.mult, ALU.add,
                            (rb, par, acc), (acc,))
                    estate[0] = c + 1
                    yield
                estate[0] = 8
                need_banks(8)
                c0, c1 = (4, 13) if hist else (0, 14)
                nc_ = c1 - c0
                RBr = rawB[4:7] if hist else rawB[2:8]
                tt("dve", dlt[:, c0:c1, :], rw[:, 8 + c0:8 + c1, 2:130], rw[:, 8 + c0:8 + c1, 3:HW], ALU.subtract, RBr, (dlt,))
                yield
                tt("dve", dlt[:, c0:c1, :], dlt[:, c0:c1, :],
                   par[:, PC["mu"] + c0:PC["mu"] + c1].unsqueeze(2).broadcast_to([128, nc_, 128]),
                   ALU.mult, (dlt, par), (dlt,))
                yield
                tt("dve", Pp[:, c0:c1, :], dlt[:, c0:c1, :], rw[:, 8 + c0:8 + c1, 3:HW], ALU.add, [dlt] + RBr, (Pp,))
                yield

            def ssd_gen(i):
                p = i % 2
                full = i >= B0T - 1
                need_front_done()
                need_conv()
                nch = 8 if full else 6
                act(th[:, 0:nch, :], acc[:, 0:nch, :], AF.Tanh, (acc,), (th,))
                stt("dve", xax[:], th[:, 0:4, :], 1.0, acc[:, 0:4, :], ALU.add, ALU.mult, (th, acc), (xax,))
                stt("dve", BCb[:, 0:nch - 4, :], th[:, 4:nch, :], 1.0, acc[:, 4:nch, :], ALU.add, ALU.mult, (th, acc), (BCb,))
                act(SM(8, 8), dtr[p][:], AF.Exp, (dtr[p],), (smB[8],))
                act(SM(8, 8), SM(8, 8), AF.Ln, (smB[8],), (smB[8],), bias=1.0)
                tt("dve", SM(16, 8), SM(8, 8), arow[:], ALU.mult, (smB[8], arow), (smB[9],))
                yield
                px = PS()
                for c in range(4):
                    mm(px[:, c * 128:(c + 1) * 128], xax[:, c, :], C("ident"), True, True, (xax, cst), (px,))
                cp("act", xs_tm[:], px[:], (px,), (xs_tm,))
                if full:
                    tt("dve", v3(Xb[:], 8), v3(xs_tm[:], 8), SM(8, 8).unsqueeze(2).broadcast_to([128, 8, 64]), ALU.mult,
                       (xs_tm, smB[8]), (Xb,))
                for g in range(2):
                    trn(pTR[:, g * 128:(g + 1) * 128], BCb[:, g, :], identb[:], (BCb, identb), (pTR,))
                cp("act", Btm[:], v3(pTR[:, 0:256], 2), (pTR,), (Btm,))
                yield
                psm = PS()
                mm(psm[:, 0:8], C("triu"), SM(16, 8), True, True, (cst, smB[9]), (psm,))
                mm(psm[:, 8:16], C("trils"), SM(16, 8), True, True, (cst, smB[9]), (psm,))
                mm(psm[:, 16:24], C("ones"), SM(16, 8), True, True, (cst, smB[9]), (psm,))
                act(SM(24, 24), psm[:, 0:24], AF.Exp, (psm,), (smB[10],))
                tt("dve", SM(48, 8), SM(8, 8), SM(32, 8), ALU.mult, (smB[8], smB[10]), (smB[11],))
                tt("dve", v3(Xd[:], 8), v3(xs_tm[:], 8), SM(48, 8).unsqueeze(2).broadcast_to([128, 8, 64]), ALU.mult,
                   (xs_tm, smB[11]), (Xd,))
                yield
                if full:
                    tt("dve", v3(dx[:], 8), v3(xs_tm[:], 8), R("dsk", 8).unsqueeze(2).broadcast_to([128, 8, 64]), ALU.mult,
                       (xs_tm, rows), (dx,))
                    psc = PS()
                    for g in range(2):
                        mm(psc[:, g * 128:(g + 1) * 128], BCb[:, g, :], BCb[:, 2 + g, :], True, True, (BCb,), (psc,))
                    tt("dve", scm[:], v3(psc[:, 0:256], 2), C("triu").unsqueeze(1).broadcast_to([128, 2, 128]), ALU.mult,
                       (psc, cst), (scm,))
                    tt("dve", rseg[:], SM(16, 8).unsqueeze(2).broadcast_to([128, 8, 128]),
                       C("triu").unsqueeze(1).broadcast_to([128, 8, 128]), ALU.mult, (smB[9], cst), (rseg,))
                    yield
                    for hh in range(2):
                        pg = PS()
                        mm(pg[:], C("trils"), rseg[:, 4 * hh:4 * hh + 4, :].rearrange("p a b -> p (a b)"), True, True,
                           (cst, rseg), (pg,))
                        act(Dm[:, 4 * hh:4 * hh + 4, :].rearrange("p a b -> p (a b)"), pg[:], AF.Exp, (pg,), (Dm,))
                    for g in range(2):
                        tt("dve", Gm[:, 4 * g:4 * g + 4, :], Dm[:, 4 * g:4 * g + 4, :],
                           scm[:, g, :].unsqueeze(1).broadcast_to([128, 4, 128]), ALU.mult, (Dm, scm), (Gm,))
                    yield
                    py1 = PS()
                    for h in range(8):
                        mm(py1[:, h * 64:(h + 1) * 64], Gm[:, h, :], Xb[:, h * 64:(h + 1) * 64], True, True, (Gm, Xb), (py1,))
                    py2 = PS()
                    for h in range(8):
                        mm(py2[:, h * 64:(h + 1) * 64], BCb[:, 2 + h // 4, :], STb[:, h, :], True, True, (BCb, STb), (py2,))
                    tt("dve", v3(ytmp[:], 8), v3(py2[:], 8), SM(24, 8).unsqueeze(2).broadcast_to([128, 8, 64]), ALU.mult,
                       (py2, smB[10]), (ytmp,))
                    tt("dve", ysd[:], ytmp[:], py1[:], ALU.add, (ytmp, py1), (ysd,))
                    tt("dve", ysd[:], ysd[:], dx[:], ALU.add, (ysd, dx), (ysd,))
                    yield
                pst = PS()
                for h in range(8):
                    mm(pst[:, h * 64:(h + 1) * 64], Btm[:, h // 4, :], Xd[:, h * 64:(h + 1) * 64], True, True, (Btm, Xd), (pst,))
                tt("dve", STf[:], STf[:], SM(40, 8).unsqueeze(2).broadcast_to([128, 8, 64]), ALU.mult, (STf, smB[10]), (STf,))
                tt("dve", STf[:].rearrange("p h d -> p (h d)"), STf[:].rearrange("p h d -> p (h d)"), pst[:], ALU.add,
                   (STf, pst), (STf,))
                if i == B0T - 1:
                    ts("dve", STf[:], STf[:], flag[:], None, ALU.mult, None, (STf, flag), (STf,))
                cp("act", STb[:], STf[:], (STf,), (STb,))
                yield
                if full:
                    tt("dve", ysd[:], ysd[:], gz[p][:], ALU.mult, (ysd, gz[p]), (ysd,))
                    for g in range(2):
                        act(ytmp[:, g * 256:(g + 1) * 256], ysd[:, g * 256:(g + 1) * 256], AF.Square, (ysd,), (ytmp, smB[12]),
                            accum=SM(56 + g))
                    ts("dve", SM(58, 2), SM(56, 2), 1.0 / 256, 1e-6, ALU.mult, ALU.add, (smB[12],), (smB[13],))
                    rsqrt_small(SM(60, 2), SM(58, 2), 2, (smB[13],), (smB[14],))
                    tt("dve", v3(ysd[:], 2), v3(ysd[:], 2), SM(60, 2).unsqueeze(2).broadcast_to([128, 2, 256]), ALU.mult,
                       (ysd, smB[14]), (ysd,))
                    tt("dve", ytok[p][:, 0:512], ysd[:], R("ssdg", 512), ALU.mult, (ysd, rows), (ytok[p],))


            def rwkv_out(i):
                p = i % 2
                full = i >= B0T - 1
                OP()
                act(lin[0:64, :], Pp[0:64, 12, :], AF.Tanh, (Pp,), (lin,))
                tt("dve", kkv[:], Pp[:, 4:8, :], par[:, PC["kk"]:PC["kk"] + 4].unsqueeze(2).broadcast_to([128, 4, 128]),
                   ALU.mult, (Pp, par), (kkv,))
                cp("act", vb[:], Pp[:, 8:12, :], (Pp,), (vb,))
                plw = PS()
                pla = PS()
                for c in range(4):
                    mm(plw[:, c * 128:(c + 1) * 128], Wlora[0:64, c * 128:(c + 1) * 128], lin[0:64, :], True, True,
                       (Wlora, lin), (plw,))
                    mm(pla[:, c * 128:(c + 1) * 128], Wlora[64:128, c * 128:(c + 1) * 128], Pp[64:128, 12, :], True, True,
                       (Wlora, Pp), (pla,))
                act(kk2[:], kkv[:], AF.Square, (kkv,), (kk2,))
                pkn = PS()
                mm(pkn[:], C("onesblk"), kk2[:].rearrange("p a b -> p (a b)"), True, True, (cst, kk2), (pkn,))
                for c in range(4):
                    act(thw[:, c, :], plw[:, c * 128:(c + 1) * 128], AF.Tanh, (plw, hw0), (thw,), bias=hw0[:, c:c + 1], scale=0.5)
                    act(tha[:, c, :], pla[:, c * 128:(c + 1) * 128], AF.Tanh, (pla, hw0), (tha,), bias=hw0[:, 4 + c:5 + c], scale=0.5)
                act(inv[:].rearrange("p a b -> p (a b)"), pkn[:], AF.Ln, (pkn,), (inv,), bias=tiny[:])
                act(inv[:], inv[:], AF.Exp, (inv,), (inv,), scale=-0.5)
                ts("dve", lw[:], thw[:], -0.30326533, -0.30326533, ALU.mult, ALU.add, (thw,), (lw,))
                ts("dve", av[:], tha[:], 0.5, 0.5, ALU.mult, ALU.add, (tha,), (av,))
                tt("dve", kkn[:], kkv[:], inv[:], ALU.mult, (kkv, inv), (kkn,))
                FP()
                OP()
                for c in range(4):
                    S.op("dve", lambda e, c=c: e.tensor_tensor_scan(out=cum[:, c, :], data0=lw[:, c, :], data1=zeros[:],
                                                                      initial=0.0, op0=ALU.add, op1=ALU.add),
                         rd(lw, zeros), rd(cum))
                act(E1[:], cum[:], AF.Exp, (cum,), (E1,))
                act(E2[:], cum[:], AF.Exp, (cum,), (E2,), scale=-1.0)
                FP()
                OP()
                stt("dve", t1[:], av[:], -1.0, par[:, PC["ka"]:PC["ka"] + 4].unsqueeze(2).broadcast_to([128, 4, 128]),
                    ALU.add, ALU.mult, (av, par), (t1,))
                stt("dve", kmod[:], t1[:], 1.0, Pp[:, 4:8, :], ALU.add, ALU.mult, (t1, Pp), (kmod,))
                FP()
                tt("dve", beta[:], kkn[:], av[:], ALU.mult, (kkn, av), (beta,))
                for hp in range(2):
                    sl = slice(hp * 64, (hp + 1) * 64)
                    stt("dve", arz[hp][sl, :, 0, :], kkn[sl, :, :], -1.0, E3[sl, :, :], ALU.mult, ALU.mult, (kkn, E3), (arz[hp],))
                tt("dve", btl[:], beta[:], E2[:], ALU.mult, (beta, E2), (btl,))
                tt("dve", ktl[:], kmod[:], E2[:], ALU.mult, (kmod, E2), (ktl,))
                if full:
                    for hp in range(2):
                        sl = slice(hp * 64, (hp + 1) * 64)
                        tt("dve", arz[hp][sl, :, 1, :], Pp[sl, 0:4, :], E1[sl, :, :], ALU.mult, (Pp, E1), (arz[hp],))
                if full:
                    tt("dve", rk[:], Pp[:, 0:4, :], kmod[:], ALU.mult, (Pp, kmod), (rk,))
                if full:
                    pbn = PS()
                    for c in range(4):
                        mm(pbn[:, 0:8], rk[:, c, :], sel[:, c, :], c == 0, c == 3, (rk, sel), (pbn,))
                    cp("act", bnsb[:], pbn[:, 0:8], (pbn,), (bnsb,))
                for c in range(4):
                    trn(pTR[:, c * 128:(c + 1) * 128], vb[:, c, :], identb[:], (vb, identb), (pTR,))
                cp("act", vtm[:], pTR[:, 0:512], (pTR,), (vtm,))
                FP()
                drain(ogen)
                if full:
                    act(gsb[:, 0:128], Pp[:, 13, :], AF.Tanh, (Pp,), (gsb,), scale=0.5)
                    ts("dve", sg[:], gsb[:, 0:128], 1.0, None, ALU.add, None, (gsb,), (sg,))
                    pgt = PS()
                    mm(pgt[:], sg[:], G2h[:], True, True, (sg, G2h), (pgt,))
                    cp("act", gsb[:], pgt[:], (pgt,), (gsb,))
                if full:
                    tt("dve", bterm[:], v3(vtm[:], 8), bnsb[:].unsqueeze(2).broadcast_to([128, 8, 64]), ALU.mult,
                       (vtm, bnsb), (bterm,))
                for c in range(4):
                    pab = PS()
                    pak = PS()
                    pnt = PS()
                    ncl = 256 if full else 128
                    for hp in range(2):
                        arv = GAz[hp][:, c * 256:c * 256 + ncl]
                        mm(pab[:, hp * 256:hp * 256 + ncl], btl[:, c, :], arv, True, True, (btl, arz[hp]), (pab,))
                        mm(pak[:, hp * 256:hp * 256 + ncl], ktl[:, c, :], arv, True, True, (ktl, arz[hp]), (pak,))
                        mm(pnt[:, hp * 128:(hp + 1) * 128], arz[hp][:, c, 0, :], btl[:, c, :], True, True, (arz[hp], btl), (pnt,))
                    tt("dve", ABs[c][:, :, 0:ncl], v3(pab[:], 2)[:, :, 0:ncl],
                       maskSI[:, 0:ncl].unsqueeze(1).broadcast_to([128, 2, ncl]), ALU.mult, (pab, maskSI), (ABs[c],))
                    tt("dve", AKs[c][:, :, 0:ncl], v3(pak[:], 2)[:, :, 0:ncl],
                       maskSI[:, 0:ncl].unsqueeze(1).broadcast_to([128, 2, ncl]), ALU.mult, (pak, maskSI), (AKs[c],))
                    tt("dve", PTm[0][c][:], v3(pnt[:, 0:256], 2), C("trils").unsqueeze(1).broadcast_to([128, 2, 128]),
                       ALU.mult, (pnt, cst), (PTm[0][c],))
                    FP()
                    EP()
                    EP()
                for c in range(4):
                    trn(pTR[:, c * 128:(c + 1) * 128], btl[:, c, :], identb[:], (btl, identb), (pTR,))
                    trn(pTR[:, 512 + c * 128:512 + (c + 1) * 128], ktl[:, c, :], identb[:], (ktl, identb), (pTR,))
                cp("act", btm[:], pTR[:, 0:512], (pTR,), (btm,))
                cp("dve", ktm[:], pTR[:, 512:1024], (pTR,), (ktm,))
                pu = pool_banks[0]
                for h in range(8):
                    c, hp = h // 2, h % 2
                    mm(pu[:, h * 64:(h + 1) * 64], arz[hp][:, c, 0, :], STrb[:, c, :], True, False, (arz[hp], STrb), (pu,))
                    mm(pu[:, h * 64:(h + 1) * 64], AKs[c][:, hp, 0:128], vtm[:, h * 64:(h + 1) * 64], False, True,
                       (AKs[c], vtm), (pu,))
                cp("act", UTf[:], pu[:], (pu,), (UTf,))
                cp("dve", UT[:].rearrange("p h d -> p (h d)"), UTf[:], (UTf,), (UT,))
                FP()
                for lvl in range(7):
                    q = lvl % 2
                    pd = PS()
                    for h in range(8):
                        c, hp = h // 2, h % 2
                        Pl, PlB = (ABs[c][:, hp, 0:128], ABs[c]) if lvl == 0 else (Pm[q][c][:, hp, :], Pm[q][c])
                        mm(pd[:, h * 64:(h + 1) * 64], Pl, UT[:, h, :], True, True, (PlB, UT), (pd,))
                    tt("dve", UTf[:], UTf[:], pd[:], ALU.add, (UTf, pd), (UTf,))
                    cp("act", UT[:].rearrange("p h d -> p (h d)"), UTf[:], (UTf,), (UT,))
                    EP()
                    SP()
                    if lvl < 6:
                        for c in range(4):
                            pp = PS()
                            for hp in range(2):
                                Pl, PlB = (ABs[c][:, hp, 0:128], ABs[c]) if lvl == 0 else (Pm[q][c][:, hp, :], Pm[q][c])
                                mm(pp[:, hp * 128:(hp + 1) * 128], PTm[q][c][:, hp, :], Pl, True, True,
                                   (PTm[q][c], PlB), (pp,))
                                mm(pp[:, 256 + hp * 128:256 + (hp + 1) * 128], Pl, PTm[q][c][:, hp, :], True, True,
                                   (PTm[q][c], PlB), (pp,))
                            cp("dve" if c % 2 == 0 else "act", PPm[1 - q][c][:].rearrange("p a b c -> p (a b c)"), pp[:],
                               (pp,), (PPm[1 - q][c],))
                            if c == 1:
                                SP()
                psr = PS()
                for c in range(4):
                    o = psr[:, c * 128:(c + 1) * 128]
                    mm(o, btm[:, c * 128:(c + 1) * 128], UT[:, 2 * c:2 * c + 2, :].rearrange("p a b -> p (a b)"), True, False,
                       (btm, UT), (psr,))
                    mm(o, ktm[:, c * 128:(c + 1) * 128], vtm[:, c * 128:(c + 1) * 128], False, True, (ktm, vtm), (psr,))
                if full:
                    pyr = PS()
                    for h in range(8):
                        c, hp = h // 2, h % 2
                        o = pyr[:, h * 64:(h + 1) * 64]
                        mm(o, arz[hp][:, c, 1, :], STrb[:, c, :], True, False, (arz[hp], STrb), (pyr,))
                        mm(o, ABs[c][:, hp, 128:256], UT[:, h, :], False, False, (ABs[c], UT), (pyr,))
                        mm(o, AKs[c][:, hp, 128:256], vtm[:, h * 64:(h + 1) * 64], False, True, (AKs[c], vtm), (pyr,))
                psr3 = v3(psr[:], 4)
                tt("dve", tmpS[0:64, :, :], STr[0:64, :, :], psr3[0:64, :, 0:64], ALU.add, (STr, psr), (tmpS,))
                tt("dve", tmpS[64:128, :, :], STr[64:128, :, :], psr3[64:128, :, 64:128], ALU.add, (STr, psr), (tmpS,))
                tt("dve", STr[:], tmpS[:], E1[:, :, 127:128].broadcast_to([128, 4, 64]), ALU.mult, (tmpS, E1), (STr,))
                if i == B0T - 1:
                    ts("dve", STr[:], STr[:], flag[:], None, ALU.mult, None, (STr, flag), (STr,))
                cp("act", STrb[:], STr[:], (STr,), (STrb,))
                SP()
                if full:
                    cp("act", yr[:].rearrange("p h d -> p (h d)"), pyr[:], (pyr,), (yr,))
                    ogen[0] = out_gen(i)

            def out_gen(i):
                p = i % 2
                slot = i - (B0T - 1)
                act(ysq[:], yr[:], AF.Square, (yr,), (ysq,))
                S.op("dve", lambda e: e.tensor_reduce(out=gn[:, 0:8], in_=yr[:], axis=AX.X, op=ALU.add), rd(yr), rd(gn))
                S.op("dve", lambda e: e.tensor_reduce(out=gn[:, 8:16], in_=ysq[:], axis=AX.X, op=ALU.add), rd(ysq), rd(gn))
                ts("dve", gn[:, 0:16], gn[:, 0:16], 1.0 / 64, None, ALU.mult, None, (gn,), (gn,))
                tt("dve", gn[:, 16:24], gn[:, 0:8], gn[:, 0:8], ALU.mult, (gn,), (gn,))
                tt("dve", gn[:, 24:32], gn[:, 8:16], gn[:, 16:24], ALU.subtract, (gn,), (gn,))
                ts("dve", gn[:, 24:32], gn[:, 24:32], 64e-5, None, ALU.add, None, (gn,), (gn,))
                rsqrt_small(gn[:, 32:40], gn[:, 24:32], 8, (gn,), (gn,))
                tt("dve", yr[:], yr[:], gn[:, 0:8].unsqueeze(2).broadcast_to([128, 8, 64]), ALU.subtract, (yr, gn), (yr,))
                tt("dve", yr[:], yr[:], gn[:, 32:40].unsqueeze(2).broadcast_to([128, 8, 64]), ALU.mult, (yr, gn), (yr,))
                yr2 = yr[:].rearrange("p h d -> p (h d)")
                tt("dve", yr2, yr2, R("lnw", 512), ALU.mult, (yr, rows), (yr,))
                tt("dve", yr2, yr2, R("lnb", 512), ALU.add, (yr, rows), (yr,))
                tt("dve", yr[:], yr[:], bterm[:], ALU.add, (yr, bterm), (yr,))
                tt("dve", ytok[p][:, 512:1024], yr2, gsb[:], ALU.mult, (yr, gsb), (ytok[p],))
                yield
                for c in range(8):
                    trn(pTR[:, c * 128:(c + 1) * 128], ytok[p][:, c * 128:(c + 1) * 128], identb[:], (ytok[p], identb), (pTR,))
                cp("act", mixT[:].rearrange("p a b -> p (a b)"), pTR[:], (pTR,), (mixT,))
                yield
                pm = [PS(), PS()]
                for hf in range(2):
                    for c in range(8):
                        mm(pm[hf][:], mixT[:, c, :], Wout[:, c, hf * 512:(hf + 1) * 512], c == 0, c == 7, (mixT, Wout), (pm[hf],))
                dma("sp", ht[:], D["x_in"][i * TT:(i + 1) * TT, :], "xr", (), (ht,))
                for hf in range(2):
                    act(B7[:, hf * 512:(hf + 1) * 512], pm[hf][:], AF.Square, (pm[hf],), (B7, hsm), accum=hs[:, hf:hf + 1])
                tt("dve", hs[:, 2:3], hs[:, 0:1], hs[:, 1:2], ALU.add, (hsm,), (hsm,))
                ts("dve", hs[:, 2:3], hs[:, 2:3], 1.0 / 1024, 1e-6, ALU.mult, ALU.add, (hsm,), (hsm,))
                rsqrt_small(hs[:, 3:4], hs[:, 2:3], 1, (hsm,), (hsm,))
                for hf in range(2):
                    stt("dve", B7[:, hf * 512:(hf + 1) * 512], pm[hf][:], hs[:, 3:4], R("pmn", 1024)[:, hf * 512:(hf + 1) * 512],
                        ALU.mult, ALU.mult, (pm[hf], hsm, rows), (B7,))
                tt("dve", ht[:], ht[:], B7[:], ALU.add, (ht, B7), (ht,))
                dma("sp", hscr[slot * TT:(slot + 1) * TT, :], ht[:], "hst", (ht,), (hscrB,))
                yield

            ogen = [None]

            def OP():
                if ogen[0] is not None:
                    next(ogen[0], None)

            fgen = [None]
            egen = [None]
            sgen = [None]

            def FP():
                if fgen[0] is not None:
                    next(fgen[0], None)

            def need_banks(n):
                while fstate[0] < n:
                    if fgen[0] is None or next(fgen[0], "done") == "done":
                        break

            def need_front_done():
                if fgen[0] is not None:
                    for _ in fgen[0]:
                        pass

            def EP():
                if egen[0] is not None:
                    next(egen[0], None)

            def need_conv():
                while estate[0] < 8:
                    if egen[0] is None or next(egen[0], "done") == "done":
                        break

            def SP():
                if sgen[0] is not None:
                    next(sgen[0], None)

            def drain(g):
                if g[0] is not None:
                    for _ in g[0]:
                        pass
                g[0] = None

            fgen[0] = front_gen(0)
            drain(fgen)
            fstate[0] = 8
            egen[0] = early_gen(0)
            drain(egen)
            estate[0] = 8
            sgen[0] = ssd_gen(0)
            drain(sgen)
            CK(2)
            for i in range(NT_ALL):
                if i + 1 < NT_ALL:
                    fgen[0] = front_gen(i + 1)
                    egen[0] = early_gen(i + 1)
                    sgen[0] = ssd_gen(i + 1)
                FP()
                FP()
                rwkv_out(i)
                drain(fgen)
                drain(egen)
                drain(sgen)
                CK(5)
            drain(ogen)
            S.barrier()
            CK(6)

        rot_set[0] = [4, 5]
        TBK = 256
        NTB = NT_MAIN * TT // TBK
        with ExitStack() as stB:
            Wup = sb("Wup", [128, 8, 5632], BF16, stB)
            Wdown = sb("Wdown", [128, 22, 1024], BF16, stB)
            pfn = sb("pfn", [128, 1024], F32, stB)
            dma("sp", pfn[:], D["rows"][:, RC["pfn"]:RC["pfn"] + 1024], "c6", (), (pfn,))
            WupB = [[Buf("wup%d_%d" % (k, hf)) for hf in range(2)] for k in range(8)]
            WdB = [Buf("wd%d" % f3) for f3 in range(8)]
            with ExitStack() as stS:
                stage = [sb("stageB%d" % q, [128, 3072], F32, stS) for q in range(3)]
                sq = [0]

                def load_rows2(src_ap, ncols, fn):
                    si = sq[0] % 3
                    stg = stage[si]
                    sq[0] += 1
                    dma("sp", stg[:, 0:ncols], src_ap, "wsu%d" % si, (), (stg,))
                    fn(stg, sq[0])

                def cast_up(stg, n, k, hf):
                    o = Wup[:, k, hf * 2816:(hf + 1) * 2816]
                    e = n % 2
                    if e == 0:
                        ts("dve", o, stg[:, 0:2816], P("g2n", k), None, ALU.mult, None, (stg, par), (WupB[k][hf],))
                    else:
                        act(o, stg[:, 0:2816], AF.Copy, (stg, par), (WupB[k][hf],), scale=P("g2n", k))
                for k in range(8):
                    for hf in range(2):
                        load_rows2(D["wup"][:, k * 5632 + hf * 2816:k * 5632 + (hf + 1) * 2816], 2816,
                                   lambda stg, n, k=k, hf=hf: cast_up(stg, n, k, hf))
                for f3 in range(8):
                    n3 = min(3, 22 - 3 * f3)
                    load_rows2(D["wdown"][:, 3 * f3 * 1024:(3 * f3 + n3) * 1024], n3 * 1024,
                               lambda stg, n, f3=f3, n3=n3: cp(("act", "dve")[n % 2], Wdown[:, 3 * f3:3 * f3 + n3, :],
                                                               v3(stg[:, 0:n3 * 1024], n3), (stg,), (WdB[f3],)))
                S.barrier()
            CK(7)
            hld = [sb("hld%d" % q, [128, 2, 1024], F32, stB) for q in range(2)]
            hbB = sb("hbB", [128, 1024], BF16, stB)
            hnB = [sb("hnB%d" % q, [128, 8, 2 + TBK], BF16, stB) for q in range(2)]
            U = [sb("U%d" % q, [128, 2, 2, 2 + TBK], F32, stB) for q in range(2)]
            cacc = [[sb("cacc%d_%d" % (q, gv), [128, 2, TBK], F32, stB) for gv in range(2)] for q in range(2)]
            sgt = [sb("sgt%d" % q, [128, 2, TBK], F32, stB) for q in range(2)]
            actT = [sb("actT%d" % q, [128, 2, TBK], BF16, stB) for q in range(2)]
            carry = sb("carry", [128, 22, 2, 2], F32, stB)
            carryB = [Buf("carry%d" % f) for f in range(22)]
            fo = [sb("fo%d" % q, [128, 1024], F32, stB) for q in range(2)]
            fs = sb("fs", [128, 16], F32, stB)
            junkF = sb("junkF", [128, 1024], BF16, stB)
            fsm = Buf("fsm")
            outB = Buf("out")
            last = []
            memset("pool", carry[:], 0.0, carryB)

            def prep_sub(row0, q, sub, dst, dcol):
                dma("sp", hld[q][:, sub, :], hscr[row0:row0 + TT, :], "hld%d" % q, (hscrB,), (hld[q],))
                act(hbB[:], hld[q][:, sub, :], AF.Square, (hld[q],), (hbB, fsm), accum=fs[:, 8:9])
                ts("dve", fs[:, 9:10], fs[:, 8:9], 1.0 / 1024, 1e-6, ALU.mult, ALU.add, (fsm,), (fsm,))
                rsqrt_small(fs[:, 10:11], fs[:, 9:10], 1, (fsm,), (fsm,))
                act(hbB[:], hld[q][:, sub, :], AF.Copy, (hld[q], fsm), (hbB,), scale=fs[:, 10:11])
                for k in range(8):
                    trn(pT[:, k * 128:(k + 1) * 128], hbB[:, k * 128:(k + 1) * 128], identb[:], (hbB, identb), (pT,))
                cp("dve", dst[:, :, dcol:dcol + TT], v3(pT[:], 8), (pT,), (dst,))

            prep_sub(0, 1, 0, hnB[1], 2)
            ts("pool", hnB[1][:, :, 128:130], hnB[1][:, :, 128:130], flag[:], None, ALU.mult, None, (hnB[1], flag), (hnB[1],))
            for f in range(22):
                pc = PS()
                for gv in range(2):
                    col = gv * 2816 + f * 128
                    for k in range(8):
                        mm(pc[:, gv * 2:gv * 2 + 2], Wup[:, k, col:col + 128], hnB[1][:, k, 128:130], k == 0, k == 7,
                           (WupB[k][gv], hnB[1]), (pc,))
                cp("act", carry[:, f, :, :].rearrange("p a b -> p (a b)"), pc[:, 0:4], (pc,), (carryB[f // 2],))

            def prep_gen(tb):
                for sub in range(2):
                    prep_sub(TT + tb * TBK + sub * TT, tb % 2, sub, hnB[tb % 2], 2 + sub * TT)
                    yield

            for _ in prep_gen(0):
                pass
            for tb in range(NTB):
                q2 = tb % 2
                pgen = prep_gen(tb + 1) if tb + 1 < NTB else None
                pf = [[pool_banks[0], pool_banks[1]], [pool_banks[2], pool_banks[3]]]
                pub = [pool_banks[4], pool_banks[5]]

                def A_pe(fp):
                    for j in range(2):
                        f = 2 * fp + j
                        for gv in range(2):
                            col = gv * 2816 + f * 128
                            for k in range(8):
                                mm(pub[j][:, gv * TBK:(gv + 1) * TBK], Wup[:, k, col:col + 128], hnB[q2][:, k, 2:2 + TBK], k == 0, k == 7,
                                   (WupB[k][gv], hnB[q2]), (pub[j],))

                def A_rest(fp):
                    q = fp % 2
                    Uq = U[q]
                    for j in range(2):
                        cp("act", Uq[:, j, :, 2:2 + TBK], v3(pub[j][:], 2), (pub[j],), (Uq,))
                    f0 = 2 * fp
                    cp("dve", Uq[:, :, :, 0:2], carry[:, f0:f0 + 2, :, :], (carryB[fp],), (Uq,))
                    cp("dve", carry[:, f0:f0 + 2, :, :], Uq[:, :, :, TBK:TBK + 2], (Uq,), (carryB[fp],))
                    ca = cacc[q]
                    for j in range(2):
                        for gv in range(2):
                            ch = gv * 22 + f0 + j
                            act(ca[gv][:, j, :], Uq[:, j, gv, 2:2 + TBK], AF.Identity, (Uq, par), (ca[gv],), bias=P("fcb", ch),
                                scale=P("fcw", ch * 3 + 2))

                def B_nonpe(fp):
                    q = fp % 2
                    Uq = U[q]
                    ca = cacc[q]
                    f0 = 2 * fp
                    for k in range(2):
                        for j in range(2):
                            for gv in range(2):
                                ch = gv * 22 + f0 + j
                                stt("dve", ca[gv][:, j, :], Uq[:, j, gv, k:k + TBK], P("fcw", ch * 3 + k), ca[gv][:, j, :], ALU.mult, ALU.add,
                                    (Uq, par, ca[gv]), (ca[gv],))
                    act(sgt[q][:], ca[0][:], AF.Silu, (ca[0],), (sgt[q],))
                    tt("dve", actT[q][:], sgt[q][:], ca[1][:], ALU.mult, (sgt[q], ca[1]), (actT[q],))

                def B_pe(fp):
                    q = fp % 2
                    a3 = actT[q]
                    for j in range(2):
                        f = 2 * fp + j
                        for sub in range(2):
                            for hf in range(2):
                                mm(pf[sub][hf][:], a3[:, j, sub * TT:(sub + 1) * TT], Wdown[:, f, hf * 512:(hf + 1) * 512], f == 0, f == 21,
                                   (a3, WdB[f // 3]), (pf[sub][hf],))

                A_pe(0)
                A_rest(0)
                for fp in range(11):
                    if fp + 1 < 11:
                        A_pe(fp + 1)
                    B_nonpe(fp)
                    B_pe(fp)
                    if fp + 1 < 11:
                        A_rest(fp + 1)
                    if pgen is not None and fp in (3, 7):
                        next(pgen, None)
                for sub in range(2):
                    for hf in range(2):
                        cp("act", fo[sub][:, hf * 512:(hf + 1) * 512], pf[sub][hf][:], (pf[sub][hf],), (fo[sub],))
                if pgen is not None:
                    for _ in pgen:
                        pass
                for sub in range(2):
                    fq = fo[sub]
                    o4 = 4 * sub
                    act(junkF[:], fq[:], AF.Square, (fq,), (junkF, fsm), accum=fs[:, o4:o4 + 1])
                    ts("dve", fs[:, o4 + 2:o4 + 3], fs[:, o4:o4 + 1], 1.0 / 1024, 1e-6, ALU.mult, ALU.add, (fsm,), (fsm,))
                    rsqrt_small(fs[:, o4 + 3:o4 + 4], fs[:, o4 + 2:o4 + 3], 1, (fsm,), (fsm,))
                    stt("dve", fq[:], fq[:], fs[:, o4 + 3:o4 + 4], pfn[:], ALU.mult, ALU.mult, (fq, fsm, pfn), (fq,))
                    tt("dve", fq[:], fq[:], hld[q2][:, sub, :], ALU.add, (fq, hld[q2]), (fq,))
                    r0 = tb * TBK + sub * TT
                    last.append(dma("sp", out[r0:r0 + TT, :], fq[:], "ost%d" % sub, (fq,), (outB,)))
            S.finish(last[-2:])
    except _Stop:
        pass
    return nc


_NC_CACHE = {}


def kernel(**inputs):
    inputs = {k: np.asarray(v) for k, v in inputs.items()}
    x = inputs["x"].astype(np.float32, copy=False)
    shared = _host_prep(inputs)
    in_maps = []
    for c in range(8):
        b, s = c // 2, c % 2
        m = dict(shared)
        if s == 0:
            m["x_in"] = np.ascontiguousarray(np.concatenate([x[b, 0:2048], x[b, 0:2048]], axis=0))
        else:
            m["x_in"] = np.ascontiguousarray(x[b])
        m["flag"] = np.full((128, 1), float(s), np.float32)
        in_maps.append(m)
    if "nc" not in _NC_CACHE:
        _NC_CACHE["nc"] = build_nc()
    res = run_bass_kernel_spmd(_NC_CACHE["nc"], in_maps, core_ids=list(range(8)))
    outp = np.empty((4, 4096, 1024), np.float32)
    for c in range(8):
        b, s = c // 2, c % 2
        outp[b, s * 2048:(s + 1) * 2048] = res.results[c]["out"]
    return outp
```

```python
from contextlib import ExitStack
import numpy as np
import concourse.bass as bass
import concourse.mybir as mybir
from concourse.bass_utils import run_bass_kernel_spmd

F32 = mybir.dt.float32
BF16 = mybir.dt.bfloat16
AF = mybir.ActivationFunctionType
ALU = mybir.AluOpType
AX = mybir.AxisListType

SAME_ENGINE_SYNC = True
NT_ALL = 32
NT_MAIN = 16
TT = 128
NFM = 22
HW = 131


class Buf:
    __slots__ = ("name", "w", "r")

    def __init__(self, name):
        self.name = name
        self.w = None
        self.r = []


class Sched:
    ENGS = ("pe", "act", "dve", "pool", "sp")

    def __init__(self, nc, stack):
        self.nc = nc
        self.stack = stack
        self.sems = {}
        self.cnt = {}
        for e in self.ENGS:
            self.sems[e] = stack.enter_context(nc.semaphore("s_" + e))
            self.cnt[e] = 0
        self.known = {e: {} for e in self.ENGS}
        self.snap = {}
        self.prog = {e: [] for e in self.ENGS}
        self.dma_sems = {}
        self.dma_cnt = {}

    def _sem_handle(self, k):
        return self.sems[k] if k in self.sems else self.dma_sems[k]

    def _need(self, eng, tok, acc):
        if tok is None:
            return
        semkey, val, src = tok
        if src == eng and (eng == "pe" or not SAME_ENGINE_SYNC):
            return
        if self.known[eng].get(semkey, 0) >= val:
            return
        if acc.get(semkey, 0) < val:
            acc[semkey] = val

    def _emit_waits(self, eng, acc):
        for semkey, val in acc.items():
            h = self._sem_handle(semkey)
            self.prog[eng].append(lambda e, h=h, val=val: e.wait_ge(h, val))
            k = self.known[eng]
            if k.get(semkey, 0) < val:
                k[semkey] = val
            if semkey in self.sems:
                sn = self.snap.get((semkey, val))
                if sn:
                    for kk, vv in sn.items():
                        if k.get(kk, 0) < vv:
                            k[kk] = vv

    def _deps(self, eng, reads, writes):
        acc = {}
        for b in reads:
            self._need(eng, b.w, acc)
        for b in writes:
            self._need(eng, b.w, acc)
            for t in b.r:
                self._need(eng, t, acc)
        self._emit_waits(eng, acc)

    def _mark(self, tok, reads, writes):
        for b in reads:
            b.r.append(tok)
            if len(b.r) > 24:
                b.r = b.r[-24:] if False else b.r
        for b in writes:
            b.w = tok
            b.r = []

    def op(self, eng, fn, reads=(), writes=()):
        self._deps(eng, reads, writes)
        self.cnt[eng] += 1
        n = self.cnt[eng]
        h = self.sems[eng]
        self.prog[eng].append(lambda e, fn=fn, h=h: fn(e).then_inc(h, 1))
        tok = (eng, n, eng)
        sn = dict(self.known[eng])
        sn[eng] = n
        self.snap[(eng, n)] = sn
        self._mark(tok, reads, writes)
        return tok

    def dma(self, queue, fn, semkey, reads=(), writes=()):
        if semkey not in self.dma_sems:
            self.dma_sems[semkey] = self.stack.enter_context(self.nc.semaphore("d_" + semkey))
            self.dma_cnt[semkey] = 0
        self._deps(queue, reads, writes)
        self.dma_cnt[semkey] += 16
        val = self.dma_cnt[semkey]
        h = self.dma_sems[semkey]
        self.prog[queue].append(lambda e, fn=fn, h=h: fn(e).then_inc(h, 16))
        tok = (semkey, val, "dma")
        self._mark(tok, reads, writes)
        return tok

    def barrier(self):
        toks = [(e, self.cnt[e], e) for e in self.ENGS if self.cnt[e] > 0]
        toks += [(k, v, "dma") for k, v in self.dma_cnt.items() if v > 0]
        for e in self.ENGS:
            acc = {}
            for t in toks:
                if t[0] == e:
                    continue
                self._need(e, t, acc)
            self._emit_waits(e, acc)

    def finish(self, final_toks):
        for t in final_toks:
            acc = {}
            self._need("sp", t, acc)
            self._emit_waits("sp", acc)
        with self.nc.Block() as block:
            @block.tensor
            def _(e):
                for f in self.prog["pe"]:
                    f(e)

            @block.scalar
            def _(e):
                for f in self.prog["act"]:
                    f(e)

            @block.vector
            def _(e):
                for f in self.prog["dve"]:
                    f(e)

            @block.gpsimd
            def _(e):
                for f in self.prog["pool"]:
                    f(e)

            @block.sync
            def _(e):
                for f in self.prog["sp"]:
                    f(e)


PC = {}
_o = 0
for _n, _w in (("cw", 32), ("cb", 8), ("mu", 14), ("w0", 4), ("a0", 4), ("kk", 4), ("ka", 4),
               ("fcw", 132), ("fcb", 44), ("g1", 8), ("g2n", 8)):
    PC[_n] = _o
    _o += _w
NPAR = _o
RC = {}
_o = 0
for _n, _w in (("dtb", 8), ("alog", 8), ("dsk", 8), ("ssdg", 512), ("lnw", 512), ("lnb", 512),
               ("pmn", 1024), ("pfn", 1024)):
    RC[_n] = _o
    _o += _w
NROW = _o
CC = {"ident": 0, "triu": 128, "trius": 256, "trils": 384, "onesblk": 512, "ones": 640}
NCONST = 768


def _cols(v, nch):
    return np.ascontiguousarray(v.reshape(nch, 128).T)


def _host_prep(inp):
    f = np.float32
    w_in = inp["w_in"][0]
    s0 = 0
    r0 = 1544
    perm = np.concatenate([
        np.arange(512, 1536),
        r0 + np.arange(0, 512),
        r0 + np.arange(576, 1088),
        r0 + np.arange(1088, 1600),
        r0 + np.arange(512, 576),
        r0 + np.arange(1600, 1664),
        r0 + np.arange(1664, 1792),
        np.arange(0, 512),
        np.arange(1536, 1544),
    ])
    win = np.ascontiguousarray(w_in[:, perm].reshape(8, 128, 3336).transpose(1, 0, 2)).reshape(128, 8 * 3336)
    wout = np.ascontiguousarray(inp["w_out"][0].reshape(8, 128, 1024).transpose(1, 0, 2)).reshape(128, 8 * 1024)
    wup = np.ascontiguousarray(inp["ffn_w_up"][0].reshape(8, 128, 5632).transpose(1, 0, 2)).reshape(128, 8 * 5632)
    wdown = np.ascontiguousarray(inp["ffn_w_down"][0].reshape(22, 128, 1024).transpose(1, 0, 2)).reshape(128, 22 * 1024)
    par = np.zeros((128, NPAR), f)
    cw = inp["ssd_conv_w"][0]
    par[:, PC["cw"]:PC["cw"] + 32] = cw.reshape(4, 8, 128).transpose(2, 1, 0).reshape(128, 32)
    par[:, PC["cb"]:PC["cb"] + 8] = _cols(inp["ssd_conv_b"][0], 8)
    mu = inp["rwkv_mu"][0]
    mup = np.concatenate([mu[0:512], mu[576:1088], mu[1088:1600], mu[512:576], mu[1600:1664], mu[1664:1792]])
    par[:, PC["mu"]:PC["mu"] + 14] = _cols(mup, 14)
    par[:, PC["w0"]:PC["w0"] + 4] = _cols(inp["rwkv_w0"][0], 4)
    par[:, PC["a0"]:PC["a0"] + 4] = _cols(inp["rwkv_a0"][0], 4)
    par[:, PC["kk"]:PC["kk"] + 4] = _cols(inp["rwkv_k_k"][0], 4)
    par[:, PC["ka"]:PC["ka"] + 4] = _cols(inp["rwkv_k_a"][0], 4)
    fw = inp["ffn_conv_w"][0]
    par[:, PC["fcw"]:PC["fcw"] + 132] = fw.reshape(3, 44, 128).transpose(2, 1, 0).reshape(128, 132)
    par[:, PC["fcb"]:PC["fcb"] + 44] = _cols(inp["ffn_conv_b"][0], 44)
    par[:, PC["g1"]:PC["g1"] + 8] = _cols(inp["pre_mix_norm"][0], 8)
    par[:, PC["g2n"]:PC["g2n"] + 8] = _cols(inp["pre_ffn_norm"][0], 8)
    rows = np.zeros((128, NROW), f)

    def put(name, v):
        rows[:, RC[name]:RC[name] + v.size] = np.broadcast_to(v.reshape(1, -1), (128, v.size))
    put("dtb", inp["ssd_dt_bias"][0])
    put("alog", inp["ssd_a_log"][0])
    put("dsk", inp["ssd_d"][0])
    put("ssdg", inp["ssd_norm"][0])
    put("lnw", inp["rwkv_ln_w"][0])
    put("lnb", inp["rwkv_ln_b"][0])
    put("pmn", inp["post_mix_norm"][0])
    put("pfn", inp["post_ffn_norm"][0])
    consts = np.zeros((128, NCONST), f)
    i = np.arange(128)
    consts[:, 0:128] = np.eye(128)
    consts[:, 128:256] = (i[:, None] <= i[None, :])
    consts[:, 256:384] = (i[:, None] < i[None, :])
    consts[:, 384:512] = (i[:, None] > i[None, :])
    consts[:, 512:640] = ((i[:, None] // 64) == (i[None, :] // 64))
    consts[:, 640:768] = 1.0
    wlora = np.concatenate([inp["rwkv_w2"][0], inp["rwkv_a2"][0]], axis=0).astype(f)
    g2 = np.ascontiguousarray(inp["rwkv_g2"][0]).astype(f)
    rk = inp["rwkv_r_k"][0]
    sel = np.zeros((128, 4, 8), f)
    for c in range(4):
        for hp in range(2):
            sel[hp * 64:(hp + 1) * 64, c, 2 * c + hp] = rk[2 * c + hp]
    shared = {"win": win, "wout": wout, "wup": wup, "wdown": wdown, "params": par, "rows": rows,
              "consts": consts, "wlora": wlora, "g2": g2, "sel": sel.reshape(128, 32)}
    return {k: np.ascontiguousarray(v, dtype=f) for k, v in shared.items()}


NSLOT = NT_MAIN + 1
DBG = 0


class _Stop(Exception):
    pass


def build_nc():
    nc = bass.Bass("TRN2", target_bir_lowering=False)
    D = {}
    for name, shape in (("x_in", [4096, 1024]), ("win", [128, 8 * 3336]), ("wout", [128, 8192]),
                        ("wup", [128, 8 * 5632]), ("wdown", [128, 22 * 1024]), ("params", [128, NPAR]),
                        ("rows", [128, NROW]), ("consts", [128, NCONST]), ("wlora", [128, 512]),
                        ("g2", [128, 512]), ("sel", [128, 32]), ("flag", [128, 1])):
        D[name] = nc.dram_tensor(name, shape, F32, kind="ExternalInput").ap()
    out = nc.dram_tensor("out", [NT_MAIN * TT, 1024], F32, kind="ExternalOutput").ap()
    hscr = nc.dram_tensor("hscr", [NSLOT * TT, 1024], F32, kind="Internal").ap()

    try:
      with ExitStack() as st:
        S = Sched(nc, st)
        dbgt = st.enter_context(nc.sbuf_tensor("sb_dbgt", [128, 64], F32))

        def CK(n, src=None):
            if DBG == n:
                if src is None:
                    S.op("dve", lambda e: e.memset(dbgt[:], 1.0), (), ())
                    tk = S.dma("sp", lambda e: e.dma_start(out=out[0:128, 0:64], in_=dbgt[:]), "dbg", (), ())
                else:
                    ap, bufs, ncol = src
                    tk = S.dma("sp", lambda e: e.dma_start(out=out[0:128, 0:ncol], in_=ap), "dbg", [b.b for b in bufs], ())
                S.finish([tk])
                raise _Stop()

        class TB:
            def __init__(self, t, name, b=None):
                self.t = t
                self.b = b if b is not None else Buf(name)

            def __getitem__(self, k):
                return self.t[k]

        class View:
            def __init__(self, ap, b):
                self.ap = ap
                self.b = b

            def __getitem__(self, k):
                return self.ap[k]

        def sb(name, shape, dt, stack=st):
            return TB(stack.enter_context(nc.sbuf_tensor("sb_" + name, shape, dt)), name)

        def psum(name, shape, dt):
            t = TB(st.enter_context(nc.psum_tensor("ps_" + name, shape, dt)), name)
            PSB.add(id(t.b))
            return t

        PSB = set()

        def rd(*xs):
            return [x.b if hasattr(x, "b") else x for x in xs]

        def _xrw(r, w):
            r = rd(*r)
            w = rd(*w)
            rr = [x for x in r if id(x) not in PSB]
            ww = list(w) + [x for x in r if id(x) in PSB and x not in w]
            return rr, ww

        def tt(eng, o, a, b, op, r, w):
            S.op(eng, lambda e: e.tensor_tensor(out=o, in0=a, in1=b, op=op), *_xrw(r, w))

        def ts(eng, o, a, s1, s2, op0, op1, r, w):
            if s2 is None:
                S.op(eng, lambda e: e.tensor_scalar(out=o, in0=a, scalar1=s1, scalar2=None, op0=op0), *_xrw(r, w))
            else:
                S.op(eng, lambda e: e.tensor_scalar(out=o, in0=a, scalar1=s1, scalar2=s2, op0=op0, op1=op1), *_xrw(r, w))

        def stt(eng, o, a, s, b, op0, op1, r, w):
            S.op(eng, lambda e: e.scalar_tensor_tensor(out=o, in0=a, scalar=s, in1=b, op0=op0, op1=op1), *_xrw(r, w))

        def act(o, a, func, r, w, bias=None, scale=None, accum=None):
            kw = {}
            if bias is not None:
                kw["bias"] = bias
            if scale is not None:
                kw["scale"] = scale
            if accum is not None:
                kw["accum_out"] = accum
            S.op("act", lambda e: e.activation(out=o, in_=a, func=func, **kw), *_xrw(r, w))

        def cp(eng, o, a, r, w):
            if eng == "act":
                act(o, a, AF.Copy, r, w)
            else:
                S.op(eng, lambda e: e.tensor_copy(out=o, in_=a), *_xrw(r, w))

        def mm(o, l, rh, start, stop, r, w):
            S.op("pe", lambda e: e.matmul(out=o, lhsT=l, rhs=rh, start=start, stop=stop, skip_group_check=True), *_xrw(r, w))

        def trn(o, a, ident, r, w):
            S.op("pe", lambda e: e.transpose(out=o, in_=a, identity=ident), *_xrw(r, w))

        def memset(eng, o, v, w):
            S.op(eng, lambda e: e.memset(o, v), (), rd(*w))

        def dma(q, o, a, key, r, w):
            return S.dma(q, lambda e: e.dma_start(out=o, in_=a), key, *_xrw(r, w))

        def v3(ap, a):
            return ap.rearrange("p (a b) -> p a b", a=a)

        par = sb("par", [128, NPAR], F32)
        cst = sb("cst", [128, NCONST], F32)
        identb = sb("identb", [128, 128], BF16)
        flag = sb("flag", [128, 1], F32)
        neghalf = sb("neghalf", [128, 512], F32)
        zeros = sb("zeros", [128, 128], F32)
        pT = psum("pT", [128, 1024], BF16)
        pTR = psum("pTR", [128, 1024], BF16)
        pool_banks = [psum("pb%d" % i, [128, 512], F32) for i in range(6)]
        rot = [0]
        rot_set = [[1, 2, 3, 4, 5]]

        def PS():
            rs = rot_set[0]
            b = pool_banks[rs[rot[0] % len(rs)]]
            rot[0] += 1
            return b

        def C(name, n=128):
            return cst[:, CC[name]:CC[name] + n]

        def P(name, c):
            return par[:, PC[name] + c:PC[name] + c + 1]

        dma("sp", par[:], D["params"], "c0", (), (par,))
        dma("sp", cst[:], D["consts"], "c2", (), (cst,))
        dma("sp", flag[:], D["flag"], "c3", (), (flag,))
        cp("dve", identb[:], C("ident"), (cst,), (identb,))
        memset("pool", neghalf[:], -0.5, (neghalf,))
        memset("pool", zeros[:], 0.0, (zeros,))
        ts("dve", par[:, PC["cw"]:PC["cw"] + 40], par[:, PC["cw"]:PC["cw"] + 40], 0.5, None, ALU.mult, None, (par,), (par,))

        def rsqrt_small(o, a, n, r, w):
            tt("pool", o, a, neghalf[:, 0:n], ALU.pow, list(r) + [neghalf], w)

        hscrB = Buf("hscr")

        with ExitStack() as stA:
            rows = sb("rows", [128, 2584], F32, stA)

            def R(name, n):
                return rows[:, RC[name]:RC[name] + n]

            dma("sp", rows[:], D["rows"][:, 0:2584], "c1", (), (rows,))
            Win = sb("Win", [128, 8, 3336], BF16, stA)
            Wout = sb("Wout", [128, 8, 1024], BF16, stA)
            Wlora = sb("Wlora", [128, 512], F32, stA)
            G2h = sb("G2h", [128, 512], BF16, stA)
            sel = sb("sel", [128, 4, 8], F32, stA)
            arow = sb("arow", [128, 8], F32, stA)
            with ExitStack() as stS:
                stage = [sb("stageA%d" % q, [128, 3336], F32, stS) for q in range(2)]
                sq = [0]

                def load_rows(src_ap, ncols, fn):
                    stg = stage[sq[0] % 2]
                    sq[0] += 1
                    dma("sp", stg[:, 0:ncols], src_ap, "wst%d" % (sq[0] % 2), (), (stg,))
                    fn(stg)
                for k in range(8):
                    load_rows(D["win"][:, k * 3336:(k + 1) * 3336], 3336,
                              lambda stg, k=k: (ts("dve", Win[:, k, :], stg[:, 0:3336], P("g1", k), None,
                                                   ALU.mult, None, (stg, par), (Win,)) if k % 2 == 0 else
                                                act(Win[:, k, :], stg[:, 0:3336], AF.Copy, (stg, par), (Win,), scale=P("g1", k))))
                ts("pool", Win[:, :, 2816:3328], Win[:, :, 2816:3328], 0.5, None, ALU.mult, None, (Win,), (Win,))
                for k in range(8):
                    load_rows(D["wout"][:, k * 1024:(k + 1) * 1024], 1024,
                              lambda stg, k=k: cp("act", Wout[:, k, :], stg[:, 0:1024], (stg,), (Wout,)))
                load_rows(D["g2"], 512, lambda stg: ts("dve", G2h[:], stg[:, 0:512], 0.5, None, ALU.mult, None, (stg,), (G2h,)))
                dma("sp", Wlora[:], D["wlora"], "c4", (), (Wlora,))
                dma("sp", sel[:].rearrange("p a b -> p (a b)"), D["sel"], "c5", (), (sel,))
                act(arow[:], R("alog", 8), AF.Exp, (rows,), (arow,))
                ts("dve", arow[:], arow[:], -1.0, None, ALU.mult, None, (arow,), (arow,))
                S.barrier()
            CK(1)

            xt1 = sb("xt", [128, 1024], F32, stA)
            xt = [xt1, xt1]
            xT = [sb("xT%d" % i, [128, 8, HW], BF16, stA) for i in range(2)]
            raw = sb("raw", [128, NFM, HW], F32, stA)
            rawB = [Buf("raw_%d" % j) for j in range(8)]
            gz = [sb("gz%d" % i, [128, 512], F32, stA) for i in range(2)]
            dtr = [sb("dtr%d" % i, [128, 8], F32, stA) for i in range(2)]
            xb = sb("xb", [128, 1024], BF16, stA)
            hb = xb
            sm = sb("sm", [128, 64], F32, stA)
            smB = [Buf("sm%d" % i) for i in range(16)]
            Pp = sb("Pp", [128, 14, 128], F32, stA)
            BB = sb("BB", [128, 2048], F32, stA)
            AC = sb("AC", [128, 1024], F32, stA)
            B2 = sb("B2", [128, 1024], F32, stA)
            B3 = sb("B3", [128, 1024], F32, stA)
            B4 = sb("B4", [128, 512], F32, stA)
            B56 = sb("B56", [128, 2048], F32, stA)
            B7 = sb("B7", [128, 1024], F32, stA)
            ht = sb("ht", [128, 1024], F32, stA)
            X1 = sb("X1", [128, 512], BF16, stA)
            X2 = sb("X2", [128, 512], BF16, stA)
            X3 = sb("X3", [128, 512], BF16, stA)
            GA = sb("GA", [128, 1024], BF16, stA)
            Btm = sb("Btm", [128, 2, 128], BF16, stA)
            STf = sb("STf", [128, 8, 64], F32, stA)
            STb = sb("STb", [128, 8, 64], BF16, stA)
            ytok = [sb("ytok%d" % q, [128, 1024], BF16, stA) for q in range(2)]
            GA0 = sb("GA0", [128, 1024], BF16, stA)
            UTfT = sb("UTfT", [128, 512], F32, stA)
            tmpST = sb("tmpST", [128, 256], F32, stA)
            mixT = sb("mixT", [128, 8, 128], BF16, stA)
            lin = sb("lin", [128, 128], F32, stA)
            sg = sb("sg", [128, 128], BF16, stA)
            btm = sb("btm", [128, 512], BF16, stA)
            ktm = sb("ktm", [128, 512], BF16, stA)
            vtm = sb("vtm", [128, 512], BF16, stA)
            ABs = [sb("ABs%d" % c, [128, 2, 256], BF16, stA) for c in range(4)]
            AKs = [sb("AKs%d" % c, [128, 2, 256], BF16, stA) for c in range(4)]
            PPm = [[sb("PPm%d_%d" % (q, c), [128, 2, 2, 128], BF16, stA) for c in range(4)] for q in range(2)]
            Pm = [[View(PPm[q][c][:, 0, :, :], PPm[q][c].b) for c in range(4)] for q in range(2)]
            PTm = [[View(PPm[q][c][:, 1, :, :], PPm[q][c].b) for c in range(4)] for q in range(2)]
            UT = sb("UT", [128, 8, 64], BF16, stA)
            STr = sb("STr", [128, 4, 64], F32, stA)
            STrb = sb("STrb", [128, 4, 64], BF16, stA)
            maskSI = sb("maskSI", [128, 256], F32, stA)
            hw0 = sb("hw0", [128, 8], F32, stA)
            tiny = sb("tiny", [128, 1], F32, stA)
            memset("pool", tiny[:], 1e-24, (tiny,))
            bnsb = sb("bnsb", [128, 8], F32, stA)
            gn = sb("gn", [128, 40], F32, stA)
            hs = sb("hs", [128, 8], F32, stA)
            hsm = Buf("hsm")
            dlt = View(v3(B56[:, 0:1792], 14), B56.b)
            th = View(v3(BB[:, 0:1024], 8), BB.b)
            xax = View(v3(BB[:, 1024:1536], 4), BB.b)
            xs_tm = View(BB[:, 1536:2048], BB.b)
            Dm = View(v3(BB[:, 0:1024], 8), BB.b)
            lw = View(v3(BB[:, 0:512], 4), BB.b)
            av = View(v3(BB[:, 512:1024], 4), BB.b)
            cum = View(v3(BB[:, 1024:1536], 4), BB.b)
            cumex = View(v3(BB[:, 1536:2048], 4), BB.b)
            yr = View(v3(B56[:, 0:512], 8), B56.b)
            ysq = View(v3(B56[:, 512:1024], 8), B56.b)
            acc = View(v3(AC[:], 8), AC.b)
            rseg = View(v3(AC[:], 8), AC.b)
            thw = View(v3(AC[:, 0:512], 4), AC.b)
            tha = View(v3(AC[:, 512:1024], 4), AC.b)
            mn = View(AC[:], AC.b)
            dx = View(B2[:, 0:512], B2.b)
            ytmp = View(B2[:, 512:1024], B2.b)
            E2 = View(v3(B2[:, 0:512], 4), B2.b)
            tmpS = View(v3(tmpST[:], 4), tmpST.b)
            ysd = View(B3[:, 0:512], B3.b)
            scm = View(v3(B3[:, 512:768], 2), B3.b)
            kkv = View(v3(B3[:, 0:512], 4), B3.b)
            UTf = View(UTfT[:], UTfT.b)
            kk2 = View(v3(B3[:, 512:1024], 4), B3.b)
            E1x = sb("E1x", [128, 4, 129], F32, stA)
            memset("pool", E1x[:], 1.0, (E1x,))
            E1 = View(E1x[:, :, 1:129], E1x.b)
            E3 = View(E1x[:, :, 0:128], E1x.b)
            inv = View(v3(B4[:, 0:512], 4), B4.b)
            kkn = View(v3(B56[:, 0:512], 4), B56.b)
            kmod = View(v3(B56[:, 512:1024], 4), B56.b)
            t1 = View(v3(B56[:, 1024:1536], 4), B56.b)
            rk = View(v3(B56[:, 1024:1536], 4), B56.b)
            beta = View(v3(B56[:, 1536:2048], 4), B56.b)
            gsb = View(B7[:, 0:512], B7.b)
            bterm = View(v3(B7[:, 512:1024], 8), B7.b)
            Xb = View(X1[:], X1.b)
            btl = View(v3(X1[:], 4), X1.b)
            Xd = View(X2[:], X2.b)
            ktl = View(v3(X2[:], 4), X2.b)
            BCb = View(v3(X3[:], 4), X3.b)
            vb = View(v3(X3[:], 4), X3.b)
            Gm = View(v3(GA[:], 8), GA.b)
            GA1 = sb("GA1", [128, 1024], BF16, stA)
            arz = [View(GA0[:].rearrange("p (c a b) -> p c a b", c=4, a=2), GA0.b),
                   View(GA1[:].rearrange("p (c a b) -> p c a b", c=4, a=2), GA1.b)]
            GAz = [GA0, GA1]
            memset("pool", GA1[:], 0.0, (GA1,))
            memset("pool", GA0[:], 0.0, (GA0,))

            cp("dve", maskSI[:, 0:128], C("trius"), (cst,), (maskSI,))
            cp("dve", maskSI[:, 128:256], C("triu"), (cst,), (maskSI,))
            memset("dve", STf[:], 0.0, (STf,))
            memset("dve", STb[:], 0.0, (STb,))
            memset("dve", STr[:], 0.0, (STr,))
            memset("dve", STrb[:], 0.0, (STrb,))
            memset("pool", xT[1][:], 0.0, (xT[1],))
            memset("pool", xT[0][:], 0.0, (xT[0],))
            ts("dve", hw0[:, 0:4], par[:, PC["w0"]:PC["w0"] + 4], 0.5, None, ALU.mult, None, (par,), (hw0,))
            ts("dve", hw0[:, 4:8], par[:, PC["a0"]:PC["a0"] + 4], 0.5, None, ALU.mult, None, (par,), (hw0,))

            def SM(i, n=1):
                return sm[:, i:i + n]

            B0T = NT_ALL - NT_MAIN

            fstate = [0]

            def front_gen(i):
                p = i % 2
                fstate[0] = 0
                dma("sp", xt[p][:], D["x_in"][i * TT:(i + 1) * TT, :], "x0", (), (xt[p],))
                act(xb[:], xt[p][:], AF.Square, (xt[p],), (xb, smB[0]), accum=SM(0))
                ts("dve", SM(1), SM(0), 1.0 / 1024, 1e-6, ALU.mult, ALU.add, (smB[0],), (smB[1],))
                rsqrt_small(SM(2), SM(1), 1, (smB[1],), (smB[2],))
                act(xb[:], xt[p][:], AF.Copy, (xt[p], smB[2]), (xb,), scale=SM(2))
                yield
                for k in range(8):
                    trn(pT[:, k * 128:(k + 1) * 128], xb[:, k * 128:(k + 1) * 128], identb[:], (xb, identb), (pT,))
                cp("dve", xT[p][:, :, 3:HW], v3(pT[:], 8), (pT,), (xT[p],))
                cp("pool", xT[p][:, :, 0:3], xT[1 - p][:, :, 128:131], (xT[1 - p],), (xT[p],))
                if i == B0T:
                    ts("pool", xT[p][:, :, 0:3], xT[p][:, :, 0:3], flag[:], None, ALU.mult, None, (xT[p], flag), (xT[p],))
                yield
                hist = i < B0T - 1
                for bnk in range(8):
                    n = min(3, NFM - 3 * bnk)
                    if hist and bnk in (2, 3, 7):
                        fstate[0] = bnk + 1
                        continue
                    pb = PS()
                    for jj in range(n):
                        j = 3 * bnk + jj
                        for k in range(8):
                            mm(pb[:, jj * HW:(jj + 1) * HW], Win[:, k, j * 128:(j + 1) * 128], xT[p][:, k, :],
                               k == 0, k == 7, (Win, xT[p]), (pb,))
                    cp("act" if bnk % 2 == 0 else "dve", raw[:, 3 * bnk:3 * bnk + n, :], v3(pb[:, 0:n * HW], n),
                       (pb,), (rawB[bnk],))
                    fstate[0] = bnk + 1
                    yield
                full = i >= B0T - 1
                if full:
                    pz = PS()
                    for k in range(8):
                        mm(pz[:, 0:512], xT[p][:, k, 3:HW], Win[:, k, 2816:3328], k == 0, k == 7, (Win, xT[p]), (pz,))
                pd = PS()
                for k in range(8):
                    mm(pd[:, 0:8], xT[p][:, k, 3:HW], Win[:, k, 3328:3336], k == 0, k == 7, (Win, xT[p]), (pd,))
                if full:
                    act(gz[p][:], pz[:, 0:512], AF.Tanh, (pz,), (gz[p],))
                    stt("dve", gz[p][:], gz[p][:], 1.0, pz[:, 0:512], ALU.add, ALU.mult, (gz[p], pz), (gz[p],))
                tt("dve", dtr[p][:], pd[:, 0:8], R("dtb", 8), ALU.add, (pd, rows), (dtr[p],))

            estate = [0]

            def early_gen(i):
                rw = raw
                estate[0] = 0
                hist = i < B0T - 1
                for c in range(6 if hist else 8):
                    need_banks(c // 3 + 1)
                    rb = rawB[c // 3]
                    ts("dve", acc[:, c, :], rw[:, c, 3:HW], P("cw", c * 4 + 3), P("cb", c), ALU.mult, ALU.add,
                       (rb, par), (acc,))
                    for k in range(3):
                        stt("dve", acc[:, c, :], rw[:, c, k:k + 128], P("cw", c * 4 + k), acc[:, c, :], ALU.mult, ALU.add,
                            (rb, par, acc), (acc,))
                    estate[0] = c + 1
                    yield
                estate[0] = 8
                need_banks(8)
                c0, c1 = (4, 13) if hist else (0, 14)
                nc_ = c1 - c0
                RBr = rawB[4:7] if hist else rawB[2:8]
                tt("dve", dlt[:, c0:c1, :], rw[:, 8 + c0:8 + c1, 2:130], rw[:, 8 + c0:8 + c1, 3:HW], ALU.subtract, RBr, (dlt,))
                yield
                tt("dve", dlt[:, c0:c1, :], dlt[:, c0:c1, :],
                   par[:, PC["mu"] + c0:PC["mu"] + c1].unsqueeze(2).broadcast_to([128, nc_, 128]),
                   ALU.mult, (dlt, par), (dlt,))
                yield
                tt("dve", Pp[:, c0:c1, :], dlt[:, c0:c1, :], rw[:, 8 + c0:8 + c1, 3:HW], ALU.add, [dlt] + RBr, (Pp,))
                yield

            def ssd_gen(i):
                p = i % 2
                full = i >= B0T - 1
                need_front_done()
                need_conv()
                nch = 8 if full else 6
                act(th[:, 0:nch, :], acc[:, 0:nch, :], AF.Tanh, (acc,), (th,))
                stt("dve", xax[:], th[:, 0:4, :], 1.0, acc[:, 0:4, :], ALU.add, ALU.mult, (th, acc), (xax,))
                stt("dve", BCb[:, 0:nch - 4, :], th[:, 4:nch, :], 1.0, acc[:, 4:nch, :], ALU.add, ALU.mult, (th, acc), (BCb,))
                act(SM(8, 8), dtr[p][:], AF.Exp, (dtr[p],), (smB[8],))
                act(SM(8, 8), SM(8, 8), AF.Ln, (smB[8],), (smB[8],), bias=1.0)
                tt("dve", SM(16, 8), SM(8, 8), arow[:], ALU.mult, (smB[8], arow), (smB[9],))
                yield
                px = PS()
                for c in range(4):
                    mm(px[:, c * 128:(c + 1) * 128], xax[:, c, :], C("ident"), True, True, (xax, cst), (px,))
                cp("act", xs_tm[:], px[:], (px,), (xs_tm,))
                if full:
                    tt("dve", v3(Xb[:], 8), v3(xs_tm[:], 8), SM(8, 8).unsqueeze(2).broadcast_to([128, 8, 64]), ALU.mult,
                       (xs_tm, smB[8]), (Xb,))
                for g in range(2):
                    trn(pTR[:, g * 128:(g + 1) * 128], BCb[:, g, :], identb[:], (BCb, identb), (pTR,))
                cp("act", Btm[:], v3(pTR[:, 0:256], 2), (pTR,), (Btm,))
                yield
                psm = PS()
                mm(psm[:, 0:8], C("triu"), SM(16, 8), True, True, (cst, smB[9]), (psm,))
                mm(psm[:, 8:16], C("trils"), SM(16, 8), True, True, (cst, smB[9]), (psm,))
                mm(psm[:, 16:24], C("ones"), SM(16, 8), True, True, (cst, smB[9]), (psm,))
                act(SM(24, 24), psm[:, 0:24], AF.Exp, (psm,), (smB[10],))
                tt("dve", SM(48, 8), SM(8, 8), SM(32, 8), ALU.mult, (smB[8], smB[10]), (smB[11],))
                tt("dve", v3(Xd[:], 8), v3(xs_tm[:], 8), SM(48, 8).unsqueeze(2).broadcast_to([128, 8, 64]), ALU.mult,
                   (xs_tm, smB[11]), (Xd,))
                yield
                if full:
                    tt("dve", v3(dx[:], 8), v3(xs_tm[:], 8), R("dsk", 8).unsqueeze(2).broadcast_to([128, 8, 64]), ALU.mult,
                       (xs_tm, rows), (dx,))
                    psc = PS()
                    for g in range(2):
                        mm(psc[:, g * 128:(g + 1) * 128], BCb[:, g, :], BCb[:, 2 + g, :], True, True, (BCb,), (psc,))
                    tt("dve", scm[:], v3(psc[:, 0:256], 2), C("triu").unsqueeze(1).broadcast_to([128, 2, 128]), ALU.mult,
                       (psc, cst), (scm,))
                    tt("dve", rseg[:], SM(16, 8).unsqueeze(2).broadcast_to([128, 8, 128]),
                       C("triu").unsqueeze(1).broadcast_to([128, 8, 128]), ALU.mult, (smB[9], cst), (rseg,))
                    yield
                    for hh in range(2):
                        pg = PS()
                        mm(pg[:], C("trils"), rseg[:, 4 * hh:4 * hh + 4, :].rearrange("p a b -> p (a b)"), True, True,
                           (cst, rseg), (pg,))
                        act(Dm[:, 4 * hh:4 * hh + 4, :].rearrange("p a b -> p (a b)"), pg[:], AF.Exp, (pg,), (Dm,))
                    for g in range(2):
                        tt("dve", Gm[:, 4 * g:4 * g + 4, :], Dm[:, 4 * g:4 * g + 4, :],
                           scm[:, g, :].unsqueeze(1).broadcast_to([128, 4, 128]), ALU.mult, (Dm, scm), (Gm,))
                    yield
                    py1 = PS()
                    for h in range(8):
                        mm(py1[:, h * 64:(h + 1) * 64], Gm[:, h, :], Xb[:, h * 64:(h + 1) * 64], True, True, (Gm, Xb), (py1,))
                    py2 = PS()
                    for h in range(8):
                        mm(py2[:, h * 64:(h + 1) * 64], BCb[:, 2 + h // 4, :], STb[:, h, :], True, True, (BCb, STb), (py2,))
                    tt("dve", v3(ytmp[:], 8), v3(py2[:], 8), SM(24, 8).unsqueeze(2).broadcast_to([128, 8, 64]), ALU.mult,
                       (py2, smB[10]), (ytmp,))
                    tt("dve", ysd[:], ytmp[:], py1[:], ALU.add, (ytmp, py1), (ysd,))
                    tt("dve", ysd[:], ysd[:], dx[:], ALU.add, (ysd, dx), (ysd,))
                    yield
                pst = PS()
                for h in range(8):
                    mm(pst[:, h * 64:(h + 1) * 64], Btm[:, h // 4, :], Xd[:, h * 64:(h + 1) * 64], True, True, (Btm, Xd), (pst,))
                tt("dve", STf[:], STf[:], SM(40, 8).unsqueeze(2).broadcast_to([128, 8, 64]), ALU.mult, (STf, smB[10]), (STf,))
                tt("dve", STf[:].rearrange("p h d -> p (h d)"), STf[:].rearrange("p h d -> p (h d)"), pst[:], ALU.add,
                   (STf, pst), (STf,))
                if i == B0T - 1:
                    ts("dve", STf[:], STf[:], flag[:], None, ALU.mult, None, (STf, flag), (STf,))
                cp("act", STb[:], STf[:], (STf,), (STb,))
                yield
                if full:
                    tt("dve", ysd[:], ysd[:], gz[p][:], ALU.mult, (ysd, gz[p]), (ysd,))
                    for g in range(2):
                        act(ytmp[:, g * 256:(g + 1) * 256], ysd[:, g * 256:(g + 1) * 256], AF.Square, (ysd,), (ytmp, smB[12]),
                            accum=SM(56 + g))
                    ts("dve", SM(58, 2), SM(56, 2), 1.0 / 256, 1e-6, ALU.mult, ALU.add, (smB[12],), (smB[13],))
                    rsqrt_small(SM(60, 2), SM(58, 2), 2, (smB[13],), (smB[14],))
                    tt("dve", v3(ysd[:], 2), v3(ysd[:], 2), SM(60, 2).unsqueeze(2).broadcast_to([128, 2, 256]), ALU.mult,
                       (ysd, smB[14]), (ysd,))
                    tt("dve", ytok[p][:, 0:512], ysd[:], R("ssdg", 512), ALU.mult, (ysd, rows), (ytok[p],))


            def rwkv_out(i):
                p = i % 2
                full = i >= B0T - 1
                act(lin[0:64, :], Pp[0:64, 12, :], AF.Tanh, (Pp,), (lin,))
                tt("dve", kkv[:], Pp[:, 4:8, :], par[:, PC["kk"]:PC["kk"] + 4].unsqueeze(2).broadcast_to([128, 4, 128]),
                   ALU.mult, (Pp, par), (kkv,))
                cp("act", vb[:], Pp[:, 8:12, :], (Pp,), (vb,))
                OP()
                plw = PS()
                pla = PS()
                for c in range(4):
                    mm(plw[:, c * 128:(c + 1) * 128], Wlora[0:64, c * 128:(c + 1) * 128], lin[0:64, :], True, True,
                       (Wlora, lin), (plw,))
                    mm(pla[:, c * 128:(c + 1) * 128], Wlora[64:128, c * 128:(c + 1) * 128], Pp[64:128, 12, :], True, True,
                       (Wlora, Pp), (pla,))
                act(kk2[:], kkv[:], AF.Square, (kkv,), (kk2,))
                pkn = PS()
                mm(pkn[:], C("onesblk"), kk2[:].rearrange("p a b -> p (a b)"), True, True, (cst, kk2), (pkn,))
                for c in range(4):
                    act(thw[:, c, :], plw[:, c * 128:(c + 1) * 128], AF.Tanh, (plw, hw0), (thw,), bias=hw0[:, c:c + 1], scale=0.5)
                    act(tha[:, c, :], pla[:, c * 128:(c + 1) * 128], AF.Tanh, (pla, hw0), (tha,), bias=hw0[:, 4 + c:5 + c], scale=0.5)
                act(inv[:].rearrange("p a b -> p (a b)"), pkn[:], AF.Ln, (pkn,), (inv,), bias=tiny[:])
                act(inv[:], inv[:], AF.Exp, (inv,), (inv,), scale=-0.5)
                ts("dve", lw[:], thw[:], -0.30326533, -0.30326533, ALU.mult, ALU.add, (thw,), (lw,))
                ts("dve", av[:], tha[:], 0.5, 0.5, ALU.mult, ALU.add, (tha,), (av,))
                tt("dve", kkn[:], kkv[:], inv[:], ALU.mult, (kkv, inv), (kkn,))
                FP()
                OP()
                for c in range(4):
                    S.op("dve", lambda e, c=c: e.tensor_tensor_scan(out=cum[:, c, :], data0=lw[:, c, :], data1=zeros[:],
                                                                      initial=0.0, op0=ALU.add, op1=ALU.add),
                         rd(lw, zeros), rd(cum))
                act(E1[:], cum[:], AF.Exp, (cum,), (E1,))
                act(E2[:], cum[:], AF.Exp, (cum,), (E2,), scale=-1.0)
                FP()
                OP()
                stt("dve", t1[:], av[:], -1.0, par[:, PC["ka"]:PC["ka"] + 4].unsqueeze(2).broadcast_to([128, 4, 128]),
                    ALU.add, ALU.mult, (av, par), (t1,))
                stt("dve", kmod[:], t1[:], 1.0, Pp[:, 4:8, :], ALU.add, ALU.mult, (t1, Pp), (kmod,))
                FP()
                tt("dve", beta[:], kkn[:], av[:], ALU.mult, (kkn, av), (beta,))
                for hp in range(2):
                    sl = slice(hp * 64, (hp + 1) * 64)
                    stt("dve", arz[hp][sl, :, 0, :], kkn[sl, :, :], -1.0, E3[sl, :, :], ALU.mult, ALU.mult, (kkn, E3), (arz[hp],))
                tt("dve", btl[:], beta[:], E2[:], ALU.mult, (beta, E2), (btl,))
                tt("dve", ktl[:], kmod[:], E2[:], ALU.mult, (kmod, E2), (ktl,))
                if full:
                    for hp in range(2):
                        sl = slice(hp * 64, (hp + 1) * 64)
                        tt("dve", arz[hp][sl, :, 1, :], Pp[sl, 0:4, :], E1[sl, :, :], ALU.mult, (Pp, E1), (arz[hp],))
                if full:
                    tt("dve", rk[:], Pp[:, 0:4, :], kmod[:], ALU.mult, (Pp, kmod), (rk,))
                if full:
                    pbn = PS()
                    for c in range(4):
                        mm(pbn[:, 0:8], rk[:, c, :], sel[:, c, :], c == 0, c == 3, (rk, sel), (pbn,))
                    cp("act", bnsb[:], pbn[:, 0:8], (pbn,), (bnsb,))
                for c in range(4):
                    trn(pTR[:, c * 128:(c + 1) * 128], vb[:, c, :], identb[:], (vb, identb), (pTR,))
                cp("act", vtm[:], pTR[:, 0:512], (pTR,), (vtm,))
                FP()
                drain(ogen)
                if full:
                    act(gsb[:, 0:128], Pp[:, 13, :], AF.Tanh, (Pp,), (gsb,), scale=0.5)
                    ts("dve", sg[:], gsb[:, 0:128], 1.0, None, ALU.add, None, (gsb,), (sg,))
                    pgt = PS()
                    mm(pgt[:], sg[:], G2h[:], True, True, (sg, G2h), (pgt,))
                    cp("act", gsb[:], pgt[:], (pgt,), (gsb,))
                if full:
                    tt("dve", bterm[:], v3(vtm[:], 8), bnsb[:].unsqueeze(2).broadcast_to([128, 8, 64]), ALU.mult,
                       (vtm, bnsb), (bterm,))
                for c in range(4):
                    pab = PS()
                    pak = PS()
                    pnt = PS()
                    ncl = 256 if full else 128
                    for hp in range(2):
                        arv = GAz[hp][:, c * 256:c * 256 + ncl]
                        mm(pab[:, hp * 256:hp * 256 + ncl], btl[:, c, :], arv, True, True, (btl, arz[hp]), (pab,))
                        mm(pak[:, hp * 256:hp * 256 + ncl], ktl[:, c, :], arv, True, True, (ktl, arz[hp]), (pak,))
                        mm(pnt[:, hp * 128:(hp + 1) * 128], arz[hp][:, c, 0, :], btl[:, c, :], True, True, (arz[hp], btl), (pnt,))
                    tt("dve", ABs[c][:, :, 0:ncl], v3(pab[:], 2)[:, :, 0:ncl],
                       maskSI[:, 0:ncl].unsqueeze(1).broadcast_to([128, 2, ncl]), ALU.mult, (pab, maskSI), (ABs[c],))
                    tt("dve", AKs[c][:, :, 0:ncl], v3(pak[:], 2)[:, :, 0:ncl],
                       maskSI[:, 0:ncl].unsqueeze(1).broadcast_to([128, 2, ncl]), ALU.mult, (pak, maskSI), (AKs[c],))
                    tt("dve", PTm[0][c][:], v3(pnt[:, 0:256], 2), C("trils").unsqueeze(1).broadcast_to([128, 2, 128]),
                       ALU.mult, (pnt, cst), (PTm[0][c],))
                    FP()
                    EP()
                    EP()
                for c in range(4):
                    trn(pTR[:, c * 128:(c + 1) * 128], btl[:, c, :], identb[:], (btl, identb), (pTR,))
                    trn(pTR[:, 512 + c * 128:512 + (c + 1) * 128], ktl[:, c, :], identb[:], (ktl, identb), (pTR,))
                cp("act", btm[:], pTR[:, 0:512], (pTR,), (btm,))
                cp("dve", ktm[:], pTR[:, 512:1024], (pTR,), (ktm,))
                pu = pool_banks[0]
                for h in range(8):
                    c, hp = h // 2, h % 2
                    mm(pu[:, h * 64:(h + 1) * 64], arz[hp][:, c, 0, :], STrb[:, c, :], True, False, (arz[hp], STrb), (pu,))
                    mm(pu[:, h * 64:(h + 1) * 64], AKs[c][:, hp, 0:128], vtm[:, h * 64:(h + 1) * 64], False, True,
                       (AKs[c], vtm), (pu,))
                cp("act", UTf[:], pu[:], (pu,), (UTf,))
                cp("dve", UT[:].rearrange("p h d -> p (h d)"), UTf[:], (UTf,), (UT,))
                FP()
                for lvl in range(7):
                    q = lvl % 2
                    pd = PS()
                    for h in range(8):
                        c, hp = h // 2, h % 2
                        Pl, PlB = (ABs[c][:, hp, 0:128], ABs[c]) if lvl == 0 else (Pm[q][c][:, hp, :], Pm[q][c])
                        mm(pd[:, h * 64:(h + 1) * 64], Pl, UT[:, h, :], True, True, (PlB, UT), (pd,))
                    tt("dve", UTf[:], UTf[:], pd[:], ALU.add, (UTf, pd), (UTf,))
                    cp("act", UT[:].rearrange("p h d -> p (h d)"), UTf[:], (UTf,), (UT,))
                    EP()
                    SP()
                    if lvl < 6:
                        for c in range(4):
                            pp = PS()
                            for hp in range(2):
                                Pl, PlB = (ABs[c][:, hp, 0:128], ABs[c]) if lvl == 0 else (Pm[q][c][:, hp, :], Pm[q][c])
                                mm(pp[:, hp * 128:(hp + 1) * 128], PTm[q][c][:, hp, :], Pl, True, True,
                                   (PTm[q][c], PlB), (pp,))
                                mm(pp[:, 256 + hp * 128:256 + (hp + 1) * 128], Pl, PTm[q][c][:, hp, :], True, True,
                                   (PTm[q][c], PlB), (pp,))
                            cp("dve" if c % 2 == 0 else "act", PPm[1 - q][c][:].rearrange("p a b c -> p (a b c)"), pp[:],
                               (pp,), (PPm[1 - q][c],))
                            if c == 1:
                                SP()
                psr = PS()
                for c in range(4):
                    o = psr[:, c * 128:(c + 1) * 128]
                    mm(o, btm[:, c * 128:(c + 1) * 128], UT[:, 2 * c:2 * c + 2, :].rearrange("p a b -> p (a b)"), True, False,
                       (btm, UT), (psr,))
                    mm(o, ktm[:, c * 128:(c + 1) * 128], vtm[:, c * 128:(c + 1) * 128], False, True, (ktm, vtm), (psr,))
                if full:
                    pyr = PS()
                    for h in range(8):
                        c, hp = h // 2, h % 2
                        o = pyr[:, h * 64:(h + 1) * 64]
                        mm(o, arz[hp][:, c, 1, :], STrb[:, c, :], True, False, (arz[hp], STrb), (pyr,))
                        mm(o, ABs[c][:, hp, 128:256], UT[:, h, :], False, False, (ABs[c], UT), (pyr,))
                        mm(o, AKs[c][:, hp, 128:256], vtm[:, h * 64:(h + 1) * 64], False, True, (AKs[c], vtm), (pyr,))
                psr3 = v3(psr[:], 4)
                tt("dve", tmpS[0:64, :, :], STr[0:64, :, :], psr3[0:64, :, 0:64], ALU.add, (STr, psr), (tmpS,))
                tt("dve", tmpS[64:128, :, :], STr[64:128, :, :], psr3[64:128, :, 64:128], ALU.add, (STr, psr), (tmpS,))
                tt("dve", STr[:], tmpS[:], E1[:, :, 127:128].broadcast_to([128, 4, 64]), ALU.mult, (tmpS, E1), (STr,))
                if i == B0T - 1:
                    ts("dve", STr[:], STr[:], flag[:], None, ALU.mult, None, (STr, flag), (STr,))
                cp("act", STrb[:], STr[:], (STr,), (STrb,))
                SP()
                if full:
                    cp("act", yr[:].rearrange("p h d -> p (h d)"), pyr[:], (pyr,), (yr,))
                    ogen[0] = out_gen(i)

            def out_gen(i):
                p = i % 2
                slot = i - (B0T - 1)
                act(ysq[:], yr[:], AF.Square, (yr,), (ysq,))
                S.op("dve", lambda e: e.tensor_reduce(out=gn[:, 0:8], in_=yr[:], axis=AX.X, op=ALU.add), rd(yr), rd(gn))
                S.op("dve", lambda e: e.tensor_reduce(out=gn[:, 8:16], in_=ysq[:], axis=AX.X, op=ALU.add), rd(ysq), rd(gn))
                ts("dve", gn[:, 0:16], gn[:, 0:16], 1.0 / 64, None, ALU.mult, None, (gn,), (gn,))
                tt("dve", gn[:, 16:24], gn[:, 0:8], gn[:, 0:8], ALU.mult, (gn,), (gn,))
                tt("dve", gn[:, 24:32], gn[:, 8:16], gn[:, 16:24], ALU.subtract, (gn,), (gn,))
                ts("dve", gn[:, 24:32], gn[:, 24:32], 64e-5, None, ALU.add, None, (gn,), (gn,))
                rsqrt_small(gn[:, 32:40], gn[:, 24:32], 8, (gn,), (gn,))
                tt("dve", yr[:], yr[:], gn[:, 0:8].unsqueeze(2).broadcast_to([128, 8, 64]), ALU.subtract, (yr, gn), (yr,))
                tt("dve", yr[:], yr[:], gn[:, 32:40].unsqueeze(2).broadcast_to([128, 8, 64]), ALU.mult, (yr, gn), (yr,))
                yr2 = yr[:].rearrange("p h d -> p (h d)")
                tt("dve", yr2, yr2, R("lnw", 512), ALU.mult, (yr, rows), (yr,))
                tt("dve", yr2, yr2, R("lnb", 512), ALU.add, (yr, rows), (yr,))
                tt("dve", yr[:], yr[:], bterm[:], ALU.add, (yr, bterm), (yr,))
                tt("dve", ytok[p][:, 512:1024], yr2, gsb[:], ALU.mult, (yr, gsb), (ytok[p],))
                yield
                for c in range(8):
                    trn(pTR[:, c * 128:(c + 1) * 128], ytok[p][:, c * 128:(c + 1) * 128], identb[:], (ytok[p], identb), (pTR,))
                cp("act", mixT[:].rearrange("p a b -> p (a b)"), pTR[:], (pTR,), (mixT,))
                yield
                pm = [PS(), PS()]
                for hf in range(2):
                    for c in range(8):
                        mm(pm[hf][:], mixT[:, c, :], Wout[:, c, hf * 512:(hf + 1) * 512], c == 0, c == 7, (mixT, Wout), (pm[hf],))
                dma("sp", ht[:], D["x_in"][i * TT:(i + 1) * TT, :], "xr", (), (ht,))
                for hf in range(2):
                    act(B7[:, hf * 512:(hf + 1) * 512], pm[hf][:], AF.Square, (pm[hf],), (B7, hsm), accum=hs[:, hf:hf + 1])
                tt("dve", hs[:, 2:3], hs[:, 0:1], hs[:, 1:2], ALU.add, (hsm,), (hsm,))
                ts("dve", hs[:, 2:3], hs[:, 2:3], 1.0 / 1024, 1e-6, ALU.mult, ALU.add, (hsm,), (hsm,))
                rsqrt_small(hs[:, 3:4], hs[:, 2:3], 1, (hsm,), (hsm,))
                for hf in range(2):
                    stt("dve", B7[:, hf * 512:(hf + 1) * 512], pm[hf][:], hs[:, 3:4], R("pmn", 1024)[:, hf * 512:(hf + 1) * 512],
                        ALU.mult, ALU.mult, (pm[hf], hsm, rows), (B7,))
                tt("dve", ht[:], ht[:], B7[:], ALU.add, (ht, B7), (ht,))
                dma("sp", hscr[slot * TT:(slot + 1) * TT, :], ht[:], "hst", (ht,), (hscrB,))
                yield

            ogen = [None]

            def OP():
                if ogen[0] is not None:
                    next(ogen[0], None)

            fgen = [None]
            egen = [None]
            sgen = [None]

            def FP():
                if fgen[0] is not None:
                    next(fgen[0], None)

            def need_banks(n):
                while fstate[0] < n:
                    if fgen[0] is None or next(fgen[0], "done") == "done":
                        break

            def need_front_done():
                if fgen[0] is not None:
                    for _ in fgen[0]:
                        pass

            def EP():
                if egen[0] is not None:
                    next(egen[0], None)

            def need_conv():
                while estate[0] < 8:
                    if egen[0] is None or next(egen[0], "done") == "done":
                        break

            def SP():
                if sgen[0] is not None:
                    next(sgen[0], None)

            def drain(g):
                if g[0] is not None:
                    for _ in g[0]:
                        pass
                g[0] = None

            fgen[0] = front_gen(0)
            drain(fgen)
            fstate[0] = 8
            egen[0] = early_gen(0)
            drain(egen)
            estate[0] = 8
            sgen[0] = ssd_gen(0)
            drain(sgen)
            CK(2)
            for i in range(NT_ALL):
                if i + 1 < NT_ALL:
                    fgen[0] = front_gen(i + 1)
                    egen[0] = early_gen(i + 1)
                    sgen[0] = ssd_gen(i + 1)
                FP()
                FP()
                rwkv_out(i)
                drain(fgen)
                drain(egen)
                drain(sgen)
                CK(5)
            drain(ogen)
            S.barrier()
            CK(6)

        rot_set[0] = [4, 5]
        TBK = 256
        NTB = NT_MAIN * TT // TBK
        with ExitStack() as stB:
            Wup = sb("Wup", [128, 8, 5632], BF16, stB)
            Wdown = sb("Wdown", [128, 22, 1024], BF16, stB)
            pfn = sb("pfn", [128, 1024], F32, stB)
            dma("sp", pfn[:], D["rows"][:, RC["pfn"]:RC["pfn"] + 1024], "c6", (), (pfn,))
            WupB = [[Buf("wup%d_%d" % (k, hf)) for hf in range(2)] for k in range(8)]
            WdB = [Buf("wd%d" % f3) for f3 in range(8)]
            with ExitStack() as stS:
                stage = [sb("stageB%d" % q, [128, 3072], F32, stS) for q in range(3)]
                sq = [0]

                def load_rows2(src_ap, ncols, fn):
                    si = sq[0] % 3
                    stg = stage[si]
                    sq[0] += 1
                    dma("sp", stg[:, 0:ncols], src_ap, "wsu%d" % si, (), (stg,))
                    fn(stg, sq[0])

                def cast_up(stg, n, k, hf):
                    o = Wup[:, k, hf * 2816:(hf + 1) * 2816]
                    e = n % 2
                    if e == 0:
                        ts("dve", o, stg[:, 0:2816], P("g2n", k), None, ALU.mult, None, (stg, par), (WupB[k][hf],))
                    else:
                        act(o, stg[:, 0:2816], AF.Copy, (stg, par), (WupB[k][hf],), scale=P("g2n", k))
                for k in range(8):
                    for hf in range(2):
                        load_rows2(D["wup"][:, k * 5632 + hf * 2816:k * 5632 + (hf + 1) * 2816], 2816,
                                   lambda stg, n, k=k, hf=hf: cast_up(stg, n, k, hf))
                for f3 in range(8):
                    n3 = min(3, 22 - 3 * f3)
                    load_rows2(D["wdown"][:, 3 * f3 * 1024:(3 * f3 + n3) * 1024], n3 * 1024,
                               lambda stg, n, f3=f3, n3=n3: cp(("act", "dve")[n % 2], Wdown[:, 3 * f3:3 * f3 + n3, :],
                                                               v3(stg[:, 0:n3 * 1024], n3), (stg,), (WdB[f3],)))
                S.barrier()
            CK(7)
            hld = [sb("hld%d" % q, [128, 2, 1024], F32, stB) for q in range(2)]
            hbB = sb("hbB", [128, 1024], BF16, stB)
            hnB = [sb("hnB%d" % q, [128, 8, 2 + TBK], BF16, stB) for q in range(2)]
            U = [sb("U%d" % q, [128, 2, 2, 2 + TBK], F32, stB) for q in range(2)]
            cacc = [[sb("cacc%d_%d" % (q, gv), [128, 2, TBK], F32, stB) for gv in range(2)] for q in range(2)]
            sgt = [sb("sgt%d" % q, [128, 2, TBK], F32, stB) for q in range(2)]
            actT = [sb("actT%d" % q, [128, 2, TBK], BF16, stB) for q in range(2)]
            carry = sb("carry", [128, 22, 2, 2], F32, stB)
            carryB = [Buf("carry%d" % f) for f in range(22)]
            fo = [sb("fo%d" % q, [128, 1024], F32, stB) for q in range(2)]
            fs = sb("fs", [128, 16], F32, stB)
            junkF = sb("junkF", [128, 1024], BF16, stB)
            fsm = Buf("fsm")
            outB = Buf("out")
            last = []
            memset("pool", carry[:], 0.0, carryB)

            def prep_sub(row0, q, sub, dst, dcol):
                dma("sp", hld[q][:, sub, :], hscr[row0:row0 + TT, :], "hld%d" % q, (hscrB,), (hld[q],))
                act(hbB[:], hld[q][:, sub, :], AF.Square, (hld[q],), (hbB, fsm), accum=fs[:, 8:9])
                ts("dve", fs[:, 9:10], fs[:, 8:9], 1.0 / 1024, 1e-6, ALU.mult, ALU.add, (fsm,), (fsm,))
                rsqrt_small(fs[:, 10:11], fs[:, 9:10], 1, (fsm,), (fsm,))
                act(hbB[:], hld[q][:, sub, :], AF.Copy, (hld[q], fsm), (hbB,), scale=fs[:, 10:11])
                for k in range(8):
                    trn(pT[:, k * 128:(k + 1) * 128], hbB[:, k * 128:(k + 1) * 128], identb[:], (hbB, identb), (pT,))
                cp("dve", dst[:, :, dcol:dcol + TT], v3(pT[:], 8), (pT,), (dst,))

            prep_sub(0, 1, 0, hnB[1], 2)
            ts("pool", hnB[1][:, :, 128:130], hnB[1][:, :, 128:130], flag[:], None, ALU.mult, None, (hnB[1], flag), (hnB[1],))
            for f in range(22):
                pc = PS()
                for gv in range(2):
                    col = gv * 2816 + f * 128
                    for k in range(8):
                        mm(pc[:, gv * 2:gv * 2 + 2], Wup[:, k, col:col + 128], hnB[1][:, k, 128:130], k == 0, k == 7,
                           (WupB[k][gv], hnB[1]), (pc,))
                cp("act", carry[:, f, :, :].rearrange("p a b -> p (a b)"), pc[:, 0:4], (pc,), (carryB[f // 2],))

            def prep_gen(tb):
                for sub in range(2):
                    prep_sub(TT + tb * TBK + sub * TT, tb % 2, sub, hnB[tb % 2], 2 + sub * TT)
                    yield

            for _ in prep_gen(0):
                pass
            for tb in range(NTB):
                q2 = tb % 2
                pgen = prep_gen(tb + 1) if tb + 1 < NTB else None
                pf = [[pool_banks[0], pool_banks[1]], [pool_banks[2], pool_banks[3]]]
                pub = [pool_banks[4], pool_banks[5]]

                def A_pe(fp):
                    for j in range(2):
                        f = 2 * fp + j
                        for gv in range(2):
                            col = gv * 2816 + f * 128
                            for k in range(8):
                                mm(pub[j][:, gv * TBK:(gv + 1) * TBK], Wup[:, k, col:col + 128], hnB[q2][:, k, 2:2 + TBK], k == 0, k == 7,
                                   (WupB[k][gv], hnB[q2]), (pub[j],))

                def A_rest(fp):
                    q = fp % 2
                    Uq = U[q]
                    for j in range(2):
                        cp("act", Uq[:, j, :, 2:2 + TBK], v3(pub[j][:], 2), (pub[j],), (Uq,))
                    f0 = 2 * fp
                    cp("dve", Uq[:, :, :, 0:2], carry[:, f0:f0 + 2, :, :], (carryB[fp],), (Uq,))
                    cp("dve", carry[:, f0:f0 + 2, :, :], Uq[:, :, :, TBK:TBK + 2], (Uq,), (carryB[fp],))
                    ca = cacc[q]
                    for j in range(2):
                        for gv in range(2):
                            ch = gv * 22 + f0 + j
                            act(ca[gv][:, j, :], Uq[:, j, gv, 2:2 + TBK], AF.Identity, (Uq, par), (ca[gv],), bias=P("fcb", ch),
                                scale=P("fcw", ch * 3 + 2))

                def B_nonpe(fp):
                    q = fp % 2
                    Uq = U[q]
                    ca = cacc[q]
                    f0 = 2 * fp
                    for k in range(2):
                        for j in range(2):
                            for gv in range(2):
                                ch = gv * 22 + f0 + j
                                stt("dve", ca[gv][:, j, :], Uq[:, j, gv, k:k + TBK], P("fcw", ch * 3 + k), ca[gv][:, j, :], ALU.mult, ALU.add,
                                    (Uq, par, ca[gv]), (ca[gv],))
                    act(sgt[q][:], ca[0][:], AF.Silu, (ca[0],), (sgt[q],))
                    tt("dve", actT[q][:], sgt[q][:], ca[1][:], ALU.mult, (sgt[q], ca[1]), (actT[q],))

                def B_pe(fp):
                    q = fp % 2
                    a3 = actT[q]
                    for j in range(2):
                        f = 2 * fp + j
                        for sub in range(2):
                            for hf in range(2):
                                mm(pf[sub][hf][:], a3[:, j, sub * TT:(sub + 1) * TT], Wdown[:, f, hf * 512:(hf + 1) * 512], f == 0, f == 21,
                                   (a3, WdB[f // 3]), (pf[sub][hf],))

                A_pe(0)
                A_rest(0)
                for fp in range(11):
                    if fp + 1 < 11:
                        A_pe(fp + 1)
                    B_nonpe(fp)
                    B_pe(fp)
                    if fp + 1 < 11:
                        A_rest(fp + 1)
                    if pgen is not None and fp in (3, 7):
                        next(pgen, None)
                for sub in range(2):
                    for hf in range(2):
                        cp("act", fo[sub][:, hf * 512:(hf + 1) * 512], pf[sub][hf][:], (pf[sub][hf],), (fo[sub],))
                if pgen is not None:
                    for _ in pgen:
                        pass
                for sub in range(2):
                    fq = fo[sub]
                    o4 = 4 * sub
                    act(junkF[:], fq[:], AF.Square, (fq,), (junkF, fsm), accum=fs[:, o4:o4 + 1])
                    ts("dve", fs[:, o4 + 2:o4 + 3], fs[:, o4:o4 + 1], 1.0 / 1024, 1e-6, ALU.mult, ALU.add, (fsm,), (fsm,))
                    rsqrt_small(fs[:, o4 + 3:o4 + 4], fs[:, o4 + 2:o4 + 3], 1, (fsm,), (fsm,))
                    stt("dve", fq[:], fq[:], fs[:, o4 + 3:o4 + 4], pfn[:], ALU.mult, ALU.mult, (fq, fsm, pfn), (fq,))
                    tt("dve", fq[:], fq[:], hld[q2][:, sub, :], ALU.add, (fq, hld[q2]), (fq,))
                    r0 = tb * TBK + sub * TT
                    last.append(dma("sp", out[r0:r0 + TT, :], fq[:], "ost%d" % sub, (fq,), (outB,)))
            S.finish(last[-2:])
    except _Stop:
        pass
    return nc


_NC_CACHE = {}


def kernel(**inputs):
    inputs = {k: np.asarray(v) for k, v in inputs.items()}
    x = inputs["x"].astype(np.float32, copy=False)
    shared = _host_prep(inputs)
    in_maps = []
    for c in range(8):
        b, s = c // 2, c % 2
        m = dict(shared)
        if s == 0:
            m["x_in"] = np.ascontiguousarray(np.concatenate([x[b, 0:2048], x[b, 0:2048]], axis=0))
        else:
            m["x_in"] = np.ascontiguousarray(x[b])
        m["flag"] = np.full((128, 1), float(s), np.float32)
        in_maps.append(m)
    if "nc" not in _NC_CACHE:
        _NC_CACHE["nc"] = build_nc()
    res = run_bass_kernel_spmd(_NC_CACHE["nc"], in_maps, core_ids=list(range(8)))
    outp = np.empty((4, 4096, 1024), np.float32)
    for c in range(8):
        b, s = c // 2, c % 2
        outp[b, s * 2048:(s + 1) * 2048] = res.results[c]["out"]
    return outp
```

```python
from contextlib import ExitStack
import numpy as np
import concourse.bass as bass
import concourse.mybir as mybir
from concourse.bass_utils import run_bass_kernel_spmd

F32 = mybir.dt.float32
BF16 = mybir.dt.bfloat16
AF = mybir.ActivationFunctionType
ALU = mybir.AluOpType
AX = mybir.AxisListType

SAME_ENGINE_SYNC = True
NT_ALL = 32
NT_MAIN = 16
TT = 128
NFM = 22
HW = 131


class Buf:
    __slots__ = ("name", "w", "r")

    def __init__(self, name):
        self.name = name
        self.w = None
        self.r = []


class Sched:
    ENGS = ("pe", "act", "dve", "pool", "sp")

    def __init__(self, nc, stack):
        self.nc = nc
        self.stack = stack
        self.sems = {}
        self.cnt = {}
        for e in self.ENGS:
            self.sems[e] = stack.enter_context(nc.semaphore("s_" + e))
            self.cnt[e] = 0
        self.known = {e: {} for e in self.ENGS}
        self.snap = {}
        self.prog = {e: [] for e in self.ENGS}
        self.dma_sems = {}
        self.dma_cnt = {}

    def _sem_handle(self, k):
        return self.sems[k] if k in self.sems else self.dma_sems[k]

    def _need(self, eng, tok, acc):
        if tok is None:
            return
        semkey, val, src = tok
        if src == eng and (eng == "pe" or not SAME_ENGINE_SYNC):
            return
        if self.known[eng].get(semkey, 0) >= val:
            return
        if acc.get(semkey, 0) < val:
            acc[semkey] = val

    def _emit_waits(self, eng, acc):
        for semkey, val in acc.items():
            h = self._sem_handle(semkey)
            self.prog[eng].append(lambda e, h=h, val=val: e.wait_ge(h, val))
            k = self.known[eng]
            if k.get(semkey, 0) < val:
                k[semkey] = val
            if semkey in self.sems:
                sn = self.snap.get((semkey, val))
                if sn:
                    for kk, vv in sn.items():
                        if k.get(kk, 0) < vv:
                            k[kk] = vv

    def _deps(self, eng, reads, writes):
        acc = {}
        for b in reads:
            self._need(eng, b.w, acc)
        for b in writes:
            self._need(eng, b.w, acc)
            for t in b.r:
                self._need(eng, t, acc)
        self._emit_waits(eng, acc)

    def _mark(self, tok, reads, writes):
        for b in reads:
            b.r.append(tok)
            if len(b.r) > 24:
                b.r = b.r[-24:] if False else b.r
        for b in writes:
            b.w = tok
            b.r = []

    def op(self, eng, fn, reads=(), writes=()):
        self._deps(eng, reads, writes)
        self.cnt[eng] += 1
        n = self.cnt[eng]
        h = self.sems[eng]
        self.prog[eng].append(lambda e, fn=fn, h=h: fn(e).then_inc(h, 1))
        tok = (eng, n, eng)
        sn = dict(self.known[eng])
        sn[eng] = n
        self.snap[(eng, n)] = sn
        self._mark(tok, reads, writes)
        return tok

    def dma(self, queue, fn, semkey, reads=(), writes=()):
        if semkey not in self.dma_sems:
            self.dma_sems[semkey] = self.stack.enter_context(self.nc.semaphore("d_" + semkey))
            self.dma_cnt[semkey] = 0
        self._deps(queue, reads, writes)
        self.dma_cnt[semkey] += 16
        val = self.dma_cnt[semkey]
        h = self.dma_sems[semkey]
        self.prog[queue].append(lambda e, fn=fn, h=h: fn(e).then_inc(h, 16))
        tok = (semkey, val, "dma")
        self._mark(tok, reads, writes)
        return tok

    def barrier(self):
        toks = [(e, self.cnt[e], e) for e in self.ENGS if self.cnt[e] > 0]
        toks += [(k, v, "dma") for k, v in self.dma_cnt.items() if v > 0]
        for e in self.ENGS:
            acc = {}
            for t in toks:
                if t[0] == e:
                    continue
                self._need(e, t, acc)
            self._emit_waits(e, acc)

    def finish(self, final_toks):
        for t in final_toks:
            acc = {}
            self._need("sp", t, acc)
            self._emit_waits("sp", acc)
        with self.nc.Block() as block:
            @block.tensor
            def _(e):
                for f in self.prog["pe"]:
                    f(e)

            @block.scalar
            def _(e):
                for f in self.prog["act"]:
                    f(e)

            @block.vector
            def _(e):
                for f in self.prog["dve"]:
                    f(e)

            @block.gpsimd
            def _(e):
                for f in self.prog["pool"]:
                    f(e)

            @block.sync
            def _(e):
                for f in self.prog["sp"]:
                    f(e)


PC = {}
_o = 0
for _n, _w in (("cw", 32), ("cb", 8), ("mu", 14), ("w0", 4), ("a0", 4), ("kk", 4), ("ka", 4),
               ("fcw", 132), ("fcb", 44), ("g1", 8), ("g2n", 8)):
    PC[_n] = _o
    _o += _w
NPAR = _o
RC = {}
_o = 0
for _n, _w in (("dtb", 8), ("alog", 8), ("dsk", 8), ("ssdg", 512), ("lnw", 512), ("lnb", 512),
               ("pmn", 1024), ("pfn", 1024)):
    RC[_n] = _o
    _o += _w
NROW = _o
CC = {"ident": 0, "triu": 128, "trius": 256, "trils": 384, "onesblk": 512, "ones": 640}
NCONST = 768


def _cols(v, nch):
    return np.ascontiguousarray(v.reshape(nch, 128).T)


def _host_prep(inp):
    f = np.float32
    w_in = inp["w_in"][0]
    s0 = 0
    r0 = 1544
    perm = np.concatenate([
        np.arange(512, 1536),
        r0 + np.arange(0, 512),
        r0 + np.arange(576, 1088),
        r0 + np.arange(1088, 1600),
        r0 + np.arange(512, 576),
        r0 + np.arange(1600, 1664),
        r0 + np.arange(1664, 1792),
        np.arange(0, 512),
        np.arange(1536, 1544),
    ])
    win = np.ascontiguousarray(w_in[:, perm].reshape(8, 128, 3336).transpose(1, 0, 2)).reshape(128, 8 * 3336)
    wout = np.ascontiguousarray(inp["w_out"][0].reshape(8, 128, 1024).transpose(1, 0, 2)).reshape(128, 8 * 1024)
    wup = np.ascontiguousarray(inp["ffn_w_up"][0].reshape(8, 128, 5632).transpose(1, 0, 2)).reshape(128, 8 * 5632)
    wdown = np.ascontiguousarray(inp["ffn_w_down"][0].reshape(22, 128, 1024).transpose(1, 0, 2)).reshape(128, 22 * 1024)
    par = np.zeros((128, NPAR), f)
    cw = inp["ssd_conv_w"][0]
    par[:, PC["cw"]:PC["cw"] + 32] = cw.reshape(4, 8, 128).transpose(2, 1, 0).reshape(128, 32)
    par[:, PC["cb"]:PC["cb"] + 8] = _cols(inp["ssd_conv_b"][0], 8)
    mu = inp["rwkv_mu"][0]
    mup = np.concatenate([mu[0:512], mu[576:1088], mu[1088:1600], mu[512:576], mu[1600:1664], mu[1664:1792]])
    par[:, PC["mu"]:PC["mu"] + 14] = _cols(mup, 14)
    par[:, PC["w0"]:PC["w0"] + 4] = _cols(inp["rwkv_w0"][0], 4)
    par[:, PC["a0"]:PC["a0"] + 4] = _cols(inp["rwkv_a0"][0], 4)
    par[:, PC["kk"]:PC["kk"] + 4] = _cols(inp["rwkv_k_k"][0], 4)
    par[:, PC["ka"]:PC["ka"] + 4] = _cols(inp["rwkv_k_a"][0], 4)
    fw = inp["ffn_conv_w"][0]
    par[:, PC["fcw"]:PC["fcw"] + 132] = fw.reshape(3, 44, 128).transpose(2, 1, 0).reshape(128, 132)
    par[:, PC["fcb"]:PC["fcb"] + 44] = _cols(inp["ffn_conv_b"][0], 44)
    par[:, PC["g1"]:PC["g1"] + 8] = _cols(inp["pre_mix_norm"][0], 8)
    par[:, PC["g2n"]:PC["g2n"] + 8] = _cols(inp["pre_ffn_norm"][0], 8)
    rows = np.zeros((128, NROW), f)

    def put(name, v):
        rows[:, RC[name]:RC[name] + v.size] = np.broadcast_to(v.reshape(1, -1), (128, v.size))
    put("dtb", inp["ssd_dt_bias"][0])
    put("alog", inp["ssd_a_log"][0])
    put("dsk", inp["ssd_d"][0])
    put("ssdg", inp["ssd_norm"][0])
    put("lnw", inp["rwkv_ln_w"][0])
    put("lnb", inp["rwkv_ln_b"][0])
    put("pmn", inp["post_mix_norm"][0])
    put("pfn", inp["post_ffn_norm"][0])
    consts = np.zeros((128, NCONST), f)
    i = np.arange(128)
    consts[:, 0:128] = np.eye(128)
    consts[:, 128:256] = (i[:, None] <= i[None, :])
    consts[:, 256:384] = (i[:, None] < i[None, :])
    consts[:, 384:512] = (i[:, None] > i[None, :])
    consts[:, 512:640] = ((i[:, None] // 64) == (i[None, :] // 64))
    consts[:, 640:768] = 1.0
    wlora = np.concatenate([inp["rwkv_w2"][0], inp["rwkv_a2"][0]], axis=0).astype(f)
    g2 = np.ascontiguousarray(inp["rwkv_g2"][0]).astype(f)
    rk = inp["rwkv_r_k"][0]
    sel = np.zeros((128, 4, 8), f)
    for c in range(4):
        for hp in range(2):
            sel[hp * 64:(hp + 1) * 64, c, 2 * c + hp] = rk[2 * c + hp]
    shared = {"win": win, "wout": wout, "wup": wup, "wdown": wdown, "params": par, "rows": rows,
              "consts": consts, "wlora": wlora, "g2": g2, "sel": sel.reshape(128, 32)}
    return {k: np.ascontiguousarray(v, dtype=f) for k, v in shared.items()}


NSLOT = NT_MAIN + 1
DBG = 0


class _Stop(Exception):
    pass


def build_nc():
    nc = bass.Bass("TRN2", target_bir_lowering=False)
    D = {}
    for name, shape in (("x_in", [4096, 1024]), ("win", [128, 8 * 3336]), ("wout", [128, 8192]),
                        ("wup", [128, 8 * 5632]), ("wdown", [128, 22 * 1024]), ("params", [128, NPAR]),
                        ("rows", [128, NROW]), ("consts", [128, NCONST]), ("wlora", [128, 512]),
                        ("g2", [128, 512]), ("sel", [128, 32]), ("flag", [128, 1])):
        D[name] = nc.dram_tensor(name, shape, F32, kind="ExternalInput").ap()
    out = nc.dram_tensor("out", [NT_MAIN * TT, 1024], F32, kind="ExternalOutput").ap()
    hscr = nc.dram_tensor("hscr", [NSLOT * TT, 1024], F32, kind="Internal").ap()

    try:
      with ExitStack() as st:
        S = Sched(nc, st)
        dbgt = st.enter_context(nc.sbuf_tensor("sb_dbgt", [128, 64], F32))

        def CK(n, src=None):
            if DBG == n:
                if src is None:
                    S.op("dve", lambda e: e.memset(dbgt[:], 1.0), (), ())
                    tk = S.dma("sp", lambda e: e.dma_start(out=out[0:128, 0:64], in_=dbgt[:]), "dbg", (), ())
                else:
                    ap, bufs, ncol = src
                    tk = S.dma("sp", lambda e: e.dma_start(out=out[0:128, 0:ncol], in_=ap), "dbg", [b.b for b in bufs], ())
                S.finish([tk])
                raise _Stop()

        class TB:
            def __init__(self, t, name, b=None):
                self.t = t
                self.b = b if b is not None else Buf(name)

            def __getitem__(self, k):
                return self.t[k]

        class View:
            def __init__(self, ap, b):
                self.ap = ap
                self.b = b

            def __getitem__(self, k):
                return self.ap[k]

        def sb(name, shape, dt, stack=st):
            return TB(stack.enter_context(nc.sbuf_tensor("sb_" + name, shape, dt)), name)

        def psum(name, shape, dt):
            t = TB(st.enter_context(nc.psum_tensor("ps_" + name, shape, dt)), name)
            PSB.add(id(t.b))
            return t

        PSB = set()

        def rd(*xs):
            return [x.b if hasattr(x, "b") else x for x in xs]

        def _xrw(r, w):
            r = rd(*r)
            w = rd(*w)
            rr = [x for x in r if id(x) not in PSB]
            ww = list(w) + [x for x in r if id(x) in PSB and x not in w]
            return rr, ww

        def tt(eng, o, a, b, op, r, w):
            S.op(eng, lambda e: e.tensor_tensor(out=o, in0=a, in1=b, op=op), *_xrw(r, w))

        def ts(eng, o, a, s1, s2, op0, op1, r, w):
            if s2 is None:
                S.op(eng, lambda e: e.tensor_scalar(out=o, in0=a, scalar1=s1, scalar2=None, op0=op0), *_xrw(r, w))
            else:
                S.op(eng, lambda e: e.tensor_scalar(out=o, in0=a, scalar1=s1, scalar2=s2, op0=op0, op1=op1), *_xrw(r, w))

        def stt(eng, o, a, s, b, op0, op1, r, w):
            S.op(eng, lambda e: e.scalar_tensor_tensor(out=o, in0=a, scalar=s, in1=b, op0=op0, op1=op1), *_xrw(r, w))

        def act(o, a, func, r, w, bias=None, scale=None, accum=None):
            kw = {}
            if bias is not None:
                kw["bias"] = bias
            if scale is not None:
                kw["scale"] = scale
            if accum is not None:
                kw["accum_out"] = accum
            S.op("act", lambda e: e.activation(out=o, in_=a, func=func, **kw), *_xrw(r, w))

        def cp(eng, o, a, r, w):
            if eng == "act":
                act(o, a, AF.Copy, r, w)
            else:
                S.op(eng, lambda e: e.tensor_copy(out=o, in_=a), *_xrw(r, w))

        def mm(o, l, rh, start, stop, r, w):
            S.op("pe", lambda e: e.matmul(out=o, lhsT=l, rhs=rh, start=start, stop=stop, skip_group_check=True), *_xrw(r, w))

        def trn(o, a, ident, r, w):
            S.op("pe", lambda e: e.transpose(out=o, in_=a, identity=ident), *_xrw(r, w))

        def memset(eng, o, v, w):
            S.op(eng, lambda e: e.memset(o, v), (), rd(*w))

        def dma(q, o, a, key, r, w):
            return S.dma(q, lambda e: e.dma_start(out=o, in_=a), key, *_xrw(r, w))

        def v3(ap, a):
            return ap.rearrange("p (a b) -> p a b", a=a)

        par = sb("par", [128, NPAR], F32)
        cst = sb("cst", [128, NCONST], F32)
        identb = sb("identb", [128, 128], BF16)
        flag = sb("flag", [128, 1], F32)
        neghalf = sb("neghalf", [128, 512], F32)
        zeros = sb("zeros", [128, 128], F32)
        pT = psum("pT", [128, 1024], BF16)
        pTR = psum("pTR", [128, 1024], BF16)
        pool_banks = [psum("pb%d" % i, [128, 512], F32) for i in range(6)]
        rot = [0]
        rot_set = [[1, 2, 3, 4, 5]]

        def PS():
            rs = rot_set[0]
            b = pool_banks[rs[rot[0] % len(rs)]]
            rot[0] += 1
            return b

        def C(name, n=128):
            return cst[:, CC[name]:CC[name] + n]

        def P(name, c):
            return par[:, PC[name] + c:PC[name] + c + 1]

        dma("sp", par[:], D["params"], "c0", (), (par,))
        dma("sp", cst[:], D["consts"], "c2", (), (cst,))
        dma("sp", flag[:], D["flag"], "c3", (), (flag,))
        cp("dve", identb[:], C("ident"), (cst,), (identb,))
        memset("pool", neghalf[:], -0.5, (neghalf,))
        memset("pool", zeros[:], 0.0, (zeros,))
        ts("dve", par[:, PC["cw"]:PC["cw"] + 40], par[:, PC["cw"]:PC["cw"] + 40], 0.5, None, ALU.mult, None, (par,), (par,))

        def rsqrt_small(o, a, n, r, w):
            tt("pool", o, a, neghalf[:, 0:n], ALU.pow, list(r) + [neghalf], w)

        hscrB = Buf("hscr")

        with ExitStack() as stA:
            rows = sb("rows", [128, 2584], F32, stA)

            def R(name, n):
                return rows[:, RC[name]:RC[name] + n]

            dma("sp", rows[:], D["rows"][:, 0:2584], "c1", (), (rows,))
            Win = sb("Win", [128, 8, 3336], BF16, stA)
            Wout = sb("Wout", [128, 8, 1024], BF16, stA)
            Wlora = sb("Wlora", [128, 512], F32, stA)
            G2h = sb("G2h", [128, 512], BF16, stA)
            sel = sb("sel", [128, 4, 8], F32, stA)
            arow = sb("arow", [128, 8], F32, stA)
            with ExitStack() as stS:
                stage = [sb("stageA%d" % q, [128, 3336], F32, stS) for q in range(2)]
                sq = [0]

                def load_rows(src_ap, ncols, fn):
                    stg = stage[sq[0] % 2]
                    sq[0] += 1
                    dma("sp", stg[:, 0:ncols], src_ap, "wst%d" % (sq[0] % 2), (), (stg,))
                    fn(stg)
                for k in range(8):
                    load_rows(D["win"][:, k * 3336:(k + 1) * 3336], 3336,
                              lambda stg, k=k: (ts("dve", Win[:, k, :], stg[:, 0:3336], P("g1", k), None,
                                                   ALU.mult, None, (stg, par), (Win,)) if k % 2 == 0 else
                                                act(Win[:, k, :], stg[:, 0:3336], AF.Copy, (stg, par), (Win,), scale=P("g1", k))))
                ts("pool", Win[:, :, 2816:3328], Win[:, :, 2816:3328], 0.5, None, ALU.mult, None, (Win,), (Win,))
                for k in range(8):
                    load_rows(D["wout"][:, k * 1024:(k + 1) * 1024], 1024,
                              lambda stg, k=k: cp("act", Wout[:, k, :], stg[:, 0:1024], (stg,), (Wout,)))
                load_rows(D["g2"], 512, lambda stg: ts("dve", G2h[:], stg[:, 0:512], 0.5, None, ALU.mult, None, (stg,), (G2h,)))
                dma("sp", Wlora[:], D["wlora"], "c4", (), (Wlora,))
                dma("sp", sel[:].rearrange("p a b -> p (a b)"), D["sel"], "c5", (), (sel,))
                act(arow[:], R("alog", 8), AF.Exp, (rows,), (arow,))
                ts("dve", arow[:], arow[:], -1.0, None, ALU.mult, None, (arow,), (arow,))
                S.barrier()
            CK(1)

            xt1 = sb("xt", [128, 1024], F32, stA)
            xt = [xt1, xt1]
            xT = [sb("xT%d" % i, [128, 8, HW], BF16, stA) for i in range(2)]
            raw = sb("raw", [128, NFM, HW], F32, stA)
            rawB = [Buf("raw_%d" % j) for j in range(8)]
            gz = [sb("gz%d" % i, [128, 512], F32, stA) for i in range(2)]
            dtr = [sb("dtr%d" % i, [128, 8], F32, stA) for i in range(2)]
            xb = sb("xb", [128, 1024], BF16, stA)
            hb = xb
            sm = sb("sm", [128, 64], F32, stA)
            smB = [Buf("sm%d" % i) for i in range(16)]
            Pp = sb("Pp", [128, 14, 128], F32, stA)
            BB = sb("BB", [128, 2048], F32, stA)
            AC = sb("AC", [128, 1024], F32, stA)
            B2 = sb("B2", [128, 1024], F32, stA)
            B3 = sb("B3", [128, 1024], F32, stA)
            B4 = sb("B4", [128, 512], F32, stA)
            B56 = sb("B56", [128, 2048], F32, stA)
            B7 = sb("B7", [128, 1024], F32, stA)
            ht = sb("ht", [128, 1024], F32, stA)
            X1 = sb("X1", [128, 512], BF16, stA)
            X2 = sb("X2", [128, 512], BF16, stA)
            X3 = sb("X3", [128, 512], BF16, stA)
            GA = sb("GA", [128, 1024], BF16, stA)
            Btm = sb("Btm", [128, 2, 128], BF16, stA)
            STf = sb("STf", [128, 8, 64], F32, stA)
            STb = sb("STb", [128, 8, 64], BF16, stA)
            ytok = [sb("ytok%d" % q, [128, 1024], BF16, stA) for q in range(2)]
            GA0 = sb("GA0", [128, 1024], BF16, stA)
            UTfT = sb("UTfT", [128, 512], F32, stA)
            tmpST = sb("tmpST", [128, 256], F32, stA)
            mixT = sb("mixT", [128, 8, 128], BF16, stA)
            lin = sb("lin", [128, 128], F32, stA)
            sg = sb("sg", [128, 128], BF16, stA)
            btm = sb("btm", [128, 512], BF16, stA)
            ktm = sb("ktm", [128, 512], BF16, stA)
            vtm = sb("vtm", [128, 512], BF16, stA)
            ABs = [sb("ABs%d" % c, [128, 2, 256], BF16, stA) for c in range(4)]
            AKs = [sb("AKs%d" % c, [128, 2, 256], BF16, stA) for c in range(4)]
            PPm = [[sb("PPm%d_%d" % (q, c), [128, 2, 2, 128], BF16, stA) for c in range(4)] for q in range(2)]
            Pm = [[View(PPm[q][c][:, 0, :, :], PPm[q][c].b) for c in range(4)] for q in range(2)]
            PTm = [[View(PPm[q][c][:, 1, :, :], PPm[q][c].b) for c in range(4)] for q in range(2)]
            UT = sb("UT", [128, 8, 64], BF16, stA)
            STr = sb("STr", [128, 4, 64], F32, stA)
            STrb = sb("STrb", [128, 4, 64], BF16, stA)
            maskSI = sb("maskSI", [128, 256], F32, stA)
            hw0 = sb("hw0", [128, 8], F32, stA)
            tiny = sb("tiny", [128, 1], F32, stA)
            memset("pool", tiny[:], 1e-24, (tiny,))
            bnsb = sb("bnsb", [128, 8], F32, stA)
            gn = sb("gn", [128, 40], F32, stA)
            hs = sb("hs", [128, 8], F32, stA)
            hsm = Buf("hsm")
            dlt = View(v3(B56[:, 0:1792], 14), B56.b)
            th = View(v3(BB[:, 0:1024], 8), BB.b)
            xax = View(v3(BB[:, 1024:1536], 4), BB.b)
            xs_tm = View(BB[:, 1536:2048], BB.b)
            Dm = View(v3(BB[:, 0:1024], 8), BB.b)
            lw = View(v3(BB[:, 0:512], 4), BB.b)
            av = View(v3(BB[:, 512:1024], 4), BB.b)
            cum = View(v3(BB[:, 1024:1536], 4), BB.b)
            cumex = View(v3(BB[:, 1536:2048], 4), BB.b)
            yr = View(v3(B56[:, 0:512], 8), B56.b)
            ysq = View(v3(B56[:, 512:1024], 8), B56.b)
            acc = View(v3(AC[:], 8), AC.b)
            rseg = View(v3(AC[:], 8), AC.b)
            thw = View(v3(AC[:, 0:512], 4), AC.b)
            tha = View(v3(AC[:, 512:1024], 4), AC.b)
            mn = View(AC[:], AC.b)
            dx = View(B2[:, 0:512], B2.b)
            ytmp = View(B2[:, 512:1024], B2.b)
            E2 = View(v3(B2[:, 0:512], 4), B2.b)
            tmpS = View(v3(tmpST[:], 4), tmpST.b)
            ysd = View(B3[:, 0:512], B3.b)
            scm = View(v3(B3[:, 512:768], 2), B3.b)
            kkv = View(v3(B3[:, 0:512], 4), B3.b)
            UTf = View(UTfT[:], UTfT.b)
            kk2 = View(v3(B3[:, 512:1024], 4), B3.b)
            E1x = sb("E1x", [128, 4, 129], F32, stA)
            memset("pool", E1x[:], 1.0, (E1x,))
            E1 = View(E1x[:, :, 1:129], E1x.b)
            E3 = View(E1x[:, :, 0:128], E1x.b)
            inv = View(v3(B4[:, 0:512], 4), B4.b)
            kkn = View(v3(B56[:, 0:512], 4), B56.b)
            kmod = View(v3(B56[:, 512:1024], 4), B56.b)
            t1 = View(v3(B56[:, 1024:1536], 4), B56.b)
            rk = View(v3(B56[:, 1024:1536], 4), B56.b)
            beta = View(v3(B56[:, 1536:2048], 4), B56.b)
            gsb = View(B7[:, 0:512], B7.b)
            bterm = View(v3(B7[:, 512:1024], 8), B7.b)
            Xb = View(X1[:], X1.b)
            btl = View(v3(X1[:], 4), X1.b)
            Xd = View(X2[:], X2.b)
            ktl = View(v3(X2[:], 4), X2.b)
            BCb = View(v3(X3[:], 4), X3.b)
            vb = View(v3(X3[:], 4), X3.b)
            Gm = View(v3(GA[:], 8), GA.b)
            GA1 = sb("GA1", [128, 1024], BF16, stA)
            arz = [View(GA0[:].rearrange("p (c a b) -> p c a b", c=4, a=2), GA0.b),
                   View(GA1[:].rearrange("p (c a b) -> p c a b", c=4, a=2), GA1.b)]
            GAz = [GA0, GA1]
            memset("pool", GA1[:], 0.0, (GA1,))
            memset("pool", GA0[:], 0.0, (GA0,))

            cp("dve", maskSI[:, 0:128], C("trius"), (cst,), (maskSI,))
            cp("dve", maskSI[:, 128:256], C("triu"), (cst,), (maskSI,))
            memset("dve", STf[:], 0.0, (STf,))
            memset("dve", STb[:], 0.0, (STb,))
            memset("dve", STr[:], 0.0, (STr,))
            memset("dve", STrb[:], 0.0, (STrb,))
            memset("pool", xT[1][:], 0.0, (xT[1],))
            memset("pool", xT[0][:], 0.0, (xT[0],))
            ts("dve", hw0[:, 0:4], par[:, PC["w0"]:PC["w0"] + 4], 0.5, None, ALU.mult, None, (par,), (hw0,))
            ts("dve", hw0[:, 4:8], par[:, PC["a0"]:PC["a0"] + 4], 0.5, None, ALU.mult, None, (par,), (hw0,))

            def SM(i, n=1):
                return sm[:, i:i + n]

            B0T = NT_ALL - NT_MAIN

            fstate = [0]

            def front_gen(i):
                p = i % 2
                fstate[0] = 0
                dma("sp", xt[p][:], D["x_in"][i * TT:(i + 1) * TT, :], "x0", (), (xt[p],))
                act(xb[:], xt[p][:], AF.Square, (xt[p],), (xb, smB[0]), accum=SM(0))
                ts("dve", SM(1), SM(0), 1.0 / 1024, 1e-6, ALU.mult, ALU.add, (smB[0],), (smB[1],))
                rsqrt_small(SM(2), SM(1), 1, (smB[1],), (smB[2],))
                act(xb[:], xt[p][:], AF.Copy, (xt[p], smB[2]), (xb,), scale=SM(2))
                yield
                for k in range(8):
                    trn(pT[:, k * 128:(k + 1) * 128], xb[:, k * 128:(k + 1) * 128], identb[:], (xb, identb), (pT,))
                cp("dve", xT[p][:, :, 3:HW], v3(pT[:], 8), (pT,), (xT[p],))
                cp("pool", xT[p][:, :, 0:3], xT[1 - p][:, :, 128:131], (xT[1 - p],), (xT[p],))
                if i == B0T:
                    ts("pool", xT[p][:, :, 0:3], xT[p][:, :, 0:3], flag[:], None, ALU.mult, None, (xT[p], flag), (xT[p],))
                yield
                hist = i < B0T - 1
                for bnk in range(8):
                    n = min(3, NFM - 3 * bnk)
                    if hist and bnk in (2, 3, 7):
                        fstate[0] = bnk + 1
                        continue
                    pb = PS()
                    for jj in range(n):
                        j = 3 * bnk + jj
                        for k in range(8):
                            mm(pb[:, jj * HW:(jj + 1) * HW], Win[:, k, j * 128:(j + 1) * 128], xT[p][:, k, :],
                               k == 0, k == 7, (Win, xT[p]), (pb,))
                    cp("act" if bnk % 2 == 0 else "dve", raw[:, 3 * bnk:3 * bnk + n, :], v3(pb[:, 0:n * HW], n),
                       (pb,), (rawB[bnk],))
                    fstate[0] = bnk + 1
                    yield
                full = i >= B0T - 1
                if full:
                    pz = PS()
                    for k in range(8):
                        mm(pz[:, 0:512], xT[p][:, k, 3:HW], Win[:, k, 2816:3328], k == 0, k == 7, (Win, xT[p]), (pz,))
                pd = PS()
                for k in range(8):
                    mm(pd[:, 0:8], xT[p][:, k, 3:HW], Win[:, k, 3328:3336], k == 0, k == 7, (Win, xT[p]), (pd,))
                if full:
                    act(gz[p][:], pz[:, 0:512], AF.Tanh, (pz,), (gz[p],))
                    stt("dve", gz[p][:], gz[p][:], 1.0, pz[:, 0:512], ALU.add, ALU.mult, (gz[p], pz), (gz[p],))
                tt("dve", dtr[p][:], pd[:, 0:8], R("dtb", 8), ALU.add, (pd, rows), (dtr[p],))

            estate = [0]

            def early_gen(i):
                rw = raw
                estate[0] = 0
                hist = i < B0T - 1
                for c in range(6 if hist else 8):
                    need_banks(c // 3 + 1)
                    rb = rawB[c // 3]
                    ts("dve", acc[:, c, :], rw[:, c, 3:HW], P("cw", c * 4 + 3), P("cb", c), ALU.mult, ALU.add,
                       (rb, par), (acc,))
                    for k in range(3):
                        stt("dve", acc[:, c, :], rw[:, c, k:k + 128], P("cw", c * 4 + k), acc[:, c, :], ALU.mult, ALU.add,
                            (rb, par, acc), (acc,))
                    estate[0] = c + 1
                    yield
                estate[0] = 8
                need_banks(8)
                c0, c1 = (4, 13) if hist else (0, 14)
                nc_ = c1 - c0
                RBr = rawB[4:7] if hist else rawB[2:8]
                tt("dve", dlt[:, c0:c1, :], rw[:, 8 + c0:8 + c1, 2:130], rw[:, 8 + c0:8 + c1, 3:HW], ALU.subtract, RBr, (dlt,))
                yield
                tt("dve", dlt[:, c0:c1, :], dlt[:, c0:c1, :],
                   par[:, PC["mu"] + c0:PC["mu"] + c1].unsqueeze(2).broadcast_to([128, nc_, 128]),
                   ALU.mult, (dlt, par), (dlt,))
                yield
                tt("dve", Pp[:, c0:c1, :], dlt[:, c0:c1, :], rw[:, 8 + c0:8 + c1, 3:HW], ALU.add, [dlt] + RBr, (Pp,))
                yield

            def ssd_gen(i):
                p = i % 2
                full = i >= B0T - 1
                need_front_done()
                need_conv()
                nch = 8 if full else 6
                act(th[:, 0:nch, :], acc[:, 0:nch, :], AF.Tanh, (acc,), (th,))
                stt("dve", xax[:], th[:, 0:4, :], 1.0, acc[:, 0:4, :], ALU.add, ALU.mult, (th, acc), (xax,))
                stt("dve", BCb[:, 0:nch - 4, :], th[:, 4:nch, :], 1.0, acc[:, 4:nch, :], ALU.add, ALU.mult, (th, acc), (BCb,))
                act(SM(8, 8), dtr[p][:], AF.Exp, (dtr[p],), (smB[8],))
                act(SM(8, 8), SM(8, 8), AF.Ln, (smB[8],), (smB[8],), bias=1.0)
                tt("dve", SM(16, 8), SM(8, 8), arow[:], ALU.mult, (smB[8], arow), (smB[9],))
                yield
                px = PS()
                for c in range(4):
                    mm(px[:, c * 128:(c + 1) * 128], xax[:, c, :], C("ident"), True, True, (xax, cst), (px,))
                cp("act", xs_tm[:], px[:], (px,), (xs_tm,))
                if full:
                    tt("dve", v3(Xb[:], 8), v3(xs_tm[:], 8), SM(8, 8).unsqueeze(2).broadcast_to([128, 8, 64]), ALU.mult,
                       (xs_tm, smB[8]), (Xb,))
                for g in range(2):
                    trn(pTR[:, g * 128:(g + 1) * 128], BCb[:, g, :], identb[:], (BCb, identb), (pTR,))
                cp("act", Btm[:], v3(pTR[:, 0:256], 2), (pTR,), (Btm,))
                yield
                psm = PS()
                mm(psm[:, 0:8], C("triu"), SM(16, 8), True, True, (cst, smB[9]), (psm,))
                mm(psm[:, 8:16], C("trils"), SM(16, 8), True, True, (cst, smB[9]), (psm,))
                mm(psm[:, 16:24], C("ones"), SM(16, 8), True, True, (cst, smB[9]), (psm,))
                act(SM(24, 24), psm[:, 0:24], AF.Exp, (psm,), (smB[10],))
                tt("dve", SM(48, 8), SM(8, 8), SM(32, 8), ALU.mult, (smB[8], smB[10]), (smB[11],))
                tt("dve", v3(Xd[:], 8), v3(xs_tm[:], 8), SM(48, 8).unsqueeze(2).broadcast_to([128, 8, 64]), ALU.mult,
                   (xs_tm, smB[11]), (Xd,))
                yield
                if full:
                    tt("dve", v3(dx[:], 8), v3(xs_tm[:], 8), R("dsk", 8).unsqueeze(2).broadcast_to([128, 8, 64]), ALU.mult,
                       (xs_tm, rows), (dx,))
                    psc = PS()
                    for g in range(2):
                        mm(psc[:, g * 128:(g + 1) * 128], BCb[:, g, :], BCb[:, 2 + g, :], True, True, (BCb,), (psc,))
                    tt("dve", scm[:], v3(psc[:, 0:256], 2), C("triu").unsqueeze(1).broadcast_to([128, 2, 128]), ALU.mult,
                       (psc, cst), (scm,))
                    tt("dve", rseg[:], SM(16, 8).unsqueeze(2).broadcast_to([128, 8, 128]),
                       C("triu").unsqueeze(1).broadcast_to([128, 8, 128]), ALU.mult, (smB[9], cst), (rseg,))
                    yield
                    for hh in range(2):
                        pg = PS()
                        mm(pg[:], C("trils"), rseg[:, 4 * hh:4 * hh + 4, :].rearrange("p a b -> p (a b)"), True, True,
                           (cst, rseg), (pg,))
                        act(Dm[:, 4 * hh:4 * hh + 4, :].rearrange("p a b -> p (a b)"), pg[:], AF.Exp, (pg,), (Dm,))
                    for g in range(2):
                        tt("dve", Gm[:, 4 * g:4 * g + 4, :], Dm[:, 4 * g:4 * g + 4, :],
                           scm[:, g, :].unsqueeze(1).broadcast_to([128, 4, 128]), ALU.mult, (Dm, scm), (Gm,))
                    yield
                    py1 = PS()
                    for h in range(8):
                        mm(py1[:, h * 64:(h + 1) * 64], Gm[:, h, :], Xb[:, h * 64:(h + 1) * 64], True, True, (Gm, Xb), (py1,))
                    py2 = PS()
                    for h in range(8):
                        mm(py2[:, h * 64:(h + 1) * 64], BCb[:, 2 + h // 4, :], STb[:, h, :], True, True, (BCb, STb), (py2,))
                    tt("dve", v3(ytmp[:], 8), v3(py2[:], 8), SM(24, 8).unsqueeze(2).broadcast_to([128, 8, 64]), ALU.mult,
                       (py2, smB[10]), (ytmp,))
                    tt("dve", ysd[:], ytmp[:], py1[:], ALU.add, (ytmp, py1), (ysd,))
                    tt("dve", ysd[:], ysd[:], dx[:], ALU.add, (ysd, dx), (ysd,))
                    yield
                pst = PS()
                for h in range(8):
                    mm(pst[:, h * 64:(h + 1) * 64], Btm[:, h // 4, :], Xd[:, h * 64:(h + 1) * 64], True, True, (Btm, Xd), (pst,))
                tt("dve", STf[:], STf[:], SM(40, 8).unsqueeze(2).broadcast_to([128, 8, 64]), ALU.mult, (STf, smB[10]), (STf,))
                tt("dve", STf[:].rearrange("p h d -> p (h d)"), STf[:].rearrange("p h d -> p (h d)"), pst[:], ALU.add,
                   (STf, pst), (STf,))
                if i == B0T - 1:
                    ts("dve", STf[:], STf[:], flag[:], None, ALU.mult, None, (STf, flag), (STf,))
                cp("act", STb[:], STf[:], (STf,), (STb,))
                yield
                if full:
                    tt("dve", ysd[:], ysd[:], gz[p][:], ALU.mult, (ysd, gz[p]), (ysd,))
                    for g in range(2):
                        act(ytmp[:, g * 256:(g + 1) * 256], ysd[:, g * 256:(g + 1) * 256], AF.Square, (ysd,), (ytmp, smB[12]),
                            accum=SM(56 + g))
                    ts("dve", SM(58, 2), SM(56, 2), 1.0 / 256, 1e-6, ALU.mult, ALU.add, (smB[12],), (smB[13],))
                    rsqrt_small(SM(60, 2), SM(58, 2), 2, (smB[13],), (smB[14],))
                    tt("dve", v3(ysd[:], 2), v3(ysd[:], 2), SM(60, 2).unsqueeze(2).broadcast_to([128, 2, 256]), ALU.mult,
                       (ysd, smB[14]), (ysd,))
                    tt("dve", ytok[p][:, 0:512], ysd[:], R("ssdg", 512), ALU.mult, (ysd, rows), (ytok[p],))


            def rwkv_out(i):
                p = i % 2
                full = i >= B0T - 1
                act(lin[0:64, :], Pp[0:64, 12, :], AF.Tanh, (Pp,), (lin,))
                tt("dve", kkv[:], Pp[:, 4:8, :], par[:, PC["kk"]:PC["kk"] + 4].unsqueeze(2).broadcast_to([128, 4, 128]),
                   ALU.mult, (Pp, par), (kkv,))
                cp("act", vb[:], Pp[:, 8:12, :], (Pp,), (vb,))
                OP()
                plw = PS()
                pla = PS()
                for c in range(4):
                    mm(plw[:, c * 128:(c + 1) * 128], Wlora[0:64, c * 128:(c + 1) * 128], lin[0:64, :], True, True,
                       (Wlora, lin), (plw,))
                    mm(pla[:, c * 128:(c + 1) * 128], Wlora[64:128, c * 128:(c + 1) * 128], Pp[64:128, 12, :], True, True,
                       (Wlora, Pp), (pla,))
                act(kk2[:], kkv[:], AF.Square, (kkv,), (kk2,))
                pkn = PS()
                mm(pkn[:], C("onesblk"), kk2[:].rearrange("p a b -> p (a b)"), True, True, (cst, kk2), (pkn,))
                for c in range(4):
                    act(thw[:, c, :], plw[:, c * 128:(c + 1) * 128], AF.Tanh, (plw, hw0), (thw,), bias=hw0[:, c:c + 1], scale=0.5)
                    act(tha[:, c, :], pla[:, c * 128:(c + 1) * 128], AF.Tanh, (pla, hw0), (tha,), bias=hw0[:, 4 + c:5 + c], scale=0.5)
                act(inv[:].rearrange("p a b -> p (a b)"), pkn[:], AF.Ln, (pkn,), (inv,), bias=tiny[:])
                act(inv[:], inv[:], AF.Exp, (inv,), (inv,), scale=-0.5)
                ts("dve", lw[:], thw[:], -0.30326533, -0.30326533, ALU.mult, ALU.add, (thw,), (lw,))
                ts("dve", av[:], tha[:], 0.5, 0.5, ALU.mult, ALU.add, (tha,), (av,))
                tt("dve", kkn[:], kkv[:], inv[:], ALU.mult, (kkv, inv), (kkn,))
                FP()
                OP()
                for c in range(4):
                    S.op("dve", lambda e, c=c: e.tensor_tensor_scan(out=cum[:, c, :], data0=lw[:, c, :], data1=zeros[:],
                                                                      initial=0.0, op0=ALU.add, op1=ALU.add),
                         rd(lw, zeros), rd(cum))
                act(E1[:], cum[:], AF.Exp, (cum,), (E1,))
                act(E2[:], cum[:], AF.Exp, (cum,), (E2,), scale=-1.0)
                FP()
                OP()
                stt("dve", t1[:], av[:], -1.0, par[:, PC["ka"]:PC["ka"] + 4].unsqueeze(2).broadcast_to([128, 4, 128]),
                    ALU.add, ALU.mult, (av, par), (t1,))
                stt("dve", kmod[:], t1[:], 1.0, Pp[:, 4:8, :], ALU.add, ALU.mult, (t1, Pp), (kmod,))
                FP()
                tt("dve", beta[:], kkn[:], av[:], ALU.mult, (kkn, av), (beta,))
                for hp in range(2):
                    sl = slice(hp * 64, (hp + 1) * 64)
                    stt("dve", arz[hp][sl, :, 0, :], kkn[sl, :, :], -1.0, E3[sl, :, :], ALU.mult, ALU.mult, (kkn, E3), (arz[hp],))
                tt("dve", btl[:], beta[:], E2[:], ALU.mult, (beta, E2), (btl,))
                tt("dve", ktl[:], kmod[:], E2[:], ALU.mult, (kmod, E2), (ktl,))
                if full:
                    for hp in range(2):
                        sl = slice(hp * 64, (hp + 1) * 64)
                        tt("dve", arz[hp][sl, :, 1, :], Pp[sl, 0:4, :], E1[sl, :, :], ALU.mult, (Pp, E1), (arz[hp],))
                if full:
                    tt("dve", rk[:], Pp[:, 0:4, :], kmod[:], ALU.mult, (Pp, kmod), (rk,))
                if full:
                    pbn = PS()
                    for c in range(4):
                        mm(pbn[:, 0:8], rk[:, c, :], sel[:, c, :], c == 0, c == 3, (rk, sel), (pbn,))
                    cp("act", bnsb[:], pbn[:, 0:8], (pbn,), (bnsb,))
                for c in range(4):
                    trn(pTR[:, c * 128:(c + 1) * 128], vb[:, c, :], identb[:], (vb, identb), (pTR,))
                cp("act", vtm[:], pTR[:, 0:512], (pTR,), (vtm,))
                FP()
                drain(ogen)
                if full:
                    act(gsb[:, 0:128], Pp[:, 13, :], AF.Tanh, (Pp,), (gsb,), scale=0.5)
                    ts("dve", sg[:], gsb[:, 0:128], 1.0, None, ALU.add, None, (gsb,), (sg,))
                    pgt = PS()
                    mm(pgt[:], sg[:], G2h[:], True, True, (sg, G2h), (pgt,))
                    cp("act", gsb[:], pgt[:], (pgt,), (gsb,))
                if full:
                    tt("dve", bterm[:], v3(vtm[:], 8), bnsb[:].unsqueeze(2).broadcast_to([128, 8, 64]), ALU.mult,
                       (vtm, bnsb), (bterm,))
                for c in range(4):
                    pab = PS()
                    pak = PS()
                    pnt = PS()
                    ncl = 256 if full else 128
                    for hp in range(2):
                        arv = GAz[hp][:, c * 256:c * 256 + ncl]
                        mm(pab[:, hp * 256:hp * 256 + ncl], btl[:, c, :], arv, True, True, (btl, arz[hp]), (pab,))
                        mm(pak[:, hp * 256:hp * 256 + ncl], ktl[:, c, :], arv, True, True, (ktl, arz[hp]), (pak,))
                        mm(pnt[:, hp * 128:(hp + 1) * 128], arz[hp][:, c, 0, :], btl[:, c, :], True, True, (arz[hp], btl), (pnt,))
                    tt("dve", ABs[c][:, :, 0:ncl], v3(pab[:], 2)[:, :, 0:ncl],
                       maskSI[:, 0:ncl].unsqueeze(1).broadcast_to([128, 2, ncl]), ALU.mult, (pab, maskSI), (ABs[c],))
                    tt("dve", AKs[c][:, :, 0:ncl], v3(pak[:], 2)[:, :, 0:ncl],
                       maskSI[:, 0:ncl].unsqueeze(1).broadcast_to([128, 2, ncl]), ALU.mult, (pak, maskSI), (AKs[c],))
                    tt("dve", PTm[0][c][:], v3(pnt[:, 0:256], 2), C("trils").unsqueeze(1).broadcast_to([128, 2, 128]),
                       ALU.mult, (pnt, cst), (PTm[0][c],))
                    FP()
                    EP()
                for c in range(4):
                    trn(pTR[:, c * 128:(c + 1) * 128], btl[:, c, :], identb[:], (btl, identb), (pTR,))
                    trn(pTR[:, 512 + c * 128:512 + (c + 1) * 128], ktl[:, c, :], identb[:], (ktl, identb), (pTR,))
                cp("act", btm[:], pTR[:, 0:512], (pTR,), (btm,))
                cp("dve", ktm[:], pTR[:, 512:1024], (pTR,), (ktm,))
                pu = pool_banks[0]
                for h in range(8):
                    c, hp = h // 2, h % 2
                    mm(pu[:, h * 64:(h + 1) * 64], arz[hp][:, c, 0, :], STrb[:, c, :], True, False, (arz[hp], STrb), (pu,))
                    mm(pu[:, h * 64:(h + 1) * 64], AKs[c][:, hp, 0:128], vtm[:, h * 64:(h + 1) * 64], False, True,
                       (AKs[c], vtm), (pu,))
                cp("act", UTf[:], pu[:], (pu,), (UTf,))
                cp("dve", UT[:].rearrange("p h d -> p (h d)"), UTf[:], (UTf,), (UT,))
                FP()
                for lvl in range(7):
                    q = lvl % 2
                    pd = PS()
                    for h in range(8):
                        c, hp = h // 2, h % 2
                        Pl, PlB = (ABs[c][:, hp, 0:128], ABs[c]) if lvl == 0 else (Pm[q][c][:, hp, :], Pm[q][c])
                        mm(pd[:, h * 64:(h + 1) * 64], Pl, UT[:, h, :], True, True, (PlB, UT), (pd,))
                    tt("dve", UTf[:], UTf[:], pd[:], ALU.add, (UTf, pd), (UTf,))
                    cp("act", UT[:].rearrange("p h d -> p (h d)"), UTf[:], (UTf,), (UT,))
                    EP()
                    if lvl < 2:
                        EP()
                    else:
                        SP()
                    if lvl < 6:
                        for c in range(4):
                            pp = PS()
                            for hp in range(2):
                                Pl, PlB = (ABs[c][:, hp, 0:128], ABs[c]) if lvl == 0 else (Pm[q][c][:, hp, :], Pm[q][c])
                                mm(pp[:, hp * 128:(hp + 1) * 128], PTm[q][c][:, hp, :], Pl, True, True,
                                   (PTm[q][c], PlB), (pp,))
                                mm(pp[:, 256 + hp * 128:256 + (hp + 1) * 128], Pl, PTm[q][c][:, hp, :], True, True,
                                   (PTm[q][c], PlB), (pp,))
                            cp("dve" if c % 2 == 0 else "act", PPm[1 - q][c][:].rearrange("p a b c -> p (a b c)"), pp[:],
                               (pp,), (PPm[1 - q][c],))
                            if c == 1 and lvl >= 2:
                                SP()
                psr = PS()
                for c in range(4):
                    o = psr[:, c * 128:(c + 1) * 128]
                    mm(o, btm[:, c * 128:(c + 1) * 128], UT[:, 2 * c:2 * c + 2, :].rearrange("p a b -> p (a b)"), True, False,
                       (btm, UT), (psr,))
                    mm(o, ktm[:, c * 128:(c + 1) * 128], vtm[:, c * 128:(c + 1) * 128], False, True, (ktm, vtm), (psr,))
                if full:
                    pyr = PS()
                    for h in range(8):
                        c, hp = h // 2, h % 2
                        o = pyr[:, h * 64:(h + 1) * 64]
                        mm(o, arz[hp][:, c, 1, :], STrb[:, c, :], True, False, (arz[hp], STrb), (pyr,))
                        mm(o, ABs[c][:, hp, 128:256], UT[:, h, :], False, False, (ABs[c], UT), (pyr,))
                        mm(o, AKs[c][:, hp, 128:256], vtm[:, h * 64:(h + 1) * 64], False, True, (AKs[c], vtm), (pyr,))
                psr3 = v3(psr[:], 4)
                tt("dve", tmpS[0:64, :, :], STr[0:64, :, :], psr3[0:64, :, 0:64], ALU.add, (STr, psr), (tmpS,))
                tt("dve", tmpS[64:128, :, :], STr[64:128, :, :], psr3[64:128, :, 64:128], ALU.add, (STr, psr), (tmpS,))
                tt("dve", STr[:], tmpS[:], E1[:, :, 127:128].broadcast_to([128, 4, 64]), ALU.mult, (tmpS, E1), (STr,))
                if i == B0T - 1:
                    ts("dve", STr[:], STr[:], flag[:], None, ALU.mult, None, (STr, flag), (STr,))
                cp("act", STrb[:], STr[:], (STr,), (STrb,))
                SP()
                if full:
                    cp("act", yr[:].rearrange("p h d -> p (h d)"), pyr[:], (pyr,), (yr,))
                    ogen[0] = out_gen(i)

            def out_gen(i):
                p = i % 2
                slot = i - (B0T - 1)
                act(ysq[:], yr[:], AF.Square, (yr,), (ysq,))
                S.op("dve", lambda e: e.tensor_reduce(out=gn[:, 0:8], in_=yr[:], axis=AX.X, op=ALU.add), rd(yr), rd(gn))
                S.op("dve", lambda e: e.tensor_reduce(out=gn[:, 8:16], in_=ysq[:], axis=AX.X, op=ALU.add), rd(ysq), rd(gn))
                ts("dve", gn[:, 0:16], gn[:, 0:16], 1.0 / 64, None, ALU.mult, None, (gn,), (gn,))
                tt("dve", gn[:, 16:24], gn[:, 0:8], gn[:, 0:8], ALU.mult, (gn,), (gn,))
                tt("dve", gn[:, 24:32], gn[:, 8:16], gn[:, 16:24], ALU.subtract, (gn,), (gn,))
                ts("dve", gn[:, 24:32], gn[:, 24:32], 64e-5, None, ALU.add, None, (gn,), (gn,))
                rsqrt_small(gn[:, 32:40], gn[:, 24:32], 8, (gn,), (gn,))
                tt("dve", yr[:], yr[:], gn[:, 0:8].unsqueeze(2).broadcast_to([128, 8, 64]), ALU.subtract, (yr, gn), (yr,))
                tt("dve", yr[:], yr[:], gn[:, 32:40].unsqueeze(2).broadcast_to([128, 8, 64]), ALU.mult, (yr, gn), (yr,))
                yr2 = yr[:].rearrange("p h d -> p (h d)")
                tt("dve", yr2, yr2, R("lnw", 512), ALU.mult, (yr, rows), (yr,))
                tt("dve", yr2, yr2, R("lnb", 512), ALU.add, (yr, rows), (yr,))
                tt("dve", yr[:], yr[:], bterm[:], ALU.add, (yr, bterm), (yr,))
                tt("dve", ytok[p][:, 512:1024], yr2, gsb[:], ALU.mult, (yr, gsb), (ytok[p],))
                yield
                for c in range(8):
                    trn(pTR[:, c * 128:(c + 1) * 128], ytok[p][:, c * 128:(c + 1) * 128], identb[:], (ytok[p], identb), (pTR,))
                cp("act", mixT[:].rearrange("p a b -> p (a b)"), pTR[:], (pTR,), (mixT,))
                yield
                pm = [PS(), PS()]
                for hf in range(2):
                    for c in range(8):
                        mm(pm[hf][:], mixT[:, c, :], Wout[:, c, hf * 512:(hf + 1) * 512], c == 0, c == 7, (mixT, Wout), (pm[hf],))
                dma("sp", ht[:], D["x_in"][i * TT:(i + 1) * TT, :], "xr", (), (ht,))
                for hf in range(2):
                    act(B7[:, hf * 512:(hf + 1) * 512], pm[hf][:], AF.Square, (pm[hf],), (B7, hsm), accum=hs[:, hf:hf + 1])
                tt("dve", hs[:, 2:3], hs[:, 0:1], hs[:, 1:2], ALU.add, (hsm,), (hsm,))
                ts("dve", hs[:, 2:3], hs[:, 2:3], 1.0 / 1024, 1e-6, ALU.mult, ALU.add, (hsm,), (hsm,))
                rsqrt_small(hs[:, 3:4], hs[:, 2:3], 1, (hsm,), (hsm,))
                for hf in range(2):
                    stt("dve", B7[:, hf * 512:(hf + 1) * 512], pm[hf][:], hs[:, 3:4], R("pmn", 1024)[:, hf * 512:(hf + 1) * 512],
                        ALU.mult, ALU.mult, (pm[hf], hsm, rows), (B7,))
                tt("dve", ht[:], ht[:], B7[:], ALU.add, (ht, B7), (ht,))
                dma("sp", hscr[slot * TT:(slot + 1) * TT, :], ht[:], "hst", (ht,), (hscrB,))
                yield

            ogen = [None]

            def OP():
                if ogen[0] is not None:
                    next(ogen[0], None)

            fgen = [None]
            egen = [None]
            sgen = [None]

            def FP():
                if fgen[0] is not None:
                    next(fgen[0], None)

            def need_banks(n):
                while fstate[0] < n:
                    if fgen[0] is None or next(fgen[0], "done") == "done":
                        break

            def need_front_done():
                if fgen[0] is not None:
                    for _ in fgen[0]:
                        pass

            def EP():
                if egen[0] is not None:
                    next(egen[0], None)

            def need_conv():
                while estate[0] < 8:
                    if egen[0] is None or next(egen[0], "done") == "done":
                        break

            def SP():
                if sgen[0] is not None:
                    next(sgen[0], None)

            def drain(g):
                if g[0] is not None:
                    for _ in g[0]:
                        pass
                g[0] = None

            fgen[0] = front_gen(0)
            drain(fgen)
            fstate[0] = 8
            egen[0] = early_gen(0)
            drain(egen)
            estate[0] = 8
            sgen[0] = ssd_gen(0)
            drain(sgen)
            CK(2)
            for i in range(NT_ALL):
                if i + 1 < NT_ALL:
                    fgen[0] = front_gen(i + 1)
                    egen[0] = early_gen(i + 1)
                    sgen[0] = ssd_gen(i + 1)
                FP()
                FP()
                rwkv_out(i)
                drain(fgen)
                drain(egen)
                drain(sgen)
                CK(5)
            drain(ogen)
            S.barrier()
            CK(6)

        rot_set[0] = [4, 5]
        TBK = 256
        NTB = NT_MAIN * TT // TBK
        with ExitStack() as stB:
            Wup = sb("Wup", [128, 8, 5632], BF16, stB)
            Wdown = sb("Wdown", [128, 22, 1024], BF16, stB)
            pfn = sb("pfn", [128, 1024], F32, stB)
            dma("sp", pfn[:], D["rows"][:, RC["pfn"]:RC["pfn"] + 1024], "c6", (), (pfn,))
            WupB = [[Buf("wup%d_%d" % (k, hf)) for hf in range(2)] for k in range(8)]
            WdB = [Buf("wd%d" % f3) for f3 in range(8)]
            with ExitStack() as stS:
                stage = [sb("stageB%d" % q, [128, 3072], F32, stS) for q in range(3)]
                sq = [0]

                def load_rows2(src_ap, ncols, fn):
                    si = sq[0] % 3
                    stg = stage[si]
                    sq[0] += 1
                    dma("sp", stg[:, 0:ncols], src_ap, "wsu%d" % si, (), (stg,))
                    fn(stg, sq[0])

                def cast_up(stg, n, k, hf):
                    o = Wup[:, k, hf * 2816:(hf + 1) * 2816]
                    e = n % 2
                    if e == 0:
                        ts("dve", o, stg[:, 0:2816], P("g2n", k), None, ALU.mult, None, (stg, par), (WupB[k][hf],))
                    else:
                        act(o, stg[:, 0:2816], AF.Copy, (stg, par), (WupB[k][hf],), scale=P("g2n", k))
                for k in range(8):
                    for hf in range(2):
                        load_rows2(D["wup"][:, k * 5632 + hf * 2816:k * 5632 + (hf + 1) * 2816], 2816,
                                   lambda stg, n, k=k, hf=hf: cast_up(stg, n, k, hf))
                for f3 in range(8):
                    n3 = min(3, 22 - 3 * f3)
                    load_rows2(D["wdown"][:, 3 * f3 * 1024:(3 * f3 + n3) * 1024], n3 * 1024,
                               lambda stg, n, f3=f3, n3=n3: cp(("act", "dve")[n % 2], Wdown[:, 3 * f3:3 * f3 + n3, :],
                                                               v3(stg[:, 0:n3 * 1024], n3), (stg,), (WdB[f3],)))
                S.barrier()
            CK(7)
            hld = [sb("hld%d" % q, [128, 2, 1024], F32, stB) for q in range(2)]
            hbB = sb("hbB", [128, 1024], BF16, stB)
            hnB = [sb("hnB%d" % q, [128, 8, 2 + TBK], BF16, stB) for q in range(2)]
            U = [sb("U%d" % q, [128, 2, 2, 2 + TBK], F32, stB) for q in range(2)]
            cacc = [[sb("cacc%d_%d" % (q, gv), [128, 2, TBK], F32, stB) for gv in range(2)] for q in range(2)]
            sgt = [sb("sgt%d" % q, [128, 2, TBK], F32, stB) for q in range(2)]
            actT = [sb("actT%d" % q, [128, 2, TBK], BF16, stB) for q in range(2)]
            carry = sb("carry", [128, 22, 2, 2], F32, stB)
            carryB = [Buf("carry%d" % f) for f in range(22)]
            fo = [sb("fo%d" % q, [128, 1024], F32, stB) for q in range(2)]
            fs = sb("fs", [128, 16], F32, stB)
            junkF = sb("junkF", [128, 1024], BF16, stB)
            fsm = Buf("fsm")
            outB = Buf("out")
            last = []
            memset("pool", carry[:], 0.0, carryB)

            def prep_sub(row0, q, sub, dst, dcol):
                dma("sp", hld[q][:, sub, :], hscr[row0:row0 + TT, :], "hld%d" % q, (hscrB,), (hld[q],))
                act(hbB[:], hld[q][:, sub, :], AF.Square, (hld[q],), (hbB, fsm), accum=fs[:, 8:9])
                ts("dve", fs[:, 9:10], fs[:, 8:9], 1.0 / 1024, 1e-6, ALU.mult, ALU.add, (fsm,), (fsm,))
                rsqrt_small(fs[:, 10:11], fs[:, 9:10], 1, (fsm,), (fsm,))
                act(hbB[:], hld[q][:, sub, :], AF.Copy, (hld[q], fsm), (hbB,), scale=fs[:, 10:11])
                for k in range(8):
                    trn(pT[:, k * 128:(k + 1) * 128], hbB[:, k * 128:(k + 1) * 128], identb[:], (hbB, identb), (pT,))
                cp("dve", dst[:, :, dcol:dcol + TT], v3(pT[:], 8), (pT,), (dst,))

            prep_sub(0, 1, 0, hnB[1], 2)
            ts("pool", hnB[1][:, :, 128:130], hnB[1][:, :, 128:130], flag[:], None, ALU.mult, None, (hnB[1], flag), (hnB[1],))
            for f in range(22):
                pc = PS()
                for gv in range(2):
                    col = gv * 2816 + f * 128
                    for k in range(8):
                        mm(pc[:, gv * 2:gv * 2 + 2], Wup[:, k, col:col + 128], hnB[1][:, k, 128:130], k == 0, k == 7,
                           (WupB[k][gv], hnB[1]), (pc,))
                cp("act", carry[:, f, :, :].rearrange("p a b -> p (a b)"), pc[:, 0:4], (pc,), (carryB[f // 2],))

            def prep_gen(tb):
                for sub in range(2):
                    prep_sub(TT + tb * TBK + sub * TT, tb % 2, sub, hnB[tb % 2], 2 + sub * TT)
                    yield

            for _ in prep_gen(0):
                pass
            for tb in range(NTB):
                q2 = tb % 2
                pgen = prep_gen(tb + 1) if tb + 1 < NTB else None
                pf = [[pool_banks[0], pool_banks[1]], [pool_banks[2], pool_banks[3]]]
                pub = [pool_banks[4], pool_banks[5]]

                def A_pe(fp):
                    for j in range(2):
                        f = 2 * fp + j
                        for gv in range(2):
                            col = gv * 2816 + f * 128
                            for k in range(8):
                                mm(pub[j][:, gv * TBK:(gv + 1) * TBK], Wup[:, k, col:col + 128], hnB[q2][:, k, 2:2 + TBK], k == 0, k == 7,
                                   (WupB[k][gv], hnB[q2]), (pub[j],))

                def A_rest(fp):
                    q = fp % 2
                    Uq = U[q]
                    for j in range(2):
                        cp("act", Uq[:, j, :, 2:2 + TBK], v3(pub[j][:], 2), (pub[j],), (Uq,))
                    f0 = 2 * fp
                    cp("dve", Uq[:, :, :, 0:2], carry[:, f0:f0 + 2, :, :], (carryB[fp],), (Uq,))
                    cp("dve", carry[:, f0:f0 + 2, :, :], Uq[:, :, :, TBK:TBK + 2], (Uq,), (carryB[fp],))
                    ca = cacc[q]
                    for j in range(2):
                        for gv in range(2):
                            ch = gv * 22 + f0 + j
                            act(ca[gv][:, j, :], Uq[:, j, gv, 2:2 + TBK], AF.Identity, (Uq, par), (ca[gv],), bias=P("fcb", ch),
                                scale=P("fcw", ch * 3 + 2))

                def B_nonpe(fp):
                    q = fp % 2
                    Uq = U[q]
                    ca = cacc[q]
                    f0 = 2 * fp
                    for k in range(2):
                        for j in range(2):
                            for gv in range(2):
                                ch = gv * 22 + f0 + j
                                stt("dve", ca[gv][:, j, :], Uq[:, j, gv, k:k + TBK], P("fcw", ch * 3 + k), ca[gv][:, j, :], ALU.mult, ALU.add,
                                    (Uq, par, ca[gv]), (ca[gv],))
                    act(sgt[q][:], ca[0][:], AF.Silu, (ca[0],), (sgt[q],))
                    tt("dve", actT[q][:], sgt[q][:], ca[1][:], ALU.mult, (sgt[q], ca[1]), (actT[q],))

                def B_pe(fp):
                    q = fp % 2
                    a3 = actT[q]
                    for j in range(2):
                        f = 2 * fp + j
                        for sub in range(2):
                            for hf in range(2):
                                mm(pf[sub][hf][:], a3[:, j, sub * TT:(sub + 1) * TT], Wdown[:, f, hf * 512:(hf + 1) * 512], f == 0, f == 21,
                                   (a3, WdB[f // 3]), (pf[sub][hf],))

                A_pe(0)
                A_rest(0)
                for fp in range(11):
                    if fp + 1 < 11:
                        A_pe(fp + 1)
                    B_nonpe(fp)
                    B_pe(fp)
                    if fp + 1 < 11:
                        A_rest(fp + 1)
                    if pgen is not None and fp in (3, 7):
                        next(pgen, None)
                for sub in range(2):
                    for hf in range(2):
                        cp("act", fo[sub][:, hf * 512:(hf + 1) * 512], pf[sub][hf][:], (pf[sub][hf],), (fo[sub],))
                if pgen is not None:
                    for _ in pgen:
                        pass
                for sub in range(2):
                    fq = fo[sub]
                    o4 = 4 * sub
                    act(junkF[:], fq[:], AF.Square, (fq,), (junkF, fsm), accum=fs[:, o4:o4 + 1])
                    ts("dve", fs[:, o4 + 2:o4 + 3], fs[:, o4:o4 + 1], 1.0 / 1024, 1e-6, ALU.mult, ALU.add, (fsm,), (fsm,))
                    rsqrt_small(fs[:, o4 + 3:o4 + 4], fs[:, o4 + 2:o4 + 3], 1, (fsm,), (fsm,))
                    stt("dve", fq[:], fq[:], fs[:, o4 + 3:o4 + 4], pfn[:], ALU.mult, ALU.mult, (fq, fsm, pfn), (fq,))
                    tt("dve", fq[:], fq[:], hld[q2][:, sub, :], ALU.add, (fq, hld[q2]), (fq,))
                    r0 = tb * TBK + sub * TT
                    last.append(dma("sp", out[r0:r0 + TT, :], fq[:], "ost%d" % sub, (fq,), (outB,)))
            S.finish(last[-2:])
    except _Stop:
        pass
    return nc


_NC_CACHE = {}


def kernel(**inputs):
    inputs = {k: np.asarray(v) for k, v in inputs.items()}
    x = inputs["x"].astype(np.float32, copy=False)
    shared = _host_prep(inputs)
    in_maps = []
    for c in range(8):
        b, s = c // 2, c % 2
        m = dict(shared)
        if s == 0:
            m["x_in"] = np.ascontiguousarray(np.concatenate([x[b, 0:2048], x[b, 0:2048]], axis=0))
        else:
            m["x_in"] = np.ascontiguousarray(x[b])
        m["flag"] = np.full((128, 1), float(s), np.float32)
        in_maps.append(m)
    if "nc" not in _NC_CACHE:
        _NC_CACHE["nc"] = build_nc()
    res = run_bass_kernel_spmd(_NC_CACHE["nc"], in_maps, core_ids=list(range(8)))
    outp = np.empty((4, 4096, 1024), np.float32)
    for c in range(8):
        b, s = c // 2, c % 2
        outp[b, s * 2048:(s + 1) * 2048] = res.results[c]["out"]
    return outp
```
